# Optimizing a Trainium2 kernel written in Bass

```python
import jax
import jax.numpy as jnp
from jax import lax
import numpy as np

D_MODEL = 1024
BATCH = 8
SEQ = 4096
DEPTH = 4

GRID_W = 64
CTX_LEN = 256
N_MIXERS = 3
N_LAYERS_SWA = (DEPTH + 2) // 3
N_LAYERS_MLA = (DEPTH + 1) // 3
N_LAYERS_LRU = DEPTH // 3
N_MOD = 6

HEAD_DIM = 64
SWA_HEADS = D_MODEL // HEAD_DIM
SWA_KV_HEADS = SWA_HEADS // 4
SWA_GROUP = SWA_HEADS // SWA_KV_HEADS
WINDOW = 128
BLOCK = 128

MLA_HEADS = D_MODEL // 64
MLA_NOPE = 64
MLA_ROPE = 32
MLA_QK = MLA_NOPE + MLA_ROPE
MLA_V = 64
Q_LORA = 384
KV_LORA = 256

LRU_WIDTH = D_MODEL
LRU_BLOCKS = 4
LRU_BW = LRU_WIDTH // LRU_BLOCKS
LRU_CONV = 4
LRU_C = 8.0

D_FF = 2816
FFN_CONV = 3

ROPE_BASE = 10000.0
EPS = 1e-6
NEG_INF = -1e30

kernel_name = "hybrid_dit_swa_mla_rglru"


def rmsnorm(x, gain):
    xf = x.astype(jnp.float32)
    y = xf * lax.rsqrt(jnp.mean(xf * xf, axis=-1, keepdims=True) + EPS)
    return (y * gain.astype(jnp.float32)).astype(x.dtype)


def modulate(h, shift, scale):
    return h * (1 + scale) + shift


def depthwise_conv(x, w, b, pad_left):
    k = w.shape[0]
    y = lax.conv_general_dilated(
        x, w[:, None, :].astype(x.dtype), (1,), [(pad_left, k - 1 - pad_left)],
        dimension_numbers=("NWC", "WIO", "NWC"), feature_group_count=x.shape[-1])
    return y + b.astype(x.dtype)


def axial_rope_tables(n_tokens, rot_dim):
    rows = n_tokens // GRID_W
    row = jnp.repeat(jnp.arange(rows, dtype=jnp.float32), GRID_W)
    col = jnp.tile(jnp.arange(GRID_W, dtype=jnp.float32), rows)
    n_freq = rot_dim // 4
    inv = ROPE_BASE ** (-jnp.arange(n_freq, dtype=jnp.float32) / n_freq)
    ang_r = row[:, None] * inv
    ang_c = col[:, None] * inv
    return (jnp.cos(ang_r), jnp.sin(ang_r), jnp.cos(ang_c), jnp.sin(ang_c))


def _rotate_half(x, cos, sin):
    n = cos.shape[-1]
    x1, x2 = x[..., :n], x[..., n:]
    cos, sin = cos[:, None, :], sin[:, None, :]
    return jnp.concatenate([x1 * cos - x2 * sin, x2 * cos + x1 * sin], axis=-1)


def apply_axial_rope(x, tables):
    cos_r, sin_r, cos_c, sin_c = tables
    half = x.shape[-1] // 2
    xf = x.astype(jnp.float32)
    out = jnp.concatenate([_rotate_half(xf[..., :half], cos_r, sin_r),
                           _rotate_half(xf[..., half:], cos_c, sin_c)], axis=-1)
    return out.astype(x.dtype)


def attend(q, k, v, mask=None, sink=None):
    s = jnp.einsum("bqhgd,bkhd->bhgqk", q, k, preferred_element_type=jnp.float32) * (q.shape[-1] ** -0.5)
    if mask is not None:
        s = jnp.where(mask, s, NEG_INF)
    m = jnp.max(s, axis=-1)
    if sink is not None:
        sk = jnp.broadcast_to(sink.astype(jnp.float32)[None, :, :, None], m.shape)
        m = jnp.maximum(m, sk)
    p = jnp.exp(s - m[..., None])
    denom = jnp.sum(p, axis=-1)
    if sink is not None:
        denom = denom + jnp.exp(sk - m)
    o = jnp.einsum("bhgqk,bkhd->bqhgd", p, v.astype(jnp.float32))
    o = o / jnp.transpose(denom, (0, 3, 1, 2))[..., None]
    return o.astype(v.dtype)


def window_gqa_mixer(h_lat, h_ctx, w_qkv, q_gain, k_gain, sink, w_o, rope, with_ctx_out):
    bsz, seq, _ = h_lat.shape
    n_blocks = seq // BLOCK
    sink_hg = sink.reshape(SWA_KV_HEADS, SWA_GROUP)

    def project(h):
        n = h.shape[1]
        q, k, v = jnp.split(h @ w_qkv, [SWA_HEADS * HEAD_DIM, (SWA_HEADS + SWA_KV_HEADS) * HEAD_DIM], axis=-1)
        q = rmsnorm(q.reshape(bsz, n, SWA_HEADS, HEAD_DIM), q_gain)
        k = rmsnorm(k.reshape(bsz, n, SWA_KV_HEADS, HEAD_DIM), k_gain)
        return q, k, v.reshape(bsz, n, SWA_KV_HEADS, HEAD_DIM)

    qc, kc, vc = project(h_ctx)
    q, k, v = project(h_lat)
    q = apply_axial_rope(q, rope).reshape(bsz, seq, SWA_KV_HEADS, SWA_GROUP, HEAD_DIM)
    k = apply_axial_rope(k, rope)
    pad = ((0, 0), (WINDOW, WINDOW), (0, 0), (0, 0))
    kp, vp = jnp.pad(k, pad), jnp.pad(v, pad)
    span = BLOCK + 2 * WINDOW
    q_idx = jnp.arange(BLOCK)[:, None]
    k_idx = jnp.arange(span)[None, :]
    ctx_mask = jnp.ones((BLOCK, kc.shape[1]), dtype=bool)

    def block(b):
        start = b * BLOCK
        qb = lax.dynamic_slice_in_dim(q, start, BLOCK, axis=1)
        kb = lax.dynamic_slice_in_dim(kp, start, span, axis=1)
        vb = lax.dynamic_slice_in_dim(vp, start, span, axis=1)
        key_pos = start - WINDOW + k_idx
        band = (jnp.abs(q_idx + WINDOW - k_idx) <= WINDOW) & (key_pos >= 0) & (key_pos < seq)
        mask = jnp.concatenate([band, ctx_mask], axis=1)
        return attend(qb, jnp.concatenate([kb, kc], axis=1), jnp.concatenate([vb, vc], axis=1), mask, sink_hg)

    o = lax.map(block, jnp.arange(n_blocks))
    o = jnp.transpose(o, (1, 0, 2, 3, 4, 5)).reshape(bsz, seq, SWA_HEADS * HEAD_DIM)
    y_lat = o @ w_o
    y_ctx = None
    if with_ctx_out:
        n_ctx = h_ctx.shape[1]
        oc = attend(qc.reshape(bsz, n_ctx, SWA_KV_HEADS, SWA_GROUP, HEAD_DIM), kc, vc, None, sink_hg)
        y_ctx = oc.reshape(bsz, n_ctx, SWA_HEADS * HEAD_DIM) @ w_o
    return y_lat, y_ctx


def mla_mixer(h_lat, h_ctx, w_down, q_lora_gain, w_uq, kv_lora_gain, w_uk, w_uv, q_gain, k_gain, w_o,
              rope, with_ctx_out):
    bsz, seq, _ = h_lat.shape
    n_blocks = seq // BLOCK

    def project(h, tables):
        n = h.shape[1]
        cq, ckv, k_rope = jnp.split(h @ w_down, [Q_LORA, Q_LORA + KV_LORA], axis=-1)
        q = (rmsnorm(cq, q_lora_gain) @ w_uq).reshape(bsz, n, MLA_HEADS, MLA_QK)
        ckv = rmsnorm(ckv, kv_lora_gain)
        k_nope = (ckv @ w_uk).reshape(bsz, n, MLA_HEADS, MLA_NOPE)
        v = (ckv @ w_uv).reshape(bsz, n, MLA_HEADS, MLA_V)
        k_rope = jnp.broadcast_to(k_rope[:, :, None, :], (bsz, n, MLA_HEADS, MLA_ROPE))
        q = rmsnorm(q, q_gain)
        k = rmsnorm(jnp.concatenate([k_nope, k_rope], axis=-1), k_gain)
        if tables is not None:
            q = jnp.concatenate([q[..., :MLA_NOPE], apply_axial_rope(q[..., MLA_NOPE:], tables)], axis=-1)
            k = jnp.concatenate([k[..., :MLA_NOPE], apply_axial_rope(k[..., MLA_NOPE:], tables)], axis=-1)
        return q[:, :, :, None, :], k, v

    qc, kc, vc = project(h_ctx, None)
    q, k, v = project(h_lat, rope)
    k_all = jnp.concatenate([k, kc], axis=1)
    v_all = jnp.concatenate([v, vc], axis=1)

    def block(b):
        qb = lax.dynamic_slice_in_dim(q, b * BLOCK, BLOCK, axis=1)
        return attend(qb, k_all, v_all)

    o = lax.map(block, jnp.arange(n_blocks))
    o = jnp.transpose(o, (1, 0, 2, 3, 4, 5)).reshape(bsz, seq, MLA_HEADS * MLA_V)
    y_lat = o @ w_o
    y_ctx = None
    if with_ctx_out:
        oc = attend(qc, kc, vc)
        y_ctx = oc.reshape(bsz, h_ctx.shape[1], MLA_HEADS * MLA_V) @ w_o
    return y_lat, y_ctx


def rglru_coeffs(x, gate_w, gate_b, lam):
    bsz, n, _ = x.shape
    xb = x.reshape(bsz, n, LRU_BLOCKS, LRU_BW)
    g = jnp.einsum("blnc,kncd->kblnd", xb, gate_w).reshape(2, bsz, n, LRU_WIDTH) + gate_b[:, None, None, :]
    g = jax.nn.sigmoid(g.astype(jnp.float32))
    r, i = g[0], g[1]
    log_a = -LRU_C * r * jax.nn.softplus(-lam.astype(jnp.float32))
    a = jnp.exp(log_a)
    b = jnp.sqrt(-jnp.expm1(2.0 * log_a)) * (i * x.astype(jnp.float32))
    return a, b


def linear_scan(a, b, h0, reverse):
    first = -1 if reverse else 0
    b = b.at[:, first].add(a[:, first] * h0)

    def combine(left, right):
        a_l, b_l = left
        a_r, b_r = right
        return a_l * a_r, a_r * b_l + b_r

    return lax.associative_scan(combine, (a, b), reverse=reverse, axis=1)[1]


def rglru_mixer(h_lat, h_ctx, w_in, conv_w, conv_b, gate_w, gate_b, lam, w_out, with_ctx_out):
    def branches(h):
        gate, xr = jnp.split(h @ w_in, 2, axis=-1)
        return jax.nn.gelu(gate, approximate=True), depthwise_conv(xr, conv_w, conv_b, LRU_CONV // 2)

    g_ctx, x_ctx = branches(h_ctx)
    g_lat, x_lat = branches(h_lat)
    h0 = jnp.zeros((h_lat.shape[0], LRU_WIDTH), jnp.float32)
    lat_states, ctx_states = [], []
    for d, reverse in enumerate((False, True)):
        a_c, b_c = rglru_coeffs(x_ctx, gate_w[d], gate_b[d], lam[d])
        s_ctx = linear_scan(a_c, b_c, h0, reverse)
        h_end = s_ctx[:, 0] if reverse else s_ctx[:, -1]
        a_l, b_l = rglru_coeffs(x_lat, gate_w[d], gate_b[d], lam[d])
        lat_states.append(linear_scan(a_l, b_l, h_end, reverse))
        ctx_states.append(s_ctx)
    r_lat = lat_states[0] + lat_states[1]
    y_lat = (g_lat * r_lat.astype(g_lat.dtype)) @ w_out
    y_ctx = None
    if with_ctx_out:
        r_ctx = ctx_states[0] + ctx_states[1]
        y_ctx = (g_ctx * r_ctx.astype(g_ctx.dtype)) @ w_out
    return y_lat, y_ctx


def conv_ffn(h, w_up, conv_w, conv_b, w_down):
    gate, val = jnp.split(h @ w_up, 2, axis=-1)
    gate = depthwise_conv(gate, conv_w, conv_b, FFN_CONV // 2)
    return (jax.nn.silu(gate) * val) @ w_down


def setup_inputs(seed: int = 0) -> dict:
    key = jax.random.key(seed)
    keys = iter(jax.random.split(key, 48))

    def nrm(shape, scale):
        return jax.random.normal(next(keys), shape, jnp.float32) * scale

    def gain(shape):
        return 1.0 + nrm(shape, 0.05)

    D = D_MODEL
    u = jax.random.uniform(next(keys), (N_LAYERS_LRU, 2, LRU_WIDTH), jnp.float32, minval=0.9, maxval=0.999)
    s = u ** (1.0 / LRU_C)
    lam = jnp.log(s) - jnp.log1p(-s)
    return {
        "x": nrm((BATCH, SEQ, D), 1.0),
        "c": nrm((BATCH, D), 1.0),
        "ctx": nrm((BATCH, CTX_LEN, D), 1.0),
        "c_ctx": nrm((D,), 1.0),
        "norm1": gain((DEPTH, D)),
        "norm2": gain((DEPTH, D)),
        "mod_w": nrm((DEPTH, D, N_MOD * D), 0.5 * D ** -0.5),
        "mod_b": nrm((DEPTH, N_MOD * D), 0.02),
        "swa_w_qkv": nrm((N_LAYERS_SWA, D, (SWA_HEADS + 2 * SWA_KV_HEADS) * HEAD_DIM), D ** -0.5),
        "swa_q_gain": gain((N_LAYERS_SWA, HEAD_DIM)),
        "swa_k_gain": gain((N_LAYERS_SWA, HEAD_DIM)),
        "swa_sink": nrm((N_LAYERS_SWA, SWA_HEADS), 1.0),
        "swa_w_o": nrm((N_LAYERS_SWA, SWA_HEADS * HEAD_DIM, D), (SWA_HEADS * HEAD_DIM) ** -0.5),
        "mla_w_down": nrm((N_LAYERS_MLA, D, Q_LORA + KV_LORA + MLA_ROPE), D ** -0.5),
        "mla_q_lora_gain": gain((N_LAYERS_MLA, Q_LORA)),
        "mla_w_uq": nrm((N_LAYERS_MLA, Q_LORA, MLA_HEADS * MLA_QK), Q_LORA ** -0.5),
        "mla_kv_lora_gain": gain((N_LAYERS_MLA, KV_LORA)),
        "mla_w_uk": nrm((N_LAYERS_MLA, KV_LORA, MLA_HEADS * MLA_NOPE), KV_LORA ** -0.5),
        "mla_w_uv": nrm((N_LAYERS_MLA, KV_LORA, MLA_HEADS * MLA_V), KV_LORA ** -0.5),
        "mla_q_gain": gain((N_LAYERS_MLA, MLA_QK)),
        "mla_k_gain": gain((N_LAYERS_MLA, MLA_QK)),
        "mla_w_o": nrm((N_LAYERS_MLA, MLA_HEADS * MLA_V, D), (MLA_HEADS * MLA_V) ** -0.5),
        "lru_w_in": nrm((N_LAYERS_LRU, D, 2 * LRU_WIDTH), D ** -0.5),
        "lru_conv_w": nrm((N_LAYERS_LRU, LRU_CONV, LRU_WIDTH), LRU_CONV ** -0.5),
        "lru_conv_b": nrm((N_LAYERS_LRU, LRU_WIDTH), 0.01),
        "lru_gate_w": nrm((N_LAYERS_LRU, 2, 2, LRU_BLOCKS, LRU_BW, LRU_BW), LRU_BW ** -0.5),
        "lru_gate_b": nrm((N_LAYERS_LRU, 2, 2, LRU_WIDTH), 0.01),
        "lru_lam": lam,
        "lru_w_out": nrm((N_LAYERS_LRU, LRU_WIDTH, D), LRU_WIDTH ** -0.5),
        "ffn_w_up": nrm((DEPTH, D, 2 * D_FF), D ** -0.5),
        "ffn_conv_w": nrm((DEPTH, FFN_CONV, D_FF), FFN_CONV ** -0.5),
        "ffn_conv_b": nrm((DEPTH, D_FF), 0.01),
        "ffn_w_down": nrm((DEPTH, D_FF, D), D_FF ** -0.5),
    }


def reference(x, c, ctx, c_ctx, norm1, norm2, mod_w, mod_b,
              swa_w_qkv, swa_q_gain, swa_k_gain, swa_sink, swa_w_o,
              mla_w_down, mla_q_lora_gain, mla_w_uq, mla_kv_lora_gain, mla_w_uk, mla_w_uv,
              mla_q_gain, mla_k_gain, mla_w_o,
              lru_w_in, lru_conv_w, lru_conv_b, lru_gate_w, lru_gate_b, lru_lam, lru_w_out,
              ffn_w_up, ffn_conv_w, ffn_conv_b, ffn_w_down):
    seq = x.shape[1]
    rope_swa = axial_rope_tables(seq, HEAD_DIM)
    rope_mla = axial_rope_tables(seq, MLA_ROPE)
    cond_lat = jax.nn.silu(c)
    cond_ctx = jax.nn.silu(c_ctx)
    for layer in range(DEPTH):
        kind, idx = layer % N_MIXERS, layer // N_MIXERS
        with_ctx_out = layer < DEPTH - 1
        mod_l = jnp.split(cond_lat @ mod_w[layer] + mod_b[layer], N_MOD, axis=-1)
        mod_c = jnp.split(cond_ctx @ mod_w[layer] + mod_b[layer], N_MOD, axis=-1)
        sh1, sc1, g1, sh2, sc2, g2 = [m[:, None, :] for m in mod_l]
        csh1, csc1, cg1, csh2, csc2, cg2 = mod_c
        h_lat = modulate(rmsnorm(x, norm1[layer]), sh1, sc1)
        h_ctx = modulate(rmsnorm(ctx, norm1[layer]), csh1, csc1)
        if kind == 0:
            y_lat, y_ctx = window_gqa_mixer(h_lat, h_ctx, swa_w_qkv[idx], swa_q_gain[idx], swa_k_gain[idx],
                                            swa_sink[idx], swa_w_o[idx], rope_swa, with_ctx_out)
        elif kind == 1:
            y_lat, y_ctx = mla_mixer(h_lat, h_ctx, mla_w_down[idx], mla_q_lora_gain[idx], mla_w_uq[idx],
                                     mla_kv_lora_gain[idx], mla_w_uk[idx], mla_w_uv[idx], mla_q_gain[idx],
                                     mla_k_gain[idx], mla_w_o[idx], rope_mla, with_ctx_out)
        else:
            y_lat, y_ctx = rglru_mixer(h_lat, h_ctx, lru_w_in[idx], lru_conv_w[idx], lru_conv_b[idx],
                                       lru_gate_w[idx], lru_gate_b[idx], lru_lam[idx], lru_w_out[idx],
                                       with_ctx_out)
        x = x + g1 * y_lat
        x = x + g2 * conv_ffn(modulate(rmsnorm(x, norm2[layer]), sh2, sc2),
                              ffn_w_up[layer], ffn_conv_w[layer], ffn_conv_b[layer], ffn_w_down[layer])
        if with_ctx_out:
            ctx = ctx + cg1 * y_ctx
            ctx = ctx + cg2 * conv_ffn(modulate(rmsnorm(ctx, norm2[layer]), csh2, csc2),
                                       ffn_w_up[layer], ffn_conv_w[layer], ffn_conv_b[layer], ffn_w_down[layer])
    return x
```

```python
from contextlib import ExitStack
import numpy as np
import concourse.bass as bass
import concourse.mybir as mybir
from concourse.bass_utils import run_bass_kernel_spmd

F32 = mybir.dt.float32
BF16 = mybir.dt.bfloat16
AF = mybir.ActivationFunctionType
ALU = mybir.AluOpType

D = 1024
L = 4096
CT = 256
T = L + CT
DEPTH = 4
DFF = 2816
NJ = DFF // 128
EPS = 1e-6
SHIFT = 16.0
TILES = [(i * 512, 512, False) for i in range(8)] + [(L, CT, True)]


class _Op:
    __slots__ = ("eng", "fn", "deps", "dkey", "sig", "sem", "val")

    def __init__(self, eng, fn, deps, dkey):
        self.eng = eng
        self.fn = fn
        self.deps = deps
        self.dkey = dkey
        self.sig = False
        self.sem = None
        self.val = 0


class Sched:
    EPOCH = 20000

    def __init__(self, nc, stack):
        self.nc = nc
        self.stack = stack
        self.ops = []
        self.last_w = {}
        self.readers = {}
        self.last_dkey = {}
        self.pending_bar = {}
        self.bar_idx = -1
        self.emitted = 0
        self.engs = {"pe": nc.tensor, "act": nc.scalar, "dve": nc.vector,
                     "pool": nc.gpsimd, "sp": nc.sync}
        self.eng_cnt = {e: 0 for e in self.engs}
        self.eng_sem = {}
        self.key_sem = {}
        self.sem_cnt = {}
        self.free_sems = []
        self.waited = {e: {} for e in self.engs}
        self.nsem = 0
        self.ninst = 0

    def add(self, eng, fn, reads=(), writes=(), dkey=None):
        i = len(self.ops)
        deps = set()
        for r in reads:
            w = self.last_w.get(r)
            if w is not None:
                deps.add(w)
        for r in writes:
            w = self.last_w.get(r)
            if w is not None:
                deps.add(w)
            for rd in self.readers.get(r, ()):
                deps.add(rd)
        if dkey is not None:
            p = self.last_dkey.get(dkey)
            if p is not None:
                deps.add(p)
            self.last_dkey[dkey] = i
        deps = set(d for d in deps if d > self.bar_idx)
        if eng in self.pending_bar:
            deps |= self.pending_bar.pop(eng)
        for r in reads:
            self.readers.setdefault(r, []).append(i)
        for r in writes:
            self.last_w[r] = i
            self.readers[r] = []
        deps.discard(i)
        red = {}
        for d in deps:
            o = self.ops[d]
            src = ("k", o.dkey) if o.dkey is not None else ("e", o.eng)
            if src == ("e", "pe") and eng == "pe" and dkey is None and d > self.bar_idx:
                continue
            if src not in red or red[src] < d:
                red[src] = d
        self.ops.append(_Op(eng, fn, sorted(red.values()), dkey))
        return i

    def barrier(self):
        last = {}
        for i in range(self.bar_idx + 1, len(self.ops)):
            o = self.ops[i]
            src = ("k", o.dkey) if o.dkey is not None else ("e", o.eng)
            last[src] = i
        deps = set(last.values())
        for d in deps:
            self.ops[d].sig = True
        self.flush()
        for e in self.engs:
            self.pending_bar[e] = set(deps) | self.pending_bar.get(e, set())
        self.bar_idx = len(self.ops) - 1
        self.free_sems.extend(self.key_sem.values())
        self.key_sem = {}

    def flush(self):
        nc = self.nc
        ops = self.ops
        for i in range(self.emitted, len(ops)):
            for d in ops[i].deps:
                ops[d].sig = True
        for i in range(self.emitted, len(ops)):
            o = ops[i]
            E = self.engs[o.eng]
            w = self.waited[o.eng]
            for d in o.deps:
                do = ops[d]
                sid = id(do.sem)
                if w.get(sid, 0) >= do.val:
                    continue
                E.wait_ge(do.sem, do.val)
                w[sid] = do.val
            inst = o.fn()
            o.fn = None
            self.ninst += 1
            if o.dkey is not None:
                if o.dkey not in self.key_sem:
                    if self.free_sems:
                        self.key_sem[o.dkey] = self.free_sems.pop()
                    else:
                        sem = self.stack.enter_context(nc.semaphore("k%d" % self.nsem))
                        self.nsem += 1
                        self.sem_cnt[id(sem)] = 0
                        self.key_sem[o.dkey] = sem
                o.sem = self.key_sem[o.dkey]
                self.sem_cnt[id(o.sem)] += 16
                o.val = self.sem_cnt[id(o.sem)]
                inst.then_inc(o.sem, 16)
            elif o.sig:
                if o.eng not in self.eng_sem or self.eng_cnt[o.eng] >= self.EPOCH:
                    self.eng_sem[o.eng] = self.stack.enter_context(nc.semaphore("e%d" % self.nsem))
                    self.nsem += 1
                    self.eng_cnt[o.eng] = 0
                self.eng_cnt[o.eng] += 1
                o.sem = self.eng_sem[o.eng]
                o.val = self.eng_cnt[o.eng]
                inst.then_inc(o.sem, 1)
        self.emitted = len(ops)

    def emit(self):
        self.flush()


class Tile:
    def __init__(self, t, key):
        self.t = t
        self.key = key

    def __getitem__(self, idx):
        return self.t[idx]


class Ring:
    def __init__(self, tiles):
        self.tiles = tiles
        self.i = 0

    def get(self):
        t = self.tiles[self.i % len(self.tiles)]
        self.i += 1
        return t


class Builder:
    def __init__(self, layers=(0, 1, 2, 3)):
        self.layers = tuple(layers)
        self.nc = bass.Bass("TRN2", target_bir_lowering=False)
        self.cnt = 0

    def sb(self, shape, dtype, name="t"):
        self.cnt += 1
        nm = "%s_%d" % (name, self.cnt)
        t = self.scope.enter_context(self.nc.sbuf_tensor(nm, list(shape), dtype))
        return Tile(t, nm)

    def ps(self, name="ps"):
        self.cnt += 1
        nm = "%s_%d" % (name, self.cnt)
        t = self.stack.enter_context(self.nc.psum_tensor(nm, [128, 512], F32))
        return Tile(t, nm)

    def ring(self, n, shape, dtype, name="r"):
        return Ring([self.sb(shape, dtype, name) for _ in range(n)])

    def din(self, name, shape):
        return self.nc.dram_tensor(name, list(shape), F32, kind="ExternalInput").ap()

    def op(self, eng, fn, r=(), w=(), dkey=None):
        return self.S.add(eng, fn, reads=[x.key if isinstance(x, Tile) else x for x in r],
                          writes=[x.key if isinstance(x, Tile) else x for x in w], dkey=dkey)

    def load(self, dst, dst_ap, src_ap, r=(), cast=False):
        nc = self.nc
        if cast:
            self.op("pool", lambda: nc.gpsimd.dma_start(out=dst_ap, in_=src_ap, max_dma_last_dim=4096),
                    r=r, w=[dst], dkey=dst.key)
        else:
            self.op("sp", lambda: nc.sync.dma_start(out=dst_ap, in_=src_ap), r=r, w=[dst], dkey=dst.key)

    def store(self, dram_key, dst_ap, src, src_ap):
        nc = self.nc
        self.op("sp", lambda: nc.sync.dma_start(out=dst_ap, in_=src_ap), r=[src], w=[dram_key], dkey=src.key)

    def mm(self, ps, out_ap, pairs, r=()):
        nc = self.nc
        n = len(pairs)
        for i, (lt, rh) in enumerate(pairs):
            self.op("pe", (lambda lt=lt, rh=rh, i=i: nc.tensor.matmul(out_ap, lt, rh, start=(i == 0), stop=(i == n - 1))),
                    r=r, w=[ps])

    def act(self, out_ap, in_ap, func, r=(), w=(), bias=None, scale=None):
        nc = self.nc
        kw = {}
        if bias is not None:
            kw["bias"] = bias
        if scale is not None:
            kw["scale"] = scale
        self.op("act", lambda: nc.scalar.activation(out=out_ap, in_=in_ap, func=func, **kw), r=r, w=w)

    def tt(self, out_ap, a, b, op, r=(), w=(), eng="dve"):
        nc = self.nc
        e = nc.vector if eng == "dve" else nc.gpsimd
        self.op(eng, lambda: e.tensor_tensor(out=out_ap, in0=a, in1=b, op=op), r=r, w=w)

    def ts(self, out_ap, a, s1, s2, op0, op1=None, r=(), w=(), eng="dve"):
        nc = self.nc
        e = nc.vector if eng == "dve" else nc.gpsimd
        if op1 is None:
            self.op(eng, lambda: e.tensor_scalar(out=out_ap, in0=a, scalar1=s1, scalar2=None, op0=op0), r=r, w=w)
        else:
            self.op(eng, lambda: e.tensor_scalar(out=out_ap, in0=a, scalar1=s1, scalar2=s2, op0=op0, op1=op1), r=r, w=w)

    def stt(self, out_ap, a, s, b, op0, op1, r=(), w=()):
        nc = self.nc
        self.op("dve", lambda: nc.vector.scalar_tensor_tensor(out=out_ap, in0=a, scalar=s, in1=b, op0=op0, op1=op1), r=r, w=w)

    def copy(self, out_ap, in_ap, r=(), w=(), eng="dve"):
        nc = self.nc
        if eng == "act":
            self.op("act", lambda: nc.scalar.copy(out=out_ap, in_=in_ap), r=r, w=w)
        else:
            e = nc.vector if eng == "dve" else nc.gpsimd
            self.op(eng, lambda: e.tensor_copy(out=out_ap, in_=in_ap), r=r, w=w)

    def memset(self, t, ap, val, eng="dve"):
        nc = self.nc
        e = nc.vector if eng == "dve" else nc.gpsimd
        self.op(eng, lambda: e.memset(ap, val), w=[t])

    def mm1(self, ps, out_ap, lhsT, rhs, start, stop, r=()):
        nc = self.nc
        self.op("pe", lambda: nc.tensor.matmul(out_ap, lhsT, rhs, start=start, stop=stop), r=r, w=[ps])

    def recip(self, out_ap, in_ap, r=(), w=()):
        nc = self.nc
        self.op("dve", lambda: nc.vector.reciprocal(out=out_ap, in_=in_ap), r=r, w=w)

    def new_scope(self):
        if self.cur_scope is not None:
            self.S.barrier()
            self.cur_scope.close()
        self.cur_scope = ExitStack()
        self.scope = self.cur_scope

    def common_rings(self, nxt=2, nh=2):
        self.XT = self.ring(nxt, [128, 8, 512], F32, "xt")
        self.SQ = self.ring(1, [128, 8, 512], BF16, "sq")
        self.SQ1 = self.ring(4, [128, 512], BF16, "sq1")
        self.RS = self.ring(2, [128, 512], F32, "rstd")
        self.TF = self.ring(5, [128, 512], F32, "tf")
        self.H = self.ring(nh, [128, 8, 512], BF16, "h")

    def build(self):
        nc = self.nc
        I = {}
        for k, shp in input_shapes().items():
            I[k] = self.din(k, shp)
        self.I = I
        self.out = nc.dram_tensor("outT", [D, L], F32, kind="ExternalOutput").ap()
        self.XS = nc.dram_tensor("xs", [D, T], F32).ap()
        self.AS = nc.dram_tensor("acts", [D, T], BF16).ap()
        self.XRS = nc.dram_tensor("xrs", [D, T], BF16).ap()
        self.xsrc_is_input = True
        with ExitStack() as stack:
            self.stack = stack
            self.scope = stack
            self.cur_scope = None
            self.S = Sched(nc, stack)
            self.PS = Ring([self.ps() for _ in range(6)])
            self.PA = Ring([self.ps() for _ in range(2)])
            self.setup_consts()
            for l in self.layers:
                self.layer(l)
            self.op("sp", lambda: nc.sync.nop(), r=["OUT%d" % i for i in range(8)])
            self.S.emit()
            if self.cur_scope is not None:
                self.cur_scope.close()
        return nc

    def xview(self, ap, t0, n):
        return ap.rearrange("(k p) t -> p k t", p=128)[:, :, t0:t0 + n]

    def load_x(self, ti):
        t0, n, isctx = TILES[ti]
        xt = self.XT.get()
        src = self.I["xin"] if self.xsrc_is_input else self.XS
        self.load(xt, xt[:, :, 0:n], self.xview(src, t0, n), r=["X%d" % ti])
        return xt

    def store_x(self, ti, xt, final=False):
        t0, n, isctx = TILES[ti]
        if final and not isctx:
            self.store("OUT%d" % ti, self.xview(self.out, t0, n), xt, xt[:, :, 0:n])
        else:
            self.store("X%d" % ti, self.xview(self.XS, t0, n), xt, xt[:, :, 0:n])

    def setup_consts(self):
        nc = self.nc
        I = self.I

        def cload(name, shape, dtype=BF16):
            t = self.sb(shape, dtype, name)
            self.load(t, t[:], I[name][:], cast=(dtype == BF16))
            return t
        self.ones1024 = cload("c_ones1024", [128, 128])
        self.blk64 = cload("c_blk64", [128, 128])
        self.ones384 = cload("c_ones384", [128, 128])
        self.ones256 = cload("c_ones256", [128, 128])
        self.ones96 = cload("c_ones96", [128, 128])
        self.perm64 = cload("c_perm64", [128, 128])
        self.perm96 = cload("c_perm96", [128, 128])
        self.epsT = self.sb([128, 1], F32, "eps")
        self.memset(self.epsT, self.epsT[:], EPS)
        self.nshift = self.sb([128, 1], F32, "nshift")
        self.memset(self.nshift, self.nshift[:], -SHIFT)
        self.one1 = self.sb([128, 1], F32, "one1")
        self.memset(self.one1, self.one1[:], 1.0)
        self.n1g = cload("n1g", [128, 4, 8], F32)
        self.n2g = cload("n2g", [128, 4, 8], F32)
        self.modb = cload("modb", [128, 4, 48], F32)
        self.fcw = cload("fcw", [128, 4, NJ, 3], F32)
        self.fcb = cload("fcb", [128, 4, NJ], F32)
        self.XB = self.sb([128, 8, 16], F32, "XB")
        self.MV = {l: self.sb([128, 6, 8, 2], F32, "mv") for l in self.layers}
        cond = cload("cond", [128, 8, 2], F32)
        condT = self.sb([128, 8, 2], F32, "condT")
        self.act(condT[:], cond[:], AF.Silu, r=[cond], w=[condT])
        self.new_scope()
        wr = self.ring(2, [128, 6144], F32, "modw")
        for l in self.layers:
            acc = self.sb([128, 48, 2], F32, "modacc")
            for k in range(8):
                wt = wr.get()
                self.load(wt, wt[:], I["modw"][l, :, k, :])
                pt = self.PS.get()
                for j in range(48):
                    self.mm1(pt, pt[:, 2 * j:2 * j + 2], wt[:, j * 128:(j + 1) * 128], condT[:, k, :], True, True,
                             r=[wt, condT])
                pv = pt[:, 0:96].rearrange("p (j s) -> p j s", s=2)
                if k == 0:
                    self.copy(acc[:], pv, r=[pt], w=[acc])
                else:
                    self.tt(acc[:], pv, acc[:], ALU.add, r=[pt, acc], w=[acc])
            mb = self.modb[:, l, :].unsqueeze(2).broadcast_to([128, 48, 2])
            self.tt(acc[:], acc[:], mb, ALU.add, r=[acc, self.modb], w=[acc])
            mv = self.MV[l]
            a4 = acc[:].rearrange("p (m k) s -> p m k s", m=6)
            for dst, srcm in ((1, 0), (2, 2), (4, 3), (5, 5)):
                self.copy(mv[:, dst], a4[:, srcm], r=[acc], w=[mv])
            for dst, srcm, g in ((0, 1, self.n1g), (3, 4, self.n2g)):
                tmp = self.sb([128, 8, 2], F32, "mtmp")
                self.ts(tmp[:], a4[:, srcm], 1.0, None, ALU.add, r=[acc], w=[tmp])
                gb = g[:, l, :].unsqueeze(2).broadcast_to([128, 8, 2])
                self.tt(mv[:, dst], tmp[:], gb, ALU.mult, r=[tmp, g], w=[mv])

    def modnorm(self, xt, n, l, which, s, h):
        mv = self.MV[l]
        sq = self.SQ.get()
        self.act(sq[:, :, 0:n], xt[:, :, 0:n], AF.Square, r=[xt], w=[sq])
        pt = self.PS.get()
        self.mm(pt, pt[:, 0:n], [(self.ones1024[:], sq[:, k, 0:n]) for k in range(8)], r=[sq, self.ones1024])
        rstd = self.RS.get()
        self.act(rstd[:, 0:n], pt[:, 0:n], AF.Ln, r=[pt, self.epsT], w=[rstd], bias=self.epsT[:, 0:1])
        self.act(rstd[:, 0:n], rstd[:, 0:n], AF.Exp, r=[rstd], w=[rstd], scale=-0.5)
        a_i, b_i = (0, 1) if which == 1 else (3, 4)
        for k in range(8):
            tmp = self.TF.get()
            self.stt(tmp[:, 0:n], xt[:, k, 0:n], mv[:, a_i, k, s:s + 1], rstd[:, 0:n], ALU.mult, ALU.mult,
                     r=[xt, mv, rstd], w=[tmp])
            self.act(h[:, k, 0:n], tmp[:, 0:n], AF.Identity, r=[tmp, mv], w=[h], bias=mv[:, b_i, k, s:s + 1])

    def headnorm(self, pt, rows, n, onesmat, gain_ap, gain_t, dst_t, dst_ap, rope=None):
        sq = self.SQ1.get()
        raw = self.TF.get()
        self.act(sq[0:rows, 0:n], pt[0:rows, 0:n], AF.Square, r=[pt], w=[sq])
        self.act(raw[0:rows, 0:n], pt[0:rows, 0:n], AF.Copy, r=[pt], w=[raw])
        pm = self.PS.get()
        self.mm(pm, pm[0:rows, 0:n], [(onesmat[0:rows, 0:rows], sq[0:rows, 0:n])], r=[sq, onesmat])
        rstd = self.RS.get()
        self.act(rstd[0:rows, 0:n], pm[0:rows, 0:n], AF.Ln, r=[pm, self.epsT], w=[rstd], bias=self.epsT[0:rows, 0:1])
        self.act(rstd[0:rows, 0:n], rstd[0:rows, 0:n], AF.Exp, r=[rstd], w=[rstd], scale=-0.5)
        if rope is None:
            self.stt(dst_ap, raw[0:rows, 0:n], gain_ap, rstd[0:rows, 0:n], ALU.mult, ALU.mult,
                     r=[raw, rstd, gain_t], w=[dst_t])
            return
        if len(rope) == 5:
            cs, sn, permT, r0, r1 = rope
            cs_ap, sn_ap = cs[:, 0:n], sn[:, 0:n]
        else:
            cs, sn, permT, r0, r1, cs_ap, sn_ap = rope
        qn = self.SQ1.get()
        self.stt(qn[0:rows, 0:n], raw[0:rows, 0:n], gain_ap, rstd[0:rows, 0:n], ALU.mult, ALU.mult,
                 r=[raw, rstd, gain_t], w=[qn])
        pw = self.PS.get()
        self.mm(pw, pw[0:rows, 0:n], [(permT[0:rows, 0:rows], qn[0:rows, 0:n])], r=[qn, permT])
        t1 = self.TF.get()
        t2 = self.TF.get()
        self.tt(t1[r0:r1, 0:n], qn[r0:r1, 0:n], cs_ap[r0:r1], ALU.mult, r=[qn, cs], w=[t1])
        self.tt(t2[r0:r1, 0:n], pw[r0:r1, 0:n], sn_ap[r0:r1], ALU.mult, r=[pw, sn], w=[t2])
        if r0 > 0:
            self.copy(dst_ap[0:r0], qn[0:r0, 0:n], r=[qn], w=[dst_t], eng="pool")
        self.tt(dst_ap[r0:r1], t1[r0:r1, 0:n], t2[r0:r1, 0:n], ALU.add, r=[t1, t2], w=[dst_t])

    def load_rope(self, csname, snname, ti):
        t0, n, isctx = TILES[ti]
        cs = self.ROPE.get()
        sn = self.ROPE.get()
        self.load(cs, cs[:, 0:n], self.I[csname][:, t0:t0 + n])
        self.load(sn, sn[:, 0:n], self.I[snname][:, t0:t0 + n])
        return cs, sn

    def resid(self, l, ti, xt, act_t, act_fn, wo):
        mv = self.MV[l]
        t0, n, isctx = TILES[ti]
        s = 1 if isctx else 0
        for oc in range(8):
            pt = self.PS.get()
            self.mm(pt, pt[:, 0:n], [(wo[:, k, oc * 128:(oc + 1) * 128], act_fn(k)) for k in range(8)], r=[act_t, wo])
            self.stt(xt[:, oc, 0:n], pt[:, 0:n], mv[:, 2, oc, s:s + 1], xt[:, oc, 0:n], ALU.mult, ALU.add,
                     r=[pt, mv, xt], w=[xt])
        if not isctx:
            self.copy(self.XB[:, :, 2 * ti:2 * ti + 1], xt[:, :, 0:1], r=[xt], w=[self.XB], eng="pool")
            self.copy(self.XB[:, :, 2 * ti + 1:2 * ti + 2], xt[:, :, n - 1:n], r=[xt], w=[self.XB], eng="pool")
        self.store_x(ti, xt)

    def phase_c_dram(self, l, wo_name_ap, tiles):
        self.new_scope()
        self.XT = self.ring(2, [128, 8, 512], F32, "xt")
        AT = self.ring(2, [128, 8, 512], BF16, "at")
        wo = self.sb([128, 8, 1024], BF16, "wo")
        for k in range(8):
            self.load(wo, wo[:, k, :], wo_name_ap[:, k, :], cast=True)
        for ti in tiles:
            t0, n, isctx = TILES[ti]
            xt = self.load_x(ti)
            at = AT.get()
            self.load(at, at[:, :, 0:n], self.xview(self.AS, t0, n), r=["AS%d" % ti])
            self.resid(l, ti, xt, at, (lambda k, at=at, n=n: at[:, k, 0:n]), wo)

    def layer(self, l):
        kind, idx = l % 3, l // 3
        last = (l == DEPTH - 1)
        final = (l == self.layers[-1])
        tiles = list(range(8)) if last else list(range(9))
        if kind == 0:
            self.swa(l, idx, tiles)
        elif kind == 1:
            self.mla(l, idx, tiles)
        else:
            self.lru(l, idx, tiles)
        self.xsrc_is_input = False
        self.ffn(l, tiles, final)

    def ffn(self, l, tiles, final):
        I = self.I
        mv = self.MV[l]
        self.new_scope()
        self.common_rings()
        WG = self.ring(3, [128, 8, 128], BF16, "wg")
        WV = self.ring(3, [128, 8, 128], BF16, "wv")
        WD = self.ring(1, [128, NJ, 1024], BF16, "wd")
        HID = self.ring(1, [128, NJ, 512], BF16, "hid")
        GB = self.sb([128, NJ, 16], F32, "gb")
        GT = self.ring(2, [128, 514], F32, "gt")
        HB = self.sb([128, 8, 16], BF16, "hb")
        self.modnorm(self.XB, 16, l, 2, 0, HB)
        for idx_t, ti in enumerate(tiles):
            t0, n, isctx = TILES[ti]
            s = 1 if isctx else 0
            xt = self.load_x(ti)
            h = self.H.get()
            self.modnorm(xt, n, l, 2, s, h)
            hid = HID.get()
            wd = WD.get()
            for j in range(NJ):
                wg = WG.get()
                wv = WV.get()
                self.load(wg, wg[:], I["wug"][l, j].rearrange("p (k m) -> p k m", k=8), cast=True)
                self.load(wv, wv[:], I["wuv"][l, j].rearrange("p (k m) -> p k m", k=8), cast=True)
                self.load(wd, wd[:, j, :], I["wdn"][l, j], cast=True)
                if idx_t == 0:
                    pb = self.PS.get()
                    self.mm(pb, pb[:, 0:16], [(wg[:, k, :], HB[:, k, :]) for k in range(8)], r=[wg, HB])
                    self.copy(GB[:, j, :], pb[:, 0:16], r=[pb], w=[GB])
                pg = self.PS.get()
                self.mm(pg, pg[:, 0:n], [(wg[:, k, :], h[:, k, 0:n]) for k in range(8)], r=[wg, h])
                pv = self.PS.get()
                self.mm(pv, pv[:, 0:n], [(wv[:, k, :], h[:, k, 0:n]) for k in range(8)], r=[wv, h])
                gt = GT.get()
                self.act(gt[:, 1:n + 1], pg[:, 0:n], AF.Copy, r=[pg], w=[gt])
                if (not isctx) and ti > 0:
                    self.copy(gt[:, 0:1], GB[:, j, 2 * (ti - 1) + 1:2 * (ti - 1) + 2], r=[GB], w=[gt], eng="pool")
                else:
                    self.memset(gt, gt[:, 0:1], 0.0, eng="pool")
                if (not isctx) and ti < 7:
                    self.copy(gt[:, n + 1:n + 2], GB[:, j, 2 * (ti + 1):2 * (ti + 1) + 1], r=[GB], w=[gt], eng="pool")
                else:
                    self.memset(gt, gt[:, n + 1:n + 2], 0.0, eng="pool")
                c1 = self.TF.get()
                self.ts(c1[:, 0:n], gt[:, 0:n], self.fcw[:, l, j, 0:1], self.fcb[:, l, j:j + 1], ALU.mult, ALU.add,
                        r=[gt, self.fcw, self.fcb], w=[c1])
                self.stt(c1[:, 0:n], gt[:, 1:n + 1], self.fcw[:, l, j, 1:2], c1[:, 0:n], ALU.mult, ALU.add,
                         r=[gt, c1, self.fcw], w=[c1])
                self.stt(c1[:, 0:n], gt[:, 2:n + 2], self.fcw[:, l, j, 2:3], c1[:, 0:n], ALU.mult, ALU.add,
                         r=[gt, c1, self.fcw], w=[c1])
                sl = self.TF.get()
                self.act(sl[:, 0:n], c1[:, 0:n], AF.Silu, r=[c1], w=[sl])
                self.tt(hid[:, j, 0:n], sl[:, 0:n], pv[:, 0:n], ALU.mult, r=[sl, pv], w=[hid])
            for oc in range(8):
                pt = self.PS.get()
                self.mm(pt, pt[:, 0:n], [(wd[:, j, oc * 128:(oc + 1) * 128], hid[:, j, 0:n]) for j in range(NJ)],
                        r=[wd, hid])
                self.stt(xt[:, oc, 0:n], pt[:, 0:n], mv[:, 5, oc, s:s + 1], xt[:, oc, 0:n], ALU.mult, ALU.add,
                         r=[pt, mv, xt], w=[xt])
            self.store_x(ti, xt, final=final)

    def swa(self, l, idx, tiles):
        nc = self.nc
        I = self.I
        self.new_scope()
        self.common_rings(nxt=1, nh=1)
        self.ROPE = self.ring(4, [128, 512], F32, "rope")
        wq = self.sb([128, 8, 1024], BF16, "wq")
        wk = self.sb([128, 8, 256], BF16, "wk")
        wv = self.sb([128, 8, 256], BF16, "wv")
        wo = self.sb([128, 8, 1024], BF16, "wo")
        for k in range(8):
            self.load(wq, wq[:, k, :], I["swq"][idx, :, k, :], cast=True)
            self.load(wo, wo[:, k, :], I["swo"][idx, :, k, :], cast=True)
        self.load(wk, wk[:], I["swk"][idx], cast=True)
        self.load(wv, wv[:], I["swv"][idx], cast=True)
        gq = self.sb([128, 1], F32, "gq")
        gk = self.sb([128, 1], F32, "gk")
        self.load(gq, gq[:], I["sqg"][idx])
        self.load(gk, gk[:], I["skg"][idx])
        es = self.sb([128, 16], F32, "es")
        self.load(es, es[:], I["ssink"][idx])
        self.act(es[:], es[:], AF.Exp, r=[es, self.nshift], w=[es], bias=self.nshift[:, 0:1])
        KT = self.sb([128, 2, T], BF16, "KT")
        VA = self.sb([128, 34, 4, 128], BF16, "VA")
        self.memset(VA, VA[:, :, :, 64:128], 1.0, eng="pool")
        QT = self.sb([128, 8, 512], BF16, "QT")
        OT = self.sb([128, 8, 512], BF16, "OT")
        PT = self.ring(3, [128, 512], BF16, "pt")
        DEN = self.ring(2, [128, 512], F32, "den")
        for ti in range(9):
            t0, n, isctx = TILES[ti]
            s = 1 if isctx else 0
            xt = self.load_x(ti)
            h = self.H.get()
            self.modnorm(xt, n, l, 1, s, h)
            rope = None
            if not isctx:
                cs, sn = self.load_rope("rcs", "rsn", ti)
                rope = (cs, sn, self.perm64, 0, 128)
            for c in range(2):
                pt = self.PS.get()
                self.mm(pt, pt[:, 0:n], [(wk[:, k, c * 128:(c + 1) * 128], h[:, k, 0:n]) for k in range(8)], r=[wk, h])
                self.headnorm(pt, 128, n, self.blk64, gk[:, 0:1], gk, KT, KT[:, c, t0:t0 + n], rope=rope)
            for tb in range(n // 128):
                pt = self.PS.get()
                self.mm(pt, pt[:, 0:256], [(h[:, k, tb * 128:(tb + 1) * 128], wv[:, k, :]) for k in range(8)], r=[wv, h])
                blk = (t0 // 128) + tb
                self.copy(VA[:, blk, :, 0:64], pt[:, 0:256].rearrange("p (g d) -> p g d", g=4), r=[pt], w=[VA], eng="act")
        for ti in tiles:
            t0, n, isctx = TILES[ti]
            s = 1 if isctx else 0
            xt = self.load_x(ti)
            h = self.H.get()
            self.modnorm(xt, n, l, 1, s, h)
            rope = None
            if not isctx:
                cs, sn = self.load_rope("rcs", "rsn", ti)
                rope = (cs, sn, self.perm64, 0, 128)
            for c in range(8):
                pt = self.PS.get()
                self.mm(pt, pt[:, 0:n], [(wq[:, k, c * 128:(c + 1) * 128], h[:, k, 0:n]) for k in range(8)], r=[wq, h])
                self.headnorm(pt, 128, n, self.blk64, gq[:, 0:1], gq, QT, QT[:, c, 0:n], rope=rope)
            for qb in range(n // 128):
                QB = t0 // 128 + qb
                if isctx:
                    kbs = [(32, 0), (33, 0)]
                else:
                    kbs = []
                    if QB > 0:
                        kbs.append((QB - 1, 1))
                    kbs.append((QB, 0))
                    if QB < 31:
                        kbs.append((QB + 1, 2))
                    kbs += [(32, 0), (33, 0)]
                for g in range(4):
                    base = 0 if g < 2 else 64
                    c0 = 4 * (g % 2)
                    kc = g % 2
                    po = self.PA.get()
                    rhs = QT[base:base + 64, c0:c0 + 4, qb * 128:(qb + 1) * 128]
                    for ki, (kb, mk) in enumerate(kbs):
                        ps_ = self.PS.get()
                        self.mm1(ps_, ps_[:], KT[base:base + 64, kc, kb * 128:(kb + 1) * 128], rhs, True, True, r=[KT, QT])
                        p = PT.get()
                        self.act(p[:], ps_[:], AF.Exp, r=[ps_, self.nshift], w=[p], bias=self.nshift[:, 0:1], scale=0.125)
                        if mk:
                            cm, st = (1, -1) if mk == 1 else (-1, 1)
                            self.op("pool", (lambda p=p, cm=cm, st=st: nc.gpsimd.affine_select(
                                out=p[:].rearrange("p (a b) -> p a b", a=4), in_=p[:].rearrange("p (a b) -> p a b", a=4),
                                pattern=[[0, 4], [st, 128]], compare_op=ALU.is_ge, fill=0.0, base=0, channel_multiplier=cm)),
                                r=[p], w=[p])
                        self.mm1(po, po[:], VA[:, kb, g, :], p[:], ki == 0, ki == len(kbs) - 1, r=[VA, p])
                    den = DEN.get()
                    esb = es[64:128, 4 * g:4 * g + 4].unsqueeze(2).broadcast_to([64, 4, 128])
                    self.tt(den[64:128, :].rearrange("p (a b) -> p a b", a=4), po[64:128, :].rearrange("p (a b) -> p a b", a=4),
                            esb, ALU.add, r=[po, es], w=[den])
                    self.recip(den[64:128, :], den[64:128, :], r=[den], w=[den])
                    self.tt(OT[base:base + 64, c0:c0 + 4, qb * 128:(qb + 1) * 128],
                            po[0:64, :].rearrange("p (a b) -> p a b", a=4),
                            den[64:128, :].rearrange("p (a b) -> p a b", a=4), ALU.mult, r=[po, den], w=[OT])
            self.resid(l, ti, xt, OT, (lambda k, n=n: OT[:, k, 0:n]), wo)

    def mla(self, l, idx, tiles):
        I = self.I
        self.new_scope()
        CQN = self.sb([128, 3, T], BF16, "CQN")
        CKVN = self.sb([128, 2, T], BF16, "CKVN")
        KRSQ = self.sb([128, T], BF16, "KRSQ")
        KRROT = self.sb([128, T], BF16, "KRROT")
        gv = self.sb([128, 8], F32, "mg")
        self.load(gv, gv[:], I["mgv"][:])
        persist = self.cur_scope
        self.cur_scope = None
        self.new_scope()
        self.common_rings(nxt=2, nh=1)
        self.ROPE = self.ring(4, [128, 512], F32, "rope")
        wdn = self.sb([128, 8, 640], BF16, "mdn")
        wrp = self.sb([128, 8, 96], BF16, "mrp")
        KRG = self.ring(2, [128, 512], BF16, "krg")
        for kt in KRG.tiles:
            self.memset(kt, kt[:], 0.0)
        for k in range(8):
            self.load(wdn, wdn[:, k, :], I["mdn"][:, k, :], cast=True)
        self.load(wrp, wrp[:], I["mrp"][:], cast=True)
        for ti in range(9):
            t0, n, isctx = TILES[ti]
            s = 1 if isctx else 0
            xt = self.load_x(ti)
            h = self.H.get()
            self.modnorm(xt, n, l, 1, s, h)
            for (nch, coff, gcol, onesm, dstT) in ((3, 0, 0, self.ones384, CQN), (2, 384, 3, self.ones256, CKVN)):
                raws, sqs = [], []
                for c in range(nch):
                    pt = self.PS.get()
                    self.mm(pt, pt[:, 0:n], [(wdn[:, k, coff + c * 128:coff + (c + 1) * 128], h[:, k, 0:n]) for k in range(8)],
                            r=[wdn, h])
                    sq = self.SQ1.get()
                    raw = self.TF.get()
                    self.act(sq[:, 0:n], pt[:, 0:n], AF.Square, r=[pt], w=[sq])
                    self.act(raw[:, 0:n], pt[:, 0:n], AF.Copy, r=[pt], w=[raw])
                    raws.append(raw)
                    sqs.append(sq)
                pm = self.PS.get()
                self.mm(pm, pm[:, 0:n], [(onesm[:], sq[:, 0:n]) for sq in sqs], r=sqs + [onesm])
                rstd = self.RS.get()
                self.act(rstd[:, 0:n], pm[:, 0:n], AF.Ln, r=[pm, self.epsT], w=[rstd], bias=self.epsT[:, 0:1])
                self.act(rstd[:, 0:n], rstd[:, 0:n], AF.Exp, r=[rstd], w=[rstd], scale=-0.5)
                for c in range(nch):
                    self.stt(dstT[:, c, t0:t0 + n], raws[c][:, 0:n], gv[:, gcol + c:gcol + c + 1], rstd[:, 0:n],
                             ALU.mult, ALU.mult, r=[raws[c], rstd, gv], w=[dstT])
            pk = self.PS.get()
            self.mm(pk, pk[0:96, 0:n], [(wrp[:, k, :], h[:, k, 0:n]) for k in range(8)], r=[wrp, h])
            self.act(KRSQ[64:96, t0:t0 + n], pk[64:96, 0:n], AF.Square, r=[pk], w=[KRSQ])
            krg = KRG.get()
            self.ts(krg[64:96, 0:n], pk[64:96, 0:n], gv[64:96, 6:7], None, ALU.mult, r=[pk, gv], w=[krg])
            if isctx:
                self.copy(KRROT[64:96, t0:t0 + n], krg[64:96, 0:n], r=[krg], w=[KRROT])
            else:
                cs, sn = self.load_rope("mcs", "msn", ti)
                pw = self.PS.get()
                self.mm(pw, pw[0:96, 0:n], [(self.perm96[0:96, 0:96], krg[0:96, 0:n])], r=[krg, self.perm96])
                t1 = self.TF.get()
                t2 = self.TF.get()
                self.tt(t1[64:96, 0:n], krg[64:96, 0:n], cs[64:96, 0:n], ALU.mult, r=[krg, cs], w=[t1])
                self.tt(t2[64:96, 0:n], pw[64:96, 0:n], sn[64:96, 0:n], ALU.mult, r=[pw, sn], w=[t2])
                self.tt(KRROT[64:96, t0:t0 + n], t1[64:96, 0:n], t2[64:96, 0:n], ALU.add, r=[t1, t2], w=[KRROT])
        self.new_scope()
        self.SQ1 = self.ring(4, [128, 512], BF16, "sq1")
        self.RS = self.ring(2, [128, 512], F32, "rstd")
        self.TF = self.ring(5, [128, 512], F32, "tf")
        RALL = self.sb([128, 2, L], F32, "ropeall")
        self.load(RALL, RALL[:, 0, :], I["mcs"][:, :])
        self.load(RALL, RALL[:, 1, :], I["msn"][:, :])
        wuq = self.sb([128, 3, 1536], BF16, "muq")
        wuk = self.sb([128, 2, 1024], BF16, "muk")
        wuv = self.sb([128, 2, 1024], BF16, "muv")
        for k in range(3):
            self.load(wuq, wuq[:, k, :], I["muq"][:, k, :], cast=True)
        for k in range(2):
            self.load(wuk, wuk[:, k, :], I["muk"][:, k, :], cast=True)
            self.load(wuv, wuv[:, k, :], I["muv"][:, k, :], cast=True)
        KTH = self.ring(2, [128, T], BF16, "KTH")
        VH = self.ring(2, [128, 34, 128], BF16, "VH")
        QTH = self.ring(2, [128, 512], BF16, "QTH")
        PT = self.ring(3, [128, 512], BF16, "pt")
        DEN = self.ring(2, [128, 512], F32, "den")
        OS = self.ring(3, [128, 512], BF16, "os")
        for v in VH.tiles:
            self.memset(v, v[:, :, 64:128], 1.0, eng="pool")
        scale = 96.0 ** -0.5
        for hd in range(16):
            kth = KTH.get()
            vh = VH.get()
            for ti in range(9):
                t0, n, isctx = TILES[ti]
                pk = self.PS.get()
                self.mm(pk, pk[0:64, 0:n], [(wuk[:, c2, hd * 64:(hd + 1) * 64], CKVN[:, c2, t0:t0 + n]) for c2 in range(2)],
                        r=[wuk, CKVN])
                sq = self.SQ1.get()
                raw = self.TF.get()
                self.act(sq[0:64, 0:n], pk[0:64, 0:n], AF.Square, r=[pk], w=[sq])
                self.act(raw[0:64, 0:n], pk[0:64, 0:n], AF.Copy, r=[pk], w=[raw])
                self.copy(sq[64:96, 0:n], KRSQ[64:96, t0:t0 + n], r=[KRSQ], w=[sq], eng="pool")
                pm = self.PS.get()
                self.mm(pm, pm[0:96, 0:n], [(self.ones96[0:96, 0:96], sq[0:96, 0:n])], r=[sq, self.ones96])
                rstd = self.RS.get()
                self.act(rstd[0:96, 0:n], pm[0:96, 0:n], AF.Ln, r=[pm, self.epsT], w=[rstd], bias=self.epsT[0:96, 0:1])
                self.act(rstd[0:96, 0:n], rstd[0:96, 0:n], AF.Exp, r=[rstd], w=[rstd], scale=-0.5)
                self.stt(kth[0:64, t0:t0 + n], raw[0:64, 0:n], gv[0:64, 6:7], rstd[0:64, 0:n], ALU.mult, ALU.mult,
                         r=[raw, rstd, gv], w=[kth])
                self.tt(kth[64:96, t0:t0 + n], KRROT[64:96, t0:t0 + n], rstd[64:96, 0:n], ALU.mult,
                        r=[KRROT, rstd], w=[kth])
                for tb in range(n // 128):
                    pv = self.PS.get()
                    self.mm(pv, pv[:, 0:64], [(CKVN[:, c2, t0 + tb * 128:t0 + (tb + 1) * 128], wuv[:, c2, hd * 64:(hd + 1) * 64])
                                              for c2 in range(2)], r=[wuv, CKVN])
                    self.copy(vh[:, t0 // 128 + tb, 0:64], pv[:, 0:64], r=[pv], w=[vh], eng="act")
            for ti in tiles:
                t0, n, isctx = TILES[ti]
                pq = self.PS.get()
                self.mm(pq, pq[0:96, 0:n], [(wuq[:, c3, hd * 96:(hd + 1) * 96], CQN[:, c3, t0:t0 + n]) for c3 in range(3)],
                        r=[wuq, CQN])
                qth = QTH.get()
                rope = None
                if not isctx:
                    rope = (RALL, RALL, self.perm96, 64, 96, RALL[:, 0, t0:t0 + n], RALL[:, 1, t0:t0 + n])
                self.headnorm(pq, 96, n, self.ones96, gv[0:96, 5:6], gv, qth, qth[0:96, 0:n], rope=rope)
                kbs = [32, 33] if isctx else list(range(34))
                po = self.PA.get()
                for ki, kb in enumerate(kbs):
                    ps_ = self.PS.get()
                    self.mm1(ps_, ps_[:, 0:n], kth[0:96, kb * 128:(kb + 1) * 128], qth[0:96, 0:n], True, True, r=[kth, qth])
                    p = PT.get()
                    self.act(p[:, 0:n], ps_[:, 0:n], AF.Exp, r=[ps_, self.nshift], w=[p], bias=self.nshift[:, 0:1], scale=scale)
                    self.mm1(po, po[:, 0:n], vh[:, kb, :], p[:, 0:n], ki == 0, ki == len(kbs) - 1, r=[vh, p])
                den = DEN.get()
                self.recip(den[64:128, 0:n], po[64:128, 0:n], r=[po], w=[den])
                hb = (hd % 2) * 64
                os_ = OS.get()
                self.tt(os_[hb:hb + 64, 0:n], po[0:64, 0:n], den[64:128, 0:n], ALU.mult, r=[po, den], w=[os_])
                dst = self.AS.rearrange("(k p) t -> p k t", p=128)[hb:hb + 64, hd // 2, t0:t0 + n]
                self.store("AS%d" % ti, dst, os_, os_[hb:hb + 64, 0:n])
        self.S.barrier()
        self.cur_scope.close()
        persist.close()
        self.cur_scope = None
        self.phase_c_dram(l, I["mwo"], tiles)

    def lru(self, l, idx, tiles):
        nc = self.nc
        I = self.I
        PADL = 2
        CTX0 = L + 6
        XW = T + 8

        def pcol(t0):
            return t0 + PADL if t0 < L else (t0 - L) + CTX0
        self.new_scope()
        self.common_rings(nxt=2, nh=2)
        win = self.sb([128, 8, 2048], BF16, "lwin")
        for k in range(8):
            self.load(win, win[:, k, :], I["lwin"][:, k, :], cast=True)
        STG = self.ring(4, [128, 512], BF16, "stg")
        for ti in range(9):
            t0, n, isctx = TILES[ti]
            s = 1 if isctx else 0
            xt = self.load_x(ti)
            h = self.H.get()
            self.modnorm(xt, n, l, 1, s, h)
            for oc in range(16):
                pt = self.PS.get()
                self.mm(pt, pt[:, 0:n], [(win[:, k, oc * 128:(oc + 1) * 128], h[:, k, 0:n]) for k in range(8)], r=[win, h])
                stg = STG.get()
                if oc < 8:
                    self.act(stg[:, 0:n], pt[:, 0:n], AF.Gelu_apprx_tanh, r=[pt], w=[stg])
                    self.store(("AS", oc, ti), self.AS[oc * 128:(oc + 1) * 128, t0:t0 + n], stg, stg[:, 0:n])
                else:
                    self.copy(stg[:, 0:n], pt[:, 0:n], r=[pt], w=[stg])
                    self.store(("XRS", oc - 8), self.XRS[(oc - 8) * 128:(oc - 7) * 128, t0:t0 + n], stg, stg[:, 0:n])
        self.new_scope()
        self.TF = self.ring(6, [128, 512], F32, "tf")
        gw = self.sb([128, 2, 2, 4, 2, 256], BF16, "lgw")
        for d in range(2):
            for wch in range(2):
                self.load(gw, gw[:, d, wch], I["lgw"][:, d, wch], cast=True)
        sv = self.sb([128, 64], F32, "lsv")
        self.load(sv, sv[:], I["lsv"][:])
        ngb = self.sb([128, 32], F32, "lngb")
        self.load(ngb, ngb[:], I["lgb"][:])
        self.ts(ngb[:], ngb[:], -1.0, None, ALU.mult, r=[ngb], w=[ngb])
        cp = self.sb([128, 16], F32, "lcp")
        cp2 = self.sb([128, 16], F32, "lcp2")
        self.act(cp[:], sv[:, 40:56], AF.Exp, r=[sv], w=[cp], scale=-1.0)
        self.act(cp[:], cp[:], AF.Ln, r=[cp, self.one1], w=[cp], bias=self.one1[:, 0:1])
        self.ts(cp2[:], cp[:], -16.0, None, ALU.mult, r=[cp], w=[cp2])
        self.ts(cp[:], cp[:], -8.0, None, ALU.mult, r=[cp], w=[cp])
        XR = self.sb([128, 2, XW], BF16, "XR")
        self.memset(XR, XR[:, :, 0:PADL], 0.0, eng="pool")
        self.memset(XR, XR[:, :, L + PADL:CTX0], 0.0, eng="pool")
        self.memset(XR, XR[:, :, CTX0 + CT:XW], 0.0, eng="pool")
        XC = self.sb([128, 2, XW], BF16, "XC")
        SF = self.sb([128, XW], F32, "SF")
        CAR = self.ring(4, [128, 1], F32, "car")
        GT_ = self.ring(3, [128, 512], BF16, "gtile")
        zero1 = self.sb([128, 1], F32, "zero1")
        self.memset(zero1, zero1[:], 0.0)
        NW = XW - 4
        for bk in range(4):
            for cc in range(2):
                c = 2 * bk + cc
                self.load(XR, XR[:, cc, PADL:PADL + L], self.XRS[c * 128:(c + 1) * 128, 0:L], r=[("XRS", c)])
                self.load(XR, XR[:, cc, CTX0:CTX0 + CT], self.XRS[c * 128:(c + 1) * 128, L:T], r=[("XRS", c)])
            for cc in range(2):
                c = 2 * bk + cc
                acc = SF
                self.ts(acc[:, 0:NW], XR[:, cc, 0:NW], sv[:, 8 + 4 * c:9 + 4 * c], sv[:, c:c + 1], ALU.mult, ALU.add,
                        r=[XR, sv], w=[acc])
                for k in (1, 2):
                    self.stt(acc[:, 0:NW], XR[:, cc, k:k + NW], sv[:, 8 + 4 * c + k:9 + 4 * c + k], acc[:, 0:NW], ALU.mult, ALU.add,
                             r=[XR, sv, acc], w=[acc])
                self.stt(XC[:, cc, 2:2 + NW], XR[:, cc, 3:3 + NW], sv[:, 8 + 4 * c + 3:9 + 4 * c + 3], acc[:, 0:NW], ALU.mult, ALU.add,
                         r=[XR, sv, acc], w=[XC])
            for cc in range(2):
                c = 2 * bk + cc
                for d in range(2):
                    order = [8] + (list(range(8)) if d == 0 else list(range(7, -1, -1)))
                    carry = zero1
                    for ti in order:
                        t0, n, isctx = TILES[ti]
                        pc = pcol(t0)
                        pr = self.PS.get()
                        self.mm(pr, pr[:, 0:n], [(gw[:, d, 0, bk, kk, cc * 128:(cc + 1) * 128], XC[:, kk, pc:pc + n]) for kk in range(2)],
                                r=[gw, XC])
                        pi = self.PS.get()
                        self.mm(pi, pi[:, 0:n], [(gw[:, d, 1, bk, kk, cc * 128:(cc + 1) * 128], XC[:, kk, pc:pc + n]) for kk in range(2)],
                                r=[gw, XC])
                        gi = (d * 2 + 0) * 8 + c
                        gi2 = (d * 2 + 1) * 8 + c
                        ta = self.TF.get()
                        tb_ = self.TF.get()
                        tcc = self.TF.get()
                        self.act(ta[:, 0:n], pr[:, 0:n], AF.Exp, r=[pr, ngb], w=[ta], bias=ngb[:, gi:gi + 1], scale=-1.0)
                        self.act(ta[:, 0:n], ta[:, 0:n], AF.Ln, r=[ta, self.one1], w=[ta], bias=self.one1[:, 0:1])
                        self.act(ta[:, 0:n], ta[:, 0:n], AF.Exp, r=[ta], w=[ta], scale=-1.0)
                        self.act(tb_[:, 0:n], ta[:, 0:n], AF.Exp, r=[ta, cp], w=[tb_], scale=cp[:, d * 8 + c:d * 8 + c + 1])
                        self.act(ta[:, 0:n], ta[:, 0:n], AF.Exp, r=[ta, cp2], w=[ta], scale=cp2[:, d * 8 + c:d * 8 + c + 1])
                        self.ts(ta[:, 0:n], ta[:, 0:n], 0.99999994, None, ALU.min, r=[ta], w=[ta])
                        self.act(ta[:, 0:n], ta[:, 0:n], AF.Ln, r=[ta, self.one1], w=[ta], bias=self.one1[:, 0:1], scale=-1.0)
                        self.act(ta[:, 0:n], ta[:, 0:n], AF.Exp, r=[ta], w=[ta], scale=0.5)
                        self.act(tcc[:, 0:n], pi[:, 0:n], AF.Exp, r=[pi, ngb], w=[tcc], bias=ngb[:, gi2:gi2 + 1], scale=-1.0)
                        self.ts(tcc[:, 0:n], tcc[:, 0:n], 1.0, None, ALU.add, r=[tcc], w=[tcc])
                        self.recip(tcc[:, 0:n], tcc[:, 0:n], r=[tcc], w=[tcc])
                        self.tt(tcc[:, 0:n], tcc[:, 0:n], XC[:, cc, pc:pc + n], ALU.mult, r=[tcc, XC], w=[tcc])
                        self.tt(tcc[:, 0:n], tcc[:, 0:n], ta[:, 0:n], ALU.mult, r=[tcc, ta], w=[tcc])
                        so = self.TF.get()
                        ncar = CAR.get()
                        if d == 0:
                            self.op("dve", (lambda so=so, tb_=tb_, tcc=tcc, n=n, carry=carry: nc.vector.tensor_tensor_scan(
                                out=so[:, 0:n], data0=tb_[:, 0:n], data1=tcc[:, 0:n], initial=carry[:, 0:1],
                                op0=ALU.mult, op1=ALU.add)), r=[tb_, tcc, carry], w=[so])
                            self.copy(ncar[:, 0:1], so[:, n - 1:n], r=[so], w=[ncar])
                            self.copy(SF[:, t0:t0 + n], so[:, 0:n], r=[so], w=[SF], eng="pool")
                        else:
                            self.op("dve", (lambda so=so, tb_=tb_, tcc=tcc, n=n, carry=carry: nc.vector.tensor_tensor_scan(
                                out=so[:, 0:n][:, ::-1], data0=tb_[:, 0:n][:, ::-1], data1=tcc[:, 0:n][:, ::-1], initial=carry[:, 0:1],
                                op0=ALU.mult, op1=ALU.add)), r=[tb_, tcc, carry], w=[so])
                            self.copy(ncar[:, 0:1], so[:, 0:1], r=[so], w=[ncar])
                            self.tt(so[:, 0:n], so[:, 0:n], SF[:, t0:t0 + n], ALU.add, r=[so, SF], w=[so])
                            gtile = GT_.get()
                            asl = self.AS[c * 128:(c + 1) * 128, t0:t0 + n]
                            self.load(gtile, gtile[:, 0:n], asl, r=[("AS", c, ti)])
                            self.tt(gtile[:, 0:n], so[:, 0:n], gtile[:, 0:n], ALU.mult, r=[so, gtile], w=[gtile])
                            self.store(("AS", c, ti), asl, gtile, gtile[:, 0:n])
                        carry = ncar
        self.phase_c_dram_multi(l, I["lwout"], tiles)

    def phase_c_dram_multi(self, l, wo_ap, tiles):
        self.new_scope()
        self.XT = self.ring(2, [128, 8, 512], F32, "xt")
        AT = self.ring(2, [128, 8, 512], BF16, "at")
        wo = self.sb([128, 8, 1024], BF16, "wo")
        for k in range(8):
            self.load(wo, wo[:, k, :], wo_ap[:, k, :], cast=True)
        for ti in tiles:
            t0, n, isctx = TILES[ti]
            xt = self.load_x(ti)
            at = AT.get()
            self.load(at, at[:, :, 0:n], self.xview(self.AS, t0, n), r=[("AS", c, ti) for c in range(8)])
            self.resid(l, ti, xt, at, (lambda k, at=at, n=n: at[:, k, 0:n]), wo)


def input_shapes():
    return {
        "xin": (D, T), "cond": (128, 8, 2), "n1g": (128, 4, 8), "n2g": (128, 4, 8),
        "modw": (4, 128, 8, 6144), "modb": (128, 4, 48),
        "wug": (4, NJ, 128, 1024), "wuv": (4, NJ, 128, 1024), "wdn": (4, NJ, 128, 1024),
        "fcw": (128, 4, NJ, 3), "fcb": (128, 4, NJ),
        "swq": (2, 128, 8, 1024), "swk": (2, 128, 8, 256), "swv": (2, 128, 8, 256), "swo": (2, 128, 8, 1024),
        "sqg": (2, 128, 1), "skg": (2, 128, 1), "ssink": (2, 128, 16),
        "rcs": (128, L), "rsn": (128, L), "mcs": (128, L), "msn": (128, L),
        "mdn": (128, 8, 640), "mrp": (128, 8, 96), "muq": (128, 3, 1536), "muk": (128, 2, 1024), "muv": (128, 2, 1024),
        "mwo": (128, 8, 1024), "mgv": (128, 8),
        "lwin": (128, 8, 2048), "lwout": (128, 8, 1024), "lgw": (128, 2, 2, 4, 2, 256), "lsv": (128, 64), "lgb": (128, 32),
        "c_ones1024": (128, 128), "c_blk64": (128, 128), "c_ones384": (128, 128), "c_ones256": (128, 128),
        "c_ones96": (128, 128), "c_perm64": (128, 128), "c_perm96": (128, 128),
    }


def _fm(v, nch):
    return np.ascontiguousarray(np.asarray(v, np.float32).reshape(nch, 128).T)


def _wfm(w):
    K, N = w.shape
    return np.ascontiguousarray(np.asarray(w, np.float32).reshape(K // 128, 128, N).transpose(1, 0, 2))


def _rope_tables(rot_dim, base_part, nrows):
    n_freq = rot_dim // 4
    half = rot_dim // 2
    t = np.arange(L)
    row = (t // 64).astype(np.float32)
    col = (t % 64).astype(np.float32)
    inv = (np.float32(10000.0) ** (-np.arange(n_freq, dtype=np.float32) / np.float32(n_freq))).astype(np.float32)
    cs = np.zeros((128, L), np.float32)
    sn = np.zeros((128, L), np.float32)
    partner = np.zeros(128, np.int64) - 1
    for p in range(nrows):
        d = p % rot_dim if base_part == 0 else p
        pp = p + base_part
        dd = d % rot_dim
        pos = row if dd < half else col
        e = dd % half
        i = e % n_freq
        ang = (pos * inv[i]).astype(np.float32)
        cs[pp] = np.cos(ang)
        sgn = -1.0 if e < n_freq else 1.0
        sn[pp] = sgn * np.sin(ang)
        partner[pp] = pp + n_freq if e < n_freq else pp - n_freq
    return cs, sn, partner


def prepare_shared(inp):
    f = lambda k: np.asarray(inp[k], np.float32)
    sh = {}
    sh["n1g"] = np.ascontiguousarray(f("norm1").reshape(4, 8, 128).transpose(2, 0, 1))
    sh["n2g"] = np.ascontiguousarray(f("norm2").reshape(4, 8, 128).transpose(2, 0, 1))
    sh["modw"] = np.ascontiguousarray(f("mod_w").reshape(4, 8, 128, 6144).transpose(0, 2, 1, 3))
    sh["modb"] = np.ascontiguousarray(f("mod_b").reshape(4, 48, 128).transpose(2, 0, 1))
    wup = f("ffn_w_up")
    def upl(w):
        return np.ascontiguousarray(w.reshape(4, 8, 128, NJ, 128).transpose(0, 3, 2, 1, 4).reshape(4, NJ, 128, 1024))
    sh["wug"] = upl(wup[:, :, :DFF])
    sh["wuv"] = upl(wup[:, :, DFF:])
    sh["wdn"] = np.ascontiguousarray(f("ffn_w_down").reshape(4, NJ, 128, 1024))
    sh["fcw"] = np.ascontiguousarray(f("ffn_conv_w").reshape(4, 3, NJ, 128).transpose(3, 0, 2, 1))
    sh["fcb"] = np.ascontiguousarray(f("ffn_conv_b").reshape(4, NJ, 128).transpose(2, 0, 1))
    wqkv = f("swa_w_qkv")
    wq = wqkv[:, :, :1024].reshape(2, 1024, 2, 8, 64).transpose(0, 1, 3, 2, 4).reshape(2, 1024, 1024)
    wk = wqkv[:, :, 1024:1280].reshape(2, 1024, 2, 2, 64).transpose(0, 1, 3, 2, 4).reshape(2, 1024, 256)
    wv = wqkv[:, :, 1280:1536]
    sh["swq"] = np.stack([_wfm(wq[i]) for i in range(2)])
    sh["swk"] = np.stack([_wfm(wk[i]) for i in range(2)])
    sh["swv"] = np.stack([_wfm(wv[i]) for i in range(2)])
    wo = f("swa_w_o").reshape(2, 2, 8, 64, 1024).transpose(0, 2, 1, 3, 4).reshape(2, 1024, 1024)
    sh["swo"] = np.stack([_wfm(wo[i]) for i in range(2)])
    sh["sqg"] = np.ascontiguousarray(np.tile(f("swa_q_gain"), (1, 2)).reshape(2, 128, 1))
    sh["skg"] = np.ascontiguousarray(np.tile(f("swa_k_gain"), (1, 2)).reshape(2, 128, 1))
    sh["ssink"] = np.ascontiguousarray(np.broadcast_to(f("swa_sink")[:, None, :], (2, 128, 16)))
    cs, sn, partner = _rope_tables(64, 0, 128)
    sh["rcs"], sh["rsn"] = cs, sn
    pm = np.zeros((128, 128), np.float32)
    for m in range(128):
        pm[partner[m], m] = 1.0
    sh["c_perm64"] = pm
    cs, sn, partner = _rope_tables(32, 64, 32)
    sh["mcs"], sh["msn"] = cs, sn
    pm = np.zeros((128, 128), np.float32)
    for m in range(64, 96):
        pm[partner[m], m] = 1.0
    sh["c_perm96"] = pm
    wd = f("mla_w_down")[0]
    sh["mdn"] = _wfm(wd[:, :640])
    wr = np.zeros((1024, 96), np.float32)
    wr[:, 64:96] = wd[:, 640:672]
    sh["mrp"] = _wfm(wr)
    sh["muq"] = _wfm(f("mla_w_uq")[0])
    sh["muk"] = _wfm(f("mla_w_uk")[0])
    sh["muv"] = _wfm(f("mla_w_uv")[0])
    sh["mwo"] = _wfm(f("mla_w_o")[0])
    gv = np.zeros((128, 8), np.float32)
    gv[:, 0:3] = _fm(f("mla_q_lora_gain")[0], 3)
    gv[:, 3:5] = _fm(f("mla_kv_lora_gain")[0], 2)
    gv[0:96, 5] = f("mla_q_gain")[0]
    gv[0:96, 6] = f("mla_k_gain")[0]
    sh["mgv"] = gv
    sh["lwin"] = _wfm(f("lru_w_in")[0])
    sh["lwout"] = _wfm(f("lru_w_out")[0])
    gw = f("lru_gate_w")[0]
    sh["lgw"] = np.ascontiguousarray(gw.reshape(2, 2, 4, 2, 128, 256).transpose(4, 0, 1, 2, 3, 5))
    sv = np.zeros((128, 64), np.float32)
    sv[:, 0:8] = _fm(f("lru_conv_b")[0], 8)
    cw = f("lru_conv_w")[0]
    sv[:, 8:40] = cw.reshape(4, 8, 128).transpose(2, 1, 0).reshape(128, 32)
    lam = f("lru_lam")[0]
    sv[:, 40:56] = lam.reshape(2, 8, 128).transpose(2, 0, 1).reshape(128, 16)
    sh["lsv"] = sv
    gb = f("lru_gate_b")[0]
    sh["lgb"] = np.ascontiguousarray(gb.reshape(2, 2, 8, 128).transpose(3, 0, 1, 2).reshape(128, 32))
    sh["c_ones1024"] = np.full((128, 128), 1.0 / 1024, np.float32)
    b = np.zeros((128, 128), np.float32)
    b[0:64, 0:64] = 1.0 / 64
    b[64:128, 64:128] = 1.0 / 64
    sh["c_blk64"] = b
    sh["c_ones384"] = np.full((128, 128), 1.0 / 384, np.float32)
    sh["c_ones256"] = np.full((128, 128), 1.0 / 256, np.float32)
    sh["c_ones96"] = np.full((128, 128), 1.0 / 96, np.float32)
    return sh


def prepare_core(inp, b):
    x = np.asarray(inp["x"][b], np.float32)
    ctx = np.asarray(inp["ctx"][b], np.float32)
    xin = np.ascontiguousarray(np.concatenate([x.T, ctx.T], axis=1))
    cond = np.stack([_fm(np.asarray(inp["c"][b]), 8), _fm(np.asarray(inp["c_ctx"]), 8)], axis=2)
    return {"xin": xin, "cond": np.ascontiguousarray(cond)}


_NC_CACHE = {}


def kernel(**inputs):
    sh = prepare_shared(inputs)
    if "nc" not in _NC_CACHE:
        _NC_CACHE["nc"] = Builder().build()
    nc = _NC_CACHE["nc"]
    in_maps = []
    for b in range(8):
        m = dict(sh)
        m.update(prepare_core(inputs, b))
        in_maps.append(m)
    res = run_bass_kernel_spmd(nc, in_maps, core_ids=list(range(8)))
    out = np.stack([np.ascontiguousarray(r["outT"].T) for r in res.results], axis=0)
    return out.astype(np.float32)
```

```python
from contextlib import ExitStack
import numpy as np
import concourse.bass as bass
import concourse.mybir as mybir
from concourse.bass_utils import run_bass_kernel_spmd

F32 = mybir.dt.float32
BF16 = mybir.dt.bfloat16
AF = mybir.ActivationFunctionType
ALU = mybir.AluOpType

D = 1024
L = 4096
CT = 256
T = L + CT
DEPTH = 4
DFF = 2816
NJ = DFF // 128
EPS = 1e-6
SHIFT = 16.0
TILES = [(i * 512, 512, False) for i in range(8)] + [(L, CT, True)]


class _Op:
    __slots__ = ("eng", "fn", "deps", "dkey", "sig", "sem", "val")

    def __init__(self, eng, fn, deps, dkey):
        self.eng = eng
        self.fn = fn
        self.deps = deps
        self.dkey = dkey
        self.sig = False
        self.sem = None
        self.val = 0


class Sched:
    EPOCH = 20000

    def __init__(self, nc, stack):
        self.nc = nc
        self.stack = stack
        self.ops = []
        self.last_w = {}
        self.readers = {}
        self.last_dkey = {}
        self.pending_bar = {}
        self.bar_idx = -1
        self.emitted = 0
        self.engs = {"pe": nc.tensor, "act": nc.scalar, "dve": nc.vector,
                     "pool": nc.gpsimd, "sp": nc.sync}
        self.eng_cnt = {e: 0 for e in self.engs}
        self.eng_sem = {}
        self.key_sem = {}
        self.sem_cnt = {}
        self.free_sems = []
        self.waited = {e: {} for e in self.engs}
        self.nsem = 0
        self.ninst = 0

    def add(self, eng, fn, reads=(), writes=(), dkey=None):
        i = len(self.ops)
        deps = set()
        for r in reads:
            w = self.last_w.get(r)
            if w is not None:
                deps.add(w)
        for r in writes:
            w = self.last_w.get(r)
            if w is not None:
                deps.add(w)
            for rd in self.readers.get(r, ()):
                deps.add(rd)
        if dkey is not None:
            p = self.last_dkey.get(dkey)
            if p is not None:
                deps.add(p)
            self.last_dkey[dkey] = i
        deps = set(d for d in deps if d > self.bar_idx)
        if eng in self.pending_bar:
            deps |= self.pending_bar.pop(eng)
        for r in reads:
            self.readers.setdefault(r, []).append(i)
        for r in writes:
            self.last_w[r] = i
            self.readers[r] = []
        deps.discard(i)
        red = {}
        for d in deps:
            o = self.ops[d]
            src = ("k", o.dkey) if o.dkey is not None else ("e", o.eng)
            if src == ("e", "pe") and eng == "pe" and dkey is None and d > self.bar_idx:
                continue
            if src not in red or red[src] < d:
                red[src] = d
        self.ops.append(_Op(eng, fn, sorted(red.values()), dkey))
        return i

    def barrier(self):
        last = {}
        for i in range(self.bar_idx + 1, len(self.ops)):
            o = self.ops[i]
            src = ("k", o.dkey) if o.dkey is not None else ("e", o.eng)
            last[src] = i
        deps = set(last.values())
        for d in deps:
            self.ops[d].sig = True
        self.flush()
        for e in self.engs:
            self.pending_bar[e] = set(deps) | self.pending_bar.get(e, set())
        self.bar_idx = len(self.ops) - 1
        self.free_sems.extend(self.key_sem.values())
        self.key_sem = {}

    def flush(self):
        nc = self.nc
        ops = self.ops
        for i in range(self.emitted, len(ops)):
            for d in ops[i].deps:
                ops[d].sig = True
        for i in range(self.emitted, len(ops)):
            o = ops[i]
            E = self.engs[o.eng]
            w = self.waited[o.eng]
            for d in o.deps:
                do = ops[d]
                sid = id(do.sem)
                if w.get(sid, 0) >= do.val:
                    continue
                E.wait_ge(do.sem, do.val)
                w[sid] = do.val
            inst = o.fn()
            o.fn = None
            self.ninst += 1
            if o.dkey is not None:
                if o.dkey not in self.key_sem:
                    if self.free_sems:
                        self.key_sem[o.dkey] = self.free_sems.pop()
                    else:
                        sem = self.stack.enter_context(nc.semaphore("k%d" % self.nsem))
                        self.nsem += 1
                        self.sem_cnt[id(sem)] = 0
                        self.key_sem[o.dkey] = sem
                o.sem = self.key_sem[o.dkey]
                self.sem_cnt[id(o.sem)] += 16
                o.val = self.sem_cnt[id(o.sem)]
                inst.then_inc(o.sem, 16)
            elif o.sig:
                if o.eng not in self.eng_sem or self.eng_cnt[o.eng] >= self.EPOCH:
                    self.eng_sem[o.eng] = self.stack.enter_context(nc.semaphore("e%d" % self.nsem))
                    self.nsem += 1
                    self.eng_cnt[o.eng] = 0
                self.eng_cnt[o.eng] += 1
                o.sem = self.eng_sem[o.eng]
                o.val = self.eng_cnt[o.eng]
                inst.then_inc(o.sem, 1)
        self.emitted = len(ops)

    def emit(self):
        self.flush()


class Tile:
    def __init__(self, t, key):
        self.t = t
        self.key = key

    def __getitem__(self, idx):
        return self.t[idx]


class Ring:
    def __init__(self, tiles):
        self.tiles = tiles
        self.i = 0

    def get(self):
        t = self.tiles[self.i % len(self.tiles)]
        self.i += 1
        return t


class Builder:
    def __init__(self, layers=(0, 1, 2, 3)):
        self.layers = tuple(layers)
        self.nc = bass.Bass("TRN2", target_bir_lowering=False)
        self.cnt = 0

    def sb(self, shape, dtype, name="t"):
        self.cnt += 1
        nm = "%s_%d" % (name, self.cnt)
        t = self.scope.enter_context(self.nc.sbuf_tensor(nm, list(shape), dtype))
        return Tile(t, nm)

    def ps(self, name="ps"):
        self.cnt += 1
        nm = "%s_%d" % (name, self.cnt)
        t = self.stack.enter_context(self.nc.psum_tensor(nm, [128, 512], F32))
        return Tile(t, nm)

    def ring(self, n, shape, dtype, name="r"):
        return Ring([self.sb(shape, dtype, name) for _ in range(n)])

    def din(self, name, shape):
        return self.nc.dram_tensor(name, list(shape), F32, kind="ExternalInput").ap()

    def op(self, eng, fn, r=(), w=(), dkey=None):
        return self.S.add(eng, fn, reads=[x.key if isinstance(x, Tile) else x for x in r],
                          writes=[x.key if isinstance(x, Tile) else x for x in w], dkey=dkey)

    def load(self, dst, dst_ap, src_ap, r=(), cast=False):
        nc = self.nc
        if cast:
            self.op("pool", lambda: nc.gpsimd.dma_start(out=dst_ap, in_=src_ap, max_dma_last_dim=4096),
                    r=r, w=[dst], dkey=dst.key)
        else:
            self.op("sp", lambda: nc.sync.dma_start(out=dst_ap, in_=src_ap), r=r, w=[dst], dkey=dst.key)

    def store(self, dram_key, dst_ap, src, src_ap):
        nc = self.nc
        self.op("sp", lambda: nc.sync.dma_start(out=dst_ap, in_=src_ap), r=[src], w=[dram_key], dkey=src.key)

    def mm(self, ps, out_ap, pairs, r=()):
        nc = self.nc
        n = len(pairs)
        for i, (lt, rh) in enumerate(pairs):
            self.op("pe", (lambda lt=lt, rh=rh, i=i: nc.tensor.matmul(out_ap, lt, rh, start=(i == 0), stop=(i == n - 1))),
                    r=r, w=[ps])

    def act(self, out_ap, in_ap, func, r=(), w=(), bias=None, scale=None):
        nc = self.nc
        kw = {}
        if bias is not None:
            kw["bias"] = bias
        if scale is not None:
            kw["scale"] = scale
        self.op("act", lambda: nc.scalar.activation(out=out_ap, in_=in_ap, func=func, **kw), r=r, w=w)

    def tt(self, out_ap, a, b, op, r=(), w=(), eng="dve"):
        nc = self.nc
        e = nc.vector if eng == "dve" else nc.gpsimd
        self.op(eng, lambda: e.tensor_tensor(out=out_ap, in0=a, in1=b, op=op), r=r, w=w)

    def ts(self, out_ap, a, s1, s2, op0, op1=None, r=(), w=(), eng="dve"):
        nc = self.nc
        e = nc.vector if eng == "dve" else nc.gpsimd
        if op1 is None:
            self.op(eng, lambda: e.tensor_scalar(out=out_ap, in0=a, scalar1=s1, scalar2=None, op0=op0), r=r, w=w)
        else:
            self.op(eng, lambda: e.tensor_scalar(out=out_ap, in0=a, scalar1=s1, scalar2=s2, op0=op0, op1=op1), r=r, w=w)

    def stt(self, out_ap, a, s, b, op0, op1, r=(), w=()):
        nc = self.nc
        self.op("dve", lambda: nc.vector.scalar_tensor_tensor(out=out_ap, in0=a, scalar=s, in1=b, op0=op0, op1=op1), r=r, w=w)

    def copy(self, out_ap, in_ap, r=(), w=(), eng="dve"):
        nc = self.nc
        if eng == "act":
            self.op("act", lambda: nc.scalar.copy(out=out_ap, in_=in_ap), r=r, w=w)
        else:
            e = nc.vector if eng == "dve" else nc.gpsimd
            self.op(eng, lambda: e.tensor_copy(out=out_ap, in_=in_ap), r=r, w=w)

    def memset(self, t, ap, val, eng="dve"):
        nc = self.nc
        e = nc.vector if eng == "dve" else nc.gpsimd
        self.op(eng, lambda: e.memset(ap, val), w=[t])

    def mm1(self, ps, out_ap, lhsT, rhs, start, stop, r=()):
        nc = self.nc
        self.op("pe", lambda: nc.tensor.matmul(out_ap, lhsT, rhs, start=start, stop=stop), r=r, w=[ps])

    def recip(self, out_ap, in_ap, r=(), w=()):
        nc = self.nc
        self.op("dve", lambda: nc.vector.reciprocal(out=out_ap, in_=in_ap), r=r, w=w)

    def new_scope(self):
        if self.cur_scope is not None:
            self.S.barrier()
            self.cur_scope.close()
        self.cur_scope = ExitStack()
        self.scope = self.cur_scope

    def common_rings(self, nxt=2, nh=2):
        self.XT = self.ring(nxt, [128, 8, 512], F32, "xt")
        self.SQ = self.ring(1, [128, 8, 512], BF16, "sq")
        self.SQ1 = self.ring(4, [128, 512], BF16, "sq1")
        self.RS = self.ring(2, [128, 512], F32, "rstd")
        self.TF = self.ring(5, [128, 512], F32, "tf")
        self.H = self.ring(nh, [128, 8, 512], BF16, "h")

    def build(self):
        nc = self.nc
        I = {}
        for k, shp in input_shapes().items():
            I[k] = self.din(k, shp)
        self.I = I
        self.out = nc.dram_tensor("outT", [D, L], F32, kind="ExternalOutput").ap()
        self.XS = nc.dram_tensor("xs", [D, T], F32).ap()
        self.AS = nc.dram_tensor("acts", [D, T], BF16).ap()
        self.XRS = nc.dram_tensor("xrs", [D, T], BF16).ap()
        self.xsrc_is_input = True
        with ExitStack() as stack:
            self.stack = stack
            self.scope = stack
            self.cur_scope = None
            self.S = Sched(nc, stack)
            self.PS = Ring([self.ps() for _ in range(6)])
            self.PA = Ring([self.ps() for _ in range(2)])
            self.setup_consts()
            for l in self.layers:
                self.layer(l)
            self.op("sp", lambda: nc.sync.nop(), r=["OUT%d" % i for i in range(8)])
            self.S.emit()
            if self.cur_scope is not None:
                self.cur_scope.close()
        return nc

    def xview(self, ap, t0, n):
        return ap.rearrange("(k p) t -> p k t", p=128)[:, :, t0:t0 + n]

    def load_x(self, ti):
        t0, n, isctx = TILES[ti]
        xt = self.XT.get()
        src = self.I["xin"] if self.xsrc_is_input else self.XS
        self.load(xt, xt[:, :, 0:n], self.xview(src, t0, n), r=["X%d" % ti])
        return xt

    def store_x(self, ti, xt, final=False):
        t0, n, isctx = TILES[ti]
        if final and not isctx:
            self.store("OUT%d" % ti, self.xview(self.out, t0, n), xt, xt[:, :, 0:n])
        else:
            self.store("X%d" % ti, self.xview(self.XS, t0, n), xt, xt[:, :, 0:n])

    def setup_consts(self):
        nc = self.nc
        I = self.I

        def cload(name, shape, dtype=BF16):
            t = self.sb(shape, dtype, name)
            self.load(t, t[:], I[name][:], cast=(dtype == BF16))
            return t
        self.ones1024 = cload("c_ones1024", [128, 128])
        self.blk64 = cload("c_blk64", [128, 128])
        self.ones384 = cload("c_ones384", [128, 128])
        self.ones256 = cload("c_ones256", [128, 128])
        self.ones96 = cload("c_ones96", [128, 128])
        self.perm64 = cload("c_perm64", [128, 128])
        self.perm96 = cload("c_perm96", [128, 128])
        self.epsT = self.sb([128, 1], F32, "eps")
        self.memset(self.epsT, self.epsT[:], EPS)
        self.nshift = self.sb([128, 1], F32, "nshift")
        self.memset(self.nshift, self.nshift[:], -SHIFT)
        self.one1 = self.sb([128, 1], F32, "one1")
        self.memset(self.one1, self.one1[:], 1.0)
        self.n1g = cload("n1g", [128, 4, 8], F32)
        self.n2g = cload("n2g", [128, 4, 8], F32)
        self.modb = cload("modb", [128, 4, 48], F32)
        self.fcw = cload("fcw", [128, 4, NJ, 3], F32)
        self.fcb = cload("fcb", [128, 4, NJ], F32)
        self.XB = self.sb([128, 8, 16], F32, "XB")
        self.MV = {l: self.sb([128, 6, 8, 2], F32, "mv") for l in self.layers}
        cond = cload("cond", [128, 8, 2], F32)
        condT = self.sb([128, 8, 2], F32, "condT")
        self.act(condT[:], cond[:], AF.Silu, r=[cond], w=[condT])
        self.new_scope()
        wr = self.ring(2, [128, 6144], F32, "modw")
        for l in self.layers:
            acc = self.sb([128, 48, 2], F32, "modacc")
            for k in range(8):
                wt = wr.get()
                self.load(wt, wt[:], I["modw"][l, :, k, :])
                pt = self.PS.get()
                for j in range(48):
                    self.mm1(pt, pt[:, 2 * j:2 * j + 2], wt[:, j * 128:(j + 1) * 128], condT[:, k, :], True, True,
                             r=[wt, condT])
                pv = pt[:, 0:96].rearrange("p (j s) -> p j s", s=2)
                if k == 0:
                    self.copy(acc[:], pv, r=[pt], w=[acc])
                else:
                    self.tt(acc[:], pv, acc[:], ALU.add, r=[pt, acc], w=[acc])
            mb = self.modb[:, l, :].unsqueeze(2).broadcast_to([128, 48, 2])
            self.tt(acc[:], acc[:], mb, ALU.add, r=[acc, self.modb], w=[acc])
            mv = self.MV[l]
            a4 = acc[:].rearrange("p (m k) s -> p m k s", m=6)
            for dst, srcm in ((1, 0), (2, 2), (4, 3), (5, 5)):
                self.copy(mv[:, dst], a4[:, srcm], r=[acc], w=[mv])
            for dst, srcm, g in ((0, 1, self.n1g), (3, 4, self.n2g)):
                tmp = self.sb([128, 8, 2], F32, "mtmp")
                self.ts(tmp[:], a4[:, srcm], 1.0, None, ALU.add, r=[acc], w=[tmp])
                gb = g[:, l, :].unsqueeze(2).broadcast_to([128, 8, 2])
                self.tt(mv[:, dst], tmp[:], gb, ALU.mult, r=[tmp, g], w=[mv])

    def modnorm(self, xt, n, l, which, s, h):
        mv = self.MV[l]
        sq = self.SQ.get()
        self.act(sq[:, :, 0:n], xt[:, :, 0:n], AF.Square, r=[xt], w=[sq])
        pt = self.PS.get()
        self.mm(pt, pt[:, 0:n], [(self.ones1024[:], sq[:, k, 0:n]) for k in range(8)], r=[sq, self.ones1024])
        rstd = self.RS.get()
        self.act(rstd[:, 0:n], pt[:, 0:n], AF.Ln, r=[pt, self.epsT], w=[rstd], bias=self.epsT[:, 0:1])
        self.act(rstd[:, 0:n], rstd[:, 0:n], AF.Exp, r=[rstd], w=[rstd], scale=-0.5)
        a_i, b_i = (0, 1) if which == 1 else (3, 4)
        for k in range(8):
            tmp = self.TF.get()
            self.stt(tmp[:, 0:n], xt[:, k, 0:n], mv[:, a_i, k, s:s + 1], rstd[:, 0:n], ALU.mult, ALU.mult,
                     r=[xt, mv, rstd], w=[tmp])
            self.act(h[:, k, 0:n], tmp[:, 0:n], AF.Identity, r=[tmp, mv], w=[h], bias=mv[:, b_i, k, s:s + 1])

    def headnorm(self, pt, rows, n, onesmat, gain_ap, gain_t, dst_t, dst_ap, rope=None):
        sq = self.SQ1.get()
        raw = self.TF.get()
        self.act(sq[0:rows, 0:n], pt[0:rows, 0:n], AF.Square, r=[pt], w=[sq])
        self.act(raw[0:rows, 0:n], pt[0:rows, 0:n], AF.Copy, r=[pt], w=[raw])
        pm = self.PS.get()
        self.mm(pm, pm[0:rows, 0:n], [(onesmat[0:rows, 0:rows], sq[0:rows, 0:n])], r=[sq, onesmat])
        rstd = self.RS.get()
        self.act(rstd[0:rows, 0:n], pm[0:rows, 0:n], AF.Ln, r=[pm, self.epsT], w=[rstd], bias=self.epsT[0:rows, 0:1])
        self.act(rstd[0:rows, 0:n], rstd[0:rows, 0:n], AF.Exp, r=[rstd], w=[rstd], scale=-0.5)
        if rope is None:
            self.stt(dst_ap, raw[0:rows, 0:n], gain_ap, rstd[0:rows, 0:n], ALU.mult, ALU.mult,
                     r=[raw, rstd, gain_t], w=[dst_t])
            return
        if len(rope) == 5:
            cs, sn, permT, r0, r1 = rope
            cs_ap, sn_ap = cs[:, 0:n], sn[:, 0:n]
        else:
            cs, sn, permT, r0, r1, cs_ap, sn_ap = rope
        qn = self.SQ1.get()
        self.stt(qn[0:rows, 0:n], raw[0:rows, 0:n], gain_ap, rstd[0:rows, 0:n], ALU.mult, ALU.mult,
                 r=[raw, rstd, gain_t], w=[qn])
        pw = self.PS.get()
        self.mm(pw, pw[0:rows, 0:n], [(permT[0:rows, 0:rows], qn[0:rows, 0:n])], r=[qn, permT])
        t1 = self.TF.get()
        t2 = self.TF.get()
        self.tt(t1[r0:r1, 0:n], qn[r0:r1, 0:n], cs_ap[r0:r1], ALU.mult, r=[qn, cs], w=[t1])
        self.tt(t2[r0:r1, 0:n], pw[r0:r1, 0:n], sn_ap[r0:r1], ALU.mult, r=[pw, sn], w=[t2])
        if r0 > 0:
            self.copy(dst_ap[0:r0], qn[0:r0, 0:n], r=[qn], w=[dst_t], eng="pool")
        self.tt(dst_ap[r0:r1], t1[r0:r1, 0:n], t2[r0:r1, 0:n], ALU.add, r=[t1, t2], w=[dst_t])

    def load_rope(self, csname, snname, ti):
        t0, n, isctx = TILES[ti]
        cs = self.ROPE.get()
        sn = self.ROPE.get()
        self.load(cs, cs[:, 0:n], self.I[csname][:, t0:t0 + n])
        self.load(sn, sn[:, 0:n], self.I[snname][:, t0:t0 + n])
        return cs, sn

    def resid(self, l, ti, xt, act_t, act_fn, wo):
        mv = self.MV[l]
        t0, n, isctx = TILES[ti]
        s = 1 if isctx else 0
        for oc in range(8):
            pt = self.PS.get()
            self.mm(pt, pt[:, 0:n], [(wo[:, k, oc * 128:(oc + 1) * 128], act_fn(k)) for k in range(8)], r=[act_t, wo])
            self.stt(xt[:, oc, 0:n], pt[:, 0:n], mv[:, 2, oc, s:s + 1], xt[:, oc, 0:n], ALU.mult, ALU.add,
                     r=[pt, mv, xt], w=[xt])
        if not isctx:
            self.copy(self.XB[:, :, 2 * ti:2 * ti + 1], xt[:, :, 0:1], r=[xt], w=[self.XB], eng="pool")
            self.copy(self.XB[:, :, 2 * ti + 1:2 * ti + 2], xt[:, :, n - 1:n], r=[xt], w=[self.XB], eng="pool")
        self.store_x(ti, xt)

    def phase_c_dram(self, l, wo_name_ap, tiles):
        self.new_scope()
        self.XT = self.ring(2, [128, 8, 512], F32, "xt")
        AT = self.ring(2, [128, 8, 512], BF16, "at")
        wo = self.sb([128, 8, 1024], BF16, "wo")
        for k in range(8):
            self.load(wo, wo[:, k, :], wo_name_ap[:, k, :], cast=True)
        for ti in tiles:
            t0, n, isctx = TILES[ti]
            xt = self.load_x(ti)
            at = AT.get()
            self.load(at, at[:, :, 0:n], self.xview(self.AS, t0, n), r=["AS%d" % ti])
            self.resid(l, ti, xt, at, (lambda k, at=at, n=n: at[:, k, 0:n]), wo)

    def layer(self, l):
        kind, idx = l % 3, l // 3
        last = (l == DEPTH - 1)
        final = (l == self.layers[-1])
        tiles = list(range(8)) if last else list(range(9))
        if kind == 0:
            self.swa(l, idx, tiles)
        elif kind == 1:
            self.mla(l, idx, tiles)
        else:
            self.lru(l, idx, tiles)
        self.xsrc_is_input = False
        self.ffn(l, tiles, final)

    def ffn(self, l, tiles, final):
        I = self.I
        mv = self.MV[l]
        self.new_scope()
        self.common_rings()
        WG = self.ring(3, [128, 8, 128], BF16, "wg")
        WV = self.ring(3, [128, 8, 128], BF16, "wv")
        WD = self.ring(1, [128, NJ, 1024], BF16, "wd")
        HID = self.ring(1, [128, NJ, 512], BF16, "hid")
        GB = self.sb([128, NJ, 16], F32, "gb")
        GT = self.ring(2, [128, 514], F32, "gt")
        HB = self.sb([128, 8, 16], BF16, "hb")
        self.modnorm(self.XB, 16, l, 2, 0, HB)
        for idx_t, ti in enumerate(tiles):
            t0, n, isctx = TILES[ti]
            s = 1 if isctx else 0
            xt = self.load_x(ti)
            h = self.H.get()
            self.modnorm(xt, n, l, 2, s, h)
            hid = HID.get()
            wd = WD.get()
            for j in range(NJ):
                wg = WG.get()
                wv = WV.get()
                self.load(wg, wg[:], I["wug"][l, j].rearrange("p (k m) -> p k m", k=8), cast=True)
                self.load(wv, wv[:], I["wuv"][l, j].rearrange("p (k m) -> p k m", k=8), cast=True)
                self.load(wd, wd[:, j, :], I["wdn"][l, j], cast=True)
                if idx_t == 0:
                    pb = self.PS.get()
                    self.mm(pb, pb[:, 0:16], [(wg[:, k, :], HB[:, k, :]) for k in range(8)], r=[wg, HB])
                    self.copy(GB[:, j, :], pb[:, 0:16], r=[pb], w=[GB])
                pg = self.PS.get()
                self.mm(pg, pg[:, 0:n], [(wg[:, k, :], h[:, k, 0:n]) for k in range(8)], r=[wg, h])
                pv = self.PS.get()
                self.mm(pv, pv[:, 0:n], [(wv[:, k, :], h[:, k, 0:n]) for k in range(8)], r=[wv, h])
                gt = GT.get()
                self.act(gt[:, 1:n + 1], pg[:, 0:n], AF.Copy, r=[pg], w=[gt])
                if (not isctx) and ti > 0:
                    self.copy(gt[:, 0:1], GB[:, j, 2 * (ti - 1) + 1:2 * (ti - 1) + 2], r=[GB], w=[gt], eng="pool")
                else:
                    self.memset(gt, gt[:, 0:1], 0.0, eng="pool")
                if (not isctx) and ti < 7:
                    self.copy(gt[:, n + 1:n + 2], GB[:, j, 2 * (ti + 1):2 * (ti + 1) + 1], r=[GB], w=[gt], eng="pool")
                else:
                    self.memset(gt, gt[:, n + 1:n + 2], 0.0, eng="pool")
                c1 = self.TF.get()
                self.ts(c1[:, 0:n], gt[:, 0:n], self.fcw[:, l, j, 0:1], self.fcb[:, l, j:j + 1], ALU.mult, ALU.add,
                        r=[gt, self.fcw, self.fcb], w=[c1])
                self.stt(c1[:, 0:n], gt[:, 1:n + 1], self.fcw[:, l, j, 1:2], c1[:, 0:n], ALU.mult, ALU.add,
                         r=[gt, c1, self.fcw], w=[c1])
                self.stt(c1[:, 0:n], gt[:, 2:n + 2], self.fcw[:, l, j, 2:3], c1[:, 0:n], ALU.mult, ALU.add,
                         r=[gt, c1, self.fcw], w=[c1])
                sl = self.TF.get()
                self.act(sl[:, 0:n], c1[:, 0:n], AF.Silu, r=[c1], w=[sl])
                self.tt(hid[:, j, 0:n], sl[:, 0:n], pv[:, 0:n], ALU.mult, r=[sl, pv], w=[hid])
            for oc in range(8):
                pt = self.PS.get()
                self.mm(pt, pt[:, 0:n], [(wd[:, j, oc * 128:(oc + 1) * 128], hid[:, j, 0:n]) for j in range(NJ)],
                        r=[wd, hid])
                self.stt(xt[:, oc, 0:n], pt[:, 0:n], mv[:, 5, oc, s:s + 1], xt[:, oc, 0:n], ALU.mult, ALU.add,
                         r=[pt, mv, xt], w=[xt])
            self.store_x(ti, xt, final=final)

    def swa(self, l, idx, tiles):
        nc = self.nc
        I = self.I
        self.new_scope()
        self.common_rings(nxt=1, nh=1)
        self.ROPE = self.ring(4, [128, 512], F32, "rope")
        wq = self.sb([128, 8, 1024], BF16, "wq")
        wk = self.sb([128, 8, 256], BF16, "wk")
        wv = self.sb([128, 8, 256], BF16, "wv")
        wo = self.sb([128, 8, 1024], BF16, "wo")
        for k in range(8):
            self.load(wq, wq[:, k, :], I["swq"][idx, :, k, :], cast=True)
            self.load(wo, wo[:, k, :], I["swo"][idx, :, k, :], cast=True)
        self.load(wk, wk[:], I["swk"][idx], cast=True)
        self.load(wv, wv[:], I["swv"][idx], cast=True)
        gq = self.sb([128, 1], F32, "gq")
        gk = self.sb([128, 1], F32, "gk")
        self.load(gq, gq[:], I["sqg"][idx])
        self.load(gk, gk[:], I["skg"][idx])
        es = self.sb([128, 16], F32, "es")
        self.load(es, es[:], I["ssink"][idx])
        self.act(es[:], es[:], AF.Exp, r=[es, self.nshift], w=[es], bias=self.nshift[:, 0:1])
        KT = self.sb([128, 2, T], BF16, "KT")
        VA = self.sb([128, 34, 4, 128], BF16, "VA")
        self.memset(VA, VA[:, :, :, 64:128], 1.0, eng="pool")
        QT = self.sb([128, 8, 512], BF16, "QT")
        OT = self.sb([128, 8, 512], BF16, "OT")
        PT = self.ring(3, [128, 512], BF16, "pt")
        DEN = self.ring(2, [128, 512], F32, "den")
        for ti in range(9):
            t0, n, isctx = TILES[ti]
            s = 1 if isctx else 0
            xt = self.load_x(ti)
            h = self.H.get()
            self.modnorm(xt, n, l, 1, s, h)
            rope = None
            if not isctx:
                cs, sn = self.load_rope("rcs", "rsn", ti)
                rope = (cs, sn, self.perm64, 0, 128)
            for c in range(2):
                pt = self.PS.get()
                self.mm(pt, pt[:, 0:n], [(wk[:, k, c * 128:(c + 1) * 128], h[:, k, 0:n]) for k in range(8)], r=[wk, h])
                self.headnorm(pt, 128, n, self.blk64, gk[:, 0:1], gk, KT, KT[:, c, t0:t0 + n], rope=rope)
            for tb in range(n // 128):
                pt = self.PS.get()
                self.mm(pt, pt[:, 0:256], [(h[:, k, tb * 128:(tb + 1) * 128], wv[:, k, :]) for k in range(8)], r=[wv, h])
                blk = (t0 // 128) + tb
                self.copy(VA[:, blk, :, 0:64], pt[:, 0:256].rearrange("p (g d) -> p g d", g=4), r=[pt], w=[VA], eng="act")
        for ti in tiles:
            t0, n, isctx = TILES[ti]
            s = 1 if isctx else 0
            xt = self.load_x(ti)
            h = self.H.get()
            self.modnorm(xt, n, l, 1, s, h)
            rope = None
            if not isctx:
                cs, sn = self.load_rope("rcs", "rsn", ti)
                rope = (cs, sn, self.perm64, 0, 128)
            for c in range(8):
                pt = self.PS.get()
                self.mm(pt, pt[:, 0:n], [(wq[:, k, c * 128:(c + 1) * 128], h[:, k, 0:n]) for k in range(8)], r=[wq, h])
                self.headnorm(pt, 128, n, self.blk64, gq[:, 0:1], gq, QT, QT[:, c, 0:n], rope=rope)
            for qb in range(n // 128):
                QB = t0 // 128 + qb
                if isctx:
                    kbs = [(32, 0), (33, 0)]
                else:
                    kbs = []
                    if QB > 0:
                        kbs.append((QB - 1, 1))
                    kbs.append((QB, 0))
                    if QB < 31:
                        kbs.append((QB + 1, 2))
                    kbs += [(32, 0), (33, 0)]
                for g in range(4):
                    base = 0 if g < 2 else 64
                    c0 = 4 * (g % 2)
                    kc = g % 2
                    po = self.PA.get()
                    rhs = QT[base:base + 64, c0:c0 + 4, qb * 128:(qb + 1) * 128]
                    for ki, (kb, mk) in enumerate(kbs):
                        ps_ = self.PS.get()
                        self.mm1(ps_, ps_[:], KT[base:base + 64, kc, kb * 128:(kb + 1) * 128], rhs, True, True, r=[KT, QT])
                        p = PT.get()
                        self.act(p[:], ps_[:], AF.Exp, r=[ps_, self.nshift], w=[p], bias=self.nshift[:, 0:1], scale=0.125)
                        if mk:
                            cm, st = (1, -1) if mk == 1 else (-1, 1)
                            self.op("pool", (lambda p=p, cm=cm, st=st: nc.gpsimd.affine_select(
                                out=p[:].rearrange("p (a b) -> p a b", a=4), in_=p[:].rearrange("p (a b) -> p a b", a=4),
                                pattern=[[0, 4], [st, 128]], compare_op=ALU.is_ge, fill=0.0, base=0, channel_multiplier=cm)),
                                r=[p], w=[p])
                        self.mm1(po, po[:], VA[:, kb, g, :], p[:], ki == 0, ki == len(kbs) - 1, r=[VA, p])
                    den = DEN.get()
                    esb = es[64:128, 4 * g:4 * g + 4].unsqueeze(2).broadcast_to([64, 4, 128])
                    self.tt(den[64:128, :].rearrange("p (a b) -> p a b", a=4), po[64:128, :].rearrange("p (a b) -> p a b", a=4),
                            esb, ALU.add, r=[po, es], w=[den])
                    self.recip(den[64:128, :], den[64:128, :], r=[den], w=[den])
                    self.tt(OT[base:base + 64, c0:c0 + 4, qb * 128:(qb + 1) * 128],
                            po[0:64, :].rearrange("p (a b) -> p a b", a=4),
                            den[64:128, :].rearrange("p (a b) -> p a b", a=4), ALU.mult, r=[po, den], w=[OT])
            self.resid(l, ti, xt, OT, (lambda k, n=n: OT[:, k, 0:n]), wo)

    def mla(self, l, idx, tiles):
        I = self.I
        self.new_scope()
        CQN = self.sb([128, 3, T], BF16, "CQN")
        CKVN = self.sb([128, 2, T], BF16, "CKVN")
        KRSQ = self.sb([128, T], BF16, "KRSQ")
        KRROT = self.sb([128, T], BF16, "KRROT")
        gv = self.sb([128, 8], F32, "mg")
        self.load(gv, gv[:], I["mgv"][:])
        persist = self.cur_scope
        self.cur_scope = None
        self.new_scope()
        self.common_rings(nxt=2, nh=1)
        self.ROPE = self.ring(4, [128, 512], F32, "rope")
        wdn = self.sb([128, 8, 640], BF16, "mdn")
        wrp = self.sb([128, 8, 96], BF16, "mrp")
        KRG = self.ring(2, [128, 512], BF16, "krg")
        for kt in KRG.tiles:
            self.memset(kt, kt[:], 0.0)
        for k in range(8):
            self.load(wdn, wdn[:, k, :], I["mdn"][:, k, :], cast=True)
        self.load(wrp, wrp[:], I["mrp"][:], cast=True)
        for ti in range(9):
            t0, n, isctx = TILES[ti]
            s = 1 if isctx else 0
            xt = self.load_x(ti)
            h = self.H.get()
            self.modnorm(xt, n, l, 1, s, h)
            for (nch, coff, gcol, onesm, dstT) in ((3, 0, 0, self.ones384, CQN), (2, 384, 3, self.ones256, CKVN)):
                raws, sqs = [], []
                for c in range(nch):
                    pt = self.PS.get()
                    self.mm(pt, pt[:, 0:n], [(wdn[:, k, coff + c * 128:coff + (c + 1) * 128], h[:, k, 0:n]) for k in range(8)],
                            r=[wdn, h])
                    sq = self.SQ1.get()
                    raw = self.TF.get()
                    self.act(sq[:, 0:n], pt[:, 0:n], AF.Square, r=[pt], w=[sq])
                    self.act(raw[:, 0:n], pt[:, 0:n], AF.Copy, r=[pt], w=[raw])
                    raws.append(raw)
                    sqs.append(sq)
                pm = self.PS.get()
                self.mm(pm, pm[:, 0:n], [(onesm[:], sq[:, 0:n]) for sq in sqs], r=sqs + [onesm])
                rstd = self.RS.get()
                self.act(rstd[:, 0:n], pm[:, 0:n], AF.Ln, r=[pm, self.epsT], w=[rstd], bias=self.epsT[:, 0:1])
                self.act(rstd[:, 0:n], rstd[:, 0:n], AF.Exp, r=[rstd], w=[rstd], scale=-0.5)
                for c in range(nch):
                    self.stt(dstT[:, c, t0:t0 + n], raws[c][:, 0:n], gv[:, gcol + c:gcol + c + 1], rstd[:, 0:n],
                             ALU.mult, ALU.mult, r=[raws[c], rstd, gv], w=[dstT])
            pk = self.PS.get()
            self.mm(pk, pk[0:96, 0:n], [(wrp[:, k, :], h[:, k, 0:n]) for k in range(8)], r=[wrp, h])
            self.act(KRSQ[64:96, t0:t0 + n], pk[64:96, 0:n], AF.Square, r=[pk], w=[KRSQ])
            krg = KRG.get()
            self.ts(krg[64:96, 0:n], pk[64:96, 0:n], gv[64:96, 6:7], None, ALU.mult, r=[pk, gv], w=[krg])
            if isctx:
                self.copy(KRROT[64:96, t0:t0 + n], krg[64:96, 0:n], r=[krg], w=[KRROT])
            else:
                cs, sn = self.load_rope("mcs", "msn", ti)
                pw = self.PS.get()
                self.mm(pw, pw[0:96, 0:n], [(self.perm96[0:96, 0:96], krg[0:96, 0:n])], r=[krg, self.perm96])
                t1 = self.TF.get()
                t2 = self.TF.get()
                self.tt(t1[64:96, 0:n], krg[64:96, 0:n], cs[64:96, 0:n], ALU.mult, r=[krg, cs], w=[t1])
                self.tt(t2[64:96, 0:n], pw[64:96, 0:n], sn[64:96, 0:n], ALU.mult, r=[pw, sn], w=[t2])
                self.tt(KRROT[64:96, t0:t0 + n], t1[64:96, 0:n], t2[64:96, 0:n], ALU.add, r=[t1, t2], w=[KRROT])
        self.new_scope()
        RALL = self.sb([128, 2, L], F32, "ropeall")
        self.load(RALL, RALL[:, 0, :], I["mcs"][:, :])
        self.load(RALL, RALL[:, 1, :], I["msn"][:, :])
        wuq = self.sb([128, 3, 1536], BF16, "muq")
        wuk = self.sb([128, 2, 1024], BF16, "muk")
        wuv = self.sb([128, 2, 1024], BF16, "muv")
        for k in range(3):
            self.load(wuq, wuq[:, k, :], I["muq"][:, k, :], cast=True)
        for k in range(2):
            self.load(wuk, wuk[:, k, :], I["muk"][:, k, :], cast=True)
            self.load(wuv, wuv[:, k, :], I["muv"][:, k, :], cast=True)
        KTH = self.ring(2, [128, T], BF16, "KTH")
        VH = self.ring(2, [128, 34, 128], BF16, "VH")
        QTH = self.ring(2, [128, 512], BF16, "QTH")
        PT = self.ring(4, [128, 512], BF16, "pt")
        DEN = self.ring(2, [128, 512], F32, "den")
        OS = self.ring(3, [128, 512], BF16, "os")
        qSQ = self.ring(2, [128, 512], BF16, "qsq")
        qTF = self.ring(3, [128, 512], F32, "qtf")
        qRS = self.ring(1, [128, 512], F32, "qrs")
        kSQ = self.ring(2, [128, 512], BF16, "ksq")
        kTF = self.ring(2, [128, 512], F32, "ktf")
        kRS = self.ring(2, [128, 512], F32, "krs")
        for v in VH.tiles:
            self.memset(v, v[:, :, 64:128], 1.0, eng="pool")
        scale = 96.0 ** -0.5
        PSS = Ring(self.PS.tiles[0:3])
        PSG = Ring(self.PS.tiles[3:6])

        def kv_steps(hd, kth, vh):
            for ti in range(9):
                t0, n, isctx = TILES[ti]
                pk = PSG.get()
                self.mm(pk, pk[0:64, 0:n], [(wuk[:, c2, hd * 64:(hd + 1) * 64], CKVN[:, c2, t0:t0 + n]) for c2 in range(2)],
                        r=[wuk, CKVN])
                sq = kSQ.get()
                raw = kTF.get()
                self.act(sq[0:64, 0:n], pk[0:64, 0:n], AF.Square, r=[pk], w=[sq])
                self.act(raw[0:64, 0:n], pk[0:64, 0:n], AF.Copy, r=[pk], w=[raw])
                self.copy(sq[64:96, 0:n], KRSQ[64:96, t0:t0 + n], r=[KRSQ], w=[sq], eng="pool")
                yield
                pm = PSG.get()
                self.mm(pm, pm[0:96, 0:n], [(self.ones96[0:96, 0:96], sq[0:96, 0:n])], r=[sq, self.ones96])
                rstd = kRS.get()
                self.act(rstd[0:96, 0:n], pm[0:96, 0:n], AF.Ln, r=[pm, self.epsT], w=[rstd], bias=self.epsT[0:96, 0:1])
                self.act(rstd[0:96, 0:n], rstd[0:96, 0:n], AF.Exp, r=[rstd], w=[rstd], scale=-0.5)
                self.stt(kth[0:64, t0:t0 + n], raw[0:64, 0:n], gv[0:64, 6:7], rstd[0:64, 0:n], ALU.mult, ALU.mult,
                         r=[raw, rstd, gv], w=[kth])
                self.tt(kth[64:96, t0:t0 + n], KRROT[64:96, t0:t0 + n], rstd[64:96, 0:n], ALU.mult,
                        r=[KRROT, rstd], w=[kth])
                for tb in range(n // 128):
                    pv = PSG.get()
                    self.mm(pv, pv[:, 0:64], [(CKVN[:, c2, t0 + tb * 128:t0 + (tb + 1) * 128], wuv[:, c2, hd * 64:(hd + 1) * 64])
                                              for c2 in range(2)], r=[wuv, CKVN])
                    self.copy(vh[:, t0 // 128 + tb, 0:64], pv[:, 0:64], r=[pv], w=[vh], eng="act")
                yield

        def q_steps(hd, ti, qth):
            t0, n, isctx = TILES[ti]
            pq = PSG.get()
            self.mm(pq, pq[0:96, 0:n], [(wuq[:, c3, hd * 96:(hd + 1) * 96], CQN[:, c3, t0:t0 + n]) for c3 in range(3)],
                    r=[wuq, CQN])
            sq = qSQ.get()
            raw = qTF.get()
            self.act(sq[0:96, 0:n], pq[0:96, 0:n], AF.Square, r=[pq], w=[sq])
            self.act(raw[0:96, 0:n], pq[0:96, 0:n], AF.Copy, r=[pq], w=[raw])
            yield
            pm = PSG.get()
            self.mm(pm, pm[0:96, 0:n], [(self.ones96[0:96, 0:96], sq[0:96, 0:n])], r=[sq, self.ones96])
            rstd = qRS.get()
            self.act(rstd[0:96, 0:n], pm[0:96, 0:n], AF.Ln, r=[pm, self.epsT], w=[rstd], bias=self.epsT[0:96, 0:1])
            self.act(rstd[0:96, 0:n], rstd[0:96, 0:n], AF.Exp, r=[rstd], w=[rstd], scale=-0.5)
            if isctx:
                self.stt(qth[0:96, 0:n], raw[0:96, 0:n], gv[0:96, 5:6], rstd[0:96, 0:n], ALU.mult, ALU.mult,
                         r=[raw, rstd, gv], w=[qth])
                return
            qn = qSQ.get()
            self.stt(qn[0:96, 0:n], raw[0:96, 0:n], gv[0:96, 5:6], rstd[0:96, 0:n], ALU.mult, ALU.mult,
                     r=[raw, rstd, gv], w=[qn])
            yield
            pw = PSG.get()
            self.mm(pw, pw[0:96, 0:n], [(self.perm96[0:96, 0:96], qn[0:96, 0:n])], r=[qn, self.perm96])
            t1 = qTF.get()
            t2 = qTF.get()
            self.tt(t1[64:96, 0:n], qn[64:96, 0:n], RALL[64:96, 0, t0:t0 + n], ALU.mult, r=[qn, RALL], w=[t1])
            self.tt(t2[64:96, 0:n], pw[64:96, 0:n], RALL[64:96, 1, t0:t0 + n], ALU.mult, r=[pw, RALL], w=[t2])
            self.copy(qth[0:64, 0:n], qn[0:64, 0:n], r=[qn], w=[qth], eng="pool")
            self.tt(qth[64:96, 0:n], t1[64:96, 0:n], t2[64:96, 0:n], ALU.add, r=[t1, t2], w=[qth])

        def step(g):
            if g is None:
                return None
            try:
                next(g)
                return g
            except StopIteration:
                return None

        def drain(g):
            while g is not None:
                g = step(g)

        units = [(hd, ti) for hd in range(16) for ti in tiles]
        kv_cur = (KTH.get(), VH.get())
        drain(kv_steps(0, kv_cur[0], kv_cur[1]))
        qth = QTH.get()
        drain(q_steps(units[0][0], units[0][1], qth))
        kv_gen = None
        kv_next = None
        LOOK = 2
        for ui, (hd, ti) in enumerate(units):
            t0, n, isctx = TILES[ti]
            if ti == tiles[0]:
                kth, vh = kv_cur
                if hd + 1 < 16:
                    kv_next = (KTH.get(), VH.get())
                    kv_gen = kv_steps(hd + 1, kv_next[0], kv_next[1])
            q_gen = None
            qth_next = None
            if ui + 1 < len(units):
                qth_next = QTH.get()
                q_gen = q_steps(units[ui + 1][0], units[ui + 1][1], qth_next)
            kbs = [32, 33] if isctx else list(range(34))
            nk = len(kbs)
            po = self.PA.get()
            pend = {}

            def issue_s(ki, kth=kth, qth=qth, n=n, kbs=kbs):
                ps_ = PSS.get()
                kb = kbs[ki]
                self.mm1(ps_, ps_[:, 0:n], kth[0:96, kb * 128:(kb + 1) * 128], qth[0:96, 0:n], True, True, r=[kth, qth])
                return ps_
            for ki in range(min(LOOK, nk)):
                pend[ki] = issue_s(ki)
            for ki in range(nk):
                ps_ = pend.pop(ki)
                p = PT.get()
                self.act(p[:, 0:n], ps_[:, 0:n], AF.Exp, r=[ps_, self.nshift], w=[p], bias=self.nshift[:, 0:1], scale=scale)
                if ki + LOOK < nk:
                    pend[ki + LOOK] = issue_s(ki + LOOK)
                self.mm1(po, po[:, 0:n], vh[:, kbs[ki], :], p[:, 0:n], ki == 0, ki == nk - 1, r=[vh, p])
                if ki % 8 == 3:
                    q_gen = step(q_gen)
                if ki % 8 == 7:
                    kv_gen = step(kv_gen)
            drain(q_gen)
            den = DEN.get()
            self.recip(den[64:128, 0:n], po[64:128, 0:n], r=[po], w=[den])
            hb = (hd % 2) * 64
            os_ = OS.get()
            self.tt(os_[hb:hb + 64, 0:n], po[0:64, 0:n], den[64:128, 0:n], ALU.mult, r=[po, den], w=[os_])
            dst = self.AS.rearrange("(k p) t -> p k t", p=128)[hb:hb + 64, hd // 2, t0:t0 + n]
            self.store("AS%d" % ti, dst, os_, os_[hb:hb + 64, 0:n])
            qth = qth_next
            if ti == tiles[-1]:
                drain(kv_gen)
                kv_gen = None
                kv_cur = kv_next
        self.S.barrier()
        self.cur_scope.close()
        persist.close()
        self.cur_scope = None
        self.phase_c_dram(l, I["mwo"], tiles)

    def lru(self, l, idx, tiles):
        nc = self.nc
        I = self.I
        PADL = 2
        CTX0 = L + 6
        XW = T + 8

        def pcol(t0):
            return t0 + PADL if t0 < L else (t0 - L) + CTX0
        self.new_scope()
        self.common_rings(nxt=2, nh=2)
        win = self.sb([128, 8, 2048], BF16, "lwin")
        for k in range(8):
            self.load(win, win[:, k, :], I["lwin"][:, k, :], cast=True)
        STG = self.ring(4, [128, 512], BF16, "stg")
        for ti in range(9):
            t0, n, isctx = TILES[ti]
            s = 1 if isctx else 0
            xt = self.load_x(ti)
            h = self.H.get()
            self.modnorm(xt, n, l, 1, s, h)
            for oc in range(16):
                pt = self.PS.get()
                self.mm(pt, pt[:, 0:n], [(win[:, k, oc * 128:(oc + 1) * 128], h[:, k, 0:n]) for k in range(8)], r=[win, h])
                stg = STG.get()
                if oc < 8:
                    self.act(stg[:, 0:n], pt[:, 0:n], AF.Gelu_apprx_tanh, r=[pt], w=[stg])
                    self.store(("AS", oc, ti), self.AS[oc * 128:(oc + 1) * 128, t0:t0 + n], stg, stg[:, 0:n])
                else:
                    self.copy(stg[:, 0:n], pt[:, 0:n], r=[pt], w=[stg])
                    self.store(("XRS", oc - 8), self.XRS[(oc - 8) * 128:(oc - 7) * 128, t0:t0 + n], stg, stg[:, 0:n])
        self.new_scope()
        self.TF = self.ring(6, [128, 512], F32, "tf")
        gw = self.sb([128, 2, 2, 4, 2, 256], BF16, "lgw")
        for d in range(2):
            for wch in range(2):
                self.load(gw, gw[:, d, wch], I["lgw"][:, d, wch], cast=True)
        sv = self.sb([128, 64], F32, "lsv")
        self.load(sv, sv[:], I["lsv"][:])
        ngb = self.sb([128, 32], F32, "lngb")
        self.load(ngb, ngb[:], I["lgb"][:])
        self.ts(ngb[:], ngb[:], -1.0, None, ALU.mult, r=[ngb], w=[ngb])
        cp = self.sb([128, 16], F32, "lcp")
        cp2 = self.sb([128, 16], F32, "lcp2")
        self.act(cp[:], sv[:, 40:56], AF.Exp, r=[sv], w=[cp], scale=-1.0)
        self.act(cp[:], cp[:], AF.Ln, r=[cp, self.one1], w=[cp], bias=self.one1[:, 0:1])
        self.ts(cp2[:], cp[:], -16.0, None, ALU.mult, r=[cp], w=[cp2])
        self.ts(cp[:], cp[:], -8.0, None, ALU.mult, r=[cp], w=[cp])
        XR = self.sb([128, 2, XW], BF16, "XR")
        self.memset(XR, XR[:, :, 0:PADL], 0.0, eng="pool")
        self.memset(XR, XR[:, :, L + PADL:CTX0], 0.0, eng="pool")
        self.memset(XR, XR[:, :, CTX0 + CT:XW], 0.0, eng="pool")
        XC = self.sb([128, 2, XW], BF16, "XC")
        SF = self.sb([128, XW], F32, "SF")
        CAR = self.ring(4, [128, 1], F32, "car")
        GT_ = self.ring(3, [128, 512], BF16, "gtile")
        zero1 = self.sb([128, 1], F32, "zero1")
        self.memset(zero1, zero1[:], 0.0)
        NW = XW - 4
        for bk in range(4):
            for cc in range(2):
                c = 2 * bk + cc
                self.load(XR, XR[:, cc, PADL:PADL + L], self.XRS[c * 128:(c + 1) * 128, 0:L], r=[("XRS", c)])
                self.load(XR, XR[:, cc, CTX0:CTX0 + CT], self.XRS[c * 128:(c + 1) * 128, L:T], r=[("XRS", c)])
            for cc in range(2):
                c = 2 * bk + cc
                acc = SF
                self.ts(acc[:, 0:NW], XR[:, cc, 0:NW], sv[:, 8 + 4 * c:9 + 4 * c], sv[:, c:c + 1], ALU.mult, ALU.add,
                        r=[XR, sv], w=[acc])
                for k in (1, 2):
                    self.stt(acc[:, 0:NW], XR[:, cc, k:k + NW], sv[:, 8 + 4 * c + k:9 + 4 * c + k], acc[:, 0:NW], ALU.mult, ALU.add,
                             r=[XR, sv, acc], w=[acc])
                self.stt(XC[:, cc, 2:2 + NW], XR[:, cc, 3:3 + NW], sv[:, 8 + 4 * c + 3:9 + 4 * c + 3], acc[:, 0:NW], ALU.mult, ALU.add,
                         r=[XR, sv, acc], w=[XC])
            for cc in range(2):
                c = 2 * bk + cc
                for d in range(2):
                    order = [8] + (list(range(8)) if d == 0 else list(range(7, -1, -1)))
                    carry = zero1
                    for ti in order:
                        t0, n, isctx = TILES[ti]
                        pc = pcol(t0)
                        pr = self.PS.get()
                        self.mm(pr, pr[:, 0:n], [(gw[:, d, 0, bk, kk, cc * 128:(cc + 1) * 128], XC[:, kk, pc:pc + n]) for kk in range(2)],
                                r=[gw, XC])
                        pi = self.PS.get()
                        self.mm(pi, pi[:, 0:n], [(gw[:, d, 1, bk, kk, cc * 128:(cc + 1) * 128], XC[:, kk, pc:pc + n]) for kk in range(2)],
                                r=[gw, XC])
                        gi = (d * 2 + 0) * 8 + c
                        gi2 = (d * 2 + 1) * 8 + c
                        ta = self.TF.get()
                        tb_ = self.TF.get()
                        tcc = self.TF.get()
                        self.act(ta[:, 0:n], pr[:, 0:n], AF.Exp, r=[pr, ngb], w=[ta], bias=ngb[:, gi:gi + 1], scale=-1.0)
                        self.act(ta[:, 0:n], ta[:, 0:n], AF.Ln, r=[ta, self.one1], w=[ta], bias=self.one1[:, 0:1])
                        self.act(ta[:, 0:n], ta[:, 0:n], AF.Exp, r=[ta], w=[ta], scale=-1.0)
                        self.act(tb_[:, 0:n], ta[:, 0:n], AF.Exp, r=[ta, cp], w=[tb_], scale=cp[:, d * 8 + c:d * 8 + c + 1])
                        self.act(ta[:, 0:n], ta[:, 0:n], AF.Exp, r=[ta, cp2], w=[ta], scale=cp2[:, d * 8 + c:d * 8 + c + 1])
                        self.ts(ta[:, 0:n], ta[:, 0:n], 0.99999994, None, ALU.min, r=[ta], w=[ta])
                        self.act(ta[:, 0:n], ta[:, 0:n], AF.Ln, r=[ta, self.one1], w=[ta], bias=self.one1[:, 0:1], scale=-1.0)
                        self.act(ta[:, 0:n], ta[:, 0:n], AF.Exp, r=[ta], w=[ta], scale=0.5)
                        self.act(tcc[:, 0:n], pi[:, 0:n], AF.Exp, r=[pi, ngb], w=[tcc], bias=ngb[:, gi2:gi2 + 1], scale=-1.0)
                        self.act(tcc[:, 0:n], tcc[:, 0:n], AF.Ln, r=[tcc, self.one1], w=[tcc], bias=self.one1[:, 0:1])
                        self.act(tcc[:, 0:n], tcc[:, 0:n], AF.Exp, r=[tcc], w=[tcc], scale=-1.0)
                        self.tt(tcc[:, 0:n], tcc[:, 0:n], XC[:, cc, pc:pc + n], ALU.mult, r=[tcc, XC], w=[tcc])
                        self.tt(tcc[:, 0:n], tcc[:, 0:n], ta[:, 0:n], ALU.mult, r=[tcc, ta], w=[tcc])
                        so = self.TF.get()
                        ncar = CAR.get()
                        if d == 0:
                            self.op("dve", (lambda so=so, tb_=tb_, tcc=tcc, n=n, carry=carry: nc.vector.tensor_tensor_scan(
                                out=so[:, 0:n], data0=tb_[:, 0:n], data1=tcc[:, 0:n], initial=carry[:, 0:1],
                                op0=ALU.mult, op1=ALU.add)), r=[tb_, tcc, carry], w=[so])
                            self.copy(ncar[:, 0:1], so[:, n - 1:n], r=[so], w=[ncar])
                            self.copy(SF[:, t0:t0 + n], so[:, 0:n], r=[so], w=[SF], eng="pool")
                        else:
                            self.op("dve", (lambda so=so, tb_=tb_, tcc=tcc, n=n, carry=carry: nc.vector.tensor_tensor_scan(
                                out=so[:, 0:n][:, ::-1], data0=tb_[:, 0:n][:, ::-1], data1=tcc[:, 0:n][:, ::-1], initial=carry[:, 0:1],
                                op0=ALU.mult, op1=ALU.add)), r=[tb_, tcc, carry], w=[so])
                            self.copy(ncar[:, 0:1], so[:, 0:1], r=[so], w=[ncar])
                            self.tt(so[:, 0:n], so[:, 0:n], SF[:, t0:t0 + n], ALU.add, r=[so, SF], w=[so])
                            gtile = GT_.get()
                            asl = self.AS[c * 128:(c + 1) * 128, t0:t0 + n]
                            self.load(gtile, gtile[:, 0:n], asl, r=[("AS", c, ti)])
                            self.tt(gtile[:, 0:n], so[:, 0:n], gtile[:, 0:n], ALU.mult, r=[so, gtile], w=[gtile])
                            self.store(("AS", c, ti), asl, gtile, gtile[:, 0:n])
                        carry = ncar
        self.phase_c_dram_multi(l, I["lwout"], tiles)

    def phase_c_dram_multi(self, l, wo_ap, tiles):
        self.new_scope()
        self.XT = self.ring(2, [128, 8, 512], F32, "xt")
        AT = self.ring(2, [128, 8, 512], BF16, "at")
        wo = self.sb([128, 8, 1024], BF16, "wo")
        for k in range(8):
            self.load(wo, wo[:, k, :], wo_ap[:, k, :], cast=True)
        for ti in tiles:
            t0, n, isctx = TILES[ti]
            xt = self.load_x(ti)
            at = AT.get()
            self.load(at, at[:, :, 0:n], self.xview(self.AS, t0, n), r=[("AS", c, ti) for c in range(8)])
            self.resid(l, ti, xt, at, (lambda k, at=at, n=n: at[:, k, 0:n]), wo)


def input_shapes():
    return {
        "xin": (D, T), "cond": (128, 8, 2), "n1g": (128, 4, 8), "n2g": (128, 4, 8),
        "modw": (4, 128, 8, 6144), "modb": (128, 4, 48),
        "wug": (4, NJ, 128, 1024), "wuv": (4, NJ, 128, 1024), "wdn": (4, NJ, 128, 1024),
        "fcw": (128, 4, NJ, 3), "fcb": (128, 4, NJ),
        "swq": (2, 128, 8, 1024), "swk": (2, 128, 8, 256), "swv": (2, 128, 8, 256), "swo": (2, 128, 8, 1024),
        "sqg": (2, 128, 1), "skg": (2, 128, 1), "ssink": (2, 128, 16),
        "rcs": (128, L), "rsn": (128, L), "mcs": (128, L), "msn": (128, L),
        "mdn": (128, 8, 640), "mrp": (128, 8, 96), "muq": (128, 3, 1536), "muk": (128, 2, 1024), "muv": (128, 2, 1024),
        "mwo": (128, 8, 1024), "mgv": (128, 8),
        "lwin": (128, 8, 2048), "lwout": (128, 8, 1024), "lgw": (128, 2, 2, 4, 2, 256), "lsv": (128, 64), "lgb": (128, 32),
        "c_ones1024": (128, 128), "c_blk64": (128, 128), "c_ones384": (128, 128), "c_ones256": (128, 128),
        "c_ones96": (128, 128), "c_perm64": (128, 128), "c_perm96": (128, 128),
    }


def _fm(v, nch):
    return np.ascontiguousarray(np.asarray(v, np.float32).reshape(nch, 128).T)


def _wfm(w):
    K, N = w.shape
    return np.ascontiguousarray(np.asarray(w, np.float32).reshape(K // 128, 128, N).transpose(1, 0, 2))


def _rope_tables(rot_dim, base_part, nrows):
    n_freq = rot_dim // 4
    half = rot_dim // 2
    t = np.arange(L)
    row = (t // 64).astype(np.float32)
    col = (t % 64).astype(np.float32)
    inv = (np.float32(10000.0) ** (-np.arange(n_freq, dtype=np.float32) / np.float32(n_freq))).astype(np.float32)
    cs = np.zeros((128, L), np.float32)
    sn = np.zeros((128, L), np.float32)
    partner = np.zeros(128, np.int64) - 1
    for p in range(nrows):
        d = p % rot_dim if base_part == 0 else p
        pp = p + base_part
        dd = d % rot_dim
        pos = row if dd < half else col
        e = dd % half
        i = e % n_freq
        ang = (pos * inv[i]).astype(np.float32)
        cs[pp] = np.cos(ang)
        sgn = -1.0 if e < n_freq else 1.0
        sn[pp] = sgn * np.sin(ang)
        partner[pp] = pp + n_freq if e < n_freq else pp - n_freq
    return cs, sn, partner


def prepare_shared(inp):
    f = lambda k: np.asarray(inp[k], np.float32)
    sh = {}
    sh["n1g"] = np.ascontiguousarray(f("norm1").reshape(4, 8, 128).transpose(2, 0, 1))
    sh["n2g"] = np.ascontiguousarray(f("norm2").reshape(4, 8, 128).transpose(2, 0, 1))
    sh["modw"] = np.ascontiguousarray(f("mod_w").reshape(4, 8, 128, 6144).transpose(0, 2, 1, 3))
    sh["modb"] = np.ascontiguousarray(f("mod_b").reshape(4, 48, 128).transpose(2, 0, 1))
    wup = f("ffn_w_up")
    def upl(w):
        return np.ascontiguousarray(w.reshape(4, 8, 128, NJ, 128).transpose(0, 3, 2, 1, 4).reshape(4, NJ, 128, 1024))
    sh["wug"] = upl(wup[:, :, :DFF])
    sh["wuv"] = upl(wup[:, :, DFF:])
    sh["wdn"] = np.ascontiguousarray(f("ffn_w_down").reshape(4, NJ, 128, 1024))
    sh["fcw"] = np.ascontiguousarray(f("ffn_conv_w").reshape(4, 3, NJ, 128).transpose(3, 0, 2, 1))
    sh["fcb"] = np.ascontiguousarray(f("ffn_conv_b").reshape(4, NJ, 128).transpose(2, 0, 1))
    wqkv = f("swa_w_qkv")
    wq = wqkv[:, :, :1024].reshape(2, 1024, 2, 8, 64).transpose(0, 1, 3, 2, 4).reshape(2, 1024, 1024)
    wk = wqkv[:, :, 1024:1280].reshape(2, 1024, 2, 2, 64).transpose(0, 1, 3, 2, 4).reshape(2, 1024, 256)
    wv = wqkv[:, :, 1280:1536]
    sh["swq"] = np.stack([_wfm(wq[i]) for i in range(2)])
    sh["swk"] = np.stack([_wfm(wk[i]) for i in range(2)])
    sh["swv"] = np.stack([_wfm(wv[i]) for i in range(2)])
    wo = f("swa_w_o").reshape(2, 2, 8, 64, 1024).transpose(0, 2, 1, 3, 4).reshape(2, 1024, 1024)
    sh["swo"] = np.stack([_wfm(wo[i]) for i in range(2)])
    sh["sqg"] = np.ascontiguousarray(np.tile(f("swa_q_gain"), (1, 2)).reshape(2, 128, 1))
    sh["skg"] = np.ascontiguousarray(np.tile(f("swa_k_gain"), (1, 2)).reshape(2, 128, 1))
    sh["ssink"] = np.ascontiguousarray(np.broadcast_to(f("swa_sink")[:, None, :], (2, 128, 16)))
    cs, sn, partner = _rope_tables(64, 0, 128)
    sh["rcs"], sh["rsn"] = cs, sn
    pm = np.zeros((128, 128), np.float32)
    for m in range(128):
        pm[partner[m], m] = 1.0
    sh["c_perm64"] = pm
    cs, sn, partner = _rope_tables(32, 64, 32)
    sh["mcs"], sh["msn"] = cs, sn
    pm = np.zeros((128, 128), np.float32)
    for m in range(64, 96):
        pm[partner[m], m] = 1.0
    sh["c_perm96"] = pm
    wd = f("mla_w_down")[0]
    sh["mdn"] = _wfm(wd[:, :640])
    wr = np.zeros((1024, 96), np.float32)
    wr[:, 64:96] = wd[:, 640:672]
    sh["mrp"] = _wfm(wr)
    sh["muq"] = _wfm(f("mla_w_uq")[0])
    sh["muk"] = _wfm(f("mla_w_uk")[0])
    sh["muv"] = _wfm(f("mla_w_uv")[0])
    sh["mwo"] = _wfm(f("mla_w_o")[0])
    gv = np.zeros((128, 8), np.float32)
    gv[:, 0:3] = _fm(f("mla_q_lora_gain")[0], 3)
    gv[:, 3:5] = _fm(f("mla_kv_lora_gain")[0], 2)
    gv[0:96, 5] = f("mla_q_gain")[0]
    gv[0:96, 6] = f("mla_k_gain")[0]
    sh["mgv"] = gv
    sh["lwin"] = _wfm(f("lru_w_in")[0])
    sh["lwout"] = _wfm(f("lru_w_out")[0])
    gw = f("lru_gate_w")[0]
    sh["lgw"] = np.ascontiguousarray(gw.reshape(2, 2, 4, 2, 128, 256).transpose(4, 0, 1, 2, 3, 5))
    sv = np.zeros((128, 64), np.float32)
    sv[:, 0:8] = _fm(f("lru_conv_b")[0], 8)
    cw = f("lru_conv_w")[0]
    sv[:, 8:40] = cw.reshape(4, 8, 128).transpose(2, 1, 0).reshape(128, 32)
    lam = f("lru_lam")[0]
    sv[:, 40:56] = lam.reshape(2, 8, 128).transpose(2, 0, 1).reshape(128, 16)
    sh["lsv"] = sv
    gb = f("lru_gate_b")[0]
    sh["lgb"] = np.ascontiguousarray(gb.reshape(2, 2, 8, 128).transpose(3, 0, 1, 2).reshape(128, 32))
    sh["c_ones1024"] = np.full((128, 128), 1.0 / 1024, np.float32)
    b = np.zeros((128, 128), np.float32)
    b[0:64, 0:64] = 1.0 / 64
    b[64:128, 64:128] = 1.0 / 64
    sh["c_blk64"] = b
    sh["c_ones384"] = np.full((128, 128), 1.0 / 384, np.float32)
    sh["c_ones256"] = np.full((128, 128), 1.0 / 256, np.float32)
    sh["c_ones96"] = np.full((128, 128), 1.0 / 96, np.float32)
    return sh


def prepare_core(inp, b):
    x = np.asarray(inp["x"][b], np.float32)
    ctx = np.asarray(inp["ctx"][b], np.float32)
    xin = np.ascontiguousarray(np.concatenate([x.T, ctx.T], axis=1))
    cond = np.stack([_fm(np.asarray(inp["c"][b]), 8), _fm(np.asarray(inp["c_ctx"]), 8)], axis=2)
    return {"xin": xin, "cond": np.ascontiguousarray(cond)}


_NC_CACHE = {}


def kernel(**inputs):
    sh = prepare_shared(inputs)
    if "nc" not in _NC_CACHE:
        _NC_CACHE["nc"] = Builder().build()
    nc = _NC_CACHE["nc"]
    in_maps = []
    for b in range(8):
        m = dict(sh)
        m.update(prepare_core(inputs, b))
        in_maps.append(m)
    res = run_bass_kernel_spmd(nc, in_maps, core_ids=list(range(8)))
    out = np.stack([np.ascontiguousarray(r["outT"].T) for r in res.results], axis=0)
    return out.astype(np.float32)
```

```python
from contextlib import ExitStack
import numpy as np
import concourse.bass as bass
import concourse.mybir as mybir
from concourse.bass_utils import run_bass_kernel_spmd

F32 = mybir.dt.float32
BF16 = mybir.dt.bfloat16
AF = mybir.ActivationFunctionType
ALU = mybir.AluOpType

D = 1024
L = 4096
CT = 256
T = L + CT
DEPTH = 4
DFF = 2816
NJ = DFF // 128
EPS = 1e-6
SHIFT = 16.0
TILES = [(i * 512, 512, False) for i in range(8)] + [(L, CT, True)]


class _Op:
    __slots__ = ("eng", "fn", "deps", "dkey", "sig", "sem", "val")

    def __init__(self, eng, fn, deps, dkey):
        self.eng = eng
        self.fn = fn
        self.deps = deps
        self.dkey = dkey
        self.sig = False
        self.sem = None
        self.val = 0


class Sched:
    EPOCH = 20000

    def __init__(self, nc, stack):
        self.nc = nc
        self.stack = stack
        self.ops = []
        self.last_w = {}
        self.readers = {}
        self.last_dkey = {}
        self.pending_bar = {}
        self.bar_idx = -1
        self.emitted = 0
        self.engs = {"pe": nc.tensor, "act": nc.scalar, "dve": nc.vector,
                     "pool": nc.gpsimd, "sp": nc.sync}
        self.eng_cnt = {e: 0 for e in self.engs}
        self.eng_sem = {}
        self.key_sem = {}
        self.sem_cnt = {}
        self.free_sems = {}
        self.waited = {e: {} for e in self.engs}
        self.nsem = 0
        self.ninst = 0

    def add(self, eng, fn, reads=(), writes=(), dkey=None):
        i = len(self.ops)
        deps = set()
        for r in reads:
            w = self.last_w.get(r)
            if w is not None:
                deps.add(w)
        for r in writes:
            w = self.last_w.get(r)
            if w is not None:
                deps.add(w)
            for rd in self.readers.get(r, ()):
                deps.add(rd)
        if dkey is not None:
            p = self.last_dkey.get(dkey)
            if p is not None:
                deps.add(p)
            self.last_dkey[dkey] = i
        deps = set(d for d in deps if d > self.bar_idx)
        if eng in self.pending_bar:
            deps |= self.pending_bar.pop(eng)
        for r in reads:
            self.readers.setdefault(r, []).append(i)
        for r in writes:
            self.last_w[r] = i
            self.readers[r] = []
        deps.discard(i)
        red = {}
        for d in deps:
            o = self.ops[d]
            src = ("k", o.dkey) if o.dkey is not None else ("e", o.eng)
            if src == ("e", "pe") and eng == "pe" and dkey is None and d > self.bar_idx:
                continue
            if src not in red or red[src] < d:
                red[src] = d
        self.ops.append(_Op(eng, fn, sorted(red.values()), dkey))
        return i

    def barrier(self):
        last = {}
        for i in range(self.bar_idx + 1, len(self.ops)):
            o = self.ops[i]
            src = ("k", o.dkey) if o.dkey is not None else ("e", o.eng)
            last[src] = i
        deps = set(last.values())
        for d in deps:
            self.ops[d].sig = True
        self.flush()
        for e in self.engs:
            self.pending_bar[e] = set(deps) | self.pending_bar.get(e, set())
        self.bar_idx = len(self.ops) - 1
        for (e, _k), sem in self.key_sem.items():
            self.free_sems.setdefault(e, []).append(sem)
        self.key_sem = {}

    def flush(self):
        nc = self.nc
        ops = self.ops
        for i in range(self.emitted, len(ops)):
            for d in ops[i].deps:
                ops[d].sig = True
        for i in range(self.emitted, len(ops)):
            o = ops[i]
            E = self.engs[o.eng]
            w = self.waited[o.eng]
            for d in o.deps:
                do = ops[d]
                sid = id(do.sem)
                if w.get(sid, 0) >= do.val:
                    continue
                E.wait_ge(do.sem, do.val)
                w[sid] = do.val
            inst = o.fn()
            o.fn = None
            self.ninst += 1
            if o.dkey is not None:
                kk = (o.eng, o.dkey)
                if kk not in self.key_sem:
                    fl = self.free_sems.setdefault(o.eng, [])
                    if fl:
                        self.key_sem[kk] = fl.pop()
                    else:
                        sem = self.stack.enter_context(nc.semaphore("k%d" % self.nsem))
                        self.nsem += 1
                        self.sem_cnt[id(sem)] = 0
                        self.key_sem[kk] = sem
                o.sem = self.key_sem[kk]
                self.sem_cnt[id(o.sem)] += 16
                o.val = self.sem_cnt[id(o.sem)]
                inst.then_inc(o.sem, 16)
            elif o.sig:
                if o.eng not in self.eng_sem or self.eng_cnt[o.eng] >= self.EPOCH:
                    self.eng_sem[o.eng] = self.stack.enter_context(nc.semaphore("e%d" % self.nsem))
                    self.nsem += 1
                    self.eng_cnt[o.eng] = 0
                self.eng_cnt[o.eng] += 1
                o.sem = self.eng_sem[o.eng]
                o.val = self.eng_cnt[o.eng]
                inst.then_inc(o.sem, 1)
        self.emitted = len(ops)

    def emit(self):
        self.flush()


class Tile:
    def __init__(self, t, key):
        self.t = t
        self.key = key

    def __getitem__(self, idx):
        return self.t[idx]


class Ring:
    def __init__(self, tiles):
        self.tiles = tiles
        self.i = 0

    def get(self):
        t = self.tiles[self.i % len(self.tiles)]
        self.i += 1
        return t


class Builder:
    def __init__(self, layers=(0, 1, 2, 3)):
        self.layers = tuple(layers)
        self.nc = bass.Bass("TRN2", target_bir_lowering=False)
        self.cnt = 0

    def sb(self, shape, dtype, name="t"):
        self.cnt += 1
        nm = "%s_%d" % (name, self.cnt)
        t = self.scope.enter_context(self.nc.sbuf_tensor(nm, list(shape), dtype))
        return Tile(t, nm)

    def ps(self, name="ps"):
        self.cnt += 1
        nm = "%s_%d" % (name, self.cnt)
        t = self.stack.enter_context(self.nc.psum_tensor(nm, [128, 512], F32))
        return Tile(t, nm)

    def ring(self, n, shape, dtype, name="r"):
        return Ring([self.sb(shape, dtype, name) for _ in range(n)])

    def din(self, name, shape):
        return self.nc.dram_tensor(name, list(shape), F32, kind="ExternalInput").ap()

    def op(self, eng, fn, r=(), w=(), dkey=None):
        return self.S.add(eng, fn, reads=[x.key if isinstance(x, Tile) else x for x in r],
                          writes=[x.key if isinstance(x, Tile) else x for x in w], dkey=dkey)

    def load(self, dst, dst_ap, src_ap, r=(), cast=False):
        nc = self.nc
        if cast:
            self.op("pool", lambda: nc.gpsimd.dma_start(out=dst_ap, in_=src_ap, max_dma_last_dim=4096),
                    r=r, w=[dst], dkey=dst.key)
        else:
            self.op("sp", lambda: nc.sync.dma_start(out=dst_ap, in_=src_ap), r=r, w=[dst], dkey=dst.key)

    def store(self, dram_key, dst_ap, src, src_ap):
        nc = self.nc
        self.op("sp", lambda: nc.sync.dma_start(out=dst_ap, in_=src_ap), r=[src], w=[dram_key], dkey=src.key)

    def mm(self, ps, out_ap, pairs, r=()):
        nc = self.nc
        n = len(pairs)
        for i, (lt, rh) in enumerate(pairs):
            self.op("pe", (lambda lt=lt, rh=rh, i=i: nc.tensor.matmul(out_ap, lt, rh, start=(i == 0), stop=(i == n - 1))),
                    r=r, w=[ps])

    def act(self, out_ap, in_ap, func, r=(), w=(), bias=None, scale=None):
        nc = self.nc
        kw = {}
        if bias is not None:
            kw["bias"] = bias
        if scale is not None:
            kw["scale"] = scale
        self.op("act", lambda: nc.scalar.activation(out=out_ap, in_=in_ap, func=func, **kw), r=r, w=w)

    def tt(self, out_ap, a, b, op, r=(), w=(), eng="dve"):
        nc = self.nc
        e = nc.vector if eng == "dve" else nc.gpsimd
        self.op(eng, lambda: e.tensor_tensor(out=out_ap, in0=a, in1=b, op=op), r=r, w=w)

    def ts(self, out_ap, a, s1, s2, op0, op1=None, r=(), w=(), eng="dve"):
        nc = self.nc
        e = nc.vector if eng == "dve" else nc.gpsimd
        if op1 is None:
            self.op(eng, lambda: e.tensor_scalar(out=out_ap, in0=a, scalar1=s1, scalar2=None, op0=op0), r=r, w=w)
        else:
            self.op(eng, lambda: e.tensor_scalar(out=out_ap, in0=a, scalar1=s1, scalar2=s2, op0=op0, op1=op1), r=r, w=w)

    def stt(self, out_ap, a, s, b, op0, op1, r=(), w=()):
        nc = self.nc
        self.op("dve", lambda: nc.vector.scalar_tensor_tensor(out=out_ap, in0=a, scalar=s, in1=b, op0=op0, op1=op1), r=r, w=w)

    def copy(self, out_ap, in_ap, r=(), w=(), eng="dve"):
        nc = self.nc
        if eng == "act":
            self.op("act", lambda: nc.scalar.copy(out=out_ap, in_=in_ap), r=r, w=w)
        else:
            e = nc.vector if eng == "dve" else nc.gpsimd
            self.op(eng, lambda: e.tensor_copy(out=out_ap, in_=in_ap), r=r, w=w)

    def memset(self, t, ap, val, eng="dve"):
        nc = self.nc
        e = nc.vector if eng == "dve" else nc.gpsimd
        self.op(eng, lambda: e.memset(ap, val), w=[t])

    def mm1(self, ps, out_ap, lhsT, rhs, start, stop, r=()):
        nc = self.nc
        self.op("pe", lambda: nc.tensor.matmul(out_ap, lhsT, rhs, start=start, stop=stop), r=r, w=[ps])

    def recip(self, out_ap, in_ap, r=(), w=()):
        nc = self.nc
        self.op("dve", lambda: nc.vector.reciprocal(out=out_ap, in_=in_ap), r=r, w=w)

    def new_scope(self):
        if self.cur_scope is not None:
            self.S.barrier()
            self.cur_scope.close()
        self.cur_scope = ExitStack()
        self.scope = self.cur_scope

    def common_rings(self, nxt=2, nh=2):
        self.XT = self.ring(nxt, [128, 8, 512], F32, "xt")
        self.SQ = self.ring(1, [128, 8, 512], BF16, "sq")
        self.SQ1 = self.ring(4, [128, 512], BF16, "sq1")
        self.RS = self.ring(2, [128, 512], F32, "rstd")
        self.TF = self.ring(5, [128, 512], F32, "tf")
        self.H = self.ring(nh, [128, 8, 512], BF16, "h")

    def build(self):
        nc = self.nc
        I = {}
        for k, shp in input_shapes().items():
            I[k] = self.din(k, shp)
        self.I = I
        self.out = nc.dram_tensor("outT", [D, L], F32, kind="ExternalOutput").ap()
        self.XS = nc.dram_tensor("xs", [D, T], F32).ap()
        self.AS = nc.dram_tensor("acts", [D, T], BF16).ap()
        self.XRS = nc.dram_tensor("xrs", [D, T], BF16).ap()
        self.xsrc_is_input = True
        with ExitStack() as stack:
            self.stack = stack
            self.scope = stack
            self.cur_scope = None
            self.S = Sched(nc, stack)
            self.PS = Ring([self.ps() for _ in range(6)])
            self.PA = Ring([self.ps() for _ in range(2)])
            self.setup_consts()
            for l in self.layers:
                self.layer(l)
            self.op("sp", lambda: nc.sync.nop(), r=["OUT%d" % i for i in range(8)])
            self.S.emit()
            if self.cur_scope is not None:
                self.cur_scope.close()
        return nc

    def xview(self, ap, t0, n):
        return ap.rearrange("(k p) t -> p k t", p=128)[:, :, t0:t0 + n]

    def load_x(self, ti):
        t0, n, isctx = TILES[ti]
        xt = self.XT.get()
        src = self.I["xin"] if self.xsrc_is_input else self.XS
        self.load(xt, xt[:, :, 0:n], self.xview(src, t0, n), r=["X%d" % ti])
        return xt

    def store_x(self, ti, xt, final=False):
        t0, n, isctx = TILES[ti]
        if final and not isctx:
            self.store("OUT%d" % ti, self.xview(self.out, t0, n), xt, xt[:, :, 0:n])
        else:
            self.store("X%d" % ti, self.xview(self.XS, t0, n), xt, xt[:, :, 0:n])

    def setup_consts(self):
        nc = self.nc
        I = self.I

        def cload(name, shape, dtype=BF16):
            t = self.sb(shape, dtype, name)
            self.load(t, t[:], I[name][:], cast=(dtype == BF16))
            return t
        self.ones1024 = cload("c_ones1024", [128, 128])
        self.blk64 = cload("c_blk64", [128, 128])
        self.ones384 = cload("c_ones384", [128, 128])
        self.ones256 = cload("c_ones256", [128, 128])
        self.ones96 = cload("c_ones96", [128, 128])
        self.perm64 = cload("c_perm64", [128, 128])
        self.perm96 = cload("c_perm96", [128, 128])
        self.epsT = self.sb([128, 1], F32, "eps")
        self.memset(self.epsT, self.epsT[:], EPS)
        self.nshift = self.sb([128, 1], F32, "nshift")
        self.memset(self.nshift, self.nshift[:], -SHIFT)
        self.one1 = self.sb([128, 1], F32, "one1")
        self.memset(self.one1, self.one1[:], 1.0)
        self.n1g = cload("n1g", [128, 4, 8], F32)
        self.n2g = cload("n2g", [128, 4, 8], F32)
        self.modb = cload("modb", [128, 4, 48], F32)
        self.fcw = cload("fcw", [128, 4, NJ, 3], F32)
        self.fcb = cload("fcb", [128, 4, NJ], F32)
        self.XB = self.sb([128, 8, 16], F32, "XB")
        self.MV = {l: self.sb([128, 6, 8, 2], F32, "mv") for l in self.layers}
        cond = cload("cond", [128, 8, 2], F32)
        condT = self.sb([128, 8, 2], F32, "condT")
        self.act(condT[:], cond[:], AF.Silu, r=[cond], w=[condT])
        self.new_scope()
        wr = self.ring(2, [128, 6144], F32, "modw")
        for l in self.layers:
            acc = self.sb([128, 48, 2], F32, "modacc")
            for k in range(8):
                wt = wr.get()
                self.load(wt, wt[:], I["modw"][l, :, k, :])
                pt = self.PS.get()
                for j in range(48):
                    self.mm1(pt, pt[:, 2 * j:2 * j + 2], wt[:, j * 128:(j + 1) * 128], condT[:, k, :], True, True,
                             r=[wt, condT])
                pv = pt[:, 0:96].rearrange("p (j s) -> p j s", s=2)
                if k == 0:
                    self.copy(acc[:], pv, r=[pt], w=[acc])
                else:
                    self.tt(acc[:], pv, acc[:], ALU.add, r=[pt, acc], w=[acc])
            mb = self.modb[:, l, :].unsqueeze(2).broadcast_to([128, 48, 2])
            self.tt(acc[:], acc[:], mb, ALU.add, r=[acc, self.modb], w=[acc])
            mv = self.MV[l]
            a4 = acc[:].rearrange("p (m k) s -> p m k s", m=6)
            for dst, srcm in ((1, 0), (2, 2), (4, 3), (5, 5)):
                self.copy(mv[:, dst], a4[:, srcm], r=[acc], w=[mv])
            for dst, srcm, g in ((0, 1, self.n1g), (3, 4, self.n2g)):
                tmp = self.sb([128, 8, 2], F32, "mtmp")
                self.ts(tmp[:], a4[:, srcm], 1.0, None, ALU.add, r=[acc], w=[tmp])
                gb = g[:, l, :].unsqueeze(2).broadcast_to([128, 8, 2])
                self.tt(mv[:, dst], tmp[:], gb, ALU.mult, r=[tmp, g], w=[mv])

    def modnorm(self, xt, n, l, which, s, h):
        mv = self.MV[l]
        sq = self.SQ.get()
        self.act(sq[:, :, 0:n], xt[:, :, 0:n], AF.Square, r=[xt], w=[sq])
        pt = self.PS.get()
        self.mm(pt, pt[:, 0:n], [(self.ones1024[:], sq[:, k, 0:n]) for k in range(8)], r=[sq, self.ones1024])
        rstd = self.RS.get()
        self.act(rstd[:, 0:n], pt[:, 0:n], AF.Ln, r=[pt, self.epsT], w=[rstd], bias=self.epsT[:, 0:1])
        self.act(rstd[:, 0:n], rstd[:, 0:n], AF.Exp, r=[rstd], w=[rstd], scale=-0.5)
        a_i, b_i = (0, 1) if which == 1 else (3, 4)
        for k in range(8):
            tmp = self.TF.get()
            self.stt(tmp[:, 0:n], xt[:, k, 0:n], mv[:, a_i, k, s:s + 1], rstd[:, 0:n], ALU.mult, ALU.mult,
                     r=[xt, mv, rstd], w=[tmp])
            self.act(h[:, k, 0:n], tmp[:, 0:n], AF.Identity, r=[tmp, mv], w=[h], bias=mv[:, b_i, k, s:s + 1])

    def headnorm(self, pt, rows, n, onesmat, gain_ap, gain_t, dst_t, dst_ap, rope=None):
        sq = self.SQ1.get()
        raw = self.TF.get()
        self.act(sq[0:rows, 0:n], pt[0:rows, 0:n], AF.Square, r=[pt], w=[sq])
        self.act(raw[0:rows, 0:n], pt[0:rows, 0:n], AF.Copy, r=[pt], w=[raw])
        pm = self.PS.get()
        self.mm(pm, pm[0:rows, 0:n], [(onesmat[0:rows, 0:rows], sq[0:rows, 0:n])], r=[sq, onesmat])
        rstd = self.RS.get()
        self.act(rstd[0:rows, 0:n], pm[0:rows, 0:n], AF.Ln, r=[pm, self.epsT], w=[rstd], bias=self.epsT[0:rows, 0:1])
        self.act(rstd[0:rows, 0:n], rstd[0:rows, 0:n], AF.Exp, r=[rstd], w=[rstd], scale=-0.5)
        if rope is None:
            self.stt(dst_ap, raw[0:rows, 0:n], gain_ap, rstd[0:rows, 0:n], ALU.mult, ALU.mult,
                     r=[raw, rstd, gain_t], w=[dst_t])
            return
        if len(rope) == 5:
            cs, sn, permT, r0, r1 = rope
            cs_ap, sn_ap = cs[:, 0:n], sn[:, 0:n]
        else:
            cs, sn, permT, r0, r1, cs_ap, sn_ap = rope
        qn = self.SQ1.get()
        self.stt(qn[0:rows, 0:n], raw[0:rows, 0:n], gain_ap, rstd[0:rows, 0:n], ALU.mult, ALU.mult,
                 r=[raw, rstd, gain_t], w=[qn])
        pw = self.PS.get()
        self.mm(pw, pw[0:rows, 0:n], [(permT[0:rows, 0:rows], qn[0:rows, 0:n])], r=[qn, permT])
        t1 = self.TF.get()
        t2 = self.TF.get()
        self.tt(t1[r0:r1, 0:n], qn[r0:r1, 0:n], cs_ap[r0:r1], ALU.mult, r=[qn, cs], w=[t1])
        self.tt(t2[r0:r1, 0:n], pw[r0:r1, 0:n], sn_ap[r0:r1], ALU.mult, r=[pw, sn], w=[t2])
        if r0 > 0:
            self.copy(dst_ap[0:r0], qn[0:r0, 0:n], r=[qn], w=[dst_t], eng="pool")
        self.tt(dst_ap[r0:r1], t1[r0:r1, 0:n], t2[r0:r1, 0:n], ALU.add, r=[t1, t2], w=[dst_t])

    def load_rope(self, csname, snname, ti):
        t0, n, isctx = TILES[ti]
        cs = self.ROPE.get()
        sn = self.ROPE.get()
        self.load(cs, cs[:, 0:n], self.I[csname][:, t0:t0 + n])
        self.load(sn, sn[:, 0:n], self.I[snname][:, t0:t0 + n])
        return cs, sn

    def resid(self, l, ti, xt, act_t, act_fn, wo):
        mv = self.MV[l]
        t0, n, isctx = TILES[ti]
        s = 1 if isctx else 0
        for oc in range(8):
            pt = self.PS.get()
            self.mm(pt, pt[:, 0:n], [(wo[:, k, oc * 128:(oc + 1) * 128], act_fn(k)) for k in range(8)], r=[act_t, wo])
            self.stt(xt[:, oc, 0:n], pt[:, 0:n], mv[:, 2, oc, s:s + 1], xt[:, oc, 0:n], ALU.mult, ALU.add,
                     r=[pt, mv, xt], w=[xt])
        if not isctx:
            self.copy(self.XB[:, :, 2 * ti:2 * ti + 1], xt[:, :, 0:1], r=[xt], w=[self.XB], eng="pool")
            self.copy(self.XB[:, :, 2 * ti + 1:2 * ti + 2], xt[:, :, n - 1:n], r=[xt], w=[self.XB], eng="pool")
        self.store_x(ti, xt)

    def phase_c_dram(self, l, wo_name_ap, tiles):
        self.new_scope()
        self.XT = self.ring(2, [128, 8, 512], F32, "xt")
        AT = self.ring(2, [128, 8, 512], BF16, "at")
        wo = self.sb([128, 8, 1024], BF16, "wo")
        for k in range(8):
            self.load(wo, wo[:, k, :], wo_name_ap[:, k, :], cast=True)
        for ti in tiles:
            t0, n, isctx = TILES[ti]
            xt = self.load_x(ti)
            at = AT.get()
            self.load(at, at[:, :, 0:n], self.xview(self.AS, t0, n), r=["AS%d" % ti])
            self.resid(l, ti, xt, at, (lambda k, at=at, n=n: at[:, k, 0:n]), wo)

    def layer(self, l):
        kind, idx = l % 3, l // 3
        last = (l == DEPTH - 1)
        final = (l == self.layers[-1])
        tiles = list(range(8)) if last else list(range(9))
        if kind == 0:
            self.swa(l, idx, tiles)
        elif kind == 1:
            self.mla(l, idx, tiles)
        else:
            self.lru(l, idx, tiles)
        self.xsrc_is_input = False
        self.ffn(l, tiles, final)

    def ffn(self, l, tiles, final):
        I = self.I
        mv = self.MV[l]
        self.new_scope()
        self.common_rings()
        WG = self.ring(3, [128, 8, 128], BF16, "wg")
        WV = self.ring(3, [128, 8, 128], BF16, "wv")
        WD = self.ring(1, [128, NJ, 1024], BF16, "wd")
        HID = self.ring(1, [128, NJ, 512], BF16, "hid")
        GB = self.sb([128, NJ, 16], F32, "gb")
        GT = self.ring(2, [128, 514], F32, "gt")
        HB = self.sb([128, 8, 16], BF16, "hb")
        self.modnorm(self.XB, 16, l, 2, 0, HB)
        for idx_t, ti in enumerate(tiles):
            t0, n, isctx = TILES[ti]
            s = 1 if isctx else 0
            xt = self.load_x(ti)
            h = self.H.get()
            self.modnorm(xt, n, l, 2, s, h)
            hid = HID.get()
            wd = WD.get()
            for j in range(NJ):
                wg = WG.get()
                wv = WV.get()
                self.load(wg, wg[:], I["wug"][l, j].rearrange("p (k m) -> p k m", k=8), cast=True)
                self.load(wv, wv[:], I["wuv"][l, j].rearrange("p (k m) -> p k m", k=8), cast=True)
                self.load(wd, wd[:, j, :], I["wdn"][l, j], cast=True)
                if idx_t == 0:
                    pb = self.PS.get()
                    self.mm(pb, pb[:, 0:16], [(wg[:, k, :], HB[:, k, :]) for k in range(8)], r=[wg, HB])
                    self.copy(GB[:, j, :], pb[:, 0:16], r=[pb], w=[GB])
                pg = self.PS.get()
                self.mm(pg, pg[:, 0:n], [(wg[:, k, :], h[:, k, 0:n]) for k in range(8)], r=[wg, h])
                pv = self.PS.get()
                self.mm(pv, pv[:, 0:n], [(wv[:, k, :], h[:, k, 0:n]) for k in range(8)], r=[wv, h])
                gt = GT.get()
                self.act(gt[:, 1:n + 1], pg[:, 0:n], AF.Copy, r=[pg], w=[gt])
                if (not isctx) and ti > 0:
                    self.copy(gt[:, 0:1], GB[:, j, 2 * (ti - 1) + 1:2 * (ti - 1) + 2], r=[GB], w=[gt], eng="pool")
                else:
                    self.memset(gt, gt[:, 0:1], 0.0, eng="pool")
                if (not isctx) and ti < 7:
                    self.copy(gt[:, n + 1:n + 2], GB[:, j, 2 * (ti + 1):2 * (ti + 1) + 1], r=[GB], w=[gt], eng="pool")
                else:
                    self.memset(gt, gt[:, n + 1:n + 2], 0.0, eng="pool")
                c1 = self.TF.get()
                self.ts(c1[:, 0:n], gt[:, 0:n], self.fcw[:, l, j, 0:1], self.fcb[:, l, j:j + 1], ALU.mult, ALU.add,
                        r=[gt, self.fcw, self.fcb], w=[c1])
                self.stt(c1[:, 0:n], gt[:, 1:n + 1], self.fcw[:, l, j, 1:2], c1[:, 0:n], ALU.mult, ALU.add,
                         r=[gt, c1, self.fcw], w=[c1])
                self.stt(c1[:, 0:n], gt[:, 2:n + 2], self.fcw[:, l, j, 2:3], c1[:, 0:n], ALU.mult, ALU.add,
                         r=[gt, c1, self.fcw], w=[c1])
                sl = self.TF.get()
                self.act(sl[:, 0:n], c1[:, 0:n], AF.Silu, r=[c1], w=[sl])
                self.tt(hid[:, j, 0:n], sl[:, 0:n], pv[:, 0:n], ALU.mult, r=[sl, pv], w=[hid])
            for oc in range(8):
                pt = self.PS.get()
                self.mm(pt, pt[:, 0:n], [(wd[:, j, oc * 128:(oc + 1) * 128], hid[:, j, 0:n]) for j in range(NJ)],
                        r=[wd, hid])
                self.stt(xt[:, oc, 0:n], pt[:, 0:n], mv[:, 5, oc, s:s + 1], xt[:, oc, 0:n], ALU.mult, ALU.add,
                         r=[pt, mv, xt], w=[xt])
            self.store_x(ti, xt, final=final)

    def swa(self, l, idx, tiles):
        nc = self.nc
        I = self.I
        self.new_scope()
        self.common_rings(nxt=1, nh=1)
        self.ROPE = self.ring(4, [128, 512], F32, "rope")
        wq = self.sb([128, 8, 1024], BF16, "wq")
        wk = self.sb([128, 8, 256], BF16, "wk")
        wv = self.sb([128, 8, 256], BF16, "wv")
        wo = self.sb([128, 8, 1024], BF16, "wo")
        for k in range(8):
            self.load(wq, wq[:, k, :], I["swq"][idx, :, k, :], cast=True)
            self.load(wo, wo[:, k, :], I["swo"][idx, :, k, :], cast=True)
        self.load(wk, wk[:], I["swk"][idx], cast=True)
        self.load(wv, wv[:], I["swv"][idx], cast=True)
        gq = self.sb([128, 1], F32, "gq")
        gk = self.sb([128, 1], F32, "gk")
        self.load(gq, gq[:], I["sqg"][idx])
        self.load(gk, gk[:], I["skg"][idx])
        es = self.sb([128, 16], F32, "es")
        self.load(es, es[:], I["ssink"][idx])
        self.act(es[:], es[:], AF.Exp, r=[es, self.nshift], w=[es], bias=self.nshift[:, 0:1])
        KT = self.sb([128, 2, T], BF16, "KT")
        VA = self.sb([128, 34, 4, 128], BF16, "VA")
        self.memset(VA, VA[:, :, :, 64:128], 1.0, eng="pool")
        QT = self.sb([128, 8, 512], BF16, "QT")
        OT = self.sb([128, 8, 512], BF16, "OT")
        PT = self.ring(3, [128, 512], BF16, "pt")
        DEN = self.ring(2, [128, 512], F32, "den")
        for ti in range(9):
            t0, n, isctx = TILES[ti]
            s = 1 if isctx else 0
            xt = self.load_x(ti)
            h = self.H.get()
            self.modnorm(xt, n, l, 1, s, h)
            rope = None
            if not isctx:
                cs, sn = self.load_rope("rcs", "rsn", ti)
                rope = (cs, sn, self.perm64, 0, 128)
            for c in range(2):
                pt = self.PS.get()
                self.mm(pt, pt[:, 0:n], [(wk[:, k, c * 128:(c + 1) * 128], h[:, k, 0:n]) for k in range(8)], r=[wk, h])
                self.headnorm(pt, 128, n, self.blk64, gk[:, 0:1], gk, KT, KT[:, c, t0:t0 + n], rope=rope)
            for tb in range(n // 128):
                pt = self.PS.get()
                self.mm(pt, pt[:, 0:256], [(h[:, k, tb * 128:(tb + 1) * 128], wv[:, k, :]) for k in range(8)], r=[wv, h])
                blk = (t0 // 128) + tb
                self.copy(VA[:, blk, :, 0:64], pt[:, 0:256].rearrange("p (g d) -> p g d", g=4), r=[pt], w=[VA], eng="act")
        for ti in tiles:
            t0, n, isctx = TILES[ti]
            s = 1 if isctx else 0
            xt = self.load_x(ti)
            h = self.H.get()
            self.modnorm(xt, n, l, 1, s, h)
            rope = None
            if not isctx:
                cs, sn = self.load_rope("rcs", "rsn", ti)
                rope = (cs, sn, self.perm64, 0, 128)
            for c in range(8):
                pt = self.PS.get()
                self.mm(pt, pt[:, 0:n], [(wq[:, k, c * 128:(c + 1) * 128], h[:, k, 0:n]) for k in range(8)], r=[wq, h])
                self.headnorm(pt, 128, n, self.blk64, gq[:, 0:1], gq, QT, QT[:, c, 0:n], rope=rope)
            for qb in range(n // 128):
                QB = t0 // 128 + qb
                if isctx:
                    kbs = [(32, 0), (33, 0)]
                else:
                    kbs = []
                    if QB > 0:
                        kbs.append((QB - 1, 1))
                    kbs.append((QB, 0))
                    if QB < 31:
                        kbs.append((QB + 1, 2))
                    kbs += [(32, 0), (33, 0)]
                for g in range(4):
                    base = 0 if g < 2 else 64
                    c0 = 4 * (g % 2)
                    kc = g % 2
                    po = self.PA.get()
                    rhs = QT[base:base + 64, c0:c0 + 4, qb * 128:(qb + 1) * 128]
                    for ki, (kb, mk) in enumerate(kbs):
                        ps_ = self.PS.get()
                        self.mm1(ps_, ps_[:], KT[base:base + 64, kc, kb * 128:(kb + 1) * 128], rhs, True, True, r=[KT, QT])
                        p = PT.get()
                        self.act(p[:], ps_[:], AF.Exp, r=[ps_, self.nshift], w=[p], bias=self.nshift[:, 0:1], scale=0.125)
                        if mk:
                            cm, st = (1, -1) if mk == 1 else (-1, 1)
                            self.op("pool", (lambda p=p, cm=cm, st=st: nc.gpsimd.affine_select(
                                out=p[:].rearrange("p (a b) -> p a b", a=4), in_=p[:].rearrange("p (a b) -> p a b", a=4),
                                pattern=[[0, 4], [st, 128]], compare_op=ALU.is_ge, fill=0.0, base=0, channel_multiplier=cm)),
                                r=[p], w=[p])
                        self.mm1(po, po[:], VA[:, kb, g, :], p[:], ki == 0, ki == len(kbs) - 1, r=[VA, p])
                    den = DEN.get()
                    esb = es[64:128, 4 * g:4 * g + 4].unsqueeze(2).broadcast_to([64, 4, 128])
                    self.tt(den[64:128, :].rearrange("p (a b) -> p a b", a=4), po[64:128, :].rearrange("p (a b) -> p a b", a=4),
                            esb, ALU.add, r=[po, es], w=[den])
                    self.recip(den[64:128, :], den[64:128, :], r=[den], w=[den])
                    self.tt(OT[base:base + 64, c0:c0 + 4, qb * 128:(qb + 1) * 128],
                            po[0:64, :].rearrange("p (a b) -> p a b", a=4),
                            den[64:128, :].rearrange("p (a b) -> p a b", a=4), ALU.mult, r=[po, den], w=[OT])
            self.resid(l, ti, xt, OT, (lambda k, n=n: OT[:, k, 0:n]), wo)

    def mla(self, l, idx, tiles):
        I = self.I
        self.new_scope()
        CQN = self.sb([128, 3, T], BF16, "CQN")
        CKVN = self.sb([128, 2, T], BF16, "CKVN")
        KRSQ = self.sb([128, T], BF16, "KRSQ")
        KRROT = self.sb([128, T], BF16, "KRROT")
        gv = self.sb([128, 8], F32, "mg")
        self.load(gv, gv[:], I["mgv"][:])
        persist = self.cur_scope
        self.cur_scope = None
        self.new_scope()
        self.common_rings(nxt=2, nh=1)
        self.ROPE = self.ring(4, [128, 512], F32, "rope")
        wdn = self.sb([128, 8, 640], BF16, "mdn")
        wrp = self.sb([128, 8, 96], BF16, "mrp")
        KRG = self.ring(2, [128, 512], BF16, "krg")
        for kt in KRG.tiles:
            self.memset(kt, kt[:], 0.0)
        for k in range(8):
            self.load(wdn, wdn[:, k, :], I["mdn"][:, k, :], cast=True)
        self.load(wrp, wrp[:], I["mrp"][:], cast=True)
        for ti in range(9):
            t0, n, isctx = TILES[ti]
            s = 1 if isctx else 0
            xt = self.load_x(ti)
            h = self.H.get()
            self.modnorm(xt, n, l, 1, s, h)
            for (nch, coff, gcol, onesm, dstT) in ((3, 0, 0, self.ones384, CQN), (2, 384, 3, self.ones256, CKVN)):
                raws, sqs = [], []
                for c in range(nch):
                    pt = self.PS.get()
                    self.mm(pt, pt[:, 0:n], [(wdn[:, k, coff + c * 128:coff + (c + 1) * 128], h[:, k, 0:n]) for k in range(8)],
                            r=[wdn, h])
                    sq = self.SQ1.get()
                    raw = self.TF.get()
                    self.act(sq[:, 0:n], pt[:, 0:n], AF.Square, r=[pt], w=[sq])
                    self.act(raw[:, 0:n], pt[:, 0:n], AF.Copy, r=[pt], w=[raw])
                    raws.append(raw)
                    sqs.append(sq)
                pm = self.PS.get()
                self.mm(pm, pm[:, 0:n], [(onesm[:], sq[:, 0:n]) for sq in sqs], r=sqs + [onesm])
                rstd = self.RS.get()
                self.act(rstd[:, 0:n], pm[:, 0:n], AF.Ln, r=[pm, self.epsT], w=[rstd], bias=self.epsT[:, 0:1])
                self.act(rstd[:, 0:n], rstd[:, 0:n], AF.Exp, r=[rstd], w=[rstd], scale=-0.5)
                for c in range(nch):
                    self.stt(dstT[:, c, t0:t0 + n], raws[c][:, 0:n], gv[:, gcol + c:gcol + c + 1], rstd[:, 0:n],
                             ALU.mult, ALU.mult, r=[raws[c], rstd, gv], w=[dstT])
            pk = self.PS.get()
            self.mm(pk, pk[0:96, 0:n], [(wrp[:, k, :], h[:, k, 0:n]) for k in range(8)], r=[wrp, h])
            self.act(KRSQ[64:96, t0:t0 + n], pk[64:96, 0:n], AF.Square, r=[pk], w=[KRSQ])
            krg = KRG.get()
            self.ts(krg[64:96, 0:n], pk[64:96, 0:n], gv[64:96, 6:7], None, ALU.mult, r=[pk, gv], w=[krg])
            if isctx:
                self.copy(KRROT[64:96, t0:t0 + n], krg[64:96, 0:n], r=[krg], w=[KRROT])
            else:
                cs, sn = self.load_rope("mcs", "msn", ti)
                pw = self.PS.get()
                self.mm(pw, pw[0:96, 0:n], [(self.perm96[0:96, 0:96], krg[0:96, 0:n])], r=[krg, self.perm96])
                t1 = self.TF.get()
                t2 = self.TF.get()
                self.tt(t1[64:96, 0:n], krg[64:96, 0:n], cs[64:96, 0:n], ALU.mult, r=[krg, cs], w=[t1])
                self.tt(t2[64:96, 0:n], pw[64:96, 0:n], sn[64:96, 0:n], ALU.mult, r=[pw, sn], w=[t2])
                self.tt(KRROT[64:96, t0:t0 + n], t1[64:96, 0:n], t2[64:96, 0:n], ALU.add, r=[t1, t2], w=[KRROT])
        self.new_scope()
        RALL = self.sb([128, 2, L], F32, "ropeall")
        self.load(RALL, RALL[:, 0, :], I["mcs"][:, :])
        self.load(RALL, RALL[:, 1, :], I["msn"][:, :])
        wuq = self.sb([128, 3, 1536], BF16, "muq")
        wuk = self.sb([128, 2, 1024], BF16, "muk")
        wuv = self.sb([128, 2, 1024], BF16, "muv")
        for k in range(3):
            self.load(wuq, wuq[:, k, :], I["muq"][:, k, :], cast=True)
        for k in range(2):
            self.load(wuk, wuk[:, k, :], I["muk"][:, k, :], cast=True)
            self.load(wuv, wuv[:, k, :], I["muv"][:, k, :], cast=True)
        KTH = self.ring(2, [128, T], BF16, "KTH")
        VH = self.ring(2, [128, 34, 128], BF16, "VH")
        QTH = self.ring(2, [128, 512], BF16, "QTH")
        PT = self.ring(4, [128, 512], BF16, "pt")
        DEN = self.ring(2, [128, 512], F32, "den")
        OS = self.ring(3, [128, 512], BF16, "os")
        qSQ = self.ring(2, [128, 512], BF16, "qsq")
        qTF = self.ring(3, [128, 512], F32, "qtf")
        qRS = self.ring(1, [128, 512], F32, "qrs")
        kSQ = self.ring(2, [128, 512], BF16, "ksq")
        kTF = self.ring(2, [128, 512], F32, "ktf")
        kRS = self.ring(2, [128, 512], F32, "krs")
        for v in VH.tiles:
            self.memset(v, v[:, :, 64:128], 1.0, eng="pool")
        scale = 96.0 ** -0.5
        PSS = Ring(self.PS.tiles[0:3])
        PSG = Ring(self.PS.tiles[3:6])

        def kv_steps(hd, kth, vh):
            for ti in range(9):
                t0, n, isctx = TILES[ti]
                pk = PSG.get()
                self.mm(pk, pk[0:64, 0:n], [(wuk[:, c2, hd * 64:(hd + 1) * 64], CKVN[:, c2, t0:t0 + n]) for c2 in range(2)],
                        r=[wuk, CKVN])
                sq = kSQ.get()
                raw = kTF.get()
                self.act(sq[0:64, 0:n], pk[0:64, 0:n], AF.Square, r=[pk], w=[sq])
                self.act(raw[0:64, 0:n], pk[0:64, 0:n], AF.Copy, r=[pk], w=[raw])
                self.copy(sq[64:96, 0:n], KRSQ[64:96, t0:t0 + n], r=[KRSQ], w=[sq], eng="pool")
                yield
                pm = PSG.get()
                self.mm(pm, pm[0:96, 0:n], [(self.ones96[0:96, 0:96], sq[0:96, 0:n])], r=[sq, self.ones96])
                rstd = kRS.get()
                self.act(rstd[0:96, 0:n], pm[0:96, 0:n], AF.Ln, r=[pm, self.epsT], w=[rstd], bias=self.epsT[0:96, 0:1])
                self.act(rstd[0:96, 0:n], rstd[0:96, 0:n], AF.Exp, r=[rstd], w=[rstd], scale=-0.5)
                self.stt(kth[0:64, t0:t0 + n], raw[0:64, 0:n], gv[0:64, 6:7], rstd[0:64, 0:n], ALU.mult, ALU.mult,
                         r=[raw, rstd, gv], w=[kth])
                self.tt(kth[64:96, t0:t0 + n], KRROT[64:96, t0:t0 + n], rstd[64:96, 0:n], ALU.mult,
                        r=[KRROT, rstd], w=[kth])
                for tb in range(n // 128):
                    pv = PSG.get()
                    self.mm(pv, pv[:, 0:64], [(CKVN[:, c2, t0 + tb * 128:t0 + (tb + 1) * 128], wuv[:, c2, hd * 64:(hd + 1) * 64])
                                              for c2 in range(2)], r=[wuv, CKVN])
                    self.copy(vh[:, t0 // 128 + tb, 0:64], pv[:, 0:64], r=[pv], w=[vh], eng="act")
                yield

        def q_steps(hd, ti, qth):
            t0, n, isctx = TILES[ti]
            pq = PSG.get()
            self.mm(pq, pq[0:96, 0:n], [(wuq[:, c3, hd * 96:(hd + 1) * 96], CQN[:, c3, t0:t0 + n]) for c3 in range(3)],
                    r=[wuq, CQN])
            sq = qSQ.get()
            raw = qTF.get()
            self.act(sq[0:96, 0:n], pq[0:96, 0:n], AF.Square, r=[pq], w=[sq])
            self.act(raw[0:96, 0:n], pq[0:96, 0:n], AF.Copy, r=[pq], w=[raw])
            yield
            pm = PSG.get()
            self.mm(pm, pm[0:96, 0:n], [(self.ones96[0:96, 0:96], sq[0:96, 0:n])], r=[sq, self.ones96])
            rstd = qRS.get()
            self.act(rstd[0:96, 0:n], pm[0:96, 0:n], AF.Ln, r=[pm, self.epsT], w=[rstd], bias=self.epsT[0:96, 0:1])
            self.act(rstd[0:96, 0:n], rstd[0:96, 0:n], AF.Exp, r=[rstd], w=[rstd], scale=-0.5)
            if isctx:
                self.stt(qth[0:96, 0:n], raw[0:96, 0:n], gv[0:96, 5:6], rstd[0:96, 0:n], ALU.mult, ALU.mult,
                         r=[raw, rstd, gv], w=[qth])
                return
            qn = qSQ.get()
            self.stt(qn[0:96, 0:n], raw[0:96, 0:n], gv[0:96, 5:6], rstd[0:96, 0:n], ALU.mult, ALU.mult,
                     r=[raw, rstd, gv], w=[qn])
            yield
            pw = PSG.get()
            self.mm(pw, pw[0:96, 0:n], [(self.perm96[0:96, 0:96], qn[0:96, 0:n])], r=[qn, self.perm96])
            t1 = qTF.get()
            t2 = qTF.get()
            self.tt(t1[64:96, 0:n], qn[64:96, 0:n], RALL[64:96, 0, t0:t0 + n], ALU.mult, r=[qn, RALL], w=[t1])
            self.tt(t2[64:96, 0:n], pw[64:96, 0:n], RALL[64:96, 1, t0:t0 + n], ALU.mult, r=[pw, RALL], w=[t2])
            self.copy(qth[0:64, 0:n], qn[0:64, 0:n], r=[qn], w=[qth], eng="pool")
            self.tt(qth[64:96, 0:n], t1[64:96, 0:n], t2[64:96, 0:n], ALU.add, r=[t1, t2], w=[qth])

        def step(g):
            if g is None:
                return None
            try:
                next(g)
                return g
            except StopIteration:
                return None

        def drain(g):
            while g is not None:
                g = step(g)

        units = [(hd, ti) for hd in range(16) for ti in tiles]
        kv_cur = (KTH.get(), VH.get())
        drain(kv_steps(0, kv_cur[0], kv_cur[1]))
        qth = QTH.get()
        drain(q_steps(units[0][0], units[0][1], qth))
        kv_gen = None
        kv_next = None
        LOOK = 2
        for ui, (hd, ti) in enumerate(units):
            t0, n, isctx = TILES[ti]
            if ti == tiles[0]:
                kth, vh = kv_cur
                if hd + 1 < 16:
                    kv_next = (KTH.get(), VH.get())
                    kv_gen = kv_steps(hd + 1, kv_next[0], kv_next[1])
            q_gen = None
            qth_next = None
            if ui + 1 < len(units):
                qth_next = QTH.get()
                q_gen = q_steps(units[ui + 1][0], units[ui + 1][1], qth_next)
            kbs = [32, 33] if isctx else list(range(34))
            nk = len(kbs)
            po = self.PA.get()
            pend = {}

            def issue_s(ki, kth=kth, qth=qth, n=n, kbs=kbs):
                ps_ = PSS.get()
                kb = kbs[ki]
                self.mm1(ps_, ps_[:, 0:n], kth[0:96, kb * 128:(kb + 1) * 128], qth[0:96, 0:n], True, True, r=[kth, qth])
                return ps_
            for ki in range(min(LOOK, nk)):
                pend[ki] = issue_s(ki)
            for ki in range(nk):
                ps_ = pend.pop(ki)
                p = PT.get()
                self.act(p[:, 0:n], ps_[:, 0:n], AF.Exp, r=[ps_, self.nshift], w=[p], bias=self.nshift[:, 0:1], scale=scale)
                if ki + LOOK < nk:
                    pend[ki + LOOK] = issue_s(ki + LOOK)
                self.mm1(po, po[:, 0:n], vh[:, kbs[ki], :], p[:, 0:n], ki == 0, ki == nk - 1, r=[vh, p])
                if ki % 8 == 3:
                    q_gen = step(q_gen)
                if ki % 8 == 7:
                    kv_gen = step(kv_gen)
            drain(q_gen)
            den = DEN.get()
            self.recip(den[64:128, 0:n], po[64:128, 0:n], r=[po], w=[den])
            hb = (hd % 2) * 64
            os_ = OS.get()
            self.tt(os_[hb:hb + 64, 0:n], po[0:64, 0:n], den[64:128, 0:n], ALU.mult, r=[po, den], w=[os_])
            dst = self.AS.rearrange("(k p) t -> p k t", p=128)[hb:hb + 64, hd // 2, t0:t0 + n]
            self.store("AS%d" % ti, dst, os_, os_[hb:hb + 64, 0:n])
            qth = qth_next
            if ti == tiles[-1]:
                drain(kv_gen)
                kv_gen = None
                kv_cur = kv_next
        self.S.barrier()
        self.cur_scope.close()
        persist.close()
        self.cur_scope = None
        self.phase_c_dram(l, I["mwo"], tiles)

    def lru(self, l, idx, tiles):
        nc = self.nc
        I = self.I
        PADL = 2
        CTX0 = L + 6
        XW = T + 8

        def pcol(t0):
            return t0 + PADL if t0 < L else (t0 - L) + CTX0
        self.new_scope()
        self.common_rings(nxt=2, nh=2)
        win = self.sb([128, 8, 2048], BF16, "lwin")
        for k in range(8):
            self.load(win, win[:, k, :], I["lwin"][:, k, :], cast=True)
        STG = self.ring(4, [128, 512], BF16, "stg")
        for ti in range(9):
            t0, n, isctx = TILES[ti]
            s = 1 if isctx else 0
            xt = self.load_x(ti)
            h = self.H.get()
            self.modnorm(xt, n, l, 1, s, h)
            for oc in range(16):
                pt = self.PS.get()
                self.mm(pt, pt[:, 0:n], [(win[:, k, oc * 128:(oc + 1) * 128], h[:, k, 0:n]) for k in range(8)], r=[win, h])
                stg = STG.get()
                if oc < 8:
                    self.act(stg[:, 0:n], pt[:, 0:n], AF.Gelu_apprx_tanh, r=[pt], w=[stg])
                    self.store(("AS", oc, ti), self.AS[oc * 128:(oc + 1) * 128, t0:t0 + n], stg, stg[:, 0:n])
                else:
                    self.copy(stg[:, 0:n], pt[:, 0:n], r=[pt], w=[stg])
                    self.store(("XRS", oc - 8), self.XRS[(oc - 8) * 128:(oc - 7) * 128, t0:t0 + n], stg, stg[:, 0:n])
        self.new_scope()
        self.TF = self.ring(6, [128, 512], F32, "tf")
        gw = self.sb([128, 2, 2, 4, 2, 256], BF16, "lgw")
        for d in range(2):
            for wch in range(2):
                self.load(gw, gw[:, d, wch], I["lgw"][:, d, wch], cast=True)
        sv = self.sb([128, 64], F32, "lsv")
        self.load(sv, sv[:], I["lsv"][:])
        ngb = self.sb([128, 32], F32, "lngb")
        self.load(ngb, ngb[:], I["lgb"][:])
        self.ts(ngb[:], ngb[:], -1.0, None, ALU.mult, r=[ngb], w=[ngb])
        cp = self.sb([128, 16], F32, "lcp")
        cp2 = self.sb([128, 16], F32, "lcp2")
        self.act(cp[:], sv[:, 40:56], AF.Exp, r=[sv], w=[cp], scale=-1.0)
        self.act(cp[:], cp[:], AF.Ln, r=[cp, self.one1], w=[cp], bias=self.one1[:, 0:1])
        self.ts(cp2[:], cp[:], -16.0, None, ALU.mult, r=[cp], w=[cp2])
        self.ts(cp[:], cp[:], -8.0, None, ALU.mult, r=[cp], w=[cp])
        XR = self.sb([128, 2, XW], BF16, "XR")
        self.memset(XR, XR[:, :, 0:PADL], 0.0, eng="pool")
        self.memset(XR, XR[:, :, L + PADL:CTX0], 0.0, eng="pool")
        self.memset(XR, XR[:, :, CTX0 + CT:XW], 0.0, eng="pool")
        XC = self.sb([128, 2, XW], BF16, "XC")
        SF = self.sb([128, XW], F32, "SF")
        CAR = self.ring(4, [128, 1], F32, "car")
        GT_ = self.ring(3, [128, 512], BF16, "gtile")
        zero1 = self.sb([128, 1], F32, "zero1")
        self.memset(zero1, zero1[:], 0.0)
        NW = XW - 4
        for bk in range(4):
            for cc in range(2):
                c = 2 * bk + cc
                self.load(XR, XR[:, cc, PADL:PADL + L], self.XRS[c * 128:(c + 1) * 128, 0:L], r=[("XRS", c)])
                self.load(XR, XR[:, cc, CTX0:CTX0 + CT], self.XRS[c * 128:(c + 1) * 128, L:T], r=[("XRS", c)])
            for cc in range(2):
                c = 2 * bk + cc
                acc = SF
                self.ts(acc[:, 0:NW], XR[:, cc, 0:NW], sv[:, 8 + 4 * c:9 + 4 * c], sv[:, c:c + 1], ALU.mult, ALU.add,
                        r=[XR, sv], w=[acc])
                for k in (1, 2):
                    self.stt(acc[:, 0:NW], XR[:, cc, k:k + NW], sv[:, 8 + 4 * c + k:9 + 4 * c + k], acc[:, 0:NW], ALU.mult, ALU.add,
                             r=[XR, sv, acc], w=[acc])
                self.stt(XC[:, cc, 2:2 + NW], XR[:, cc, 3:3 + NW], sv[:, 8 + 4 * c + 3:9 + 4 * c + 3], acc[:, 0:NW], ALU.mult, ALU.add,
                         r=[XR, sv, acc], w=[XC])
            for cc in range(2):
                c = 2 * bk + cc
                for d in range(2):
                    order = [8] + (list(range(8)) if d == 0 else list(range(7, -1, -1)))
                    carry = zero1
                    for ti in order:
                        t0, n, isctx = TILES[ti]
                        pc = pcol(t0)
                        pr = self.PS.get()
                        self.mm(pr, pr[:, 0:n], [(gw[:, d, 0, bk, kk, cc * 128:(cc + 1) * 128], XC[:, kk, pc:pc + n]) for kk in range(2)],
                                r=[gw, XC])
                        pi = self.PS.get()
                        self.mm(pi, pi[:, 0:n], [(gw[:, d, 1, bk, kk, cc * 128:(cc + 1) * 128], XC[:, kk, pc:pc + n]) for kk in range(2)],
                                r=[gw, XC])
                        gi = (d * 2 + 0) * 8 + c
                        gi2 = (d * 2 + 1) * 8 + c
                        ta = self.TF.get()
                        tb_ = self.TF.get()
                        tcc = self.TF.get()
                        self.act(ta[:, 0:n], pr[:, 0:n], AF.Exp, r=[pr, ngb], w=[ta], bias=ngb[:, gi:gi + 1], scale=-1.0)
                        self.act(ta[:, 0:n], ta[:, 0:n], AF.Ln, r=[ta, self.one1], w=[ta], bias=self.one1[:, 0:1])
                        self.act(ta[:, 0:n], ta[:, 0:n], AF.Exp, r=[ta], w=[ta], scale=-1.0)
                        self.act(tb_[:, 0:n], ta[:, 0:n], AF.Exp, r=[ta, cp], w=[tb_], scale=cp[:, d * 8 + c:d * 8 + c + 1])
                        self.act(ta[:, 0:n], ta[:, 0:n], AF.Exp, r=[ta, cp2], w=[ta], scale=cp2[:, d * 8 + c:d * 8 + c + 1])
                        self.ts(ta[:, 0:n], ta[:, 0:n], 0.99999994, None, ALU.min, r=[ta], w=[ta])
                        self.act(ta[:, 0:n], ta[:, 0:n], AF.Ln, r=[ta, self.one1], w=[ta], bias=self.one1[:, 0:1], scale=-1.0)
                        self.act(ta[:, 0:n], ta[:, 0:n], AF.Exp, r=[ta], w=[ta], scale=0.5)
                        self.act(tcc[:, 0:n], pi[:, 0:n], AF.Exp, r=[pi, ngb], w=[tcc], bias=ngb[:, gi2:gi2 + 1], scale=-1.0)
                        self.act(tcc[:, 0:n], tcc[:, 0:n], AF.Ln, r=[tcc, self.one1], w=[tcc], bias=self.one1[:, 0:1])
                        self.act(tcc[:, 0:n], tcc[:, 0:n], AF.Exp, r=[tcc], w=[tcc], scale=-1.0)
                        self.tt(tcc[:, 0:n], tcc[:, 0:n], XC[:, cc, pc:pc + n], ALU.mult, r=[tcc, XC], w=[tcc])
                        self.tt(tcc[:, 0:n], tcc[:, 0:n], ta[:, 0:n], ALU.mult, r=[tcc, ta], w=[tcc])
                        so = self.TF.get()
                        ncar = CAR.get()
                        if d == 0:
                            self.op("dve", (lambda so=so, tb_=tb_, tcc=tcc, n=n, carry=carry: nc.vector.tensor_tensor_scan(
                                out=so[:, 0:n], data0=tb_[:, 0:n], data1=tcc[:, 0:n], initial=carry[:, 0:1],
                                op0=ALU.mult, op1=ALU.add)), r=[tb_, tcc, carry], w=[so])
                            self.copy(ncar[:, 0:1], so[:, n - 1:n], r=[so], w=[ncar])
                            self.copy(SF[:, t0:t0 + n], so[:, 0:n], r=[so], w=[SF], eng="pool")
                        else:
                            self.op("dve", (lambda so=so, tb_=tb_, tcc=tcc, n=n, carry=carry: nc.vector.tensor_tensor_scan(
                                out=so[:, 0:n][:, ::-1], data0=tb_[:, 0:n][:, ::-1], data1=tcc[:, 0:n][:, ::-1], initial=carry[:, 0:1],
                                op0=ALU.mult, op1=ALU.add)), r=[tb_, tcc, carry], w=[so])
                            self.copy(ncar[:, 0:1], so[:, 0:1], r=[so], w=[ncar])
                            self.tt(so[:, 0:n], so[:, 0:n], SF[:, t0:t0 + n], ALU.add, r=[so, SF], w=[so])
                            gtile = GT_.get()
                            asl = self.AS[c * 128:(c + 1) * 128, t0:t0 + n]
                            self.load(gtile, gtile[:, 0:n], asl, r=[("AS", c, ti)])
                            self.tt(gtile[:, 0:n], so[:, 0:n], gtile[:, 0:n], ALU.mult, r=[so, gtile], w=[gtile])
                            self.store(("AS", c, ti), asl, gtile, gtile[:, 0:n])
                        carry = ncar
        self.phase_c_dram_multi(l, I["lwout"], tiles)

    def phase_c_dram_multi(self, l, wo_ap, tiles):
        self.new_scope()
        self.XT = self.ring(2, [128, 8, 512], F32, "xt")
        AT = self.ring(2, [128, 8, 512], BF16, "at")
        wo = self.sb([128, 8, 1024], BF16, "wo")
        for k in range(8):
            self.load(wo, wo[:, k, :], wo_ap[:, k, :], cast=True)
        for ti in tiles:
            t0, n, isctx = TILES[ti]
            xt = self.load_x(ti)
            at = AT.get()
            self.load(at, at[:, :, 0:n], self.xview(self.AS, t0, n), r=[("AS", c, ti) for c in range(8)])
            self.resid(l, ti, xt, at, (lambda k, at=at, n=n: at[:, k, 0:n]), wo)


def input_shapes():
    return {
        "xin": (D, T), "cond": (128, 8, 2), "n1g": (128, 4, 8), "n2g": (128, 4, 8),
        "modw": (4, 128, 8, 6144), "modb": (128, 4, 48),
        "wug": (4, NJ, 128, 1024), "wuv": (4, NJ, 128, 1024), "wdn": (4, NJ, 128, 1024),
        "fcw": (128, 4, NJ, 3), "fcb": (128, 4, NJ),
        "swq": (2, 128, 8, 1024), "swk": (2, 128, 8, 256), "swv": (2, 128, 8, 256), "swo": (2, 128, 8, 1024),
        "sqg": (2, 128, 1), "skg": (2, 128, 1), "ssink": (2, 128, 16),
        "rcs": (128, L), "rsn": (128, L), "mcs": (128, L), "msn": (128, L),
        "mdn": (128, 8, 640), "mrp": (128, 8, 96), "muq": (128, 3, 1536), "muk": (128, 2, 1024), "muv": (128, 2, 1024),
        "mwo": (128, 8, 1024), "mgv": (128, 8),
        "lwin": (128, 8, 2048), "lwout": (128, 8, 1024), "lgw": (128, 2, 2, 4, 2, 256), "lsv": (128, 64), "lgb": (128, 32),
        "c_ones1024": (128, 128), "c_blk64": (128, 128), "c_ones384": (128, 128), "c_ones256": (128, 128),
        "c_ones96": (128, 128), "c_perm64": (128, 128), "c_perm96": (128, 128),
    }


def _fm(v, nch):
    return np.ascontiguousarray(np.asarray(v, np.float32).reshape(nch, 128).T)


def _wfm(w):
    K, N = w.shape
    return np.ascontiguousarray(np.asarray(w, np.float32).reshape(K // 128, 128, N).transpose(1, 0, 2))


def _rope_tables(rot_dim, base_part, nrows):
    n_freq = rot_dim // 4
    half = rot_dim // 2
    t = np.arange(L)
    row = (t // 64).astype(np.float32)
    col = (t % 64).astype(np.float32)
    inv = (np.float32(10000.0) ** (-np.arange(n_freq, dtype=np.float32) / np.float32(n_freq))).astype(np.float32)
    cs = np.zeros((128, L), np.float32)
    sn = np.zeros((128, L), np.float32)
    partner = np.zeros(128, np.int64) - 1
    for p in range(nrows):
        d = p % rot_dim if base_part == 0 else p
        pp = p + base_part
        dd = d % rot_dim
        pos = row if dd < half else col
        e = dd % half
        i = e % n_freq
        ang = (pos * inv[i]).astype(np.float32)
        cs[pp] = np.cos(ang)
        sgn = -1.0 if e < n_freq else 1.0
        sn[pp] = sgn * np.sin(ang)
        partner[pp] = pp + n_freq if e < n_freq else pp - n_freq
    return cs, sn, partner


def prepare_shared(inp):
    f = lambda k: np.asarray(inp[k], np.float32)
    sh = {}
    sh["n1g"] = np.ascontiguousarray(f("norm1").reshape(4, 8, 128).transpose(2, 0, 1))
    sh["n2g"] = np.ascontiguousarray(f("norm2").reshape(4, 8, 128).transpose(2, 0, 1))
    sh["modw"] = np.ascontiguousarray(f("mod_w").reshape(4, 8, 128, 6144).transpose(0, 2, 1, 3))
    sh["modb"] = np.ascontiguousarray(f("mod_b").reshape(4, 48, 128).transpose(2, 0, 1))
    wup = f("ffn_w_up")
    def upl(w):
        return np.ascontiguousarray(w.reshape(4, 8, 128, NJ, 128).transpose(0, 3, 2, 1, 4).reshape(4, NJ, 128, 1024))
    sh["wug"] = upl(wup[:, :, :DFF])
    sh["wuv"] = upl(wup[:, :, DFF:])
    sh["wdn"] = np.ascontiguousarray(f("ffn_w_down").reshape(4, NJ, 128, 1024))
    sh["fcw"] = np.ascontiguousarray(f("ffn_conv_w").reshape(4, 3, NJ, 128).transpose(3, 0, 2, 1))
    sh["fcb"] = np.ascontiguousarray(f("ffn_conv_b").reshape(4, NJ, 128).transpose(2, 0, 1))
    wqkv = f("swa_w_qkv")
    wq = wqkv[:, :, :1024].reshape(2, 1024, 2, 8, 64).transpose(0, 1, 3, 2, 4).reshape(2, 1024, 1024)
    wk = wqkv[:, :, 1024:1280].reshape(2, 1024, 2, 2, 64).transpose(0, 1, 3, 2, 4).reshape(2, 1024, 256)
    wv = wqkv[:, :, 1280:1536]
    sh["swq"] = np.stack([_wfm(wq[i]) for i in range(2)])
    sh["swk"] = np.stack([_wfm(wk[i]) for i in range(2)])
    sh["swv"] = np.stack([_wfm(wv[i]) for i in range(2)])
    wo = f("swa_w_o").reshape(2, 2, 8, 64, 1024).transpose(0, 2, 1, 3, 4).reshape(2, 1024, 1024)
    sh["swo"] = np.stack([_wfm(wo[i]) for i in range(2)])
    sh["sqg"] = np.ascontiguousarray(np.tile(f("swa_q_gain"), (1, 2)).reshape(2, 128, 1))
    sh["skg"] = np.ascontiguousarray(np.tile(f("swa_k_gain"), (1, 2)).reshape(2, 128, 1))
    sh["ssink"] = np.ascontiguousarray(np.broadcast_to(f("swa_sink")[:, None, :], (2, 128, 16)))
    cs, sn, partner = _rope_tables(64, 0, 128)
    sh["rcs"], sh["rsn"] = cs, sn
    pm = np.zeros((128, 128), np.float32)
    for m in range(128):
        pm[partner[m], m] = 1.0
    sh["c_perm64"] = pm
    cs, sn, partner = _rope_tables(32, 64, 32)
    sh["mcs"], sh["msn"] = cs, sn
    pm = np.zeros((128, 128), np.float32)
    for m in range(64, 96):
        pm[partner[m], m] = 1.0
    sh["c_perm96"] = pm
    wd = f("mla_w_down")[0]
    sh["mdn"] = _wfm(wd[:, :640])
    wr = np.zeros((1024, 96), np.float32)
    wr[:, 64:96] = wd[:, 640:672]
    sh["mrp"] = _wfm(wr)
    sh["muq"] = _wfm(f("mla_w_uq")[0])
    sh["muk"] = _wfm(f("mla_w_uk")[0])
    sh["muv"] = _wfm(f("mla_w_uv")[0])
    sh["mwo"] = _wfm(f("mla_w_o")[0])
    gv = np.zeros((128, 8), np.float32)
    gv[:, 0:3] = _fm(f("mla_q_lora_gain")[0], 3)
    gv[:, 3:5] = _fm(f("mla_kv_lora_gain")[0], 2)
    gv[0:96, 5] = f("mla_q_gain")[0]
    gv[0:96, 6] = f("mla_k_gain")[0]
    sh["mgv"] = gv
    sh["lwin"] = _wfm(f("lru_w_in")[0])
    sh["lwout"] = _wfm(f("lru_w_out")[0])
    gw = f("lru_gate_w")[0]
    sh["lgw"] = np.ascontiguousarray(gw.reshape(2, 2, 4, 2, 128, 256).transpose(4, 0, 1, 2, 3, 5))
    sv = np.zeros((128, 64), np.float32)
    sv[:, 0:8] = _fm(f("lru_conv_b")[0], 8)
    cw = f("lru_conv_w")[0]
    sv[:, 8:40] = cw.reshape(4, 8, 128).transpose(2, 1, 0).reshape(128, 32)
    lam = f("lru_lam")[0]
    sv[:, 40:56] = lam.reshape(2, 8, 128).transpose(2, 0, 1).reshape(128, 16)
    sh["lsv"] = sv
    gb = f("lru_gate_b")[0]
    sh["lgb"] = np.ascontiguousarray(gb.reshape(2, 2, 8, 128).transpose(3, 0, 1, 2).reshape(128, 32))
    sh["c_ones1024"] = np.full((128, 128), 1.0 / 1024, np.float32)
    b = np.zeros((128, 128), np.float32)
    b[0:64, 0:64] = 1.0 / 64
    b[64:128, 64:128] = 1.0 / 64
    sh["c_blk64"] = b
    sh["c_ones384"] = np.full((128, 128), 1.0 / 384, np.float32)
    sh["c_ones256"] = np.full((128, 128), 1.0 / 256, np.float32)
    sh["c_ones96"] = np.full((128, 128), 1.0 / 96, np.float32)
    return sh


def prepare_core(inp, b):
    x = np.asarray(inp["x"][b], np.float32)
    ctx = np.asarray(inp["ctx"][b], np.float32)
    xin = np.ascontiguousarray(np.concatenate([x.T, ctx.T], axis=1))
    cond = np.stack([_fm(np.asarray(inp["c"][b]), 8), _fm(np.asarray(inp["c_ctx"]), 8)], axis=2)
    return {"xin": xin, "cond": np.ascontiguousarray(cond)}


_NC_CACHE = {}


def kernel(**inputs):
    sh = prepare_shared(inputs)
    if "nc" not in _NC_CACHE:
        _NC_CACHE["nc"] = Builder().build()
    nc = _NC_CACHE["nc"]
    in_maps = []
    for b in range(8):
        m = dict(sh)
        m.update(prepare_core(inputs, b))
        in_maps.append(m)
    res = run_bass_kernel_spmd(nc, in_maps, core_ids=list(range(8)))
    out = np.stack([np.ascontiguousarray(r["outT"].T) for r in res.results], axis=0)
    return out.astype(np.float32)
```

```python
from contextlib import ExitStack
import numpy as np
import concourse.bass as bass
import concourse.mybir as mybir
from concourse.bass_utils import run_bass_kernel_spmd

F32 = mybir.dt.float32
BF16 = mybir.dt.bfloat16
AF = mybir.ActivationFunctionType
ALU = mybir.AluOpType

D = 1024
L = 4096
CT = 256
T = L + CT
DEPTH = 4
DFF = 2816
NJ = DFF // 128
EPS = 1e-6
SHIFT = 16.0
TILES = [(i * 512, 512, False) for i in range(8)] + [(L, CT, True)]


class _Op:
    __slots__ = ("eng", "fn", "deps", "dkey", "sig", "sem", "val")

    def __init__(self, eng, fn, deps, dkey):
        self.eng = eng
        self.fn = fn
        self.deps = deps
        self.dkey = dkey
        self.sig = False
        self.sem = None
        self.val = 0


class Sched:
    EPOCH = 20000

    def __init__(self, nc, stack):
        self.nc = nc
        self.stack = stack
        self.ops = []
        self.last_w = {}
        self.readers = {}
        self.last_dkey = {}
        self.pending_bar = {}
        self.bar_idx = -1
        self.emitted = 0
        self.engs = {"pe": nc.tensor, "act": nc.scalar, "dve": nc.vector,
                     "pool": nc.gpsimd, "sp": nc.sync}
        self.eng_cnt = {e: 0 for e in self.engs}
        self.eng_sem = {}
        self.key_sem = {}
        self.sem_cnt = {}
        self.free_sems = {}
        self.waited = {e: {} for e in self.engs}
        self.nsem = 0
        self.ninst = 0

    def add(self, eng, fn, reads=(), writes=(), dkey=None):
        i = len(self.ops)
        deps = set()
        for r in reads:
            w = self.last_w.get(r)
            if w is not None:
                deps.add(w)
        for r in writes:
            w = self.last_w.get(r)
            if w is not None:
                deps.add(w)
            for rd in self.readers.get(r, ()):
                deps.add(rd)
        if dkey is not None:
            p = self.last_dkey.get(dkey)
            if p is not None:
                deps.add(p)
            self.last_dkey[dkey] = i
        deps = set(d for d in deps if d > self.bar_idx)
        if eng in self.pending_bar:
            deps |= self.pending_bar.pop(eng)
        for r in reads:
            self.readers.setdefault(r, []).append(i)
        for r in writes:
            self.last_w[r] = i
            self.readers[r] = []
        deps.discard(i)
        red = {}
        for d in deps:
            o = self.ops[d]
            src = ("k", o.dkey) if o.dkey is not None else ("e", o.eng)
            if src == ("e", "pe") and eng == "pe" and dkey is None and d > self.bar_idx:
                continue
            if src not in red or red[src] < d:
                red[src] = d
        self.ops.append(_Op(eng, fn, sorted(red.values()), dkey))
        return i

    def barrier(self):
        last = {}
        for i in range(self.bar_idx + 1, len(self.ops)):
            o = self.ops[i]
            src = ("k", o.dkey) if o.dkey is not None else ("e", o.eng)
            last[src] = i
        deps = set(last.values())
        for d in deps:
            self.ops[d].sig = True
        self.flush()
        for e in self.engs:
            self.pending_bar[e] = set(deps) | self.pending_bar.get(e, set())
        self.bar_idx = len(self.ops) - 1
        for (e, _k), sem in self.key_sem.items():
            self.free_sems.setdefault(e, []).append(sem)
        self.key_sem = {}

    def flush(self):
        nc = self.nc
        ops = self.ops
        for i in range(self.emitted, len(ops)):
            for d in ops[i].deps:
                ops[d].sig = True
        for i in range(self.emitted, len(ops)):
            o = ops[i]
            E = self.engs[o.eng]
            w = self.waited[o.eng]
            for d in o.deps:
                do = ops[d]
                sid = id(do.sem)
                if w.get(sid, 0) >= do.val:
                    continue
                E.wait_ge(do.sem, do.val)
                w[sid] = do.val
            inst = o.fn()
            o.fn = None
            self.ninst += 1
            if o.dkey is not None:
                kk = (o.eng, o.dkey)
                if kk not in self.key_sem:
                    fl = self.free_sems.setdefault(o.eng, [])
                    if fl:
                        self.key_sem[kk] = fl.pop()
                    else:
                        sem = self.stack.enter_context(nc.semaphore("k%d" % self.nsem))
                        self.nsem += 1
                        self.sem_cnt[id(sem)] = 0
                        self.key_sem[kk] = sem
                o.sem = self.key_sem[kk]
                self.sem_cnt[id(o.sem)] += 16
                o.val = self.sem_cnt[id(o.sem)]
                inst.then_inc(o.sem, 16)
            elif o.sig:
                if o.eng not in self.eng_sem or self.eng_cnt[o.eng] >= self.EPOCH:
                    self.eng_sem[o.eng] = self.stack.enter_context(nc.semaphore("e%d" % self.nsem))
                    self.nsem += 1
                    self.eng_cnt[o.eng] = 0
                self.eng_cnt[o.eng] += 1
                o.sem = self.eng_sem[o.eng]
                o.val = self.eng_cnt[o.eng]
                inst.then_inc(o.sem, 1)
        self.emitted = len(ops)

    def emit(self):
        self.flush()


class Tile:
    def __init__(self, t, key):
        self.t = t
        self.key = key

    def __getitem__(self, idx):
        return self.t[idx]


class Ring:
    def __init__(self, tiles):
        self.tiles = tiles
        self.i = 0

    def get(self):
        t = self.tiles[self.i % len(self.tiles)]
        self.i += 1
        return t


class Builder:
    def __init__(self, layers=(0, 1, 2, 3)):
        self.layers = tuple(layers)
        self.nc = bass.Bass("TRN2", target_bir_lowering=False)
        self.cnt = 0

    def sb(self, shape, dtype, name="t"):
        self.cnt += 1
        nm = "%s_%d" % (name, self.cnt)
        t = self.scope.enter_context(self.nc.sbuf_tensor(nm, list(shape), dtype))
        return Tile(t, nm)

    def ps(self, name="ps"):
        self.cnt += 1
        nm = "%s_%d" % (name, self.cnt)
        t = self.stack.enter_context(self.nc.psum_tensor(nm, [128, 512], F32))
        return Tile(t, nm)

    def ring(self, n, shape, dtype, name="r"):
        return Ring([self.sb(shape, dtype, name) for _ in range(n)])

    def din(self, name, shape):
        return self.nc.dram_tensor(name, list(shape), F32, kind="ExternalInput").ap()

    def op(self, eng, fn, r=(), w=(), dkey=None):
        return self.S.add(eng, fn, reads=[x.key if isinstance(x, Tile) else x for x in r],
                          writes=[x.key if isinstance(x, Tile) else x for x in w], dkey=dkey)

    def load(self, dst, dst_ap, src_ap, r=(), cast=False):
        nc = self.nc
        if cast:
            self.op("pool", lambda: nc.gpsimd.dma_start(out=dst_ap, in_=src_ap, max_dma_last_dim=4096),
                    r=r, w=[dst], dkey=dst.key)
        else:
            self.op("sp", lambda: nc.sync.dma_start(out=dst_ap, in_=src_ap), r=r, w=[dst], dkey=dst.key)

    def store(self, dram_key, dst_ap, src, src_ap):
        nc = self.nc
        self.op("sp", lambda: nc.sync.dma_start(out=dst_ap, in_=src_ap), r=[src], w=[dram_key], dkey=src.key)

    def mm(self, ps, out_ap, pairs, r=()):
        nc = self.nc
        n = len(pairs)
        for i, (lt, rh) in enumerate(pairs):
            self.op("pe", (lambda lt=lt, rh=rh, i=i: nc.tensor.matmul(out_ap, lt, rh, start=(i == 0), stop=(i == n - 1))),
                    r=r, w=[ps])

    def act(self, out_ap, in_ap, func, r=(), w=(), bias=None, scale=None):
        nc = self.nc
        kw = {}
        if bias is not None:
            kw["bias"] = bias
        if scale is not None:
            kw["scale"] = scale
        self.op("act", lambda: nc.scalar.activation(out=out_ap, in_=in_ap, func=func, **kw), r=r, w=w)

    def tt(self, out_ap, a, b, op, r=(), w=(), eng="dve"):
        nc = self.nc
        e = nc.vector if eng == "dve" else nc.gpsimd
        self.op(eng, lambda: e.tensor_tensor(out=out_ap, in0=a, in1=b, op=op), r=r, w=w)

    def ts(self, out_ap, a, s1, s2, op0, op1=None, r=(), w=(), eng="dve"):
        nc = self.nc
        e = nc.vector if eng == "dve" else nc.gpsimd
        if op1 is None:
            self.op(eng, lambda: e.tensor_scalar(out=out_ap, in0=a, scalar1=s1, scalar2=None, op0=op0), r=r, w=w)
        else:
            self.op(eng, lambda: e.tensor_scalar(out=out_ap, in0=a, scalar1=s1, scalar2=s2, op0=op0, op1=op1), r=r, w=w)

    def stt(self, out_ap, a, s, b, op0, op1, r=(), w=()):
        nc = self.nc
        self.op("dve", lambda: nc.vector.scalar_tensor_tensor(out=out_ap, in0=a, scalar=s, in1=b, op0=op0, op1=op1), r=r, w=w)

    def copy(self, out_ap, in_ap, r=(), w=(), eng="dve"):
        nc = self.nc
        if eng == "act":
            self.op("act", lambda: nc.scalar.copy(out=out_ap, in_=in_ap), r=r, w=w)
        else:
            e = nc.vector if eng == "dve" else nc.gpsimd
            self.op(eng, lambda: e.tensor_copy(out=out_ap, in_=in_ap), r=r, w=w)

    def memset(self, t, ap, val, eng="dve"):
        nc = self.nc
        e = nc.vector if eng == "dve" else nc.gpsimd
        self.op(eng, lambda: e.memset(ap, val), w=[t])

    def mm1(self, ps, out_ap, lhsT, rhs, start, stop, r=()):
        nc = self.nc
        self.op("pe", lambda: nc.tensor.matmul(out_ap, lhsT, rhs, start=start, stop=stop), r=r, w=[ps])

    def recip(self, out_ap, in_ap, r=(), w=()):
        nc = self.nc
        self.op("dve", lambda: nc.vector.reciprocal(out=out_ap, in_=in_ap), r=r, w=w)

    def new_scope(self):
        if self.cur_scope is not None:
            self.S.barrier()
            self.cur_scope.close()
        self.cur_scope = ExitStack()
        self.scope = self.cur_scope

    def common_rings(self, nxt=2, nh=2, nsq1=4, ntf=5):
        self.XT = self.ring(nxt, [128, 8, 512], F32, "xt")
        self.SQ = self.ring(1, [128, 8, 512], BF16, "sq")
        self.SQ1 = self.ring(nsq1, [128, 512], BF16, "sq1") if nsq1 else None
        self.RS = self.ring(2, [128, 512], F32, "rstd")
        self.TF = self.ring(ntf, [128, 512], F32, "tf")
        self.H = self.ring(nh, [128, 8, 512], BF16, "h")

    def build(self):
        nc = self.nc
        I = {}
        for k, shp in input_shapes().items():
            I[k] = self.din(k, shp)
        self.I = I
        self.out = nc.dram_tensor("outT", [D, L], F32, kind="ExternalOutput").ap()
        self.XS = nc.dram_tensor("xs", [D, T], F32).ap()
        self.AS = nc.dram_tensor("acts", [D, T], BF16).ap()
        self.XRS = nc.dram_tensor("xrs", [D, T], BF16).ap()
        self.xsrc_is_input = True
        with ExitStack() as stack:
            self.stack = stack
            self.scope = stack
            self.cur_scope = None
            self.S = Sched(nc, stack)
            self.PS = Ring([self.ps() for _ in range(6)])
            self.PA = Ring([self.ps() for _ in range(2)])
            self.setup_consts()
            for l in self.layers:
                self.layer(l)
            self.op("sp", lambda: nc.sync.nop(), r=["OUT%d" % i for i in range(8)])
            self.S.emit()
            if self.cur_scope is not None:
                self.cur_scope.close()
        return nc

    def xview(self, ap, t0, n):
        return ap.rearrange("(k p) t -> p k t", p=128)[:, :, t0:t0 + n]

    def load_x(self, ti):
        t0, n, isctx = TILES[ti]
        xt = self.XT.get()
        src = self.I["xin"] if self.xsrc_is_input else self.XS
        self.load(xt, xt[:, :, 0:n], self.xview(src, t0, n), r=["X%d" % ti])
        return xt

    def store_x(self, ti, xt, final=False):
        t0, n, isctx = TILES[ti]
        if final and not isctx:
            self.store("OUT%d" % ti, self.xview(self.out, t0, n), xt, xt[:, :, 0:n])
        else:
            self.store("X%d" % ti, self.xview(self.XS, t0, n), xt, xt[:, :, 0:n])

    def setup_consts(self):
        nc = self.nc
        I = self.I

        def cload(name, shape, dtype=BF16):
            t = self.sb(shape, dtype, name)
            self.load(t, t[:], I[name][:], cast=(dtype == BF16))
            return t
        self.ones1024 = cload("c_ones1024", [128, 128])
        self.blk64 = cload("c_blk64", [128, 128])
        self.ones384 = cload("c_ones384", [128, 128])
        self.ones256 = cload("c_ones256", [128, 128])
        self.ones96 = cload("c_ones96", [128, 128])
        self.perm64 = cload("c_perm64", [128, 128])
        self.perm96 = cload("c_perm96", [128, 128])
        self.epsT = self.sb([128, 1], F32, "eps")
        self.memset(self.epsT, self.epsT[:], EPS)
        self.nshift = self.sb([128, 1], F32, "nshift")
        self.memset(self.nshift, self.nshift[:], -SHIFT)
        self.one1 = self.sb([128, 1], F32, "one1")
        self.memset(self.one1, self.one1[:], 1.0)
        self.n1g = cload("n1g", [128, 4, 8], F32)
        self.n2g = cload("n2g", [128, 4, 8], F32)
        self.modb = cload("modb", [128, 4, 48], F32)
        self.fcw = cload("fcw", [128, 4, NJ, 3], F32)
        self.fcb = cload("fcb", [128, 4, NJ], F32)
        self.XB = self.sb([128, 8, 16], F32, "XB")
        self.MV = {l: self.sb([128, 6, 8, 2], F32, "mv") for l in self.layers}
        cond = cload("cond", [128, 8, 2], F32)
        condT = self.sb([128, 8, 2], F32, "condT")
        self.act(condT[:], cond[:], AF.Silu, r=[cond], w=[condT])
        self.new_scope()
        wr = self.ring(2, [128, 6144], F32, "modw")
        for l in self.layers:
            acc = self.sb([128, 48, 2], F32, "modacc")
            for k in range(8):
                wt = wr.get()
                self.load(wt, wt[:], I["modw"][l, :, k, :])
                pt = self.PS.get()
                for j in range(48):
                    self.mm1(pt, pt[:, 2 * j:2 * j + 2], wt[:, j * 128:(j + 1) * 128], condT[:, k, :], True, True,
                             r=[wt, condT])
                pv = pt[:, 0:96].rearrange("p (j s) -> p j s", s=2)
                if k == 0:
                    self.copy(acc[:], pv, r=[pt], w=[acc])
                else:
                    self.tt(acc[:], pv, acc[:], ALU.add, r=[pt, acc], w=[acc])
            mb = self.modb[:, l, :].unsqueeze(2).broadcast_to([128, 48, 2])
            self.tt(acc[:], acc[:], mb, ALU.add, r=[acc, self.modb], w=[acc])
            mv = self.MV[l]
            a4 = acc[:].rearrange("p (m k) s -> p m k s", m=6)
            for dst, srcm in ((1, 0), (2, 2), (4, 3), (5, 5)):
                self.copy(mv[:, dst], a4[:, srcm], r=[acc], w=[mv])
            for dst, srcm, g in ((0, 1, self.n1g), (3, 4, self.n2g)):
                tmp = self.sb([128, 8, 2], F32, "mtmp")
                self.ts(tmp[:], a4[:, srcm], 1.0, None, ALU.add, r=[acc], w=[tmp])
                gb = g[:, l, :].unsqueeze(2).broadcast_to([128, 8, 2])
                self.tt(mv[:, dst], tmp[:], gb, ALU.mult, r=[tmp, g], w=[mv])

    def modnorm(self, xt, n, l, which, s, h):
        mv = self.MV[l]
        sq = self.SQ.get()
        self.act(sq[:, :, 0:n], xt[:, :, 0:n], AF.Square, r=[xt], w=[sq])
        pt = self.PS.get()
        self.mm(pt, pt[:, 0:n], [(self.ones1024[:], sq[:, k, 0:n]) for k in range(8)], r=[sq, self.ones1024])
        rstd = self.RS.get()
        self.act(rstd[:, 0:n], pt[:, 0:n], AF.Ln, r=[pt, self.epsT], w=[rstd], bias=self.epsT[:, 0:1])
        self.act(rstd[:, 0:n], rstd[:, 0:n], AF.Exp, r=[rstd], w=[rstd], scale=-0.5)
        a_i, b_i = (0, 1) if which == 1 else (3, 4)
        for k in range(8):
            tmp = self.TF.get()
            self.stt(tmp[:, 0:n], xt[:, k, 0:n], mv[:, a_i, k, s:s + 1], rstd[:, 0:n], ALU.mult, ALU.mult,
                     r=[xt, mv, rstd], w=[tmp])
            self.act(h[:, k, 0:n], tmp[:, 0:n], AF.Identity, r=[tmp, mv], w=[h], bias=mv[:, b_i, k, s:s + 1])

    def headnorm(self, pt, rows, n, onesmat, gain_ap, gain_t, dst_t, dst_ap, rope=None):
        sq = self.SQ1.get()
        raw = self.TF.get()
        self.act(sq[0:rows, 0:n], pt[0:rows, 0:n], AF.Square, r=[pt], w=[sq])
        self.act(raw[0:rows, 0:n], pt[0:rows, 0:n], AF.Copy, r=[pt], w=[raw])
        pm = self.PS.get()
        self.mm(pm, pm[0:rows, 0:n], [(onesmat[0:rows, 0:rows], sq[0:rows, 0:n])], r=[sq, onesmat])
        rstd = self.RS.get()
        self.act(rstd[0:rows, 0:n], pm[0:rows, 0:n], AF.Ln, r=[pm, self.epsT], w=[rstd], bias=self.epsT[0:rows, 0:1])
        self.act(rstd[0:rows, 0:n], rstd[0:rows, 0:n], AF.Exp, r=[rstd], w=[rstd], scale=-0.5)
        if rope is None:
            self.stt(dst_ap, raw[0:rows, 0:n], gain_ap, rstd[0:rows, 0:n], ALU.mult, ALU.mult,
                     r=[raw, rstd, gain_t], w=[dst_t])
            return
        if len(rope) == 5:
            cs, sn, permT, r0, r1 = rope
            cs_ap, sn_ap = cs[:, 0:n], sn[:, 0:n]
        else:
            cs, sn, permT, r0, r1, cs_ap, sn_ap = rope
        qn = self.SQ1.get()
        self.stt(qn[0:rows, 0:n], raw[0:rows, 0:n], gain_ap, rstd[0:rows, 0:n], ALU.mult, ALU.mult,
                 r=[raw, rstd, gain_t], w=[qn])
        pw = self.PS.get()
        self.mm(pw, pw[0:rows, 0:n], [(permT[0:rows, 0:rows], qn[0:rows, 0:n])], r=[qn, permT])
        t1 = self.TF.get()
        t2 = self.TF.get()
        self.tt(t1[r0:r1, 0:n], qn[r0:r1, 0:n], cs_ap[r0:r1], ALU.mult, r=[qn, cs], w=[t1])
        self.tt(t2[r0:r1, 0:n], pw[r0:r1, 0:n], sn_ap[r0:r1], ALU.mult, r=[pw, sn], w=[t2])
        if r0 > 0:
            self.copy(dst_ap[0:r0], qn[0:r0, 0:n], r=[qn], w=[dst_t], eng="pool")
        self.tt(dst_ap[r0:r1], t1[r0:r1, 0:n], t2[r0:r1, 0:n], ALU.add, r=[t1, t2], w=[dst_t])

    def load_rope(self, csname, snname, ti):
        t0, n, isctx = TILES[ti]
        cs = self.ROPE.get()
        sn = self.ROPE.get()
        self.load(cs, cs[:, 0:n], self.I[csname][:, t0:t0 + n])
        self.load(sn, sn[:, 0:n], self.I[snname][:, t0:t0 + n])
        return cs, sn

    def resid(self, l, ti, xt, act_t, act_fn, wo):
        mv = self.MV[l]
        t0, n, isctx = TILES[ti]
        s = 1 if isctx else 0
        for oc in range(8):
            pt = self.PS.get()
            self.mm(pt, pt[:, 0:n], [(wo[:, k, oc * 128:(oc + 1) * 128], act_fn(k)) for k in range(8)], r=[act_t, wo])
            self.stt(xt[:, oc, 0:n], pt[:, 0:n], mv[:, 2, oc, s:s + 1], xt[:, oc, 0:n], ALU.mult, ALU.add,
                     r=[pt, mv, xt], w=[xt])
        if not isctx:
            self.copy(self.XB[:, :, 2 * ti:2 * ti + 1], xt[:, :, 0:1], r=[xt], w=[self.XB], eng="pool")
            self.copy(self.XB[:, :, 2 * ti + 1:2 * ti + 2], xt[:, :, n - 1:n], r=[xt], w=[self.XB], eng="pool")
        self.store_x(ti, xt)

    def phase_c_dram(self, l, wo_name_ap, tiles):
        self.new_scope()
        self.XT = self.ring(2, [128, 8, 512], F32, "xt")
        AT = self.ring(2, [128, 8, 512], BF16, "at")
        wo = self.sb([128, 8, 1024], BF16, "wo")
        for k in range(8):
            self.load(wo, wo[:, k, :], wo_name_ap[:, k, :], cast=True)
        for ti in tiles:
            t0, n, isctx = TILES[ti]
            xt = self.load_x(ti)
            at = AT.get()
            self.load(at, at[:, :, 0:n], self.xview(self.AS, t0, n), r=["AS%d" % ti])
            self.resid(l, ti, xt, at, (lambda k, at=at, n=n: at[:, k, 0:n]), wo)

    def layer(self, l):
        kind, idx = l % 3, l // 3
        last = (l == DEPTH - 1)
        final = (l == self.layers[-1])
        tiles = list(range(8)) if last else list(range(9))
        if kind == 0:
            self.swa(l, idx, tiles)
        elif kind == 1:
            self.mla(l, idx, tiles)
        else:
            self.lru(l, idx, tiles)
        self.xsrc_is_input = False
        self.ffn(l, tiles, final)

    def ffn(self, l, tiles, final):
        I = self.I
        mv = self.MV[l]
        self.new_scope()
        self.common_rings()
        WG = self.ring(3, [128, 8, 128], BF16, "wg")
        WV = self.ring(3, [128, 8, 128], BF16, "wv")
        WD = self.ring(1, [128, NJ, 1024], BF16, "wd")
        HID = self.ring(1, [128, NJ, 512], BF16, "hid")
        GB = self.sb([128, NJ, 16], F32, "gb")
        GT = self.ring(2, [128, 514], F32, "gt")
        HB = self.sb([128, 8, 16], BF16, "hb")
        self.modnorm(self.XB, 16, l, 2, 0, HB)
        for idx_t, ti in enumerate(tiles):
            t0, n, isctx = TILES[ti]
            s = 1 if isctx else 0
            xt = self.load_x(ti)
            h = self.H.get()
            self.modnorm(xt, n, l, 2, s, h)
            hid = HID.get()
            wd = WD.get()
            for j in range(NJ):
                wg = WG.get()
                wv = WV.get()
                self.load(wg, wg[:], I["wug"][l, j].rearrange("p (k m) -> p k m", k=8), cast=True)
                self.load(wv, wv[:], I["wuv"][l, j].rearrange("p (k m) -> p k m", k=8), cast=True)
                self.load(wd, wd[:, j, :], I["wdn"][l, j], cast=True)
                if idx_t == 0:
                    pb = self.PS.get()
                    self.mm(pb, pb[:, 0:16], [(wg[:, k, :], HB[:, k, :]) for k in range(8)], r=[wg, HB])
                    self.copy(GB[:, j, :], pb[:, 0:16], r=[pb], w=[GB])
                pg = self.PS.get()
                self.mm(pg, pg[:, 0:n], [(wg[:, k, :], h[:, k, 0:n]) for k in range(8)], r=[wg, h])
                pv = self.PS.get()
                self.mm(pv, pv[:, 0:n], [(wv[:, k, :], h[:, k, 0:n]) for k in range(8)], r=[wv, h])
                gt = GT.get()
                self.act(gt[:, 1:n + 1], pg[:, 0:n], AF.Copy, r=[pg], w=[gt])
                if (not isctx) and ti > 0:
                    self.copy(gt[:, 0:1], GB[:, j, 2 * (ti - 1) + 1:2 * (ti - 1) + 2], r=[GB], w=[gt], eng="pool")
                else:
                    self.memset(gt, gt[:, 0:1], 0.0, eng="pool")
                if (not isctx) and ti < 7:
                    self.copy(gt[:, n + 1:n + 2], GB[:, j, 2 * (ti + 1):2 * (ti + 1) + 1], r=[GB], w=[gt], eng="pool")
                else:
                    self.memset(gt, gt[:, n + 1:n + 2], 0.0, eng="pool")
                c1 = self.TF.get()
                self.ts(c1[:, 0:n], gt[:, 0:n], self.fcw[:, l, j, 0:1], self.fcb[:, l, j:j + 1], ALU.mult, ALU.add,
                        r=[gt, self.fcw, self.fcb], w=[c1])
                self.stt(c1[:, 0:n], gt[:, 1:n + 1], self.fcw[:, l, j, 1:2], c1[:, 0:n], ALU.mult, ALU.add,
                         r=[gt, c1, self.fcw], w=[c1])
                self.stt(c1[:, 0:n], gt[:, 2:n + 2], self.fcw[:, l, j, 2:3], c1[:, 0:n], ALU.mult, ALU.add,
                         r=[gt, c1, self.fcw], w=[c1])
                sl = self.TF.get()
                self.act(sl[:, 0:n], c1[:, 0:n], AF.Silu, r=[c1], w=[sl])
                self.tt(hid[:, j, 0:n], sl[:, 0:n], pv[:, 0:n], ALU.mult, r=[sl, pv], w=[hid])
            for oc in range(8):
                pt = self.PS.get()
                self.mm(pt, pt[:, 0:n], [(wd[:, j, oc * 128:(oc + 1) * 128], hid[:, j, 0:n]) for j in range(NJ)],
                        r=[wd, hid])
                self.stt(xt[:, oc, 0:n], pt[:, 0:n], mv[:, 5, oc, s:s + 1], xt[:, oc, 0:n], ALU.mult, ALU.add,
                         r=[pt, mv, xt], w=[xt])
            self.store_x(ti, xt, final=final)

    def swa(self, l, idx, tiles):
        nc = self.nc
        I = self.I
        self.new_scope()
        self.common_rings(nxt=1, nh=1, nsq1=0, ntf=4)
        self.ROPE = self.ring(4, [128, 512], F32, "rope")
        wq = self.sb([128, 8, 1024], BF16, "wq")
        wk = self.sb([128, 8, 256], BF16, "wk")
        wv = self.sb([128, 8, 256], BF16, "wv")
        wo = self.sb([128, 8, 1024], BF16, "wo")
        for k in range(8):
            self.load(wq, wq[:, k, :], I["swq"][idx, :, k, :], cast=True)
            self.load(wo, wo[:, k, :], I["swo"][idx, :, k, :], cast=True)
        self.load(wk, wk[:], I["swk"][idx], cast=True)
        self.load(wv, wv[:], I["swv"][idx], cast=True)
        gq = self.sb([128, 1], F32, "gq")
        gk = self.sb([128, 1], F32, "gk")
        self.load(gq, gq[:], I["sqg"][idx])
        self.load(gk, gk[:], I["skg"][idx])
        es = self.sb([128, 16], F32, "es")
        self.load(es, es[:], I["ssink"][idx])
        self.act(es[:], es[:], AF.Exp, r=[es, self.nshift], w=[es], bias=self.nshift[:, 0:1])
        KT = self.sb([128, 2, T], BF16, "KT")
        VA = self.sb([128, 34, 4, 128], BF16, "VA")
        self.memset(VA, VA[:, :, :, 64:128], 1.0, eng="pool")
        QT = self.sb([128, 8, 512], BF16, "QT")
        OT = self.sb([128, 8, 512], BF16, "OT")
        PT = self.ring(4, [128, 512], BF16, "pt")
        DEN = self.ring(2, [128, 512], F32, "den")
        NG = 4
        gSQ = [self.sb([128, 512], BF16, "gsq") for _ in range(NG)]
        gRAW = [self.sb([128, 512], F32, "graw") for _ in range(NG)]
        gQN = [self.sb([128, 512], BF16, "gqn") for _ in range(NG)]
        gRS = [self.sb([128, 512], F32, "grs") for _ in range(NG)]
        PSS = Ring(self.PS.tiles[0:3])
        PSG = Ring(self.PS.tiles[3:6])

        def hn_steps(slot, w_t, c, h, n, gain, dst_t, dst_ap, rope):
            pt = PSG.get()
            self.mm(pt, pt[:, 0:n], [(w_t[:, k, c * 128:(c + 1) * 128], h[:, k, 0:n]) for k in range(8)], r=[w_t, h])
            sq, raw, qn, rstd = gSQ[slot], gRAW[slot], gQN[slot], gRS[slot]
            self.act(sq[:, 0:n], pt[:, 0:n], AF.Square, r=[pt], w=[sq])
            self.act(raw[:, 0:n], pt[:, 0:n], AF.Copy, r=[pt], w=[raw])
            yield
            pm = PSG.get()
            self.mm(pm, pm[:, 0:n], [(self.blk64[:], sq[:, 0:n])], r=[sq, self.blk64])
            self.act(rstd[:, 0:n], pm[:, 0:n], AF.Ln, r=[pm, self.epsT], w=[rstd], bias=self.epsT[:, 0:1])
            self.act(rstd[:, 0:n], rstd[:, 0:n], AF.Exp, r=[rstd], w=[rstd], scale=-0.5)
            if rope is None:
                self.stt(dst_ap, raw[:, 0:n], gain[:, 0:1], rstd[:, 0:n], ALU.mult, ALU.mult, r=[raw, rstd, gain], w=[dst_t])
                return
            cs, sn = rope
            self.stt(qn[:, 0:n], raw[:, 0:n], gain[:, 0:1], rstd[:, 0:n], ALU.mult, ALU.mult, r=[raw, rstd, gain], w=[qn])
            yield
            pw = PSG.get()
            self.mm(pw, pw[:, 0:n], [(self.perm64[:], qn[:, 0:n])], r=[qn, self.perm64])
            t1 = self.TF.get()
            t2 = self.TF.get()
            self.tt(t1[:, 0:n], qn[:, 0:n], cs[:, 0:n], ALU.mult, r=[qn, cs], w=[t1])
            self.tt(t2[:, 0:n], pw[:, 0:n], sn[:, 0:n], ALU.mult, r=[pw, sn], w=[t2])
            self.tt(dst_ap, t1[:, 0:n], t2[:, 0:n], ALU.add, r=[t1, t2], w=[dst_t])

        def run_group(gens):
            gens = list(gens)
            while gens:
                nxt = []
                for g_ in gens:
                    try:
                        next(g_)
                        nxt.append(g_)
                    except StopIteration:
                        pass
                gens = nxt

        for ti in range(9):
            t0, n, isctx = TILES[ti]
            s = 1 if isctx else 0
            xt = self.load_x(ti)
            h = self.H.get()
            self.modnorm(xt, n, l, 1, s, h)
            rope = None
            if not isctx:
                rope = self.load_rope("rcs", "rsn", ti)
            run_group([hn_steps(c, wk, c, h, n, gk, KT, KT[:, c, t0:t0 + n], rope) for c in range(2)])
            for tb in range(n // 128):
                pt = PSG.get()
                self.mm(pt, pt[:, 0:256], [(h[:, k, tb * 128:(tb + 1) * 128], wv[:, k, :]) for k in range(8)], r=[wv, h])
                blk = (t0 // 128) + tb
                self.copy(VA[:, blk, :, 0:64], pt[:, 0:256].rearrange("p (g d) -> p g d", g=4), r=[pt], w=[VA], eng="act")
        LOOK = 2
        for ti in tiles:
            t0, n, isctx = TILES[ti]
            s = 1 if isctx else 0
            xt = self.load_x(ti)
            h = self.H.get()
            self.modnorm(xt, n, l, 1, s, h)
            rope = None
            if not isctx:
                rope = self.load_rope("rcs", "rsn", ti)
            for c0_ in (0, 4):
                run_group([hn_steps(j, wq, c0_ + j, h, n, gq, QT, QT[:, c0_ + j, 0:n], rope) for j in range(NG)])
            flat = []
            for qb in range(n // 128):
                QB = t0 // 128 + qb
                if isctx:
                    kbs = [(32, 0), (33, 0)]
                else:
                    kbs = []
                    if QB > 0:
                        kbs.append((QB - 1, 1))
                    kbs.append((QB, 0))
                    if QB < 31:
                        kbs.append((QB + 1, 2))
                    kbs += [(32, 0), (33, 0)]
                for g in range(4):
                    for ki, (kb, mk) in enumerate(kbs):
                        flat.append((qb, g, ki, len(kbs), kb, mk))

            def issue_s(item):
                qb, g, ki, nk, kb, mk = item
                base = 0 if g < 2 else 64
                c0 = 4 * (g % 2)
                kc = g % 2
                ps_ = PSS.get()
                rhs = QT[base:base + 64, c0:c0 + 4, qb * 128:(qb + 1) * 128]
                self.mm1(ps_, ps_[:], KT[base:base + 64, kc, kb * 128:(kb + 1) * 128], rhs, True, True, r=[KT, QT])
                return ps_
            pend = {}
            for i in range(min(LOOK, len(flat))):
                pend[i] = issue_s(flat[i])
            po = None
            for i, item in enumerate(flat):
                qb, g, ki, nk, kb, mk = item
                base = 0 if g < 2 else 64
                c0 = 4 * (g % 2)
                if ki == 0:
                    po = self.PA.get()
                ps_ = pend.pop(i)
                p = PT.get()
                self.act(p[:], ps_[:], AF.Exp, r=[ps_, self.nshift], w=[p], bias=self.nshift[:, 0:1], scale=0.125)
                if i + LOOK < len(flat):
                    pend[i + LOOK] = issue_s(flat[i + LOOK])
                if mk:
                    cm, st = (1, -1) if mk == 1 else (-1, 1)
                    self.op("pool", (lambda p=p, cm=cm, st=st: nc.gpsimd.affine_select(
                        out=p[:].rearrange("p (a b) -> p a b", a=4), in_=p[:].rearrange("p (a b) -> p a b", a=4),
                        pattern=[[0, 4], [st, 128]], compare_op=ALU.is_ge, fill=0.0, base=0, channel_multiplier=cm)),
                        r=[p], w=[p])
                self.mm1(po, po[:], VA[:, kb, g, :], p[:], ki == 0, ki == nk - 1, r=[VA, p])
                if ki == nk - 1:
                    den = DEN.get()
                    esb = es[64:128, 4 * g:4 * g + 4].unsqueeze(2).broadcast_to([64, 4, 128])
                    self.tt(den[64:128, :].rearrange("p (a b) -> p a b", a=4), po[64:128, :].rearrange("p (a b) -> p a b", a=4),
                            esb, ALU.add, r=[po, es], w=[den])
                    self.act(den[64:128, :], den[64:128, :], AF.Ln, r=[den], w=[den])
                    self.act(den[64:128, :], den[64:128, :], AF.Exp, r=[den], w=[den], scale=-1.0)
                    self.tt(OT[base:base + 64, c0:c0 + 4, qb * 128:(qb + 1) * 128],
                            po[0:64, :].rearrange("p (a b) -> p a b", a=4),
                            den[64:128, :].rearrange("p (a b) -> p a b", a=4), ALU.mult, r=[po, den], w=[OT])
            self.resid(l, ti, xt, OT, (lambda k, n=n: OT[:, k, 0:n]), wo)

    def mla(self, l, idx, tiles):
        I = self.I
        self.new_scope()
        CQN = self.sb([128, 3, T], BF16, "CQN")
        CKVN = self.sb([128, 2, T], BF16, "CKVN")
        KRSQ = self.sb([128, T], BF16, "KRSQ")
        KRROT = self.sb([128, T], BF16, "KRROT")
        gv = self.sb([128, 8], F32, "mg")
        self.load(gv, gv[:], I["mgv"][:])
        persist = self.cur_scope
        self.cur_scope = None
        self.new_scope()
        self.common_rings(nxt=2, nh=1)
        self.ROPE = self.ring(4, [128, 512], F32, "rope")
        wdn = self.sb([128, 8, 640], BF16, "mdn")
        wrp = self.sb([128, 8, 96], BF16, "mrp")
        KRG = self.ring(2, [128, 512], BF16, "krg")
        for kt in KRG.tiles:
            self.memset(kt, kt[:], 0.0)
        for k in range(8):
            self.load(wdn, wdn[:, k, :], I["mdn"][:, k, :], cast=True)
        self.load(wrp, wrp[:], I["mrp"][:], cast=True)
        for ti in range(9):
            t0, n, isctx = TILES[ti]
            s = 1 if isctx else 0
            xt = self.load_x(ti)
            h = self.H.get()
            self.modnorm(xt, n, l, 1, s, h)
            for (nch, coff, gcol, onesm, dstT) in ((3, 0, 0, self.ones384, CQN), (2, 384, 3, self.ones256, CKVN)):
                raws, sqs = [], []
                for c in range(nch):
                    pt = self.PS.get()
                    self.mm(pt, pt[:, 0:n], [(wdn[:, k, coff + c * 128:coff + (c + 1) * 128], h[:, k, 0:n]) for k in range(8)],
                            r=[wdn, h])
                    sq = self.SQ1.get()
                    raw = self.TF.get()
                    self.act(sq[:, 0:n], pt[:, 0:n], AF.Square, r=[pt], w=[sq])
                    self.act(raw[:, 0:n], pt[:, 0:n], AF.Copy, r=[pt], w=[raw])
                    raws.append(raw)
                    sqs.append(sq)
                pm = self.PS.get()
                self.mm(pm, pm[:, 0:n], [(onesm[:], sq[:, 0:n]) for sq in sqs], r=sqs + [onesm])
                rstd = self.RS.get()
                self.act(rstd[:, 0:n], pm[:, 0:n], AF.Ln, r=[pm, self.epsT], w=[rstd], bias=self.epsT[:, 0:1])
                self.act(rstd[:, 0:n], rstd[:, 0:n], AF.Exp, r=[rstd], w=[rstd], scale=-0.5)
                for c in range(nch):
                    self.stt(dstT[:, c, t0:t0 + n], raws[c][:, 0:n], gv[:, gcol + c:gcol + c + 1], rstd[:, 0:n],
                             ALU.mult, ALU.mult, r=[raws[c], rstd, gv], w=[dstT])
            pk = self.PS.get()
            self.mm(pk, pk[0:96, 0:n], [(wrp[:, k, :], h[:, k, 0:n]) for k in range(8)], r=[wrp, h])
            self.act(KRSQ[64:96, t0:t0 + n], pk[64:96, 0:n], AF.Square, r=[pk], w=[KRSQ])
            krg = KRG.get()
            self.ts(krg[64:96, 0:n], pk[64:96, 0:n], gv[64:96, 6:7], None, ALU.mult, r=[pk, gv], w=[krg])
            if isctx:
                self.copy(KRROT[64:96, t0:t0 + n], krg[64:96, 0:n], r=[krg], w=[KRROT])
            else:
                cs, sn = self.load_rope("mcs", "msn", ti)
                pw = self.PS.get()
                self.mm(pw, pw[0:96, 0:n], [(self.perm96[0:96, 0:96], krg[0:96, 0:n])], r=[krg, self.perm96])
                t1 = self.TF.get()
                t2 = self.TF.get()
                self.tt(t1[64:96, 0:n], krg[64:96, 0:n], cs[64:96, 0:n], ALU.mult, r=[krg, cs], w=[t1])
                self.tt(t2[64:96, 0:n], pw[64:96, 0:n], sn[64:96, 0:n], ALU.mult, r=[pw, sn], w=[t2])
                self.tt(KRROT[64:96, t0:t0 + n], t1[64:96, 0:n], t2[64:96, 0:n], ALU.add, r=[t1, t2], w=[KRROT])
        self.new_scope()
        RALL = self.sb([128, 2, L], F32, "ropeall")
        self.load(RALL, RALL[:, 0, :], I["mcs"][:, :])
        self.load(RALL, RALL[:, 1, :], I["msn"][:, :])
        wuq = self.sb([128, 3, 1536], BF16, "muq")
        wuk = self.sb([128, 2, 1024], BF16, "muk")
        wuv = self.sb([128, 2, 1024], BF16, "muv")
        for k in range(3):
            self.load(wuq, wuq[:, k, :], I["muq"][:, k, :], cast=True)
        for k in range(2):
            self.load(wuk, wuk[:, k, :], I["muk"][:, k, :], cast=True)
            self.load(wuv, wuv[:, k, :], I["muv"][:, k, :], cast=True)
        KTH = self.ring(2, [128, T], BF16, "KTH")
        VH = self.ring(2, [128, 34, 128], BF16, "VH")
        QTH = self.ring(2, [128, 512], BF16, "QTH")
        PT = self.ring(4, [128, 512], BF16, "pt")
        DEN = self.ring(2, [128, 512], F32, "den")
        OS = self.ring(3, [128, 512], BF16, "os")
        qSQ = self.ring(2, [128, 512], BF16, "qsq")
        qTF = self.ring(3, [128, 512], F32, "qtf")
        qRS = self.ring(1, [128, 512], F32, "qrs")
        kSQ = self.ring(2, [128, 512], BF16, "ksq")
        kTF = self.ring(2, [128, 512], F32, "ktf")
        kRS = self.ring(2, [128, 512], F32, "krs")
        for v in VH.tiles:
            self.memset(v, v[:, :, 64:128], 1.0, eng="pool")
        scale = 96.0 ** -0.5
        PSS = Ring(self.PS.tiles[0:3])
        PSG = Ring(self.PS.tiles[3:6])

        def kv_steps(hd, kth, vh):
            for ti in range(9):
                t0, n, isctx = TILES[ti]
                pk = PSG.get()
                self.mm(pk, pk[0:64, 0:n], [(wuk[:, c2, hd * 64:(hd + 1) * 64], CKVN[:, c2, t0:t0 + n]) for c2 in range(2)],
                        r=[wuk, CKVN])
                sq = kSQ.get()
                raw = kTF.get()
                self.act(sq[0:64, 0:n], pk[0:64, 0:n], AF.Square, r=[pk], w=[sq])
                self.act(raw[0:64, 0:n], pk[0:64, 0:n], AF.Copy, r=[pk], w=[raw])
                self.copy(sq[64:96, 0:n], KRSQ[64:96, t0:t0 + n], r=[KRSQ], w=[sq], eng="pool")
                yield
                pm = PSG.get()
                self.mm(pm, pm[0:96, 0:n], [(self.ones96[0:96, 0:96], sq[0:96, 0:n])], r=[sq, self.ones96])
                rstd = kRS.get()
                self.act(rstd[0:96, 0:n], pm[0:96, 0:n], AF.Ln, r=[pm, self.epsT], w=[rstd], bias=self.epsT[0:96, 0:1])
                self.act(rstd[0:96, 0:n], rstd[0:96, 0:n], AF.Exp, r=[rstd], w=[rstd], scale=-0.5)
                self.stt(kth[0:64, t0:t0 + n], raw[0:64, 0:n], gv[0:64, 6:7], rstd[0:64, 0:n], ALU.mult, ALU.mult,
                         r=[raw, rstd, gv], w=[kth])
                self.tt(kth[64:96, t0:t0 + n], KRROT[64:96, t0:t0 + n], rstd[64:96, 0:n], ALU.mult,
                        r=[KRROT, rstd], w=[kth])
                for tb in range(n // 128):
                    pv = PSG.get()
                    self.mm(pv, pv[:, 0:64], [(CKVN[:, c2, t0 + tb * 128:t0 + (tb + 1) * 128], wuv[:, c2, hd * 64:(hd + 1) * 64])
                                              for c2 in range(2)], r=[wuv, CKVN])
                    self.copy(vh[:, t0 // 128 + tb, 0:64], pv[:, 0:64], r=[pv], w=[vh], eng="act")
                yield

        def q_steps(hd, ti, qth):
            t0, n, isctx = TILES[ti]
            pq = PSG.get()
            self.mm(pq, pq[0:96, 0:n], [(wuq[:, c3, hd * 96:(hd + 1) * 96], CQN[:, c3, t0:t0 + n]) for c3 in range(3)],
                    r=[wuq, CQN])
            sq = qSQ.get()
            raw = qTF.get()
            self.act(sq[0:96, 0:n], pq[0:96, 0:n], AF.Square, r=[pq], w=[sq])
            self.act(raw[0:96, 0:n], pq[0:96, 0:n], AF.Copy, r=[pq], w=[raw])
            yield
            pm = PSG.get()
            self.mm(pm, pm[0:96, 0:n], [(self.ones96[0:96, 0:96], sq[0:96, 0:n])], r=[sq, self.ones96])
            rstd = qRS.get()
            self.act(rstd[0:96, 0:n], pm[0:96, 0:n], AF.Ln, r=[pm, self.epsT], w=[rstd], bias=self.epsT[0:96, 0:1])
            self.act(rstd[0:96, 0:n], rstd[0:96, 0:n], AF.Exp, r=[rstd], w=[rstd], scale=-0.5)
            if isctx:
                self.stt(qth[0:96, 0:n], raw[0:96, 0:n], gv[0:96, 5:6], rstd[0:96, 0:n], ALU.mult, ALU.mult,
                         r=[raw, rstd, gv], w=[qth])
                return
            qn = qSQ.get()
            self.stt(qn[0:96, 0:n], raw[0:96, 0:n], gv[0:96, 5:6], rstd[0:96, 0:n], ALU.mult, ALU.mult,
                     r=[raw, rstd, gv], w=[qn])
            yield
            pw = PSG.get()
            self.mm(pw, pw[0:96, 0:n], [(self.perm96[0:96, 0:96], qn[0:96, 0:n])], r=[qn, self.perm96])
            t1 = qTF.get()
            t2 = qTF.get()
            self.tt(t1[64:96, 0:n], qn[64:96, 0:n], RALL[64:96, 0, t0:t0 + n], ALU.mult, r=[qn, RALL], w=[t1])
            self.tt(t2[64:96, 0:n], pw[64:96, 0:n], RALL[64:96, 1, t0:t0 + n], ALU.mult, r=[pw, RALL], w=[t2])
            self.copy(qth[0:64, 0:n], qn[0:64, 0:n], r=[qn], w=[qth], eng="pool")
            self.tt(qth[64:96, 0:n], t1[64:96, 0:n], t2[64:96, 0:n], ALU.add, r=[t1, t2], w=[qth])

        def step(g):
            if g is None:
                return None
            try:
                next(g)
                return g
            except StopIteration:
                return None

        def drain(g):
            while g is not None:
                g = step(g)

        units = [(hd, ti) for hd in range(16) for ti in tiles]
        kv_cur = (KTH.get(), VH.get())
        drain(kv_steps(0, kv_cur[0], kv_cur[1]))
        qth = QTH.get()
        drain(q_steps(units[0][0], units[0][1], qth))
        kv_gen = None
        kv_next = None
        LOOK = 2
        for ui, (hd, ti) in enumerate(units):
            t0, n, isctx = TILES[ti]
            if ti == tiles[0]:
                kth, vh = kv_cur
                if hd + 1 < 16:
                    kv_next = (KTH.get(), VH.get())
                    kv_gen = kv_steps(hd + 1, kv_next[0], kv_next[1])
            q_gen = None
            qth_next = None
            if ui + 1 < len(units):
                qth_next = QTH.get()
                q_gen = q_steps(units[ui + 1][0], units[ui + 1][1], qth_next)
            kbs = [32, 33] if isctx else list(range(34))
            nk = len(kbs)
            po = self.PA.get()
            pend = {}

            def issue_s(ki, kth=kth, qth=qth, n=n, kbs=kbs):
                ps_ = PSS.get()
                kb = kbs[ki]
                self.mm1(ps_, ps_[:, 0:n], kth[0:96, kb * 128:(kb + 1) * 128], qth[0:96, 0:n], True, True, r=[kth, qth])
                return ps_
            for ki in range(min(LOOK, nk)):
                pend[ki] = issue_s(ki)
            for ki in range(nk):
                ps_ = pend.pop(ki)
                p = PT.get()
                self.act(p[:, 0:n], ps_[:, 0:n], AF.Exp, r=[ps_, self.nshift], w=[p], bias=self.nshift[:, 0:1], scale=scale)
                if ki + LOOK < nk:
                    pend[ki + LOOK] = issue_s(ki + LOOK)
                self.mm1(po, po[:, 0:n], vh[:, kbs[ki], :], p[:, 0:n], ki == 0, ki == nk - 1, r=[vh, p])
                if ki % 8 == 3:
                    q_gen = step(q_gen)
                if ki % 8 == 7:
                    kv_gen = step(kv_gen)
            drain(q_gen)
            den = DEN.get()
            self.recip(den[64:128, 0:n], po[64:128, 0:n], r=[po], w=[den])
            hb = (hd % 2) * 64
            os_ = OS.get()
            self.tt(os_[hb:hb + 64, 0:n], po[0:64, 0:n], den[64:128, 0:n], ALU.mult, r=[po, den], w=[os_])
            dst = self.AS.rearrange("(k p) t -> p k t", p=128)[hb:hb + 64, hd // 2, t0:t0 + n]
            self.store("AS%d" % ti, dst, os_, os_[hb:hb + 64, 0:n])
            qth = qth_next
            if ti == tiles[-1]:
                drain(kv_gen)
                kv_gen = None
                kv_cur = kv_next
        self.S.barrier()
        self.cur_scope.close()
        persist.close()
        self.cur_scope = None
        self.phase_c_dram(l, I["mwo"], tiles)

    def lru(self, l, idx, tiles):
        nc = self.nc
        I = self.I
        PADL = 2
        CTX0 = L + 6
        XW = T + 8

        def pcol(t0):
            return t0 + PADL if t0 < L else (t0 - L) + CTX0
        self.new_scope()
        self.common_rings(nxt=2, nh=2)
        win = self.sb([128, 8, 2048], BF16, "lwin")
        for k in range(8):
            self.load(win, win[:, k, :], I["lwin"][:, k, :], cast=True)
        STG = self.ring(4, [128, 512], BF16, "stg")
        for ti in range(9):
            t0, n, isctx = TILES[ti]
            s = 1 if isctx else 0
            xt = self.load_x(ti)
            h = self.H.get()
            self.modnorm(xt, n, l, 1, s, h)
            for oc in range(16):
                pt = self.PS.get()
                self.mm(pt, pt[:, 0:n], [(win[:, k, oc * 128:(oc + 1) * 128], h[:, k, 0:n]) for k in range(8)], r=[win, h])
                stg = STG.get()
                if oc < 8:
                    self.act(stg[:, 0:n], pt[:, 0:n], AF.Gelu_apprx_tanh, r=[pt], w=[stg])
                    self.store(("AS", oc, ti), self.AS[oc * 128:(oc + 1) * 128, t0:t0 + n], stg, stg[:, 0:n])
                else:
                    self.copy(stg[:, 0:n], pt[:, 0:n], r=[pt], w=[stg])
                    self.store(("XRS", oc - 8), self.XRS[(oc - 8) * 128:(oc - 7) * 128, t0:t0 + n], stg, stg[:, 0:n])
        self.new_scope()
        self.TF = self.ring(6, [128, 512], F32, "tf")
        gw = self.sb([128, 2, 2, 4, 2, 256], BF16, "lgw")
        for d in range(2):
            for wch in range(2):
                self.load(gw, gw[:, d, wch], I["lgw"][:, d, wch], cast=True)
        sv = self.sb([128, 64], F32, "lsv")
        self.load(sv, sv[:], I["lsv"][:])
        ngb = self.sb([128, 32], F32, "lngb")
        self.load(ngb, ngb[:], I["lgb"][:])
        self.ts(ngb[:], ngb[:], -1.0, None, ALU.mult, r=[ngb], w=[ngb])
        cp = self.sb([128, 16], F32, "lcp")
        cp2 = self.sb([128, 16], F32, "lcp2")
        self.act(cp[:], sv[:, 40:56], AF.Exp, r=[sv], w=[cp], scale=-1.0)
        self.act(cp[:], cp[:], AF.Ln, r=[cp, self.one1], w=[cp], bias=self.one1[:, 0:1])
        self.ts(cp2[:], cp[:], -16.0, None, ALU.mult, r=[cp], w=[cp2])
        self.ts(cp[:], cp[:], -8.0, None, ALU.mult, r=[cp], w=[cp])
        XR = self.sb([128, 2, XW], BF16, "XR")
        self.memset(XR, XR[:, :, 0:PADL], 0.0, eng="pool")
        self.memset(XR, XR[:, :, L + PADL:CTX0], 0.0, eng="pool")
        self.memset(XR, XR[:, :, CTX0 + CT:XW], 0.0, eng="pool")
        XC = self.sb([128, 2, XW], BF16, "XC")
        SF = self.sb([128, XW], F32, "SF")
        CAR = self.ring(4, [128, 1], F32, "car")
        GT_ = self.ring(3, [128, 512], BF16, "gtile")
        zero1 = self.sb([128, 1], F32, "zero1")
        self.memset(zero1, zero1[:], 0.0)
        NW = XW - 4
        for bk in range(4):
            for cc in range(2):
                c = 2 * bk + cc
                self.load(XR, XR[:, cc, PADL:PADL + L], self.XRS[c * 128:(c + 1) * 128, 0:L], r=[("XRS", c)])
                self.load(XR, XR[:, cc, CTX0:CTX0 + CT], self.XRS[c * 128:(c + 1) * 128, L:T], r=[("XRS", c)])
            for cc in range(2):
                c = 2 * bk + cc
                acc = SF
                self.ts(acc[:, 0:NW], XR[:, cc, 0:NW], sv[:, 8 + 4 * c:9 + 4 * c], sv[:, c:c + 1], ALU.mult, ALU.add,
                        r=[XR, sv], w=[acc])
                for k in (1, 2):
                    self.stt(acc[:, 0:NW], XR[:, cc, k:k + NW], sv[:, 8 + 4 * c + k:9 + 4 * c + k], acc[:, 0:NW], ALU.mult, ALU.add,
                             r=[XR, sv, acc], w=[acc])
                self.stt(XC[:, cc, 2:2 + NW], XR[:, cc, 3:3 + NW], sv[:, 8 + 4 * c + 3:9 + 4 * c + 3], acc[:, 0:NW], ALU.mult, ALU.add,
                         r=[XR, sv, acc], w=[XC])
            for cc in range(2):
                c = 2 * bk + cc
                for d in range(2):
                    order = [8] + (list(range(8)) if d == 0 else list(range(7, -1, -1)))
                    carry = zero1
                    for ti in order:
                        t0, n, isctx = TILES[ti]
                        pc = pcol(t0)
                        pr = self.PS.get()
                        self.mm(pr, pr[:, 0:n], [(gw[:, d, 0, bk, kk, cc * 128:(cc + 1) * 128], XC[:, kk, pc:pc + n]) for kk in range(2)],
                                r=[gw, XC])
                        pi = self.PS.get()
                        self.mm(pi, pi[:, 0:n], [(gw[:, d, 1, bk, kk, cc * 128:(cc + 1) * 128], XC[:, kk, pc:pc + n]) for kk in range(2)],
                                r=[gw, XC])
                        gi = (d * 2 + 0) * 8 + c
                        gi2 = (d * 2 + 1) * 8 + c
                        ta = self.TF.get()
                        tb_ = self.TF.get()
                        tcc = self.TF.get()
                        self.act(ta[:, 0:n], pr[:, 0:n], AF.Exp, r=[pr, ngb], w=[ta], bias=ngb[:, gi:gi + 1], scale=-1.0)
                        self.act(ta[:, 0:n], ta[:, 0:n], AF.Ln, r=[ta, self.one1], w=[ta], bias=self.one1[:, 0:1])
                        self.act(ta[:, 0:n], ta[:, 0:n], AF.Exp, r=[ta], w=[ta], scale=-1.0)
                        self.act(tb_[:, 0:n], ta[:, 0:n], AF.Exp, r=[ta, cp], w=[tb_], scale=cp[:, d * 8 + c:d * 8 + c + 1])
                        self.act(ta[:, 0:n], ta[:, 0:n], AF.Exp, r=[ta, cp2], w=[ta], scale=cp2[:, d * 8 + c:d * 8 + c + 1])
                        self.ts(ta[:, 0:n], ta[:, 0:n], 0.99999994, None, ALU.min, r=[ta], w=[ta])
                        self.act(ta[:, 0:n], ta[:, 0:n], AF.Ln, r=[ta, self.one1], w=[ta], bias=self.one1[:, 0:1], scale=-1.0)
                        self.act(ta[:, 0:n], ta[:, 0:n], AF.Exp, r=[ta], w=[ta], scale=0.5)
                        self.act(tcc[:, 0:n], pi[:, 0:n], AF.Exp, r=[pi, ngb], w=[tcc], bias=ngb[:, gi2:gi2 + 1], scale=-1.0)
                        self.act(tcc[:, 0:n], tcc[:, 0:n], AF.Ln, r=[tcc, self.one1], w=[tcc], bias=self.one1[:, 0:1])
                        self.act(tcc[:, 0:n], tcc[:, 0:n], AF.Exp, r=[tcc], w=[tcc], scale=-1.0)
                        self.tt(tcc[:, 0:n], tcc[:, 0:n], XC[:, cc, pc:pc + n], ALU.mult, r=[tcc, XC], w=[tcc])
                        self.tt(tcc[:, 0:n], tcc[:, 0:n], ta[:, 0:n], ALU.mult, r=[tcc, ta], w=[tcc])
                        so = self.TF.get()
                        ncar = CAR.get()
                        if d == 0:
                            self.op("dve", (lambda so=so, tb_=tb_, tcc=tcc, n=n, carry=carry: nc.vector.tensor_tensor_scan(
                                out=so[:, 0:n], data0=tb_[:, 0:n], data1=tcc[:, 0:n], initial=carry[:, 0:1],
                                op0=ALU.mult, op1=ALU.add)), r=[tb_, tcc, carry], w=[so])
                            self.copy(ncar[:, 0:1], so[:, n - 1:n], r=[so], w=[ncar])
                            self.copy(SF[:, t0:t0 + n], so[:, 0:n], r=[so], w=[SF], eng="pool")
                        else:
                            self.op("dve", (lambda so=so, tb_=tb_, tcc=tcc, n=n, carry=carry: nc.vector.tensor_tensor_scan(
                                out=so[:, 0:n][:, ::-1], data0=tb_[:, 0:n][:, ::-1], data1=tcc[:, 0:n][:, ::-1], initial=carry[:, 0:1],
                                op0=ALU.mult, op1=ALU.add)), r=[tb_, tcc, carry], w=[so])
                            self.copy(ncar[:, 0:1], so[:, 0:1], r=[so], w=[ncar])
                            self.tt(so[:, 0:n], so[:, 0:n], SF[:, t0:t0 + n], ALU.add, r=[so, SF], w=[so])
                            gtile = GT_.get()
                            asl = self.AS[c * 128:(c + 1) * 128, t0:t0 + n]
                            self.load(gtile, gtile[:, 0:n], asl, r=[("AS", c, ti)])
                            self.tt(gtile[:, 0:n], so[:, 0:n], gtile[:, 0:n], ALU.mult, r=[so, gtile], w=[gtile])
                            self.store(("AS", c, ti), asl, gtile, gtile[:, 0:n])
                        carry = ncar
        self.phase_c_dram_multi(l, I["lwout"], tiles)

    def phase_c_dram_multi(self, l, wo_ap, tiles):
        self.new_scope()
        self.XT = self.ring(2, [128, 8, 512], F32, "xt")
        AT = self.ring(2, [128, 8, 512], BF16, "at")
        wo = self.sb([128, 8, 1024], BF16, "wo")
        for k in range(8):
            self.load(wo, wo[:, k, :], wo_ap[:, k, :], cast=True)
        for ti in tiles:
            t0, n, isctx = TILES[ti]
            xt = self.load_x(ti)
            at = AT.get()
            self.load(at, at[:, :, 0:n], self.xview(self.AS, t0, n), r=[("AS", c, ti) for c in range(8)])
            self.resid(l, ti, xt, at, (lambda k, at=at, n=n: at[:, k, 0:n]), wo)


def input_shapes():
    return {
        "xin": (D, T), "cond": (128, 8, 2), "n1g": (128, 4, 8), "n2g": (128, 4, 8),
        "modw": (4, 128, 8, 6144), "modb": (128, 4, 48),
        "wug": (4, NJ, 128, 1024), "wuv": (4, NJ, 128, 1024), "wdn": (4, NJ, 128, 1024),
        "fcw": (128, 4, NJ, 3), "fcb": (128, 4, NJ),
        "swq": (2, 128, 8, 1024), "swk": (2, 128, 8, 256), "swv": (2, 128, 8, 256), "swo": (2, 128, 8, 1024),
        "sqg": (2, 128, 1), "skg": (2, 128, 1), "ssink": (2, 128, 16),
        "rcs": (128, L), "rsn": (128, L), "mcs": (128, L), "msn": (128, L),
        "mdn": (128, 8, 640), "mrp": (128, 8, 96), "muq": (128, 3, 1536), "muk": (128, 2, 1024), "muv": (128, 2, 1024),
        "mwo": (128, 8, 1024), "mgv": (128, 8),
        "lwin": (128, 8, 2048), "lwout": (128, 8, 1024), "lgw": (128, 2, 2, 4, 2, 256), "lsv": (128, 64), "lgb": (128, 32),
        "c_ones1024": (128, 128), "c_blk64": (128, 128), "c_ones384": (128, 128), "c_ones256": (128, 128),
        "c_ones96": (128, 128), "c_perm64": (128, 128), "c_perm96": (128, 128),
    }


def _fm(v, nch):
    return np.ascontiguousarray(np.asarray(v, np.float32).reshape(nch, 128).T)


def _wfm(w):
    K, N = w.shape
    return np.ascontiguousarray(np.asarray(w, np.float32).reshape(K // 128, 128, N).transpose(1, 0, 2))


def _rope_tables(rot_dim, base_part, nrows):
    n_freq = rot_dim // 4
    half = rot_dim // 2
    t = np.arange(L)
    row = (t // 64).astype(np.float32)
    col = (t % 64).astype(np.float32)
    inv = (np.float32(10000.0) ** (-np.arange(n_freq, dtype=np.float32) / np.float32(n_freq))).astype(np.float32)
    cs = np.zeros((128, L), np.float32)
    sn = np.zeros((128, L), np.float32)
    partner = np.zeros(128, np.int64) - 1
    for p in range(nrows):
        d = p % rot_dim if base_part == 0 else p
        pp = p + base_part
        dd = d % rot_dim
        pos = row if dd < half else col
        e = dd % half
        i = e % n_freq
        ang = (pos * inv[i]).astype(np.float32)
        cs[pp] = np.cos(ang)
        sgn = -1.0 if e < n_freq else 1.0
        sn[pp] = sgn * np.sin(ang)
        partner[pp] = pp + n_freq if e < n_freq else pp - n_freq
    return cs, sn, partner


def prepare_shared(inp):
    f = lambda k: np.asarray(inp[k], np.float32)
    sh = {}
    sh["n1g"] = np.ascontiguousarray(f("norm1").reshape(4, 8, 128).transpose(2, 0, 1))
    sh["n2g"] = np.ascontiguousarray(f("norm2").reshape(4, 8, 128).transpose(2, 0, 1))
    sh["modw"] = np.ascontiguousarray(f("mod_w").reshape(4, 8, 128, 6144).transpose(0, 2, 1, 3))
    sh["modb"] = np.ascontiguousarray(f("mod_b").reshape(4, 48, 128).transpose(2, 0, 1))
    wup = f("ffn_w_up")
    def upl(w):
        return np.ascontiguousarray(w.reshape(4, 8, 128, NJ, 128).transpose(0, 3, 2, 1, 4).reshape(4, NJ, 128, 1024))
    sh["wug"] = upl(wup[:, :, :DFF])
    sh["wuv"] = upl(wup[:, :, DFF:])
    sh["wdn"] = np.ascontiguousarray(f("ffn_w_down").reshape(4, NJ, 128, 1024))
    sh["fcw"] = np.ascontiguousarray(f("ffn_conv_w").reshape(4, 3, NJ, 128).transpose(3, 0, 2, 1))
    sh["fcb"] = np.ascontiguousarray(f("ffn_conv_b").reshape(4, NJ, 128).transpose(2, 0, 1))
    wqkv = f("swa_w_qkv")
    wq = wqkv[:, :, :1024].reshape(2, 1024, 2, 8, 64).transpose(0, 1, 3, 2, 4).reshape(2, 1024, 1024)
    wk = wqkv[:, :, 1024:1280].reshape(2, 1024, 2, 2, 64).transpose(0, 1, 3, 2, 4).reshape(2, 1024, 256)
    wv = wqkv[:, :, 1280:1536]
    sh["swq"] = np.stack([_wfm(wq[i]) for i in range(2)])
    sh["swk"] = np.stack([_wfm(wk[i]) for i in range(2)])
    sh["swv"] = np.stack([_wfm(wv[i]) for i in range(2)])
    wo = f("swa_w_o").reshape(2, 2, 8, 64, 1024).transpose(0, 2, 1, 3, 4).reshape(2, 1024, 1024)
    sh["swo"] = np.stack([_wfm(wo[i]) for i in range(2)])
    sh["sqg"] = np.ascontiguousarray(np.tile(f("swa_q_gain"), (1, 2)).reshape(2, 128, 1))
    sh["skg"] = np.ascontiguousarray(np.tile(f("swa_k_gain"), (1, 2)).reshape(2, 128, 1))
    sh["ssink"] = np.ascontiguousarray(np.broadcast_to(f("swa_sink")[:, None, :], (2, 128, 16)))
    cs, sn, partner = _rope_tables(64, 0, 128)
    sh["rcs"], sh["rsn"] = cs, sn
    pm = np.zeros((128, 128), np.float32)
    for m in range(128):
        pm[partner[m], m] = 1.0
    sh["c_perm64"] = pm
    cs, sn, partner = _rope_tables(32, 64, 32)
    sh["mcs"], sh["msn"] = cs, sn
    pm = np.zeros((128, 128), np.float32)
    for m in range(64, 96):
        pm[partner[m], m] = 1.0
    sh["c_perm96"] = pm
    wd = f("mla_w_down")[0]
    sh["mdn"] = _wfm(wd[:, :640])
    wr = np.zeros((1024, 96), np.float32)
    wr[:, 64:96] = wd[:, 640:672]
    sh["mrp"] = _wfm(wr)
    sh["muq"] = _wfm(f("mla_w_uq")[0])
    sh["muk"] = _wfm(f("mla_w_uk")[0])
    sh["muv"] = _wfm(f("mla_w_uv")[0])
    sh["mwo"] = _wfm(f("mla_w_o")[0])
    gv = np.zeros((128, 8), np.float32)
    gv[:, 0:3] = _fm(f("mla_q_lora_gain")[0], 3)
    gv[:, 3:5] = _fm(f("mla_kv_lora_gain")[0], 2)
    gv[0:96, 5] = f("mla_q_gain")[0]
    gv[0:96, 6] = f("mla_k_gain")[0]
    sh["mgv"] = gv
    sh["lwin"] = _wfm(f("lru_w_in")[0])
    sh["lwout"] = _wfm(f("lru_w_out")[0])
    gw = f("lru_gate_w")[0]
    sh["lgw"] = np.ascontiguousarray(gw.reshape(2, 2, 4, 2, 128, 256).transpose(4, 0, 1, 2, 3, 5))
    sv = np.zeros((128, 64), np.float32)
    sv[:, 0:8] = _fm(f("lru_conv_b")[0], 8)
    cw = f("lru_conv_w")[0]
    sv[:, 8:40] = cw.reshape(4, 8, 128).transpose(2, 1, 0).reshape(128, 32)
    lam = f("lru_lam")[0]
    sv[:, 40:56] = lam.reshape(2, 8, 128).transpose(2, 0, 1).reshape(128, 16)
    sh["lsv"] = sv
    gb = f("lru_gate_b")[0]
    sh["lgb"] = np.ascontiguousarray(gb.reshape(2, 2, 8, 128).transpose(3, 0, 1, 2).reshape(128, 32))
    sh["c_ones1024"] = np.full((128, 128), 1.0 / 1024, np.float32)
    b = np.zeros((128, 128), np.float32)
    b[0:64, 0:64] = 1.0 / 64
    b[64:128, 64:128] = 1.0 / 64
    sh["c_blk64"] = b
    sh["c_ones384"] = np.full((128, 128), 1.0 / 384, np.float32)
    sh["c_ones256"] = np.full((128, 128), 1.0 / 256, np.float32)
    sh["c_ones96"] = np.full((128, 128), 1.0 / 96, np.float32)
    return sh


def prepare_core(inp, b):
    x = np.asarray(inp["x"][b], np.float32)
    ctx = np.asarray(inp["ctx"][b], np.float32)
    xin = np.ascontiguousarray(np.concatenate([x.T, ctx.T], axis=1))
    cond = np.stack([_fm(np.asarray(inp["c"][b]), 8), _fm(np.asarray(inp["c_ctx"]), 8)], axis=2)
    return {"xin": xin, "cond": np.ascontiguousarray(cond)}


_NC_CACHE = {}


def kernel(**inputs):
    sh = prepare_shared(inputs)
    if "nc" not in _NC_CACHE:
        _NC_CACHE["nc"] = Builder().build()
    nc = _NC_CACHE["nc"]
    in_maps = []
    for b in range(8):
        m = dict(sh)
        m.update(prepare_core(inputs, b))
        in_maps.append(m)
    res = run_bass_kernel_spmd(nc, in_maps, core_ids=list(range(8)))
    out = np.stack([np.ascontiguousarray(r["outT"].T) for r in res.results], axis=0)
    return out.astype(np.float32)
```

```python
from contextlib import ExitStack
import numpy as np
import concourse.bass as bass
import concourse.mybir as mybir
from concourse.bass_utils import run_bass_kernel_spmd

F32 = mybir.dt.float32
BF16 = mybir.dt.bfloat16
AF = mybir.ActivationFunctionType
ALU = mybir.AluOpType

D = 1024
L = 4096
CT = 256
T = L + CT
DEPTH = 4
DFF = 2816
NJ = DFF // 128
EPS = 1e-6
SHIFT = 16.0
TILES = [(i * 512, 512, False) for i in range(8)] + [(L, CT, True)]


class _Op:
    __slots__ = ("eng", "fn", "deps", "dkey", "sig", "sem", "val")

    def __init__(self, eng, fn, deps, dkey):
        self.eng = eng
        self.fn = fn
        self.deps = deps
        self.dkey = dkey
        self.sig = False
        self.sem = None
        self.val = 0


class Sched:
    EPOCH = 20000

    def __init__(self, nc, stack):
        self.nc = nc
        self.stack = stack
        self.ops = []
        self.last_w = {}
        self.readers = {}
        self.last_dkey = {}
        self.pending_bar = {}
        self.bar_idx = -1
        self.emitted = 0
        self.engs = {"pe": nc.tensor, "act": nc.scalar, "dve": nc.vector,
                     "pool": nc.gpsimd, "sp": nc.sync}
        self.eng_cnt = {e: 0 for e in self.engs}
        self.eng_sem = {}
        self.key_sem = {}
        self.sem_cnt = {}
        self.free_sems = {}
        self.waited = {e: {} for e in self.engs}
        self.nsem = 0
        self.ninst = 0

    def add(self, eng, fn, reads=(), writes=(), dkey=None):
        i = len(self.ops)
        deps = set()
        for r in reads:
            w = self.last_w.get(r)
            if w is not None:
                deps.add(w)
        for r in writes:
            w = self.last_w.get(r)
            if w is not None:
                deps.add(w)
            for rd in self.readers.get(r, ()):
                deps.add(rd)
        if dkey is not None:
            p = self.last_dkey.get(dkey)
            if p is not None:
                deps.add(p)
            self.last_dkey[dkey] = i
        deps = set(d for d in deps if d > self.bar_idx)
        if eng in self.pending_bar:
            deps |= self.pending_bar.pop(eng)
        for r in reads:
            self.readers.setdefault(r, []).append(i)
        for r in writes:
            self.last_w[r] = i
            self.readers[r] = []
        deps.discard(i)
        red = {}
        for d in deps:
            o = self.ops[d]
            src = ("k", o.dkey) if o.dkey is not None else ("e", o.eng)
            if src == ("e", "pe") and eng == "pe" and dkey is None and d > self.bar_idx:
                continue
            if src not in red or red[src] < d:
                red[src] = d
        self.ops.append(_Op(eng, fn, sorted(red.values()), dkey))
        return i

    def barrier(self):
        last = {}
        for i in range(self.bar_idx + 1, len(self.ops)):
            o = self.ops[i]
            src = ("k", o.dkey) if o.dkey is not None else ("e", o.eng)
            last[src] = i
        deps = set(last.values())
        for d in deps:
            self.ops[d].sig = True
        self.flush()
        for e in self.engs:
            self.pending_bar[e] = set(deps) | self.pending_bar.get(e, set())
        self.bar_idx = len(self.ops) - 1
        for (e, _k), sem in self.key_sem.items():
            self.free_sems.setdefault(e, []).append(sem)
        self.key_sem = {}

    def flush(self):
        nc = self.nc
        ops = self.ops
        for i in range(self.emitted, len(ops)):
            for d in ops[i].deps:
                ops[d].sig = True
        for i in range(self.emitted, len(ops)):
            o = ops[i]
            E = self.engs[o.eng]
            w = self.waited[o.eng]
            for d in o.deps:
                do = ops[d]
                sid = id(do.sem)
                if w.get(sid, 0) >= do.val:
                    continue
                E.wait_ge(do.sem, do.val)
                w[sid] = do.val
            inst = o.fn()
            o.fn = None
            self.ninst += 1
            if o.dkey is not None:
                kk = (o.eng, o.dkey)
                if kk not in self.key_sem:
                    fl = self.free_sems.setdefault(o.eng, [])
                    if fl:
                        self.key_sem[kk] = fl.pop()
                    else:
                        sem = self.stack.enter_context(nc.semaphore("k%d" % self.nsem))
                        self.nsem += 1
                        self.sem_cnt[id(sem)] = 0
                        self.key_sem[kk] = sem
                o.sem = self.key_sem[kk]
                self.sem_cnt[id(o.sem)] += 16
                o.val = self.sem_cnt[id(o.sem)]
                inst.then_inc(o.sem, 16)
            elif o.sig:
                if o.eng not in self.eng_sem or self.eng_cnt[o.eng] >= self.EPOCH:
                    self.eng_sem[o.eng] = self.stack.enter_context(nc.semaphore("e%d" % self.nsem))
                    self.nsem += 1
                    self.eng_cnt[o.eng] = 0
                self.eng_cnt[o.eng] += 1
                o.sem = self.eng_sem[o.eng]
                o.val = self.eng_cnt[o.eng]
                inst.then_inc(o.sem, 1)
        self.emitted = len(ops)

    def emit(self):
        self.flush()


class Tile:
    def __init__(self, t, key):
        self.t = t
        self.key = key

    def __getitem__(self, idx):
        return self.t[idx]


class Ring:
    def __init__(self, tiles):
        self.tiles = tiles
        self.i = 0

    def get(self):
        t = self.tiles[self.i % len(self.tiles)]
        self.i += 1
        return t


class Builder:
    def __init__(self, layers=(0, 1, 2, 3)):
        self.layers = tuple(layers)
        self.nc = bass.Bass("TRN2", target_bir_lowering=False)
        self.cnt = 0

    def sb(self, shape, dtype, name="t"):
        self.cnt += 1
        nm = "%s_%d" % (name, self.cnt)
        t = self.scope.enter_context(self.nc.sbuf_tensor(nm, list(shape), dtype))
        return Tile(t, nm)

    def ps(self, name="ps"):
        self.cnt += 1
        nm = "%s_%d" % (name, self.cnt)
        t = self.stack.enter_context(self.nc.psum_tensor(nm, [128, 512], F32))
        return Tile(t, nm)

    def ring(self, n, shape, dtype, name="r"):
        return Ring([self.sb(shape, dtype, name) for _ in range(n)])

    def din(self, name, shape):
        return self.nc.dram_tensor(name, list(shape), F32, kind="ExternalInput").ap()

    def op(self, eng, fn, r=(), w=(), dkey=None):
        return self.S.add(eng, fn, reads=[x.key if isinstance(x, Tile) else x for x in r],
                          writes=[x.key if isinstance(x, Tile) else x for x in w], dkey=dkey)

    def load(self, dst, dst_ap, src_ap, r=(), cast=False):
        nc = self.nc
        if cast:
            self.op("pool", lambda: nc.gpsimd.dma_start(out=dst_ap, in_=src_ap, max_dma_last_dim=4096),
                    r=r, w=[dst], dkey=dst.key)
        else:
            self.op("sp", lambda: nc.sync.dma_start(out=dst_ap, in_=src_ap), r=r, w=[dst], dkey=dst.key)

    def store(self, dram_key, dst_ap, src, src_ap):
        nc = self.nc
        self.op("sp", lambda: nc.sync.dma_start(out=dst_ap, in_=src_ap), r=[src], w=[dram_key], dkey=src.key)

    def mm(self, ps, out_ap, pairs, r=()):
        nc = self.nc
        n = len(pairs)
        for i, (lt, rh) in enumerate(pairs):
            self.op("pe", (lambda lt=lt, rh=rh, i=i: nc.tensor.matmul(out_ap, lt, rh, start=(i == 0), stop=(i == n - 1))),
                    r=r, w=[ps])

    def act(self, out_ap, in_ap, func, r=(), w=(), bias=None, scale=None):
        nc = self.nc
        kw = {}
        if bias is not None:
            kw["bias"] = bias
        if scale is not None:
            kw["scale"] = scale
        self.op("act", lambda: nc.scalar.activation(out=out_ap, in_=in_ap, func=func, **kw), r=r, w=w)

    def tt(self, out_ap, a, b, op, r=(), w=(), eng="dve"):
        nc = self.nc
        e = nc.vector if eng == "dve" else nc.gpsimd
        self.op(eng, lambda: e.tensor_tensor(out=out_ap, in0=a, in1=b, op=op), r=r, w=w)

    def ts(self, out_ap, a, s1, s2, op0, op1=None, r=(), w=(), eng="dve"):
        nc = self.nc
        e = nc.vector if eng == "dve" else nc.gpsimd
        if op1 is None:
            self.op(eng, lambda: e.tensor_scalar(out=out_ap, in0=a, scalar1=s1, scalar2=None, op0=op0), r=r, w=w)
        else:
            self.op(eng, lambda: e.tensor_scalar(out=out_ap, in0=a, scalar1=s1, scalar2=s2, op0=op0, op1=op1), r=r, w=w)

    def stt(self, out_ap, a, s, b, op0, op1, r=(), w=()):
        nc = self.nc
        self.op("dve", lambda: nc.vector.scalar_tensor_tensor(out=out_ap, in0=a, scalar=s, in1=b, op0=op0, op1=op1), r=r, w=w)

    def copy(self, out_ap, in_ap, r=(), w=(), eng="dve"):
        nc = self.nc
        if eng == "act":
            self.op("act", lambda: nc.scalar.copy(out=out_ap, in_=in_ap), r=r, w=w)
        else:
            e = nc.vector if eng == "dve" else nc.gpsimd
            self.op(eng, lambda: e.tensor_copy(out=out_ap, in_=in_ap), r=r, w=w)

    def memset(self, t, ap, val, eng="dve"):
        nc = self.nc
        e = nc.vector if eng == "dve" else nc.gpsimd
        self.op(eng, lambda: e.memset(ap, val), w=[t])

    def mm1(self, ps, out_ap, lhsT, rhs, start, stop, r=()):
        nc = self.nc
        self.op("pe", lambda: nc.tensor.matmul(out_ap, lhsT, rhs, start=start, stop=stop), r=r, w=[ps])

    def recip(self, out_ap, in_ap, r=(), w=()):
        nc = self.nc
        self.op("dve", lambda: nc.vector.reciprocal(out=out_ap, in_=in_ap), r=r, w=w)

    def new_scope(self):
        if self.cur_scope is not None:
            self.S.barrier()
            self.cur_scope.close()
        self.cur_scope = ExitStack()
        self.scope = self.cur_scope

    def common_rings(self, nxt=2, nh=2, nsq1=4, ntf=5):
        self.XT = self.ring(nxt, [128, 8, 512], F32, "xt")
        self.SQ = self.ring(1, [128, 8, 512], BF16, "sq")
        self.SQ1 = self.ring(nsq1, [128, 512], BF16, "sq1") if nsq1 else None
        self.RS = self.ring(2, [128, 512], F32, "rstd")
        self.TF = self.ring(ntf, [128, 512], F32, "tf")
        self.H = self.ring(nh, [128, 8, 512], BF16, "h")

    def build(self):
        nc = self.nc
        I = {}
        for k, shp in input_shapes().items():
            I[k] = self.din(k, shp)
        self.I = I
        self.out = nc.dram_tensor("outT", [D, L], F32, kind="ExternalOutput").ap()
        self.XS = nc.dram_tensor("xs", [D, T], F32).ap()
        self.AS = nc.dram_tensor("acts", [D, T], BF16).ap()
        self.XRS = nc.dram_tensor("xrs", [D, T], BF16).ap()
        self.xsrc_is_input = True
        with ExitStack() as stack:
            self.stack = stack
            self.scope = stack
            self.cur_scope = None
            self.S = Sched(nc, stack)
            self.PS = Ring([self.ps() for _ in range(6)])
            self.PA = Ring([self.ps() for _ in range(2)])
            self.setup_consts()
            for l in self.layers:
                self.layer(l)
            self.op("sp", lambda: nc.sync.nop(), r=["OUT%d" % i for i in range(8)])
            self.S.emit()
            if self.cur_scope is not None:
                self.cur_scope.close()
        return nc

    def xview(self, ap, t0, n):
        return ap.rearrange("(k p) t -> p k t", p=128)[:, :, t0:t0 + n]

    def load_x(self, ti):
        t0, n, isctx = TILES[ti]
        xt = self.XT.get()
        src = self.I["xin"] if self.xsrc_is_input else self.XS
        self.load(xt, xt[:, :, 0:n], self.xview(src, t0, n), r=["X%d" % ti])
        return xt

    def store_x(self, ti, xt, final=False):
        t0, n, isctx = TILES[ti]
        if final and not isctx:
            self.store("OUT%d" % ti, self.xview(self.out, t0, n), xt, xt[:, :, 0:n])
        else:
            self.store("X%d" % ti, self.xview(self.XS, t0, n), xt, xt[:, :, 0:n])

    def setup_consts(self):
        nc = self.nc
        I = self.I

        def cload(name, shape, dtype=BF16):
            t = self.sb(shape, dtype, name)
            self.load(t, t[:], I[name][:], cast=(dtype == BF16))
            return t
        self.ones1024 = cload("c_ones1024", [128, 128])
        self.blk64 = cload("c_blk64", [128, 128])
        self.ones384 = cload("c_ones384", [128, 128])
        self.ones256 = cload("c_ones256", [128, 128])
        self.ones96 = cload("c_ones96", [128, 128])
        self.perm64 = cload("c_perm64", [128, 128])
        self.perm96 = cload("c_perm96", [128, 128])
        self.epsT = self.sb([128, 1], F32, "eps")
        self.memset(self.epsT, self.epsT[:], EPS)
        self.nshift = self.sb([128, 1], F32, "nshift")
        self.memset(self.nshift, self.nshift[:], -SHIFT)
        self.one1 = self.sb([128, 1], F32, "one1")
        self.memset(self.one1, self.one1[:], 1.0)
        self.n1g = cload("n1g", [128, 4, 8], F32)
        self.n2g = cload("n2g", [128, 4, 8], F32)
        self.modb = cload("modb", [128, 4, 48], F32)
        self.fcw = cload("fcw", [128, 4, NJ, 3], F32)
        self.fcb = cload("fcb", [128, 4, NJ], F32)
        self.XB = self.sb([128, 8, 16], F32, "XB")
        self.MV = {l: self.sb([128, 6, 8, 2], F32, "mv") for l in self.layers}
        cond = cload("cond", [128, 8, 2], F32)
        condT = self.sb([128, 8, 2], F32, "condT")
        self.act(condT[:], cond[:], AF.Silu, r=[cond], w=[condT])
        self.condB = self.sb([128, 8, 2], BF16, "condB")
        self.copy(self.condB[:], condT[:], r=[condT], w=[self.condB])
        self.new_scope()
        self.drain(self.mv_steps(self.layers[0]))

    @staticmethod
    def step(g):
        if g is None:
            return None
        try:
            next(g)
            return g
        except StopIteration:
            return None

    def drain(self, g):
        while g is not None:
            g = self.step(g)

    def mv_steps(self, l):
        I = self.I
        wr = self.ring(2, [128, 6144], BF16, "modw")
        acc = self.sb([128, 48, 2], F32, "modacc")
        tmps = [self.sb([128, 8, 2], F32, "mtmp") for _ in range(2)]
        wts = {}
        wts[0] = wr.get()
        self.load(wts[0], wts[0][:], I["modw"][l, :, 0, :], cast=True)
        yield
        for k in range(8):
            if k + 1 < 8:
                wts[k + 1] = wr.get()
                self.load(wts[k + 1], wts[k + 1][:], I["modw"][l, :, k + 1, :], cast=True)
            wt = wts.pop(k)
            pt = self.PS.get()
            for j in range(48):
                self.mm1(pt, pt[:, 2 * j:2 * j + 2], wt[:, j * 128:(j + 1) * 128], self.condB[:, k, :], True, True,
                         r=[wt, self.condB])
            pv = pt[:, 0:96].rearrange("p (j s) -> p j s", s=2)
            if k == 0:
                self.copy(acc[:], pv, r=[pt], w=[acc])
            else:
                self.tt(acc[:], pv, acc[:], ALU.add, r=[pt, acc], w=[acc])
            yield
        mb = self.modb[:, l, :].unsqueeze(2).broadcast_to([128, 48, 2])
        self.tt(acc[:], acc[:], mb, ALU.add, r=[acc, self.modb], w=[acc])
        mv = self.MV[l]
        a4 = acc[:].rearrange("p (m k) s -> p m k s", m=6)
        for dst, srcm in ((1, 0), (2, 2), (4, 3), (5, 5)):
            self.copy(mv[:, dst], a4[:, srcm], r=[acc], w=[mv])
        for ii, (dst, srcm, g) in enumerate(((0, 1, self.n1g), (3, 4, self.n2g))):
            tmp = tmps[ii]
            self.ts(tmp[:], a4[:, srcm], 1.0, None, ALU.add, r=[acc], w=[tmp])
            gb = g[:, l, :].unsqueeze(2).broadcast_to([128, 8, 2])
            self.tt(mv[:, dst], tmp[:], gb, ALU.mult, r=[tmp, g], w=[mv])

    def modnorm(self, xt, n, l, which, s, h):
        for _ in self.modnorm_steps(xt, n, l, which, s, h):
            pass

    def modnorm_steps(self, xt, n, l, which, s, h, ps_tile=None):
        mv = self.MV[l]
        sq = self.SQ.get()
        self.act(sq[:, :, 0:n], xt[:, :, 0:n], AF.Square, r=[xt], w=[sq])
        pt = ps_tile if ps_tile is not None else self.PS.get()
        self.mm(pt, pt[:, 0:n], [(self.ones1024[:], sq[:, k, 0:n]) for k in range(8)], r=[sq, self.ones1024])
        yield
        rstd = self.RS.get()
        self.act(rstd[:, 0:n], pt[:, 0:n], AF.Ln, r=[pt, self.epsT], w=[rstd], bias=self.epsT[:, 0:1])
        self.act(rstd[:, 0:n], rstd[:, 0:n], AF.Exp, r=[rstd], w=[rstd], scale=-0.5)
        a_i, b_i = (0, 1) if which == 1 else (3, 4)
        for k in range(8):
            tmp = self.TF.get()
            self.stt(tmp[:, 0:n], xt[:, k, 0:n], mv[:, a_i, k, s:s + 1], rstd[:, 0:n], ALU.mult, ALU.mult,
                     r=[xt, mv, rstd], w=[tmp])
            self.act(h[:, k, 0:n], tmp[:, 0:n], AF.Identity, r=[tmp, mv], w=[h], bias=mv[:, b_i, k, s:s + 1])

    def headnorm(self, pt, rows, n, onesmat, gain_ap, gain_t, dst_t, dst_ap, rope=None):
        sq = self.SQ1.get()
        raw = self.TF.get()
        self.act(sq[0:rows, 0:n], pt[0:rows, 0:n], AF.Square, r=[pt], w=[sq])
        self.act(raw[0:rows, 0:n], pt[0:rows, 0:n], AF.Copy, r=[pt], w=[raw])
        pm = self.PS.get()
        self.mm(pm, pm[0:rows, 0:n], [(onesmat[0:rows, 0:rows], sq[0:rows, 0:n])], r=[sq, onesmat])
        rstd = self.RS.get()
        self.act(rstd[0:rows, 0:n], pm[0:rows, 0:n], AF.Ln, r=[pm, self.epsT], w=[rstd], bias=self.epsT[0:rows, 0:1])
        self.act(rstd[0:rows, 0:n], rstd[0:rows, 0:n], AF.Exp, r=[rstd], w=[rstd], scale=-0.5)
        if rope is None:
            self.stt(dst_ap, raw[0:rows, 0:n], gain_ap, rstd[0:rows, 0:n], ALU.mult, ALU.mult,
                     r=[raw, rstd, gain_t], w=[dst_t])
            return
        if len(rope) == 5:
            cs, sn, permT, r0, r1 = rope
            cs_ap, sn_ap = cs[:, 0:n], sn[:, 0:n]
        else:
            cs, sn, permT, r0, r1, cs_ap, sn_ap = rope
        qn = self.SQ1.get()
        self.stt(qn[0:rows, 0:n], raw[0:rows, 0:n], gain_ap, rstd[0:rows, 0:n], ALU.mult, ALU.mult,
                 r=[raw, rstd, gain_t], w=[qn])
        pw = self.PS.get()
        self.mm(pw, pw[0:rows, 0:n], [(permT[0:rows, 0:rows], qn[0:rows, 0:n])], r=[qn, permT])
        t1 = self.TF.get()
        t2 = self.TF.get()
        self.tt(t1[r0:r1, 0:n], qn[r0:r1, 0:n], cs_ap[r0:r1], ALU.mult, r=[qn, cs], w=[t1])
        self.tt(t2[r0:r1, 0:n], pw[r0:r1, 0:n], sn_ap[r0:r1], ALU.mult, r=[pw, sn], w=[t2])
        if r0 > 0:
            self.copy(dst_ap[0:r0], qn[0:r0, 0:n], r=[qn], w=[dst_t], eng="pool")
        self.tt(dst_ap[r0:r1], t1[r0:r1, 0:n], t2[r0:r1, 0:n], ALU.add, r=[t1, t2], w=[dst_t])

    def load_rope(self, csname, snname, ti):
        t0, n, isctx = TILES[ti]
        cs = self.ROPE.get()
        sn = self.ROPE.get()
        self.load(cs, cs[:, 0:n], self.I[csname][:, t0:t0 + n])
        self.load(sn, sn[:, 0:n], self.I[snname][:, t0:t0 + n])
        return cs, sn

    def resid(self, l, ti, xt, act_t, act_fn, wo):
        mv = self.MV[l]
        t0, n, isctx = TILES[ti]
        s = 1 if isctx else 0
        for oc in range(8):
            pt = self.PS.get()
            self.mm(pt, pt[:, 0:n], [(wo[:, k, oc * 128:(oc + 1) * 128], act_fn(k)) for k in range(8)], r=[act_t, wo])
            self.stt(xt[:, oc, 0:n], pt[:, 0:n], mv[:, 2, oc, s:s + 1], xt[:, oc, 0:n], ALU.mult, ALU.add,
                     r=[pt, mv, xt], w=[xt])
        if not isctx:
            self.copy(self.XB[:, :, 2 * ti:2 * ti + 1], xt[:, :, 0:1], r=[xt], w=[self.XB], eng="pool")
            self.copy(self.XB[:, :, 2 * ti + 1:2 * ti + 2], xt[:, :, n - 1:n], r=[xt], w=[self.XB], eng="pool")
        self.store_x(ti, xt)

    def phase_c_dram(self, l, wo_name_ap, tiles):
        self.new_scope()
        self.XT = self.ring(2, [128, 8, 512], F32, "xt")
        AT = self.ring(2, [128, 8, 512], BF16, "at")
        wo = self.sb([128, 8, 1024], BF16, "wo")
        for k in range(8):
            self.load(wo, wo[:, k, :], wo_name_ap[:, k, :], cast=True)
        for ti in tiles:
            t0, n, isctx = TILES[ti]
            xt = self.load_x(ti)
            at = AT.get()
            self.load(at, at[:, :, 0:n], self.xview(self.AS, t0, n), r=["AS%d" % ti])
            self.resid(l, ti, xt, at, (lambda k, at=at, n=n: at[:, k, 0:n]), wo)

    def layer(self, l):
        kind, idx = l % 3, l // 3
        last = (l == DEPTH - 1)
        final = (l == self.layers[-1])
        tiles = list(range(8)) if last else list(range(9))
        if kind == 0:
            self.swa(l, idx, tiles)
        elif kind == 1:
            self.mla(l, idx, tiles)
        else:
            self.lru(l, idx, tiles)
        self.xsrc_is_input = False
        self.ffn(l, tiles, final)

    def ffn(self, l, tiles, final):
        I = self.I
        mv = self.MV[l]
        self.new_scope()
        self.common_rings()
        WG = self.ring(3, [128, 8, 128], BF16, "wg")
        WV = self.ring(3, [128, 8, 128], BF16, "wv")
        WD = self.ring(1, [128, NJ, 1024], BF16, "wd")
        HID = self.ring(1, [128, NJ, 512], BF16, "hid")
        GB = self.sb([128, NJ, 16], F32, "gb")
        GT = self.ring(2, [128, 514], F32, "gt")
        HB = self.sb([128, 8, 16], BF16, "hb")
        self.modnorm(self.XB, 16, l, 2, 0, HB)
        li = self.layers.index(l)
        mv_gen = self.mv_steps(self.layers[li + 1]) if li + 1 < len(self.layers) else None
        prepped = {}

        def prep_steps(ti_):
            t0_, n_, c_ = TILES[ti_]
            xt_ = self.load_x(ti_)
            h_ = self.H.get()
            prepped[ti_] = (xt_, h_)
            yield from self.modnorm_steps(xt_, n_, l, 2, 1 if c_ else 0, h_, ps_tile=self.PA.tiles[0])
        self.drain(prep_steps(tiles[0]))
        for idx_t, ti in enumerate(tiles):
            t0, n, isctx = TILES[ti]
            s = 1 if isctx else 0
            xt, h = prepped.pop(ti)
            nxt_gen = prep_steps(tiles[idx_t + 1]) if idx_t + 1 < len(tiles) else None
            hid = HID.get()
            wd = WD.get()
            for j in range(NJ):
                wg = WG.get()
                wv = WV.get()
                self.load(wg, wg[:], I["wug"][l, j].rearrange("p (k m) -> p k m", k=8), cast=True)
                self.load(wv, wv[:], I["wuv"][l, j].rearrange("p (k m) -> p k m", k=8), cast=True)
                self.load(wd, wd[:, j, :], I["wdn"][l, j], cast=True)
                if idx_t == 0:
                    pb = self.PS.get()
                    self.mm(pb, pb[:, 0:16], [(wg[:, k, :], HB[:, k, :]) for k in range(8)], r=[wg, HB])
                    self.copy(GB[:, j, :], pb[:, 0:16], r=[pb], w=[GB])
                pg = self.PS.get()
                self.mm(pg, pg[:, 0:n], [(wg[:, k, :], h[:, k, 0:n]) for k in range(8)], r=[wg, h])
                pv = self.PS.get()
                self.mm(pv, pv[:, 0:n], [(wv[:, k, :], h[:, k, 0:n]) for k in range(8)], r=[wv, h])
                gt = GT.get()
                self.act(gt[:, 1:n + 1], pg[:, 0:n], AF.Copy, r=[pg], w=[gt])
                if (not isctx) and ti > 0:
                    self.copy(gt[:, 0:1], GB[:, j, 2 * (ti - 1) + 1:2 * (ti - 1) + 2], r=[GB], w=[gt], eng="pool")
                else:
                    self.memset(gt, gt[:, 0:1], 0.0, eng="pool")
                if (not isctx) and ti < 7:
                    self.copy(gt[:, n + 1:n + 2], GB[:, j, 2 * (ti + 1):2 * (ti + 1) + 1], r=[GB], w=[gt], eng="pool")
                else:
                    self.memset(gt, gt[:, n + 1:n + 2], 0.0, eng="pool")
                c1 = self.TF.get()
                self.ts(c1[:, 0:n], gt[:, 0:n], self.fcw[:, l, j, 0:1], self.fcb[:, l, j:j + 1], ALU.mult, ALU.add,
                        r=[gt, self.fcw, self.fcb], w=[c1])
                self.stt(c1[:, 0:n], gt[:, 1:n + 1], self.fcw[:, l, j, 1:2], c1[:, 0:n], ALU.mult, ALU.add,
                         r=[gt, c1, self.fcw], w=[c1])
                self.stt(c1[:, 0:n], gt[:, 2:n + 2], self.fcw[:, l, j, 2:3], c1[:, 0:n], ALU.mult, ALU.add,
                         r=[gt, c1, self.fcw], w=[c1])
                sl = self.TF.get()
                self.act(sl[:, 0:n], c1[:, 0:n], AF.Silu, r=[c1], w=[sl])
                self.tt(hid[:, j, 0:n], sl[:, 0:n], pv[:, 0:n], ALU.mult, r=[sl, pv], w=[hid])
                if j in (8, 13):
                    nxt_gen = self.step(nxt_gen)
                if j in (4, 17):
                    mv_gen = self.step(mv_gen)
            self.drain(nxt_gen)
            for oc in range(8):
                pt = self.PS.get()
                self.mm(pt, pt[:, 0:n], [(wd[:, j, oc * 128:(oc + 1) * 128], hid[:, j, 0:n]) for j in range(NJ)],
                        r=[wd, hid])
                self.stt(xt[:, oc, 0:n], pt[:, 0:n], mv[:, 5, oc, s:s + 1], xt[:, oc, 0:n], ALU.mult, ALU.add,
                         r=[pt, mv, xt], w=[xt])
            self.store_x(ti, xt, final=final)
        self.drain(mv_gen)

    def swa(self, l, idx, tiles):
        nc = self.nc
        I = self.I
        self.new_scope()
        self.common_rings(nxt=1, nh=1, nsq1=0, ntf=4)
        self.ROPE = self.ring(4, [128, 512], F32, "rope")
        wq = self.sb([128, 8, 1024], BF16, "wq")
        wk = self.sb([128, 8, 256], BF16, "wk")
        wv = self.sb([128, 8, 256], BF16, "wv")
        wo = self.sb([128, 8, 1024], BF16, "wo")
        for k in range(8):
            self.load(wq, wq[:, k, :], I["swq"][idx, :, k, :], cast=True)
            self.load(wo, wo[:, k, :], I["swo"][idx, :, k, :], cast=True)
        self.load(wk, wk[:], I["swk"][idx], cast=True)
        self.load(wv, wv[:], I["swv"][idx], cast=True)
        gq = self.sb([128, 1], F32, "gq")
        gk = self.sb([128, 1], F32, "gk")
        self.load(gq, gq[:], I["sqg"][idx])
        self.load(gk, gk[:], I["skg"][idx])
        es = self.sb([128, 16], F32, "es")
        self.load(es, es[:], I["ssink"][idx])
        self.act(es[:], es[:], AF.Exp, r=[es, self.nshift], w=[es], bias=self.nshift[:, 0:1])
        KT = self.sb([128, 2, T], BF16, "KT")
        VA = self.sb([128, 34, 4, 128], BF16, "VA")
        self.memset(VA, VA[:, :, :, 64:128], 1.0, eng="pool")
        QT = self.sb([128, 8, 512], BF16, "QT")
        OT = self.sb([128, 8, 512], BF16, "OT")
        PT = self.ring(4, [128, 512], BF16, "pt")
        DEN = self.ring(2, [128, 512], F32, "den")
        NG = 4
        gSQ = [self.sb([128, 512], BF16, "gsq") for _ in range(NG)]
        gRAW = [self.sb([128, 512], F32, "graw") for _ in range(NG)]
        gQN = [self.sb([128, 512], BF16, "gqn") for _ in range(NG)]
        gRS = [self.sb([128, 512], F32, "grs") for _ in range(NG)]
        PSS = Ring(self.PS.tiles[0:3])
        PSG = Ring(self.PS.tiles[3:6])

        def hn_steps(slot, w_t, c, h, n, gain, dst_t, dst_ap, rope):
            pt = PSG.get()
            self.mm(pt, pt[:, 0:n], [(w_t[:, k, c * 128:(c + 1) * 128], h[:, k, 0:n]) for k in range(8)], r=[w_t, h])
            sq, raw, qn, rstd = gSQ[slot], gRAW[slot], gQN[slot], gRS[slot]
            self.act(sq[:, 0:n], pt[:, 0:n], AF.Square, r=[pt], w=[sq])
            self.act(raw[:, 0:n], pt[:, 0:n], AF.Copy, r=[pt], w=[raw])
            yield
            pm = PSG.get()
            self.mm(pm, pm[:, 0:n], [(self.blk64[:], sq[:, 0:n])], r=[sq, self.blk64])
            self.act(rstd[:, 0:n], pm[:, 0:n], AF.Ln, r=[pm, self.epsT], w=[rstd], bias=self.epsT[:, 0:1])
            self.act(rstd[:, 0:n], rstd[:, 0:n], AF.Exp, r=[rstd], w=[rstd], scale=-0.5)
            if rope is None:
                self.stt(dst_ap, raw[:, 0:n], gain[:, 0:1], rstd[:, 0:n], ALU.mult, ALU.mult, r=[raw, rstd, gain], w=[dst_t])
                return
            cs, sn = rope
            self.stt(qn[:, 0:n], raw[:, 0:n], gain[:, 0:1], rstd[:, 0:n], ALU.mult, ALU.mult, r=[raw, rstd, gain], w=[qn])
            yield
            pw = PSG.get()
            self.mm(pw, pw[:, 0:n], [(self.perm64[:], qn[:, 0:n])], r=[qn, self.perm64])
            t1 = self.TF.get()
            t2 = self.TF.get()
            self.tt(t1[:, 0:n], qn[:, 0:n], cs[:, 0:n], ALU.mult, r=[qn, cs], w=[t1])
            self.tt(t2[:, 0:n], pw[:, 0:n], sn[:, 0:n], ALU.mult, r=[pw, sn], w=[t2])
            self.tt(dst_ap, t1[:, 0:n], t2[:, 0:n], ALU.add, r=[t1, t2], w=[dst_t])

        def run_group(gens):
            gens = list(gens)
            while gens:
                nxt = []
                for g_ in gens:
                    try:
                        next(g_)
                        nxt.append(g_)
                    except StopIteration:
                        pass
                gens = nxt

        for ti in range(9):
            t0, n, isctx = TILES[ti]
            s = 1 if isctx else 0
            xt = self.load_x(ti)
            h = self.H.get()
            self.modnorm(xt, n, l, 1, s, h)
            rope = None
            if not isctx:
                rope = self.load_rope("rcs", "rsn", ti)
            run_group([hn_steps(c, wk, c, h, n, gk, KT, KT[:, c, t0:t0 + n], rope) for c in range(2)])
            for tb in range(n // 128):
                pt = PSG.get()
                self.mm(pt, pt[:, 0:256], [(h[:, k, tb * 128:(tb + 1) * 128], wv[:, k, :]) for k in range(8)], r=[wv, h])
                blk = (t0 // 128) + tb
                self.copy(VA[:, blk, :, 0:64], pt[:, 0:256].rearrange("p (g d) -> p g d", g=4), r=[pt], w=[VA], eng="act")
        LOOK = 2
        for ti in tiles:
            t0, n, isctx = TILES[ti]
            s = 1 if isctx else 0
            xt = self.load_x(ti)
            h = self.H.get()
            self.modnorm(xt, n, l, 1, s, h)
            rope = None
            if not isctx:
                rope = self.load_rope("rcs", "rsn", ti)
            for c0_ in (0, 4):
                run_group([hn_steps(j, wq, c0_ + j, h, n, gq, QT, QT[:, c0_ + j, 0:n], rope) for j in range(NG)])
            flat = []
            for qb in range(n // 128):
                QB = t0 // 128 + qb
                if isctx:
                    kbs = [(32, 0), (33, 0)]
                else:
                    kbs = []
                    if QB > 0:
                        kbs.append((QB - 1, 1))
                    kbs.append((QB, 0))
                    if QB < 31:
                        kbs.append((QB + 1, 2))
                    kbs += [(32, 0), (33, 0)]
                for g in range(4):
                    for ki, (kb, mk) in enumerate(kbs):
                        flat.append((qb, g, ki, len(kbs), kb, mk))

            def issue_s(item):
                qb, g, ki, nk, kb, mk = item
                base = 0 if g < 2 else 64
                c0 = 4 * (g % 2)
                kc = g % 2
                ps_ = PSS.get()
                rhs = QT[base:base + 64, c0:c0 + 4, qb * 128:(qb + 1) * 128]
                self.mm1(ps_, ps_[:], KT[base:base + 64, kc, kb * 128:(kb + 1) * 128], rhs, True, True, r=[KT, QT])
                return ps_
            pend = {}
            for i in range(min(LOOK, len(flat))):
                pend[i] = issue_s(flat[i])
            po = None
            for i, item in enumerate(flat):
                qb, g, ki, nk, kb, mk = item
                base = 0 if g < 2 else 64
                c0 = 4 * (g % 2)
                if ki == 0:
                    po = self.PA.get()
                ps_ = pend.pop(i)
                p = PT.get()
                self.act(p[:], ps_[:], AF.Exp, r=[ps_, self.nshift], w=[p], bias=self.nshift[:, 0:1], scale=0.125)
                if i + LOOK < len(flat):
                    pend[i + LOOK] = issue_s(flat[i + LOOK])
                if mk:
                    cm, st = (1, -1) if mk == 1 else (-1, 1)
                    self.op("pool", (lambda p=p, cm=cm, st=st: nc.gpsimd.affine_select(
                        out=p[:].rearrange("p (a b) -> p a b", a=4), in_=p[:].rearrange("p (a b) -> p a b", a=4),
                        pattern=[[0, 4], [st, 128]], compare_op=ALU.is_ge, fill=0.0, base=0, channel_multiplier=cm)),
                        r=[p], w=[p])
                self.mm1(po, po[:], VA[:, kb, g, :], p[:], ki == 0, ki == nk - 1, r=[VA, p])
                if ki == nk - 1:
                    den = DEN.get()
                    esb = es[64:128, 4 * g:4 * g + 4].unsqueeze(2).broadcast_to([64, 4, 128])
                    self.tt(den[64:128, :].rearrange("p (a b) -> p a b", a=4), po[64:128, :].rearrange("p (a b) -> p a b", a=4),
                            esb, ALU.add, r=[po, es], w=[den])
                    self.act(den[64:128, :], den[64:128, :], AF.Ln, r=[den], w=[den])
                    self.act(den[64:128, :], den[64:128, :], AF.Exp, r=[den], w=[den], scale=-1.0)
                    self.tt(OT[base:base + 64, c0:c0 + 4, qb * 128:(qb + 1) * 128],
                            po[0:64, :].rearrange("p (a b) -> p a b", a=4),
                            den[64:128, :].rearrange("p (a b) -> p a b", a=4), ALU.mult, r=[po, den], w=[OT])
            self.resid(l, ti, xt, OT, (lambda k, n=n: OT[:, k, 0:n]), wo)

    def mla(self, l, idx, tiles):
        I = self.I
        self.new_scope()
        CQN = self.sb([128, 3, T], BF16, "CQN")
        CKVN = self.sb([128, 2, T], BF16, "CKVN")
        KRSQ = self.sb([128, T], BF16, "KRSQ")
        KRROT = self.sb([128, T], BF16, "KRROT")
        gv = self.sb([128, 8], F32, "mg")
        self.load(gv, gv[:], I["mgv"][:])
        persist = self.cur_scope
        self.cur_scope = None
        self.new_scope()
        self.common_rings(nxt=2, nh=1)
        self.ROPE = self.ring(4, [128, 512], F32, "rope")
        wdn = self.sb([128, 8, 640], BF16, "mdn")
        wrp = self.sb([128, 8, 96], BF16, "mrp")
        KRG = self.ring(2, [128, 512], BF16, "krg")
        for kt in KRG.tiles:
            self.memset(kt, kt[:], 0.0)
        for k in range(8):
            self.load(wdn, wdn[:, k, :], I["mdn"][:, k, :], cast=True)
        self.load(wrp, wrp[:], I["mrp"][:], cast=True)
        for ti in range(9):
            t0, n, isctx = TILES[ti]
            s = 1 if isctx else 0
            xt = self.load_x(ti)
            h = self.H.get()
            self.modnorm(xt, n, l, 1, s, h)
            for (nch, coff, gcol, onesm, dstT) in ((3, 0, 0, self.ones384, CQN), (2, 384, 3, self.ones256, CKVN)):
                raws, sqs = [], []
                for c in range(nch):
                    pt = self.PS.get()
                    self.mm(pt, pt[:, 0:n], [(wdn[:, k, coff + c * 128:coff + (c + 1) * 128], h[:, k, 0:n]) for k in range(8)],
                            r=[wdn, h])
                    sq = self.SQ1.get()
                    raw = self.TF.get()
                    self.act(sq[:, 0:n], pt[:, 0:n], AF.Square, r=[pt], w=[sq])
                    self.act(raw[:, 0:n], pt[:, 0:n], AF.Copy, r=[pt], w=[raw])
                    raws.append(raw)
                    sqs.append(sq)
                pm = self.PS.get()
                self.mm(pm, pm[:, 0:n], [(onesm[:], sq[:, 0:n]) for sq in sqs], r=sqs + [onesm])
                rstd = self.RS.get()
                self.act(rstd[:, 0:n], pm[:, 0:n], AF.Ln, r=[pm, self.epsT], w=[rstd], bias=self.epsT[:, 0:1])
                self.act(rstd[:, 0:n], rstd[:, 0:n], AF.Exp, r=[rstd], w=[rstd], scale=-0.5)
                for c in range(nch):
                    self.stt(dstT[:, c, t0:t0 + n], raws[c][:, 0:n], gv[:, gcol + c:gcol + c + 1], rstd[:, 0:n],
                             ALU.mult, ALU.mult, r=[raws[c], rstd, gv], w=[dstT])
            pk = self.PS.get()
            self.mm(pk, pk[0:96, 0:n], [(wrp[:, k, :], h[:, k, 0:n]) for k in range(8)], r=[wrp, h])
            self.act(KRSQ[64:96, t0:t0 + n], pk[64:96, 0:n], AF.Square, r=[pk], w=[KRSQ])
            krg = KRG.get()
            self.ts(krg[64:96, 0:n], pk[64:96, 0:n], gv[64:96, 6:7], None, ALU.mult, r=[pk, gv], w=[krg])
            if isctx:
                self.copy(KRROT[64:96, t0:t0 + n], krg[64:96, 0:n], r=[krg], w=[KRROT])
            else:
                cs, sn = self.load_rope("mcs", "msn", ti)
                pw = self.PS.get()
                self.mm(pw, pw[0:96, 0:n], [(self.perm96[0:96, 0:96], krg[0:96, 0:n])], r=[krg, self.perm96])
                t1 = self.TF.get()
                t2 = self.TF.get()
                self.tt(t1[64:96, 0:n], krg[64:96, 0:n], cs[64:96, 0:n], ALU.mult, r=[krg, cs], w=[t1])
                self.tt(t2[64:96, 0:n], pw[64:96, 0:n], sn[64:96, 0:n], ALU.mult, r=[pw, sn], w=[t2])
                self.tt(KRROT[64:96, t0:t0 + n], t1[64:96, 0:n], t2[64:96, 0:n], ALU.add, r=[t1, t2], w=[KRROT])
        self.new_scope()
        RALL = self.sb([128, 2, L], F32, "ropeall")
        self.load(RALL, RALL[:, 0, :], I["mcs"][:, :])
        self.load(RALL, RALL[:, 1, :], I["msn"][:, :])
        wuq = self.sb([128, 3, 1536], BF16, "muq")
        wuk = self.sb([128, 2, 1024], BF16, "muk")
        wuv = self.sb([128, 2, 1024], BF16, "muv")
        for k in range(3):
            self.load(wuq, wuq[:, k, :], I["muq"][:, k, :], cast=True)
        for k in range(2):
            self.load(wuk, wuk[:, k, :], I["muk"][:, k, :], cast=True)
            self.load(wuv, wuv[:, k, :], I["muv"][:, k, :], cast=True)
        KTH = self.ring(2, [128, T], BF16, "KTH")
        VH = self.ring(2, [128, 34, 128], BF16, "VH")
        QTH = self.ring(2, [128, 512], BF16, "QTH")
        PT = self.ring(4, [128, 512], BF16, "pt")
        DEN = self.ring(2, [128, 512], F32, "den")
        OS = self.ring(3, [128, 512], BF16, "os")
        qSQ = self.ring(2, [128, 512], BF16, "qsq")
        qTF = self.ring(3, [128, 512], F32, "qtf")
        qRS = self.ring(1, [128, 512], F32, "qrs")
        kSQ = self.ring(2, [128, 512], BF16, "ksq")
        kTF = self.ring(2, [128, 512], F32, "ktf")
        kRS = self.ring(2, [128, 512], F32, "krs")
        for v in VH.tiles:
            self.memset(v, v[:, :, 64:128], 1.0, eng="pool")
        scale = 96.0 ** -0.5
        PSS = Ring(self.PS.tiles[0:3])
        PSG = Ring(self.PS.tiles[3:6])

        def kv_steps(hd, kth, vh):
            for ti in range(9):
                t0, n, isctx = TILES[ti]
                pk = PSG.get()
                self.mm(pk, pk[0:64, 0:n], [(wuk[:, c2, hd * 64:(hd + 1) * 64], CKVN[:, c2, t0:t0 + n]) for c2 in range(2)],
                        r=[wuk, CKVN])
                sq = kSQ.get()
                raw = kTF.get()
                self.act(sq[0:64, 0:n], pk[0:64, 0:n], AF.Square, r=[pk], w=[sq])
                self.act(raw[0:64, 0:n], pk[0:64, 0:n], AF.Copy, r=[pk], w=[raw])
                self.copy(sq[64:96, 0:n], KRSQ[64:96, t0:t0 + n], r=[KRSQ], w=[sq], eng="pool")
                yield
                pm = PSG.get()
                self.mm(pm, pm[0:96, 0:n], [(self.ones96[0:96, 0:96], sq[0:96, 0:n])], r=[sq, self.ones96])
                rstd = kRS.get()
                self.act(rstd[0:96, 0:n], pm[0:96, 0:n], AF.Ln, r=[pm, self.epsT], w=[rstd], bias=self.epsT[0:96, 0:1])
                self.act(rstd[0:96, 0:n], rstd[0:96, 0:n], AF.Exp, r=[rstd], w=[rstd], scale=-0.5)
                self.stt(kth[0:64, t0:t0 + n], raw[0:64, 0:n], gv[0:64, 6:7], rstd[0:64, 0:n], ALU.mult, ALU.mult,
                         r=[raw, rstd, gv], w=[kth])
                self.tt(kth[64:96, t0:t0 + n], KRROT[64:96, t0:t0 + n], rstd[64:96, 0:n], ALU.mult,
                        r=[KRROT, rstd], w=[kth])
                for tb in range(n // 128):
                    pv = PSG.get()
                    self.mm(pv, pv[:, 0:64], [(CKVN[:, c2, t0 + tb * 128:t0 + (tb + 1) * 128], wuv[:, c2, hd * 64:(hd + 1) * 64])
                                              for c2 in range(2)], r=[wuv, CKVN])
                    self.copy(vh[:, t0 // 128 + tb, 0:64], pv[:, 0:64], r=[pv], w=[vh], eng="act")
                yield

        def q_steps(hd, ti, qth):
            t0, n, isctx = TILES[ti]
            pq = PSG.get()
            self.mm(pq, pq[0:96, 0:n], [(wuq[:, c3, hd * 96:(hd + 1) * 96], CQN[:, c3, t0:t0 + n]) for c3 in range(3)],
                    r=[wuq, CQN])
            sq = qSQ.get()
            raw = qTF.get()
            self.act(sq[0:96, 0:n], pq[0:96, 0:n], AF.Square, r=[pq], w=[sq])
            self.act(raw[0:96, 0:n], pq[0:96, 0:n], AF.Copy, r=[pq], w=[raw])
            yield
            pm = PSG.get()
            self.mm(pm, pm[0:96, 0:n], [(self.ones96[0:96, 0:96], sq[0:96, 0:n])], r=[sq, self.ones96])
            rstd = qRS.get()
            self.act(rstd[0:96, 0:n], pm[0:96, 0:n], AF.Ln, r=[pm, self.epsT], w=[rstd], bias=self.epsT[0:96, 0:1])
            self.act(rstd[0:96, 0:n], rstd[0:96, 0:n], AF.Exp, r=[rstd], w=[rstd], scale=-0.5)
            if isctx:
                self.stt(qth[0:96, 0:n], raw[0:96, 0:n], gv[0:96, 5:6], rstd[0:96, 0:n], ALU.mult, ALU.mult,
                         r=[raw, rstd, gv], w=[qth])
                return
            qn = qSQ.get()
            self.stt(qn[0:96, 0:n], raw[0:96, 0:n], gv[0:96, 5:6], rstd[0:96, 0:n], ALU.mult, ALU.mult,
                     r=[raw, rstd, gv], w=[qn])
            yield
            pw = PSG.get()
            self.mm(pw, pw[0:96, 0:n], [(self.perm96[0:96, 0:96], qn[0:96, 0:n])], r=[qn, self.perm96])
            t1 = qTF.get()
            t2 = qTF.get()
            self.tt(t1[64:96, 0:n], qn[64:96, 0:n], RALL[64:96, 0, t0:t0 + n], ALU.mult, r=[qn, RALL], w=[t1])
            self.tt(t2[64:96, 0:n], pw[64:96, 0:n], RALL[64:96, 1, t0:t0 + n], ALU.mult, r=[pw, RALL], w=[t2])
            self.copy(qth[0:64, 0:n], qn[0:64, 0:n], r=[qn], w=[qth], eng="pool")
            self.tt(qth[64:96, 0:n], t1[64:96, 0:n], t2[64:96, 0:n], ALU.add, r=[t1, t2], w=[qth])

        def step(g):
            if g is None:
                return None
            try:
                next(g)
                return g
            except StopIteration:
                return None

        def drain(g):
            while g is not None:
                g = step(g)

        units = [(hd, ti) for hd in range(16) for ti in tiles]
        kv_cur = (KTH.get(), VH.get())
        drain(kv_steps(0, kv_cur[0], kv_cur[1]))
        qth = QTH.get()
        drain(q_steps(units[0][0], units[0][1], qth))
        kv_gen = None
        kv_next = None
        LOOK = 2
        for ui, (hd, ti) in enumerate(units):
            t0, n, isctx = TILES[ti]
            if ti == tiles[0]:
                kth, vh = kv_cur
                if hd + 1 < 16:
                    kv_next = (KTH.get(), VH.get())
                    kv_gen = kv_steps(hd + 1, kv_next[0], kv_next[1])
            q_gen = None
            qth_next = None
            if ui + 1 < len(units):
                qth_next = QTH.get()
                q_gen = q_steps(units[ui + 1][0], units[ui + 1][1], qth_next)
            kbs = [32, 33] if isctx else list(range(34))
            nk = len(kbs)
            po = self.PA.get()
            pend = {}

            def issue_s(ki, kth=kth, qth=qth, n=n, kbs=kbs):
                ps_ = PSS.get()
                kb = kbs[ki]
                self.mm1(ps_, ps_[:, 0:n], kth[0:96, kb * 128:(kb + 1) * 128], qth[0:96, 0:n], True, True, r=[kth, qth])
                return ps_
            for ki in range(min(LOOK, nk)):
                pend[ki] = issue_s(ki)
            for ki in range(nk):
                ps_ = pend.pop(ki)
                p = PT.get()
                self.act(p[:, 0:n], ps_[:, 0:n], AF.Exp, r=[ps_, self.nshift], w=[p], bias=self.nshift[:, 0:1], scale=scale)
                if ki + LOOK < nk:
                    pend[ki + LOOK] = issue_s(ki + LOOK)
                self.mm1(po, po[:, 0:n], vh[:, kbs[ki], :], p[:, 0:n], ki == 0, ki == nk - 1, r=[vh, p])
                if ki % 8 == 3:
                    q_gen = step(q_gen)
                if ki % 8 == 7:
                    kv_gen = step(kv_gen)
            drain(q_gen)
            den = DEN.get()
            self.recip(den[64:128, 0:n], po[64:128, 0:n], r=[po], w=[den])
            hb = (hd % 2) * 64
            os_ = OS.get()
            self.tt(os_[hb:hb + 64, 0:n], po[0:64, 0:n], den[64:128, 0:n], ALU.mult, r=[po, den], w=[os_])
            dst = self.AS.rearrange("(k p) t -> p k t", p=128)[hb:hb + 64, hd // 2, t0:t0 + n]
            self.store("AS%d" % ti, dst, os_, os_[hb:hb + 64, 0:n])
            qth = qth_next
            if ti == tiles[-1]:
                drain(kv_gen)
                kv_gen = None
                kv_cur = kv_next
        self.S.barrier()
        self.cur_scope.close()
        persist.close()
        self.cur_scope = None
        self.phase_c_dram(l, I["mwo"], tiles)

    def lru(self, l, idx, tiles):
        nc = self.nc
        I = self.I
        PADL = 2
        CTX0 = L + 6
        XW = T + 8

        def pcol(t0):
            return t0 + PADL if t0 < L else (t0 - L) + CTX0
        self.new_scope()
        self.common_rings(nxt=2, nh=2)
        win = self.sb([128, 8, 2048], BF16, "lwin")
        for k in range(8):
            self.load(win, win[:, k, :], I["lwin"][:, k, :], cast=True)
        STG = self.ring(4, [128, 512], BF16, "stg")
        for ti in range(9):
            t0, n, isctx = TILES[ti]
            s = 1 if isctx else 0
            xt = self.load_x(ti)
            h = self.H.get()
            self.modnorm(xt, n, l, 1, s, h)
            for oc in range(16):
                pt = self.PS.get()
                self.mm(pt, pt[:, 0:n], [(win[:, k, oc * 128:(oc + 1) * 128], h[:, k, 0:n]) for k in range(8)], r=[win, h])
                stg = STG.get()
                if oc < 8:
                    self.act(stg[:, 0:n], pt[:, 0:n], AF.Gelu_apprx_tanh, r=[pt], w=[stg])
                    self.store(("AS", oc, ti), self.AS[oc * 128:(oc + 1) * 128, t0:t0 + n], stg, stg[:, 0:n])
                else:
                    self.copy(stg[:, 0:n], pt[:, 0:n], r=[pt], w=[stg])
                    self.store(("XRS", oc - 8), self.XRS[(oc - 8) * 128:(oc - 7) * 128, t0:t0 + n], stg, stg[:, 0:n])
        self.new_scope()
        self.TF = self.ring(6, [128, 512], F32, "tf")
        gw = self.sb([128, 2, 2, 4, 2, 256], BF16, "lgw")
        for d in range(2):
            for wch in range(2):
                self.load(gw, gw[:, d, wch], I["lgw"][:, d, wch], cast=True)
        sv = self.sb([128, 64], F32, "lsv")
        self.load(sv, sv[:], I["lsv"][:])
        ngb = self.sb([128, 32], F32, "lngb")
        self.load(ngb, ngb[:], I["lgb"][:])
        self.ts(ngb[:], ngb[:], -1.0, None, ALU.mult, r=[ngb], w=[ngb])
        cp = self.sb([128, 16], F32, "lcp")
        cp2 = self.sb([128, 16], F32, "lcp2")
        self.act(cp[:], sv[:, 40:56], AF.Exp, r=[sv], w=[cp], scale=-1.0)
        self.act(cp[:], cp[:], AF.Ln, r=[cp, self.one1], w=[cp], bias=self.one1[:, 0:1])
        self.ts(cp2[:], cp[:], -16.0, None, ALU.mult, r=[cp], w=[cp2])
        self.ts(cp[:], cp[:], -8.0, None, ALU.mult, r=[cp], w=[cp])
        XR = self.sb([128, 2, XW], BF16, "XR")
        self.memset(XR, XR[:, :, 0:PADL], 0.0, eng="pool")
        self.memset(XR, XR[:, :, L + PADL:CTX0], 0.0, eng="pool")
        self.memset(XR, XR[:, :, CTX0 + CT:XW], 0.0, eng="pool")
        XC = self.sb([128, 2, XW], BF16, "XC")
        SF = self.sb([128, XW], F32, "SF")
        CAR = self.ring(4, [128, 1], F32, "car")
        GT_ = self.ring(3, [128, 512], BF16, "gtile")
        zero1 = self.sb([128, 1], F32, "zero1")
        self.memset(zero1, zero1[:], 0.0)
        NW = XW - 4
        for bk in range(4):
            for cc in range(2):
                c = 2 * bk + cc
                self.load(XR, XR[:, cc, PADL:PADL + L], self.XRS[c * 128:(c + 1) * 128, 0:L], r=[("XRS", c)])
                self.load(XR, XR[:, cc, CTX0:CTX0 + CT], self.XRS[c * 128:(c + 1) * 128, L:T], r=[("XRS", c)])
            for cc in range(2):
                c = 2 * bk + cc
                acc = SF
                self.ts(acc[:, 0:NW], XR[:, cc, 0:NW], sv[:, 8 + 4 * c:9 + 4 * c], sv[:, c:c + 1], ALU.mult, ALU.add,
                        r=[XR, sv], w=[acc])
                for k in (1, 2):
                    self.stt(acc[:, 0:NW], XR[:, cc, k:k + NW], sv[:, 8 + 4 * c + k:9 + 4 * c + k], acc[:, 0:NW], ALU.mult, ALU.add,
                             r=[XR, sv, acc], w=[acc])
                self.stt(XC[:, cc, 2:2 + NW], XR[:, cc, 3:3 + NW], sv[:, 8 + 4 * c + 3:9 + 4 * c + 3], acc[:, 0:NW], ALU.mult, ALU.add,
                         r=[XR, sv, acc], w=[XC])
            for cc in range(2):
                c = 2 * bk + cc
                for d in range(2):
                    order = [8] + (list(range(8)) if d == 0 else list(range(7, -1, -1)))
                    carry = zero1
                    for ti in order:
                        t0, n, isctx = TILES[ti]
                        pc = pcol(t0)
                        pr = self.PS.get()
                        self.mm(pr, pr[:, 0:n], [(gw[:, d, 0, bk, kk, cc * 128:(cc + 1) * 128], XC[:, kk, pc:pc + n]) for kk in range(2)],
                                r=[gw, XC])
                        pi = self.PS.get()
                        self.mm(pi, pi[:, 0:n], [(gw[:, d, 1, bk, kk, cc * 128:(cc + 1) * 128], XC[:, kk, pc:pc + n]) for kk in range(2)],
                                r=[gw, XC])
                        gi = (d * 2 + 0) * 8 + c
                        gi2 = (d * 2 + 1) * 8 + c
                        ta = self.TF.get()
                        tb_ = self.TF.get()
                        tcc = self.TF.get()
                        self.act(ta[:, 0:n], pr[:, 0:n], AF.Exp, r=[pr, ngb], w=[ta], bias=ngb[:, gi:gi + 1], scale=-1.0)
                        self.act(ta[:, 0:n], ta[:, 0:n], AF.Ln, r=[ta, self.one1], w=[ta], bias=self.one1[:, 0:1])
                        self.act(ta[:, 0:n], ta[:, 0:n], AF.Exp, r=[ta], w=[ta], scale=-1.0)
                        self.act(tb_[:, 0:n], ta[:, 0:n], AF.Exp, r=[ta, cp], w=[tb_], scale=cp[:, d * 8 + c:d * 8 + c + 1])
                        self.act(ta[:, 0:n], ta[:, 0:n], AF.Exp, r=[ta, cp2], w=[ta], scale=cp2[:, d * 8 + c:d * 8 + c + 1])
                        self.ts(ta[:, 0:n], ta[:, 0:n], 0.99999994, None, ALU.min, r=[ta], w=[ta])
                        self.act(ta[:, 0:n], ta[:, 0:n], AF.Ln, r=[ta, self.one1], w=[ta], bias=self.one1[:, 0:1], scale=-1.0)
                        self.act(ta[:, 0:n], ta[:, 0:n], AF.Exp, r=[ta], w=[ta], scale=0.5)
                        self.act(tcc[:, 0:n], pi[:, 0:n], AF.Exp, r=[pi, ngb], w=[tcc], bias=ngb[:, gi2:gi2 + 1], scale=-1.0)
                        self.act(tcc[:, 0:n], tcc[:, 0:n], AF.Ln, r=[tcc, self.one1], w=[tcc], bias=self.one1[:, 0:1])
                        self.act(tcc[:, 0:n], tcc[:, 0:n], AF.Exp, r=[tcc], w=[tcc], scale=-1.0)
                        self.tt(tcc[:, 0:n], tcc[:, 0:n], XC[:, cc, pc:pc + n], ALU.mult, r=[tcc, XC], w=[tcc])
                        self.tt(tcc[:, 0:n], tcc[:, 0:n], ta[:, 0:n], ALU.mult, r=[tcc, ta], w=[tcc])
                        so = self.TF.get()
                        ncar = CAR.get()
                        if d == 0:
                            self.op("dve", (lambda so=so, tb_=tb_, tcc=tcc, n=n, carry=carry: nc.vector.tensor_tensor_scan(
                                out=so[:, 0:n], data0=tb_[:, 0:n], data1=tcc[:, 0:n], initial=carry[:, 0:1],
                                op0=ALU.mult, op1=ALU.add)), r=[tb_, tcc, carry], w=[so])
                            self.copy(ncar[:, 0:1], so[:, n - 1:n], r=[so], w=[ncar])
                            self.copy(SF[:, t0:t0 + n], so[:, 0:n], r=[so], w=[SF], eng="pool")
                        else:
                            self.op("dve", (lambda so=so, tb_=tb_, tcc=tcc, n=n, carry=carry: nc.vector.tensor_tensor_scan(
                                out=so[:, 0:n][:, ::-1], data0=tb_[:, 0:n][:, ::-1], data1=tcc[:, 0:n][:, ::-1], initial=carry[:, 0:1],
                                op0=ALU.mult, op1=ALU.add)), r=[tb_, tcc, carry], w=[so])
                            self.copy(ncar[:, 0:1], so[:, 0:1], r=[so], w=[ncar])
                            self.tt(so[:, 0:n], so[:, 0:n], SF[:, t0:t0 + n], ALU.add, r=[so, SF], w=[so])
                            gtile = GT_.get()
                            asl = self.AS[c * 128:(c + 1) * 128, t0:t0 + n]
                            self.load(gtile, gtile[:, 0:n], asl, r=[("AS", c, ti)])
                            self.tt(gtile[:, 0:n], so[:, 0:n], gtile[:, 0:n], ALU.mult, r=[so, gtile], w=[gtile])
                            self.store(("AS", c, ti), asl, gtile, gtile[:, 0:n])
                        carry = ncar
        self.phase_c_dram_multi(l, I["lwout"], tiles)

    def phase_c_dram_multi(self, l, wo_ap, tiles):
        self.new_scope()
        self.XT = self.ring(2, [128, 8, 512], F32, "xt")
        AT = self.ring(2, [128, 8, 512], BF16, "at")
        wo = self.sb([128, 8, 1024], BF16, "wo")
        for k in range(8):
            self.load(wo, wo[:, k, :], wo_ap[:, k, :], cast=True)
        for ti in tiles:
            t0, n, isctx = TILES[ti]
            xt = self.load_x(ti)
            at = AT.get()
            self.load(at, at[:, :, 0:n], self.xview(self.AS, t0, n), r=[("AS", c, ti) for c in range(8)])
            self.resid(l, ti, xt, at, (lambda k, at=at, n=n: at[:, k, 0:n]), wo)


def input_shapes():
    return {
        "xin": (D, T), "cond": (128, 8, 2), "n1g": (128, 4, 8), "n2g": (128, 4, 8),
        "modw": (4, 128, 8, 6144), "modb": (128, 4, 48),
        "wug": (4, NJ, 128, 1024), "wuv": (4, NJ, 128, 1024), "wdn": (4, NJ, 128, 1024),
        "fcw": (128, 4, NJ, 3), "fcb": (128, 4, NJ),
        "swq": (2, 128, 8, 1024), "swk": (2, 128, 8, 256), "swv": (2, 128, 8, 256), "swo": (2, 128, 8, 1024),
        "sqg": (2, 128, 1), "skg": (2, 128, 1), "ssink": (2, 128, 16),
        "rcs": (128, L), "rsn": (128, L), "mcs": (128, L), "msn": (128, L),
        "mdn": (128, 8, 640), "mrp": (128, 8, 96), "muq": (128, 3, 1536), "muk": (128, 2, 1024), "muv": (128, 2, 1024),
        "mwo": (128, 8, 1024), "mgv": (128, 8),
        "lwin": (128, 8, 2048), "lwout": (128, 8, 1024), "lgw": (128, 2, 2, 4, 2, 256), "lsv": (128, 64), "lgb": (128, 32),
        "c_ones1024": (128, 128), "c_blk64": (128, 128), "c_ones384": (128, 128), "c_ones256": (128, 128),
        "c_ones96": (128, 128), "c_perm64": (128, 128), "c_perm96": (128, 128),
    }


def _fm(v, nch):
    return np.ascontiguousarray(np.asarray(v, np.float32).reshape(nch, 128).T)


def _wfm(w):
    K, N = w.shape
    return np.ascontiguousarray(np.asarray(w, np.float32).reshape(K // 128, 128, N).transpose(1, 0, 2))


def _rope_tables(rot_dim, base_part, nrows):
    n_freq = rot_dim // 4
    half = rot_dim // 2
    t = np.arange(L)
    row = (t // 64).astype(np.float32)
    col = (t % 64).astype(np.float32)
    inv = (np.float32(10000.0) ** (-np.arange(n_freq, dtype=np.float32) / np.float32(n_freq))).astype(np.float32)
    cs = np.zeros((128, L), np.float32)
    sn = np.zeros((128, L), np.float32)
    partner = np.zeros(128, np.int64) - 1
    for p in range(nrows):
        d = p % rot_dim if base_part == 0 else p
        pp = p + base_part
        dd = d % rot_dim
        pos = row if dd < half else col
        e = dd % half
        i = e % n_freq
        ang = (pos * inv[i]).astype(np.float32)
        cs[pp] = np.cos(ang)
        sgn = -1.0 if e < n_freq else 1.0
        sn[pp] = sgn * np.sin(ang)
        partner[pp] = pp + n_freq if e < n_freq else pp - n_freq
    return cs, sn, partner


def prepare_shared(inp):
    f = lambda k: np.asarray(inp[k], np.float32)
    sh = {}
    sh["n1g"] = np.ascontiguousarray(f("norm1").reshape(4, 8, 128).transpose(2, 0, 1))
    sh["n2g"] = np.ascontiguousarray(f("norm2").reshape(4, 8, 128).transpose(2, 0, 1))
    sh["modw"] = np.ascontiguousarray(f("mod_w").reshape(4, 8, 128, 6144).transpose(0, 2, 1, 3))
    sh["modb"] = np.ascontiguousarray(f("mod_b").reshape(4, 48, 128).transpose(2, 0, 1))
    wup = f("ffn_w_up")
    def upl(w):
        return np.ascontiguousarray(w.reshape(4, 8, 128, NJ, 128).transpose(0, 3, 2, 1, 4).reshape(4, NJ, 128, 1024))
    sh["wug"] = upl(wup[:, :, :DFF])
    sh["wuv"] = upl(wup[:, :, DFF:])
    sh["wdn"] = np.ascontiguousarray(f("ffn_w_down").reshape(4, NJ, 128, 1024))
    sh["fcw"] = np.ascontiguousarray(f("ffn_conv_w").reshape(4, 3, NJ, 128).transpose(3, 0, 2, 1))
    sh["fcb"] = np.ascontiguousarray(f("ffn_conv_b").reshape(4, NJ, 128).transpose(2, 0, 1))
    wqkv = f("swa_w_qkv")
    wq = wqkv[:, :, :1024].reshape(2, 1024, 2, 8, 64).transpose(0, 1, 3, 2, 4).reshape(2, 1024, 1024)
    wk = wqkv[:, :, 1024:1280].reshape(2, 1024, 2, 2, 64).transpose(0, 1, 3, 2, 4).reshape(2, 1024, 256)
    wv = wqkv[:, :, 1280:1536]
    sh["swq"] = np.stack([_wfm(wq[i]) for i in range(2)])
    sh["swk"] = np.stack([_wfm(wk[i]) for i in range(2)])
    sh["swv"] = np.stack([_wfm(wv[i]) for i in range(2)])
    wo = f("swa_w_o").reshape(2, 2, 8, 64, 1024).transpose(0, 2, 1, 3, 4).reshape(2, 1024, 1024)
    sh["swo"] = np.stack([_wfm(wo[i]) for i in range(2)])
    sh["sqg"] = np.ascontiguousarray(np.tile(f("swa_q_gain"), (1, 2)).reshape(2, 128, 1))
    sh["skg"] = np.ascontiguousarray(np.tile(f("swa_k_gain"), (1, 2)).reshape(2, 128, 1))
    sh["ssink"] = np.ascontiguousarray(np.broadcast_to(f("swa_sink")[:, None, :], (2, 128, 16)))
    cs, sn, partner = _rope_tables(64, 0, 128)
    sh["rcs"], sh["rsn"] = cs, sn
    pm = np.zeros((128, 128), np.float32)
    for m in range(128):
        pm[partner[m], m] = 1.0
    sh["c_perm64"] = pm
    cs, sn, partner = _rope_tables(32, 64, 32)
    sh["mcs"], sh["msn"] = cs, sn
    pm = np.zeros((128, 128), np.float32)
    for m in range(64, 96):
        pm[partner[m], m] = 1.0
    sh["c_perm96"] = pm
    wd = f("mla_w_down")[0]
    sh["mdn"] = _wfm(wd[:, :640])
    wr = np.zeros((1024, 96), np.float32)
    wr[:, 64:96] = wd[:, 640:672]
    sh["mrp"] = _wfm(wr)
    sh["muq"] = _wfm(f("mla_w_uq")[0])
    sh["muk"] = _wfm(f("mla_w_uk")[0])
    sh["muv"] = _wfm(f("mla_w_uv")[0])
    sh["mwo"] = _wfm(f("mla_w_o")[0])
    gv = np.zeros((128, 8), np.float32)
    gv[:, 0:3] = _fm(f("mla_q_lora_gain")[0], 3)
    gv[:, 3:5] = _fm(f("mla_kv_lora_gain")[0], 2)
    gv[0:96, 5] = f("mla_q_gain")[0]
    gv[0:96, 6] = f("mla_k_gain")[0]
    sh["mgv"] = gv
    sh["lwin"] = _wfm(f("lru_w_in")[0])
    sh["lwout"] = _wfm(f("lru_w_out")[0])
    gw = f("lru_gate_w")[0]
    sh["lgw"] = np.ascontiguousarray(gw.reshape(2, 2, 4, 2, 128, 256).transpose(4, 0, 1, 2, 3, 5))
    sv = np.zeros((128, 64), np.float32)
    sv[:, 0:8] = _fm(f("lru_conv_b")[0], 8)
    cw = f("lru_conv_w")[0]
    sv[:, 8:40] = cw.reshape(4, 8, 128).transpose(2, 1, 0).reshape(128, 32)
    lam = f("lru_lam")[0]
    sv[:, 40:56] = lam.reshape(2, 8, 128).transpose(2, 0, 1).reshape(128, 16)
    sh["lsv"] = sv
    gb = f("lru_gate_b")[0]
    sh["lgb"] = np.ascontiguousarray(gb.reshape(2, 2, 8, 128).transpose(3, 0, 1, 2).reshape(128, 32))
    sh["c_ones1024"] = np.full((128, 128), 1.0 / 1024, np.float32)
    b = np.zeros((128, 128), np.float32)
    b[0:64, 0:64] = 1.0 / 64
    b[64:128, 64:128] = 1.0 / 64
    sh["c_blk64"] = b
    sh["c_ones384"] = np.full((128, 128), 1.0 / 384, np.float32)
    sh["c_ones256"] = np.full((128, 128), 1.0 / 256, np.float32)
    sh["c_ones96"] = np.full((128, 128), 1.0 / 96, np.float32)
    return sh


def prepare_core(inp, b):
    x = np.asarray(inp["x"][b], np.float32)
    ctx = np.asarray(inp["ctx"][b], np.float32)
    xin = np.ascontiguousarray(np.concatenate([x.T, ctx.T], axis=1))
    cond = np.stack([_fm(np.asarray(inp["c"][b]), 8), _fm(np.asarray(inp["c_ctx"]), 8)], axis=2)
    return {"xin": xin, "cond": np.ascontiguousarray(cond)}


_NC_CACHE = {}


def kernel(**inputs):
    sh = prepare_shared(inputs)
    if "nc" not in _NC_CACHE:
        _NC_CACHE["nc"] = Builder().build()
    nc = _NC_CACHE["nc"]
    in_maps = []
    for b in range(8):
        m = dict(sh)
        m.update(prepare_core(inputs, b))
        in_maps.append(m)
    res = run_bass_kernel_spmd(nc, in_maps, core_ids=list(range(8)))
    out = np.stack([np.ascontiguousarray(r["outT"].T) for r in res.results], axis=0)
    return out.astype(np.float32)
```

```python
from contextlib import ExitStack
import numpy as np
import concourse.bass as bass
import concourse.mybir as mybir
from concourse.bass_utils import run_bass_kernel_spmd

F32 = mybir.dt.float32
BF16 = mybir.dt.bfloat16
AF = mybir.ActivationFunctionType
ALU = mybir.AluOpType

D = 1024
L = 4096
CT = 256
T = L + CT
DEPTH = 4
DFF = 2816
NJ = DFF // 128
EPS = 1e-6
SHIFT = 16.0
TILES = [(i * 512, 512, False) for i in range(8)] + [(L, CT, True)]


class _Op:
    __slots__ = ("eng", "fn", "deps", "dkey", "sig", "sem", "val")

    def __init__(self, eng, fn, deps, dkey):
        self.eng = eng
        self.fn = fn
        self.deps = deps
        self.dkey = dkey
        self.sig = False
        self.sem = None
        self.val = 0


class Sched:
    EPOCH = 20000

    def __init__(self, nc, stack):
        self.nc = nc
        self.stack = stack
        self.ops = []
        self.last_w = {}
        self.readers = {}
        self.last_dkey = {}
        self.pending_bar = {}
        self.bar_idx = -1
        self.emitted = 0
        self.engs = {"pe": nc.tensor, "act": nc.scalar, "dve": nc.vector,
                     "pool": nc.gpsimd, "sp": nc.sync}
        self.eng_cnt = {e: 0 for e in self.engs}
        self.eng_sem = {}
        self.key_sem = {}
        self.sem_cnt = {}
        self.free_sems = {}
        self.waited = {e: {} for e in self.engs}
        self.nsem = 0
        self.ninst = 0

    def add(self, eng, fn, reads=(), writes=(), dkey=None):
        i = len(self.ops)
        deps = set()
        for r in reads:
            w = self.last_w.get(r)
            if w is not None:
                deps.add(w)
        for r in writes:
            w = self.last_w.get(r)
            if w is not None:
                deps.add(w)
            for rd in self.readers.get(r, ()):
                deps.add(rd)
        if dkey is not None:
            p = self.last_dkey.get(dkey)
            if p is not None:
                deps.add(p)
            self.last_dkey[dkey] = i
        deps = set(d for d in deps if d > self.bar_idx)
        if eng in self.pending_bar:
            deps |= self.pending_bar.pop(eng)
        for r in reads:
            self.readers.setdefault(r, []).append(i)
        for r in writes:
            self.last_w[r] = i
            self.readers[r] = []
        deps.discard(i)
        red = {}
        for d in deps:
            o = self.ops[d]
            src = ("k", o.dkey) if o.dkey is not None else ("e", o.eng)
            if src == ("e", "pe") and eng == "pe" and dkey is None and d > self.bar_idx:
                continue
            if src not in red or red[src] < d:
                red[src] = d
        self.ops.append(_Op(eng, fn, sorted(red.values()), dkey))
        return i

    def barrier(self):
        last = {}
        for i in range(self.bar_idx + 1, len(self.ops)):
            o = self.ops[i]
            src = ("k", o.dkey) if o.dkey is not None else ("e", o.eng)
            last[src] = i
        deps = set(last.values())
        for d in deps:
            self.ops[d].sig = True
        self.flush()
        for e in self.engs:
            self.pending_bar[e] = set(deps) | self.pending_bar.get(e, set())
        self.bar_idx = len(self.ops) - 1
        for (e, _k), sem in self.key_sem.items():
            self.free_sems.setdefault(e, []).append(sem)
        self.key_sem = {}

    def flush(self):
        nc = self.nc
        ops = self.ops
        for i in range(self.emitted, len(ops)):
            for d in ops[i].deps:
                ops[d].sig = True
        for i in range(self.emitted, len(ops)):
            o = ops[i]
            E = self.engs[o.eng]
            w = self.waited[o.eng]
            for d in o.deps:
                do = ops[d]
                sid = id(do.sem)
                if w.get(sid, 0) >= do.val:
                    continue
                E.wait_ge(do.sem, do.val)
                w[sid] = do.val
            inst = o.fn()
            o.fn = None
            self.ninst += 1
            if o.dkey is not None:
                kk = (o.eng, o.dkey)
                if kk not in self.key_sem:
                    fl = self.free_sems.setdefault(o.eng, [])
                    if fl:
                        self.key_sem[kk] = fl.pop()
                    else:
                        sem = self.stack.enter_context(nc.semaphore("k%d" % self.nsem))
                        self.nsem += 1
                        self.sem_cnt[id(sem)] = 0
                        self.key_sem[kk] = sem
                o.sem = self.key_sem[kk]
                self.sem_cnt[id(o.sem)] += 16
                o.val = self.sem_cnt[id(o.sem)]
                inst.then_inc(o.sem, 16)
            elif o.sig:
                if o.eng not in self.eng_sem or self.eng_cnt[o.eng] >= self.EPOCH:
                    self.eng_sem[o.eng] = self.stack.enter_context(nc.semaphore("e%d" % self.nsem))
                    self.nsem += 1
                    self.eng_cnt[o.eng] = 0
                self.eng_cnt[o.eng] += 1
                o.sem = self.eng_sem[o.eng]
                o.val = self.eng_cnt[o.eng]
                inst.then_inc(o.sem, 1)
        self.emitted = len(ops)

    def emit(self):
        self.flush()


class Tile:
    def __init__(self, t, key):
        self.t = t
        self.key = key

    def __getitem__(self, idx):
        return self.t[idx]


class Ring:
    def __init__(self, tiles):
        self.tiles = tiles
        self.i = 0

    def get(self):
        t = self.tiles[self.i % len(self.tiles)]
        self.i += 1
        return t


class Builder:
    def __init__(self, layers=(0, 1, 2, 3)):
        self.layers = tuple(layers)
        self.nc = bass.Bass("TRN2", target_bir_lowering=False)
        self.cnt = 0

    def sb(self, shape, dtype, name="t"):
        self.cnt += 1
        nm = "%s_%d" % (name, self.cnt)
        t = self.scope.enter_context(self.nc.sbuf_tensor(nm, list(shape), dtype))
        return Tile(t, nm)

    def ps(self, name="ps"):
        self.cnt += 1
        nm = "%s_%d" % (name, self.cnt)
        t = self.stack.enter_context(self.nc.psum_tensor(nm, [128, 512], F32))
        return Tile(t, nm)

    def ring(self, n, shape, dtype, name="r"):
        return Ring([self.sb(shape, dtype, name) for _ in range(n)])

    def din(self, name, shape):
        return self.nc.dram_tensor(name, list(shape), F32, kind="ExternalInput").ap()

    def op(self, eng, fn, r=(), w=(), dkey=None):
        return self.S.add(eng, fn, reads=[x.key if isinstance(x, Tile) else x for x in r],
                          writes=[x.key if isinstance(x, Tile) else x for x in w], dkey=dkey)

    def load(self, dst, dst_ap, src_ap, r=(), cast=False):
        nc = self.nc
        if cast:
            self.op("pool", lambda: nc.gpsimd.dma_start(out=dst_ap, in_=src_ap, max_dma_last_dim=4096),
                    r=r, w=[dst], dkey=dst.key)
        else:
            self.op("sp", lambda: nc.sync.dma_start(out=dst_ap, in_=src_ap), r=r, w=[dst], dkey=dst.key)

    def store(self, dram_key, dst_ap, src, src_ap):
        nc = self.nc
        self.op("sp", lambda: nc.sync.dma_start(out=dst_ap, in_=src_ap), r=[src], w=[dram_key], dkey=src.key)

    def mm(self, ps, out_ap, pairs, r=()):
        nc = self.nc
        n = len(pairs)
        for i, (lt, rh) in enumerate(pairs):
            self.op("pe", (lambda lt=lt, rh=rh, i=i: nc.tensor.matmul(out_ap, lt, rh, start=(i == 0), stop=(i == n - 1))),
                    r=r, w=[ps])

    def act(self, out_ap, in_ap, func, r=(), w=(), bias=None, scale=None):
        nc = self.nc
        kw = {}
        if bias is not None:
            kw["bias"] = bias
        if scale is not None:
            kw["scale"] = scale
        self.op("act", lambda: nc.scalar.activation(out=out_ap, in_=in_ap, func=func, **kw), r=r, w=w)

    def tt(self, out_ap, a, b, op, r=(), w=(), eng="dve"):
        nc = self.nc
        e = nc.vector if eng == "dve" else nc.gpsimd
        self.op(eng, lambda: e.tensor_tensor(out=out_ap, in0=a, in1=b, op=op), r=r, w=w)

    def ts(self, out_ap, a, s1, s2, op0, op1=None, r=(), w=(), eng="dve"):
        nc = self.nc
        e = nc.vector if eng == "dve" else nc.gpsimd
        if op1 is None:
            self.op(eng, lambda: e.tensor_scalar(out=out_ap, in0=a, scalar1=s1, scalar2=None, op0=op0), r=r, w=w)
        else:
            self.op(eng, lambda: e.tensor_scalar(out=out_ap, in0=a, scalar1=s1, scalar2=s2, op0=op0, op1=op1), r=r, w=w)

    def stt(self, out_ap, a, s, b, op0, op1, r=(), w=()):
        nc = self.nc
        self.op("dve", lambda: nc.vector.scalar_tensor_tensor(out=out_ap, in0=a, scalar=s, in1=b, op0=op0, op1=op1), r=r, w=w)

    def copy(self, out_ap, in_ap, r=(), w=(), eng="dve"):
        nc = self.nc
        if eng == "act":
            self.op("act", lambda: nc.scalar.copy(out=out_ap, in_=in_ap), r=r, w=w)
        else:
            e = nc.vector if eng == "dve" else nc.gpsimd
            self.op(eng, lambda: e.tensor_copy(out=out_ap, in_=in_ap), r=r, w=w)

    def memset(self, t, ap, val, eng="dve"):
        nc = self.nc
        e = nc.vector if eng == "dve" else nc.gpsimd
        self.op(eng, lambda: e.memset(ap, val), w=[t])

    def mm1(self, ps, out_ap, lhsT, rhs, start, stop, r=()):
        nc = self.nc
        self.op("pe", lambda: nc.tensor.matmul(out_ap, lhsT, rhs, start=start, stop=stop), r=r, w=[ps])

    def recip(self, out_ap, in_ap, r=(), w=()):
        nc = self.nc
        self.op("dve", lambda: nc.vector.reciprocal(out=out_ap, in_=in_ap), r=r, w=w)

    def new_scope(self):
        if self.cur_scope is not None:
            self.S.barrier()
            self.cur_scope.close()
        self.cur_scope = ExitStack()
        self.scope = self.cur_scope

    def common_rings(self, nxt=2, nh=2, nsq1=4, ntf=5):
        self.XT = self.ring(nxt, [128, 8, 512], F32, "xt")
        self.SQ = self.ring(1, [128, 8, 512], BF16, "sq")
        self.SQ1 = self.ring(nsq1, [128, 512], BF16, "sq1") if nsq1 else None
        self.RS = self.ring(2, [128, 512], F32, "rstd")
        self.TF = self.ring(ntf, [128, 512], F32, "tf")
        self.H = self.ring(nh, [128, 8, 512], BF16, "h")

    def build(self):
        nc = self.nc
        I = {}
        for k, shp in input_shapes().items():
            I[k] = self.din(k, shp)
        self.I = I
        self.out = nc.dram_tensor("outT", [D, L], F32, kind="ExternalOutput").ap()
        self.XS = nc.dram_tensor("xs", [D, T], F32).ap()
        self.AS = nc.dram_tensor("acts", [D, T], BF16).ap()
        self.XRS = nc.dram_tensor("xrs", [D, T], BF16).ap()
        self.xsrc_is_input = True
        with ExitStack() as stack:
            self.stack = stack
            self.scope = stack
            self.cur_scope = None
            self.S = Sched(nc, stack)
            self.PS = Ring([self.ps() for _ in range(6)])
            self.PA = Ring([self.ps() for _ in range(2)])
            self.setup_consts()
            for l in self.layers:
                self.layer(l)
            self.op("sp", lambda: nc.sync.nop(), r=["OUT%d" % i for i in range(8)])
            self.S.emit()
            if self.cur_scope is not None:
                self.cur_scope.close()
        return nc

    def xview(self, ap, t0, n):
        return ap.rearrange("(k p) t -> p k t", p=128)[:, :, t0:t0 + n]

    def load_x(self, ti):
        t0, n, isctx = TILES[ti]
        xt = self.XT.get()
        src = self.I["xin"] if self.xsrc_is_input else self.XS
        self.load(xt, xt[:, :, 0:n], self.xview(src, t0, n), r=["X%d" % ti])
        return xt

    def store_x(self, ti, xt, final=False):
        t0, n, isctx = TILES[ti]
        if final and not isctx:
            self.store("OUT%d" % ti, self.xview(self.out, t0, n), xt, xt[:, :, 0:n])
        else:
            self.store("X%d" % ti, self.xview(self.XS, t0, n), xt, xt[:, :, 0:n])

    def setup_consts(self):
        nc = self.nc
        I = self.I

        def cload(name, shape, dtype=BF16):
            t = self.sb(shape, dtype, name)
            self.load(t, t[:], I[name][:], cast=(dtype == BF16))
            return t
        self.ones1024 = cload("c_ones1024", [128, 128])
        self.blk64 = cload("c_blk64", [128, 128])
        self.ones384 = cload("c_ones384", [128, 128])
        self.ones256 = cload("c_ones256", [128, 128])
        self.ones96 = cload("c_ones96", [128, 128])
        self.perm64 = cload("c_perm64", [128, 128])
        self.perm96 = cload("c_perm96", [128, 128])
        self.epsT = self.sb([128, 1], F32, "eps")
        self.memset(self.epsT, self.epsT[:], EPS)
        self.nshift = self.sb([128, 1], F32, "nshift")
        self.memset(self.nshift, self.nshift[:], -SHIFT)
        self.one1 = self.sb([128, 1], F32, "one1")
        self.memset(self.one1, self.one1[:], 1.0)
        self.n1g = cload("n1g", [128, 4, 8], F32)
        self.n2g = cload("n2g", [128, 4, 8], F32)
        self.modb = cload("modb", [128, 4, 48], F32)
        self.fcw = cload("fcw", [128, 4, NJ, 3], F32)
        self.fcb = cload("fcb", [128, 4, NJ], F32)
        self.XB = self.sb([128, 8, 16], F32, "XB")
        self.MV = {l: self.sb([128, 6, 8, 2], F32, "mv") for l in self.layers}
        cond = cload("cond", [128, 8, 2], F32)
        condT = self.sb([128, 8, 2], F32, "condT")
        self.act(condT[:], cond[:], AF.Silu, r=[cond], w=[condT])
        self.condB = self.sb([128, 8, 2], BF16, "condB")
        self.copy(self.condB[:], condT[:], r=[condT], w=[self.condB])
        self.new_scope()
        self.drain(self.mv_steps(self.layers[0]))

    @staticmethod
    def step(g):
        if g is None:
            return None
        try:
            next(g)
            return g
        except StopIteration:
            return None

    def drain(self, g):
        while g is not None:
            g = self.step(g)

    def mv_steps(self, l):
        I = self.I
        wr = self.ring(2, [128, 6144], BF16, "modw")
        acc = self.sb([128, 48, 2], F32, "modacc")
        tmps = [self.sb([128, 8, 2], F32, "mtmp") for _ in range(2)]
        wts = {}
        wts[0] = wr.get()
        self.load(wts[0], wts[0][:], I["modw"][l, :, 0, :], cast=True)
        yield
        for k in range(8):
            if k + 1 < 8:
                wts[k + 1] = wr.get()
                self.load(wts[k + 1], wts[k + 1][:], I["modw"][l, :, k + 1, :], cast=True)
            wt = wts.pop(k)
            pt = self.PS.get()
            for j in range(48):
                self.mm1(pt, pt[:, 2 * j:2 * j + 2], wt[:, j * 128:(j + 1) * 128], self.condB[:, k, :], True, True,
                         r=[wt, self.condB])
            pv = pt[:, 0:96].rearrange("p (j s) -> p j s", s=2)
            if k == 0:
                self.copy(acc[:], pv, r=[pt], w=[acc])
            else:
                self.tt(acc[:], pv, acc[:], ALU.add, r=[pt, acc], w=[acc])
            yield
        mb = self.modb[:, l, :].unsqueeze(2).broadcast_to([128, 48, 2])
        self.tt(acc[:], acc[:], mb, ALU.add, r=[acc, self.modb], w=[acc])
        mv = self.MV[l]
        a4 = acc[:].rearrange("p (m k) s -> p m k s", m=6)
        for dst, srcm in ((1, 0), (2, 2), (4, 3), (5, 5)):
            self.copy(mv[:, dst], a4[:, srcm], r=[acc], w=[mv])
        for ii, (dst, srcm, g) in enumerate(((0, 1, self.n1g), (3, 4, self.n2g))):
            tmp = tmps[ii]
            self.ts(tmp[:], a4[:, srcm], 1.0, None, ALU.add, r=[acc], w=[tmp])
            gb = g[:, l, :].unsqueeze(2).broadcast_to([128, 8, 2])
            self.tt(mv[:, dst], tmp[:], gb, ALU.mult, r=[tmp, g], w=[mv])

    def modnorm(self, xt, n, l, which, s, h):
        for _ in self.modnorm_steps(xt, n, l, which, s, h):
            pass

    def modnorm_steps(self, xt, n, l, which, s, h, ps_tile=None):
        mv = self.MV[l]
        sq = self.SQ.get()
        self.act(sq[:, :, 0:n], xt[:, :, 0:n], AF.Square, r=[xt], w=[sq])
        pt = ps_tile if ps_tile is not None else self.PS.get()
        self.mm(pt, pt[:, 0:n], [(self.ones1024[:], sq[:, k, 0:n]) for k in range(8)], r=[sq, self.ones1024])
        yield
        rstd = self.RS.get()
        self.act(rstd[:, 0:n], pt[:, 0:n], AF.Ln, r=[pt, self.epsT], w=[rstd], bias=self.epsT[:, 0:1])
        self.act(rstd[:, 0:n], rstd[:, 0:n], AF.Exp, r=[rstd], w=[rstd], scale=-0.5)
        a_i, b_i = (0, 1) if which == 1 else (3, 4)
        for k in range(8):
            tmp = self.TF.get()
            self.stt(tmp[:, 0:n], xt[:, k, 0:n], mv[:, a_i, k, s:s + 1], rstd[:, 0:n], ALU.mult, ALU.mult,
                     r=[xt, mv, rstd], w=[tmp])
            self.act(h[:, k, 0:n], tmp[:, 0:n], AF.Identity, r=[tmp, mv], w=[h], bias=mv[:, b_i, k, s:s + 1])

    def headnorm(self, pt, rows, n, onesmat, gain_ap, gain_t, dst_t, dst_ap, rope=None):
        sq = self.SQ1.get()
        raw = self.TF.get()
        self.act(sq[0:rows, 0:n], pt[0:rows, 0:n], AF.Square, r=[pt], w=[sq])
        self.act(raw[0:rows, 0:n], pt[0:rows, 0:n], AF.Copy, r=[pt], w=[raw])
        pm = self.PS.get()
        self.mm(pm, pm[0:rows, 0:n], [(onesmat[0:rows, 0:rows], sq[0:rows, 0:n])], r=[sq, onesmat])
        rstd = self.RS.get()
        self.act(rstd[0:rows, 0:n], pm[0:rows, 0:n], AF.Ln, r=[pm, self.epsT], w=[rstd], bias=self.epsT[0:rows, 0:1])
        self.act(rstd[0:rows, 0:n], rstd[0:rows, 0:n], AF.Exp, r=[rstd], w=[rstd], scale=-0.5)
        if rope is None:
            self.stt(dst_ap, raw[0:rows, 0:n], gain_ap, rstd[0:rows, 0:n], ALU.mult, ALU.mult,
                     r=[raw, rstd, gain_t], w=[dst_t])
            return
        if len(rope) == 5:
            cs, sn, permT, r0, r1 = rope
            cs_ap, sn_ap = cs[:, 0:n], sn[:, 0:n]
        else:
            cs, sn, permT, r0, r1, cs_ap, sn_ap = rope
        qn = self.SQ1.get()
        self.stt(qn[0:rows, 0:n], raw[0:rows, 0:n], gain_ap, rstd[0:rows, 0:n], ALU.mult, ALU.mult,
                 r=[raw, rstd, gain_t], w=[qn])
        pw = self.PS.get()
        self.mm(pw, pw[0:rows, 0:n], [(permT[0:rows, 0:rows], qn[0:rows, 0:n])], r=[qn, permT])
        t1 = self.TF.get()
        t2 = self.TF.get()
        self.tt(t1[r0:r1, 0:n], qn[r0:r1, 0:n], cs_ap[r0:r1], ALU.mult, r=[qn, cs], w=[t1])
        self.tt(t2[r0:r1, 0:n], pw[r0:r1, 0:n], sn_ap[r0:r1], ALU.mult, r=[pw, sn], w=[t2])
        if r0 > 0:
            self.copy(dst_ap[0:r0], qn[0:r0, 0:n], r=[qn], w=[dst_t], eng="pool")
        self.tt(dst_ap[r0:r1], t1[r0:r1, 0:n], t2[r0:r1, 0:n], ALU.add, r=[t1, t2], w=[dst_t])

    def load_rope(self, csname, snname, ti):
        t0, n, isctx = TILES[ti]
        cs = self.ROPE.get()
        sn = self.ROPE.get()
        self.load(cs, cs[:, 0:n], self.I[csname][:, t0:t0 + n])
        self.load(sn, sn[:, 0:n], self.I[snname][:, t0:t0 + n])
        return cs, sn

    def resid(self, l, ti, xt, act_t, act_fn, wo):
        mv = self.MV[l]
        t0, n, isctx = TILES[ti]
        s = 1 if isctx else 0
        for oc in range(8):
            pt = self.PS.get()
            self.mm(pt, pt[:, 0:n], [(wo[:, k, oc * 128:(oc + 1) * 128], act_fn(k)) for k in range(8)], r=[act_t, wo])
            self.stt(xt[:, oc, 0:n], pt[:, 0:n], mv[:, 2, oc, s:s + 1], xt[:, oc, 0:n], ALU.mult, ALU.add,
                     r=[pt, mv, xt], w=[xt])
        if not isctx:
            self.copy(self.XB[:, :, 2 * ti:2 * ti + 1], xt[:, :, 0:1], r=[xt], w=[self.XB], eng="pool")
            self.copy(self.XB[:, :, 2 * ti + 1:2 * ti + 2], xt[:, :, n - 1:n], r=[xt], w=[self.XB], eng="pool")
        self.store_x(ti, xt)

    def phase_c_dram(self, l, wo_name_ap, tiles):
        self.new_scope()
        self.XT = self.ring(2, [128, 8, 512], F32, "xt")
        AT = self.ring(2, [128, 8, 512], BF16, "at")
        wo = self.sb([128, 8, 1024], BF16, "wo")
        for k in range(8):
            self.load(wo, wo[:, k, :], wo_name_ap[:, k, :], cast=True)
        for ti in tiles:
            t0, n, isctx = TILES[ti]
            xt = self.load_x(ti)
            at = AT.get()
            self.load(at, at[:, :, 0:n], self.xview(self.AS, t0, n), r=["AS%d" % ti])
            self.resid(l, ti, xt, at, (lambda k, at=at, n=n: at[:, k, 0:n]), wo)

    def layer(self, l):
        kind, idx = l % 3, l // 3
        last = (l == DEPTH - 1)
        final = (l == self.layers[-1])
        tiles = list(range(8)) if last else list(range(9))
        if kind == 0:
            self.swa(l, idx, tiles)
        elif kind == 1:
            self.mla(l, idx, tiles)
        else:
            self.lru(l, idx, tiles)
        self.xsrc_is_input = False
        self.ffn(l, tiles, final)

    def ffn(self, l, tiles, final):
        I = self.I
        mv = self.MV[l]
        self.new_scope()
        self.common_rings()
        WG = self.ring(5, [128, 8, 128], BF16, "wg")
        WV = self.ring(5, [128, 8, 128], BF16, "wv")
        WD = self.ring(1, [128, NJ, 1024], BF16, "wd")
        HID = self.ring(1, [128, NJ, 512], BF16, "hid")
        GB = self.sb([128, NJ, 16], F32, "gb")
        GT = self.ring(2, [128, 514], F32, "gt")
        HB = self.sb([128, 8, 16], BF16, "hb")
        self.modnorm(self.XB, 16, l, 2, 0, HB)
        li = self.layers.index(l)
        mv_gen = self.mv_steps(self.layers[li + 1]) if li + 1 < len(self.layers) else None
        prepped = {}

        def prep_steps(ti_):
            t0_, n_, c_ = TILES[ti_]
            xt_ = self.load_x(ti_)
            h_ = self.H.get()
            prepped[ti_] = (xt_, h_)
            yield from self.modnorm_steps(xt_, n_, l, 2, 1 if c_ else 0, h_, ps_tile=self.PA.tiles[0])
        self.drain(prep_steps(tiles[0]))
        for idx_t, ti in enumerate(tiles):
            t0, n, isctx = TILES[ti]
            s = 1 if isctx else 0
            xt, h = prepped.pop(ti)
            nxt_gen = prep_steps(tiles[idx_t + 1]) if idx_t + 1 < len(tiles) else None
            hid = HID.get()
            wd = WD.get()
            for j in range(NJ):
                wg = WG.get()
                wv = WV.get()
                self.load(wg, wg[:], I["wug"][l, j].rearrange("p (k m) -> p k m", k=8), cast=True)
                self.load(wv, wv[:], I["wuv"][l, j].rearrange("p (k m) -> p k m", k=8), cast=True)
                self.load(wd, wd[:, j, :], I["wdn"][l, j], cast=True)
                if idx_t == 0:
                    pb = self.PS.get()
                    self.mm(pb, pb[:, 0:16], [(wg[:, k, :], HB[:, k, :]) for k in range(8)], r=[wg, HB])
                    self.copy(GB[:, j, :], pb[:, 0:16], r=[pb], w=[GB])
                pg = self.PS.get()
                self.mm(pg, pg[:, 0:n], [(wg[:, k, :], h[:, k, 0:n]) for k in range(8)], r=[wg, h])
                pv = self.PS.get()
                self.mm(pv, pv[:, 0:n], [(wv[:, k, :], h[:, k, 0:n]) for k in range(8)], r=[wv, h])
                gt = GT.get()
                self.act(gt[:, 1:n + 1], pg[:, 0:n], AF.Copy, r=[pg], w=[gt])
                if (not isctx) and ti > 0:
                    self.copy(gt[:, 0:1], GB[:, j, 2 * (ti - 1) + 1:2 * (ti - 1) + 2], r=[GB], w=[gt], eng="pool")
                else:
                    self.memset(gt, gt[:, 0:1], 0.0, eng="pool")
                if (not isctx) and ti < 7:
                    self.copy(gt[:, n + 1:n + 2], GB[:, j, 2 * (ti + 1):2 * (ti + 1) + 1], r=[GB], w=[gt], eng="pool")
                else:
                    self.memset(gt, gt[:, n + 1:n + 2], 0.0, eng="pool")
                c1 = self.TF.get()
                self.ts(c1[:, 0:n], gt[:, 0:n], self.fcw[:, l, j, 0:1], self.fcb[:, l, j:j + 1], ALU.mult, ALU.add,
                        r=[gt, self.fcw, self.fcb], w=[c1])
                self.stt(c1[:, 0:n], gt[:, 1:n + 1], self.fcw[:, l, j, 1:2], c1[:, 0:n], ALU.mult, ALU.add,
                         r=[gt, c1, self.fcw], w=[c1])
                self.stt(c1[:, 0:n], gt[:, 2:n + 2], self.fcw[:, l, j, 2:3], c1[:, 0:n], ALU.mult, ALU.add,
                         r=[gt, c1, self.fcw], w=[c1])
                sl = self.TF.get()
                self.act(sl[:, 0:n], c1[:, 0:n], AF.Silu, r=[c1], w=[sl])
                self.tt(hid[:, j, 0:n], sl[:, 0:n], pv[:, 0:n], ALU.mult, r=[sl, pv], w=[hid])
                if j in (8, 13):
                    nxt_gen = self.step(nxt_gen)
                if j in (4, 17):
                    mv_gen = self.step(mv_gen)
            self.drain(nxt_gen)
            for oc in range(8):
                pt = self.PS.get()
                self.mm(pt, pt[:, 0:n], [(wd[:, j, oc * 128:(oc + 1) * 128], hid[:, j, 0:n]) for j in range(NJ)],
                        r=[wd, hid])
                self.stt(xt[:, oc, 0:n], pt[:, 0:n], mv[:, 5, oc, s:s + 1], xt[:, oc, 0:n], ALU.mult, ALU.add,
                         r=[pt, mv, xt], w=[xt])
            self.store_x(ti, xt, final=final)
        self.drain(mv_gen)

    def swa(self, l, idx, tiles):
        nc = self.nc
        I = self.I
        self.new_scope()
        self.common_rings(nxt=1, nh=1, nsq1=0, ntf=4)
        self.ROPE = self.ring(4, [128, 512], F32, "rope")
        wq = self.sb([128, 8, 1024], BF16, "wq")
        wk = self.sb([128, 8, 256], BF16, "wk")
        wv = self.sb([128, 8, 256], BF16, "wv")
        wo = self.sb([128, 8, 1024], BF16, "wo")
        for k in range(8):
            self.load(wq, wq[:, k, :], I["swq"][idx, :, k, :], cast=True)
            self.load(wo, wo[:, k, :], I["swo"][idx, :, k, :], cast=True)
        self.load(wk, wk[:], I["swk"][idx], cast=True)
        self.load(wv, wv[:], I["swv"][idx], cast=True)
        gq = self.sb([128, 1], F32, "gq")
        gk = self.sb([128, 1], F32, "gk")
        self.load(gq, gq[:], I["sqg"][idx])
        self.load(gk, gk[:], I["skg"][idx])
        es = self.sb([128, 16], F32, "es")
        self.load(es, es[:], I["ssink"][idx])
        self.act(es[:], es[:], AF.Exp, r=[es, self.nshift], w=[es], bias=self.nshift[:, 0:1])
        KT = self.sb([128, 2, T], BF16, "KT")
        VA = self.sb([128, 34, 4, 128], BF16, "VA")
        self.memset(VA, VA[:, :, :, 64:128], 1.0, eng="pool")
        QT = self.sb([128, 8, 512], BF16, "QT")
        OT = self.sb([128, 8, 512], BF16, "OT")
        PT = self.ring(4, [128, 512], BF16, "pt")
        DEN = self.ring(2, [128, 512], F32, "den")
        NG = 4
        gSQ = [self.sb([128, 512], BF16, "gsq") for _ in range(NG)]
        gRAW = [self.sb([128, 512], F32, "graw") for _ in range(NG)]
        gQN = [self.sb([128, 512], BF16, "gqn") for _ in range(NG)]
        gRS = [self.sb([128, 512], F32, "grs") for _ in range(NG)]
        PSS = Ring(self.PS.tiles[0:3])
        PSG = Ring(self.PS.tiles[3:6])

        def hn_steps(slot, w_t, c, h, n, gain, dst_t, dst_ap, rope):
            pt = PSG.get()
            self.mm(pt, pt[:, 0:n], [(w_t[:, k, c * 128:(c + 1) * 128], h[:, k, 0:n]) for k in range(8)], r=[w_t, h])
            sq, raw, qn, rstd = gSQ[slot], gRAW[slot], gQN[slot], gRS[slot]
            self.act(sq[:, 0:n], pt[:, 0:n], AF.Square, r=[pt], w=[sq])
            self.act(raw[:, 0:n], pt[:, 0:n], AF.Copy, r=[pt], w=[raw])
            yield
            pm = PSG.get()
            self.mm(pm, pm[:, 0:n], [(self.blk64[:], sq[:, 0:n])], r=[sq, self.blk64])
            self.act(rstd[:, 0:n], pm[:, 0:n], AF.Ln, r=[pm, self.epsT], w=[rstd], bias=self.epsT[:, 0:1])
            self.act(rstd[:, 0:n], rstd[:, 0:n], AF.Exp, r=[rstd], w=[rstd], scale=-0.5)
            if rope is None:
                self.stt(dst_ap, raw[:, 0:n], gain[:, 0:1], rstd[:, 0:n], ALU.mult, ALU.mult, r=[raw, rstd, gain], w=[dst_t])
                return
            cs, sn = rope
            self.stt(qn[:, 0:n], raw[:, 0:n], gain[:, 0:1], rstd[:, 0:n], ALU.mult, ALU.mult, r=[raw, rstd, gain], w=[qn])
            yield
            pw = PSG.get()
            self.mm(pw, pw[:, 0:n], [(self.perm64[:], qn[:, 0:n])], r=[qn, self.perm64])
            t1 = self.TF.get()
            t2 = self.TF.get()
            self.tt(t1[:, 0:n], qn[:, 0:n], cs[:, 0:n], ALU.mult, r=[qn, cs], w=[t1])
            self.tt(t2[:, 0:n], pw[:, 0:n], sn[:, 0:n], ALU.mult, r=[pw, sn], w=[t2])
            self.tt(dst_ap, t1[:, 0:n], t2[:, 0:n], ALU.add, r=[t1, t2], w=[dst_t])

        def run_group(gens):
            gens = list(gens)
            while gens:
                nxt = []
                for g_ in gens:
                    try:
                        next(g_)
                        nxt.append(g_)
                    except StopIteration:
                        pass
                gens = nxt

        for ti in range(9):
            t0, n, isctx = TILES[ti]
            s = 1 if isctx else 0
            xt = self.load_x(ti)
            h = self.H.get()
            self.modnorm(xt, n, l, 1, s, h)
            rope = None
            if not isctx:
                rope = self.load_rope("rcs", "rsn", ti)
            run_group([hn_steps(c, wk, c, h, n, gk, KT, KT[:, c, t0:t0 + n], rope) for c in range(2)])
            for tb in range(n // 128):
                pt = PSG.get()
                self.mm(pt, pt[:, 0:256], [(h[:, k, tb * 128:(tb + 1) * 128], wv[:, k, :]) for k in range(8)], r=[wv, h])
                blk = (t0 // 128) + tb
                self.copy(VA[:, blk, :, 0:64], pt[:, 0:256].rearrange("p (g d) -> p g d", g=4), r=[pt], w=[VA], eng="act")
        LOOK = 2
        for ti in tiles:
            t0, n, isctx = TILES[ti]
            s = 1 if isctx else 0
            xt = self.load_x(ti)
            h = self.H.get()
            self.modnorm(xt, n, l, 1, s, h)
            rope = None
            if not isctx:
                rope = self.load_rope("rcs", "rsn", ti)
            for c0_ in (0, 4):
                run_group([hn_steps(j, wq, c0_ + j, h, n, gq, QT, QT[:, c0_ + j, 0:n], rope) for j in range(NG)])
            flat = []
            for qb in range(n // 128):
                QB = t0 // 128 + qb
                if isctx:
                    kbs = [(32, 0), (33, 0)]
                else:
                    kbs = []
                    if QB > 0:
                        kbs.append((QB - 1, 1))
                    kbs.append((QB, 0))
                    if QB < 31:
                        kbs.append((QB + 1, 2))
                    kbs += [(32, 0), (33, 0)]
                for g in range(4):
                    for ki, (kb, mk) in enumerate(kbs):
                        flat.append((qb, g, ki, len(kbs), kb, mk))

            def issue_s(item):
                qb, g, ki, nk, kb, mk = item
                base = 0 if g < 2 else 64
                c0 = 4 * (g % 2)
                kc = g % 2
                ps_ = PSS.get()
                rhs = QT[base:base + 64, c0:c0 + 4, qb * 128:(qb + 1) * 128]
                self.mm1(ps_, ps_[:], KT[base:base + 64, kc, kb * 128:(kb + 1) * 128], rhs, True, True, r=[KT, QT])
                return ps_
            pend = {}
            for i in range(min(LOOK, len(flat))):
                pend[i] = issue_s(flat[i])
            po = None
            for i, item in enumerate(flat):
                qb, g, ki, nk, kb, mk = item
                base = 0 if g < 2 else 64
                c0 = 4 * (g % 2)
                if ki == 0:
                    po = self.PA.get()
                ps_ = pend.pop(i)
                p = PT.get()
                self.act(p[:], ps_[:], AF.Exp, r=[ps_, self.nshift], w=[p], bias=self.nshift[:, 0:1], scale=0.125)
                if i + LOOK < len(flat):
                    pend[i + LOOK] = issue_s(flat[i + LOOK])
                if mk:
                    cm, st = (1, -1) if mk == 1 else (-1, 1)
                    self.op("pool", (lambda p=p, cm=cm, st=st: nc.gpsimd.affine_select(
                        out=p[:].rearrange("p (a b) -> p a b", a=4), in_=p[:].rearrange("p (a b) -> p a b", a=4),
                        pattern=[[0, 4], [st, 128]], compare_op=ALU.is_ge, fill=0.0, base=0, channel_multiplier=cm)),
                        r=[p], w=[p])
                self.mm1(po, po[:], VA[:, kb, g, :], p[:], ki == 0, ki == nk - 1, r=[VA, p])
                if ki == nk - 1:
                    den = DEN.get()
                    esb = es[64:128, 4 * g:4 * g + 4].unsqueeze(2).broadcast_to([64, 4, 128])
                    self.tt(den[64:128, :].rearrange("p (a b) -> p a b", a=4), po[64:128, :].rearrange("p (a b) -> p a b", a=4),
                            esb, ALU.add, r=[po, es], w=[den])
                    self.act(den[64:128, :], den[64:128, :], AF.Ln, r=[den], w=[den])
                    self.act(den[64:128, :], den[64:128, :], AF.Exp, r=[den], w=[den], scale=-1.0)
                    self.tt(OT[base:base + 64, c0:c0 + 4, qb * 128:(qb + 1) * 128],
                            po[0:64, :].rearrange("p (a b) -> p a b", a=4),
                            den[64:128, :].rearrange("p (a b) -> p a b", a=4), ALU.mult, r=[po, den], w=[OT])
            self.resid(l, ti, xt, OT, (lambda k, n=n: OT[:, k, 0:n]), wo)

    def mla(self, l, idx, tiles):
        I = self.I
        self.new_scope()
        CQN = self.sb([128, 3, T], BF16, "CQN")
        CKVN = self.sb([128, 2, T], BF16, "CKVN")
        KRSQ = self.sb([128, T], BF16, "KRSQ")
        KRROT = self.sb([128, T], BF16, "KRROT")
        gv = self.sb([128, 8], F32, "mg")
        self.load(gv, gv[:], I["mgv"][:])
        persist = self.cur_scope
        self.cur_scope = None
        self.new_scope()
        self.common_rings(nxt=2, nh=1)
        self.ROPE = self.ring(4, [128, 512], F32, "rope")
        wdn = self.sb([128, 8, 640], BF16, "mdn")
        wrp = self.sb([128, 8, 96], BF16, "mrp")
        KRG = self.ring(2, [128, 512], BF16, "krg")
        for kt in KRG.tiles:
            self.memset(kt, kt[:], 0.0)
        for k in range(8):
            self.load(wdn, wdn[:, k, :], I["mdn"][:, k, :], cast=True)
        self.load(wrp, wrp[:], I["mrp"][:], cast=True)
        for ti in range(9):
            t0, n, isctx = TILES[ti]
            s = 1 if isctx else 0
            xt = self.load_x(ti)
            h = self.H.get()
            self.modnorm(xt, n, l, 1, s, h)
            for (nch, coff, gcol, onesm, dstT) in ((3, 0, 0, self.ones384, CQN), (2, 384, 3, self.ones256, CKVN)):
                raws, sqs = [], []
                for c in range(nch):
                    pt = self.PS.get()
                    self.mm(pt, pt[:, 0:n], [(wdn[:, k, coff + c * 128:coff + (c + 1) * 128], h[:, k, 0:n]) for k in range(8)],
                            r=[wdn, h])
                    sq = self.SQ1.get()
                    raw = self.TF.get()
                    self.act(sq[:, 0:n], pt[:, 0:n], AF.Square, r=[pt], w=[sq])
                    self.act(raw[:, 0:n], pt[:, 0:n], AF.Copy, r=[pt], w=[raw])
                    raws.append(raw)
                    sqs.append(sq)
                pm = self.PS.get()
                self.mm(pm, pm[:, 0:n], [(onesm[:], sq[:, 0:n]) for sq in sqs], r=sqs + [onesm])
                rstd = self.RS.get()
                self.act(rstd[:, 0:n], pm[:, 0:n], AF.Ln, r=[pm, self.epsT], w=[rstd], bias=self.epsT[:, 0:1])
                self.act(rstd[:, 0:n], rstd[:, 0:n], AF.Exp, r=[rstd], w=[rstd], scale=-0.5)
                for c in range(nch):
                    self.stt(dstT[:, c, t0:t0 + n], raws[c][:, 0:n], gv[:, gcol + c:gcol + c + 1], rstd[:, 0:n],
                             ALU.mult, ALU.mult, r=[raws[c], rstd, gv], w=[dstT])
            pk = self.PS.get()
            self.mm(pk, pk[0:96, 0:n], [(wrp[:, k, :], h[:, k, 0:n]) for k in range(8)], r=[wrp, h])
            self.act(KRSQ[64:96, t0:t0 + n], pk[64:96, 0:n], AF.Square, r=[pk], w=[KRSQ])
            krg = KRG.get()
            self.ts(krg[64:96, 0:n], pk[64:96, 0:n], gv[64:96, 6:7], None, ALU.mult, r=[pk, gv], w=[krg])
            if isctx:
                self.copy(KRROT[64:96, t0:t0 + n], krg[64:96, 0:n], r=[krg], w=[KRROT])
            else:
                cs, sn = self.load_rope("mcs", "msn", ti)
                pw = self.PS.get()
                self.mm(pw, pw[0:96, 0:n], [(self.perm96[0:96, 0:96], krg[0:96, 0:n])], r=[krg, self.perm96])
                t1 = self.TF.get()
                t2 = self.TF.get()
                self.tt(t1[64:96, 0:n], krg[64:96, 0:n], cs[64:96, 0:n], ALU.mult, r=[krg, cs], w=[t1])
                self.tt(t2[64:96, 0:n], pw[64:96, 0:n], sn[64:96, 0:n], ALU.mult, r=[pw, sn], w=[t2])
                self.tt(KRROT[64:96, t0:t0 + n], t1[64:96, 0:n], t2[64:96, 0:n], ALU.add, r=[t1, t2], w=[KRROT])
        self.new_scope()
        RALL = self.sb([128, 2, L], F32, "ropeall")
        self.load(RALL, RALL[:, 0, :], I["mcs"][:, :])
        self.load(RALL, RALL[:, 1, :], I["msn"][:, :])
        wuq = self.sb([128, 3, 1536], BF16, "muq")
        wuk = self.sb([128, 2, 1024], BF16, "muk")
        wuv = self.sb([128, 2, 1024], BF16, "muv")
        for k in range(3):
            self.load(wuq, wuq[:, k, :], I["muq"][:, k, :], cast=True)
        for k in range(2):
            self.load(wuk, wuk[:, k, :], I["muk"][:, k, :], cast=True)
            self.load(wuv, wuv[:, k, :], I["muv"][:, k, :], cast=True)
        KTH = self.ring(2, [128, T], BF16, "KTH")
        VH = self.ring(2, [128, 34, 128], BF16, "VH")
        QTH = self.ring(2, [128, 512], BF16, "QTH")
        PT = self.ring(4, [128, 512], BF16, "pt")
        DEN = self.ring(2, [128, 512], F32, "den")
        OS = self.ring(3, [128, 512], BF16, "os")
        qSQ = self.ring(2, [128, 512], BF16, "qsq")
        qTF = self.ring(3, [128, 512], F32, "qtf")
        qRS = self.ring(1, [128, 512], F32, "qrs")
        kSQ = self.ring(2, [128, 512], BF16, "ksq")
        kTF = self.ring(2, [128, 512], F32, "ktf")
        kRS = self.ring(2, [128, 512], F32, "krs")
        for v in VH.tiles:
            self.memset(v, v[:, :, 64:128], 1.0, eng="pool")
        scale = 96.0 ** -0.5
        PSS = Ring(self.PS.tiles[0:3])
        PSG = Ring(self.PS.tiles[3:6])

        def kv_steps(hd, kth, vh):
            for ti in range(9):
                t0, n, isctx = TILES[ti]
                pk = PSG.get()
                self.mm(pk, pk[0:64, 0:n], [(wuk[:, c2, hd * 64:(hd + 1) * 64], CKVN[:, c2, t0:t0 + n]) for c2 in range(2)],
                        r=[wuk, CKVN])
                sq = kSQ.get()
                raw = kTF.get()
                self.act(sq[0:64, 0:n], pk[0:64, 0:n], AF.Square, r=[pk], w=[sq])
                self.act(raw[0:64, 0:n], pk[0:64, 0:n], AF.Copy, r=[pk], w=[raw])
                self.copy(sq[64:96, 0:n], KRSQ[64:96, t0:t0 + n], r=[KRSQ], w=[sq], eng="pool")
                yield
                pm = PSG.get()
                self.mm(pm, pm[0:96, 0:n], [(self.ones96[0:96, 0:96], sq[0:96, 0:n])], r=[sq, self.ones96])
                rstd = kRS.get()
                self.act(rstd[0:96, 0:n], pm[0:96, 0:n], AF.Ln, r=[pm, self.epsT], w=[rstd], bias=self.epsT[0:96, 0:1])
                self.act(rstd[0:96, 0:n], rstd[0:96, 0:n], AF.Exp, r=[rstd], w=[rstd], scale=-0.5)
                self.stt(kth[0:64, t0:t0 + n], raw[0:64, 0:n], gv[0:64, 6:7], rstd[0:64, 0:n], ALU.mult, ALU.mult,
                         r=[raw, rstd, gv], w=[kth])
                self.tt(kth[64:96, t0:t0 + n], KRROT[64:96, t0:t0 + n], rstd[64:96, 0:n], ALU.mult,
                        r=[KRROT, rstd], w=[kth])
                for tb in range(n // 128):
                    pv = PSG.get()
                    self.mm(pv, pv[:, 0:64], [(CKVN[:, c2, t0 + tb * 128:t0 + (tb + 1) * 128], wuv[:, c2, hd * 64:(hd + 1) * 64])
                                              for c2 in range(2)], r=[wuv, CKVN])
                    self.copy(vh[:, t0 // 128 + tb, 0:64], pv[:, 0:64], r=[pv], w=[vh], eng="act")
                yield

        def q_steps(hd, ti, qth):
            t0, n, isctx = TILES[ti]
            pq = PSG.get()
            self.mm(pq, pq[0:96, 0:n], [(wuq[:, c3, hd * 96:(hd + 1) * 96], CQN[:, c3, t0:t0 + n]) for c3 in range(3)],
                    r=[wuq, CQN])
            sq = qSQ.get()
            raw = qTF.get()
            self.act(sq[0:96, 0:n], pq[0:96, 0:n], AF.Square, r=[pq], w=[sq])
            self.act(raw[0:96, 0:n], pq[0:96, 0:n], AF.Copy, r=[pq], w=[raw])
            yield
            pm = PSG.get()
            self.mm(pm, pm[0:96, 0:n], [(self.ones96[0:96, 0:96], sq[0:96, 0:n])], r=[sq, self.ones96])
            rstd = qRS.get()
            self.act(rstd[0:96, 0:n], pm[0:96, 0:n], AF.Ln, r=[pm, self.epsT], w=[rstd], bias=self.epsT[0:96, 0:1])
            self.act(rstd[0:96, 0:n], rstd[0:96, 0:n], AF.Exp, r=[rstd], w=[rstd], scale=-0.5)
            if isctx:
                self.stt(qth[0:96, 0:n], raw[0:96, 0:n], gv[0:96, 5:6], rstd[0:96, 0:n], ALU.mult, ALU.mult,
                         r=[raw, rstd, gv], w=[qth])
                return
            qn = qSQ.get()
            self.stt(qn[0:96, 0:n], raw[0:96, 0:n], gv[0:96, 5:6], rstd[0:96, 0:n], ALU.mult, ALU.mult,
                     r=[raw, rstd, gv], w=[qn])
            yield
            pw = PSG.get()
            self.mm(pw, pw[0:96, 0:n], [(self.perm96[0:96, 0:96], qn[0:96, 0:n])], r=[qn, self.perm96])
            t1 = qTF.get()
            t2 = qTF.get()
            self.tt(t1[64:96, 0:n], qn[64:96, 0:n], RALL[64:96, 0, t0:t0 + n], ALU.mult, r=[qn, RALL], w=[t1])
            self.tt(t2[64:96, 0:n], pw[64:96, 0:n], RALL[64:96, 1, t0:t0 + n], ALU.mult, r=[pw, RALL], w=[t2])
            self.copy(qth[0:64, 0:n], qn[0:64, 0:n], r=[qn], w=[qth], eng="pool")
            self.tt(qth[64:96, 0:n], t1[64:96, 0:n], t2[64:96, 0:n], ALU.add, r=[t1, t2], w=[qth])

        def step(g):
            if g is None:
                return None
            try:
                next(g)
                return g
            except StopIteration:
                return None

        def drain(g):
            while g is not None:
                g = step(g)

        units = [(hd, ti) for hd in range(16) for ti in tiles]
        kv_cur = (KTH.get(), VH.get())
        drain(kv_steps(0, kv_cur[0], kv_cur[1]))
        qth = QTH.get()
        drain(q_steps(units[0][0], units[0][1], qth))
        kv_gen = None
        kv_next = None
        LOOK = 2
        for ui, (hd, ti) in enumerate(units):
            t0, n, isctx = TILES[ti]
            if ti == tiles[0]:
                kth, vh = kv_cur
                if hd + 1 < 16:
                    kv_next = (KTH.get(), VH.get())
                    kv_gen = kv_steps(hd + 1, kv_next[0], kv_next[1])
            q_gen = None
            qth_next = None
            if ui + 1 < len(units):
                qth_next = QTH.get()
                q_gen = q_steps(units[ui + 1][0], units[ui + 1][1], qth_next)
            kbs = [32, 33] if isctx else list(range(34))
            nk = len(kbs)
            po = self.PA.get()
            pend = {}

            def issue_s(ki, kth=kth, qth=qth, n=n, kbs=kbs):
                ps_ = PSS.get()
                kb = kbs[ki]
                self.mm1(ps_, ps_[:, 0:n], kth[0:96, kb * 128:(kb + 1) * 128], qth[0:96, 0:n], True, True, r=[kth, qth])
                return ps_
            for ki in range(min(LOOK, nk)):
                pend[ki] = issue_s(ki)
            for ki in range(nk):
                ps_ = pend.pop(ki)
                p = PT.get()
                self.act(p[:, 0:n], ps_[:, 0:n], AF.Exp, r=[ps_, self.nshift], w=[p], bias=self.nshift[:, 0:1], scale=scale)
                if ki + LOOK < nk:
                    pend[ki + LOOK] = issue_s(ki + LOOK)
                self.mm1(po, po[:, 0:n], vh[:, kbs[ki], :], p[:, 0:n], ki == 0, ki == nk - 1, r=[vh, p])
                if ki % 8 == 3:
                    q_gen = step(q_gen)
                if ki % 8 == 7:
                    kv_gen = step(kv_gen)
            drain(q_gen)
            den = DEN.get()
            self.recip(den[64:128, 0:n], po[64:128, 0:n], r=[po], w=[den])
            hb = (hd % 2) * 64
            os_ = OS.get()
            self.tt(os_[hb:hb + 64, 0:n], po[0:64, 0:n], den[64:128, 0:n], ALU.mult, r=[po, den], w=[os_])
            dst = self.AS.rearrange("(k p) t -> p k t", p=128)[hb:hb + 64, hd // 2, t0:t0 + n]
            self.store("AS%d" % ti, dst, os_, os_[hb:hb + 64, 0:n])
            qth = qth_next
            if ti == tiles[-1]:
                drain(kv_gen)
                kv_gen = None
                kv_cur = kv_next
        self.S.barrier()
        self.cur_scope.close()
        persist.close()
        self.cur_scope = None
        self.phase_c_dram(l, I["mwo"], tiles)

    def lru(self, l, idx, tiles):
        nc = self.nc
        I = self.I
        PADL = 2
        CTX0 = L + 6
        XW = T + 8

        def pcol(t0):
            return t0 + PADL if t0 < L else (t0 - L) + CTX0
        self.new_scope()
        self.common_rings(nxt=2, nh=2)
        win = self.sb([128, 8, 2048], BF16, "lwin")
        for k in range(8):
            self.load(win, win[:, k, :], I["lwin"][:, k, :], cast=True)
        STG = self.ring(4, [128, 512], BF16, "stg")
        for ti in range(9):
            t0, n, isctx = TILES[ti]
            s = 1 if isctx else 0
            xt = self.load_x(ti)
            h = self.H.get()
            self.modnorm(xt, n, l, 1, s, h)
            for oc in range(16):
                pt = self.PS.get()
                self.mm(pt, pt[:, 0:n], [(win[:, k, oc * 128:(oc + 1) * 128], h[:, k, 0:n]) for k in range(8)], r=[win, h])
                stg = STG.get()
                if oc < 8:
                    self.act(stg[:, 0:n], pt[:, 0:n], AF.Gelu_apprx_tanh, r=[pt], w=[stg])
                    self.store(("AS", oc, ti), self.AS[oc * 128:(oc + 1) * 128, t0:t0 + n], stg, stg[:, 0:n])
                else:
                    self.copy(stg[:, 0:n], pt[:, 0:n], r=[pt], w=[stg])
                    self.store(("XRS", oc - 8), self.XRS[(oc - 8) * 128:(oc - 7) * 128, t0:t0 + n], stg, stg[:, 0:n])
        self.new_scope()
        self.TF = self.ring(6, [128, 512], F32, "tf")
        gw = self.sb([128, 2, 2, 4, 2, 256], BF16, "lgw")
        for d in range(2):
            for wch in range(2):
                self.load(gw, gw[:, d, wch], I["lgw"][:, d, wch], cast=True)
        sv = self.sb([128, 64], F32, "lsv")
        self.load(sv, sv[:], I["lsv"][:])
        ngb = self.sb([128, 32], F32, "lngb")
        self.load(ngb, ngb[:], I["lgb"][:])
        self.ts(ngb[:], ngb[:], -1.0, None, ALU.mult, r=[ngb], w=[ngb])
        cp = self.sb([128, 16], F32, "lcp")
        cp2 = self.sb([128, 16], F32, "lcp2")
        self.act(cp[:], sv[:, 40:56], AF.Exp, r=[sv], w=[cp], scale=-1.0)
        self.act(cp[:], cp[:], AF.Ln, r=[cp, self.one1], w=[cp], bias=self.one1[:, 0:1])
        self.ts(cp2[:], cp[:], -16.0, None, ALU.mult, r=[cp], w=[cp2])
        self.ts(cp[:], cp[:], -8.0, None, ALU.mult, r=[cp], w=[cp])
        XR = self.sb([128, 2, XW], BF16, "XR")
        self.memset(XR, XR[:, :, 0:PADL], 0.0, eng="pool")
        self.memset(XR, XR[:, :, L + PADL:CTX0], 0.0, eng="pool")
        self.memset(XR, XR[:, :, CTX0 + CT:XW], 0.0, eng="pool")
        XC = self.sb([128, 2, XW], BF16, "XC")
        SF = self.sb([128, XW], F32, "SF")
        CAR = self.ring(4, [128, 1], F32, "car")
        GT_ = self.ring(3, [128, 512], BF16, "gtile")
        zero1 = self.sb([128, 1], F32, "zero1")
        self.memset(zero1, zero1[:], 0.0)
        NW = XW - 4
        for bk in range(4):
            for cc in range(2):
                c = 2 * bk + cc
                self.load(XR, XR[:, cc, PADL:PADL + L], self.XRS[c * 128:(c + 1) * 128, 0:L], r=[("XRS", c)])
                self.load(XR, XR[:, cc, CTX0:CTX0 + CT], self.XRS[c * 128:(c + 1) * 128, L:T], r=[("XRS", c)])
            for cc in range(2):
                c = 2 * bk + cc
                acc = SF
                self.ts(acc[:, 0:NW], XR[:, cc, 0:NW], sv[:, 8 + 4 * c:9 + 4 * c], sv[:, c:c + 1], ALU.mult, ALU.add,
                        r=[XR, sv], w=[acc])
                for k in (1, 2):
                    self.stt(acc[:, 0:NW], XR[:, cc, k:k + NW], sv[:, 8 + 4 * c + k:9 + 4 * c + k], acc[:, 0:NW], ALU.mult, ALU.add,
                             r=[XR, sv, acc], w=[acc])
                self.stt(XC[:, cc, 2:2 + NW], XR[:, cc, 3:3 + NW], sv[:, 8 + 4 * c + 3:9 + 4 * c + 3], acc[:, 0:NW], ALU.mult, ALU.add,
                         r=[XR, sv, acc], w=[XC])
            for cc in range(2):
                c = 2 * bk + cc
                for d in range(2):
                    order = [8] + (list(range(8)) if d == 0 else list(range(7, -1, -1)))
                    carry = zero1
                    for ti in order:
                        t0, n, isctx = TILES[ti]
                        pc = pcol(t0)
                        pr = self.PS.get()
                        self.mm(pr, pr[:, 0:n], [(gw[:, d, 0, bk, kk, cc * 128:(cc + 1) * 128], XC[:, kk, pc:pc + n]) for kk in range(2)],
                                r=[gw, XC])
                        pi = self.PS.get()
                        self.mm(pi, pi[:, 0:n], [(gw[:, d, 1, bk, kk, cc * 128:(cc + 1) * 128], XC[:, kk, pc:pc + n]) for kk in range(2)],
                                r=[gw, XC])
                        gi = (d * 2 + 0) * 8 + c
                        gi2 = (d * 2 + 1) * 8 + c
                        ta = self.TF.get()
                        tb_ = self.TF.get()
                        tcc = self.TF.get()
                        self.act(ta[:, 0:n], pr[:, 0:n], AF.Exp, r=[pr, ngb], w=[ta], bias=ngb[:, gi:gi + 1], scale=-1.0)
                        self.act(ta[:, 0:n], ta[:, 0:n], AF.Ln, r=[ta, self.one1], w=[ta], bias=self.one1[:, 0:1])
                        self.act(ta[:, 0:n], ta[:, 0:n], AF.Exp, r=[ta], w=[ta], scale=-1.0)
                        self.act(tb_[:, 0:n], ta[:, 0:n], AF.Exp, r=[ta, cp], w=[tb_], scale=cp[:, d * 8 + c:d * 8 + c + 1])
                        self.act(ta[:, 0:n], ta[:, 0:n], AF.Exp, r=[ta, cp2], w=[ta], scale=cp2[:, d * 8 + c:d * 8 + c + 1])
                        self.ts(ta[:, 0:n], ta[:, 0:n], 0.99999994, None, ALU.min, r=[ta], w=[ta])
                        self.act(ta[:, 0:n], ta[:, 0:n], AF.Ln, r=[ta, self.one1], w=[ta], bias=self.one1[:, 0:1], scale=-1.0)
                        self.act(ta[:, 0:n], ta[:, 0:n], AF.Exp, r=[ta], w=[ta], scale=0.5)
                        self.act(tcc[:, 0:n], pi[:, 0:n], AF.Exp, r=[pi, ngb], w=[tcc], bias=ngb[:, gi2:gi2 + 1], scale=-1.0)
                        self.act(tcc[:, 0:n], tcc[:, 0:n], AF.Ln, r=[tcc, self.one1], w=[tcc], bias=self.one1[:, 0:1])
                        self.act(tcc[:, 0:n], tcc[:, 0:n], AF.Exp, r=[tcc], w=[tcc], scale=-1.0)
                        self.tt(tcc[:, 0:n], tcc[:, 0:n], XC[:, cc, pc:pc + n], ALU.mult, r=[tcc, XC], w=[tcc])
                        self.tt(tcc[:, 0:n], tcc[:, 0:n], ta[:, 0:n], ALU.mult, r=[tcc, ta], w=[tcc])
                        so = self.TF.get()
                        ncar = CAR.get()
                        if d == 0:
                            self.op("dve", (lambda so=so, tb_=tb_, tcc=tcc, n=n, carry=carry: nc.vector.tensor_tensor_scan(
                                out=so[:, 0:n], data0=tb_[:, 0:n], data1=tcc[:, 0:n], initial=carry[:, 0:1],
                                op0=ALU.mult, op1=ALU.add)), r=[tb_, tcc, carry], w=[so])
                            self.copy(ncar[:, 0:1], so[:, n - 1:n], r=[so], w=[ncar])
                            self.copy(SF[:, t0:t0 + n], so[:, 0:n], r=[so], w=[SF], eng="pool")
                        else:
                            self.op("dve", (lambda so=so, tb_=tb_, tcc=tcc, n=n, carry=carry: nc.vector.tensor_tensor_scan(
                                out=so[:, 0:n][:, ::-1], data0=tb_[:, 0:n][:, ::-1], data1=tcc[:, 0:n][:, ::-1], initial=carry[:, 0:1],
                                op0=ALU.mult, op1=ALU.add)), r=[tb_, tcc, carry], w=[so])
                            self.copy(ncar[:, 0:1], so[:, 0:1], r=[so], w=[ncar])
                            self.tt(so[:, 0:n], so[:, 0:n], SF[:, t0:t0 + n], ALU.add, r=[so, SF], w=[so])
                            gtile = GT_.get()
                            asl = self.AS[c * 128:(c + 1) * 128, t0:t0 + n]
                            self.load(gtile, gtile[:, 0:n], asl, r=[("AS", c, ti)])
                            self.tt(gtile[:, 0:n], so[:, 0:n], gtile[:, 0:n], ALU.mult, r=[so, gtile], w=[gtile])
                            self.store(("AS", c, ti), asl, gtile, gtile[:, 0:n])
                        carry = ncar
        self.phase_c_dram_multi(l, I["lwout"], tiles)

    def phase_c_dram_multi(self, l, wo_ap, tiles):
        self.new_scope()
        self.XT = self.ring(2, [128, 8, 512], F32, "xt")
        AT = self.ring(2, [128, 8, 512], BF16, "at")
        wo = self.sb([128, 8, 1024], BF16, "wo")
        for k in range(8):
            self.load(wo, wo[:, k, :], wo_ap[:, k, :], cast=True)
        for ti in tiles:
            t0, n, isctx = TILES[ti]
            xt = self.load_x(ti)
            at = AT.get()
            self.load(at, at[:, :, 0:n], self.xview(self.AS, t0, n), r=[("AS", c, ti) for c in range(8)])
            self.resid(l, ti, xt, at, (lambda k, at=at, n=n: at[:, k, 0:n]), wo)


def input_shapes():
    return {
        "xin": (D, T), "cond": (128, 8, 2), "n1g": (128, 4, 8), "n2g": (128, 4, 8),
        "modw": (4, 128, 8, 6144), "modb": (128, 4, 48),
        "wug": (4, NJ, 128, 1024), "wuv": (4, NJ, 128, 1024), "wdn": (4, NJ, 128, 1024),
        "fcw": (128, 4, NJ, 3), "fcb": (128, 4, NJ),
        "swq": (2, 128, 8, 1024), "swk": (2, 128, 8, 256), "swv": (2, 128, 8, 256), "swo": (2, 128, 8, 1024),
        "sqg": (2, 128, 1), "skg": (2, 128, 1), "ssink": (2, 128, 16),
        "rcs": (128, L), "rsn": (128, L), "mcs": (128, L), "msn": (128, L),
        "mdn": (128, 8, 640), "mrp": (128, 8, 96), "muq": (128, 3, 1536), "muk": (128, 2, 1024), "muv": (128, 2, 1024),
        "mwo": (128, 8, 1024), "mgv": (128, 8),
        "lwin": (128, 8, 2048), "lwout": (128, 8, 1024), "lgw": (128, 2, 2, 4, 2, 256), "lsv": (128, 64), "lgb": (128, 32),
        "c_ones1024": (128, 128), "c_blk64": (128, 128), "c_ones384": (128, 128), "c_ones256": (128, 128),
        "c_ones96": (128, 128), "c_perm64": (128, 128), "c_perm96": (128, 128),
    }


def _fm(v, nch):
    return np.ascontiguousarray(np.asarray(v, np.float32).reshape(nch, 128).T)


def _wfm(w):
    K, N = w.shape
    return np.ascontiguousarray(np.asarray(w, np.float32).reshape(K // 128, 128, N).transpose(1, 0, 2))


def _rope_tables(rot_dim, base_part, nrows):
    n_freq = rot_dim // 4
    half = rot_dim // 2
    t = np.arange(L)
    row = (t // 64).astype(np.float32)
    col = (t % 64).astype(np.float32)
    inv = (np.float32(10000.0) ** (-np.arange(n_freq, dtype=np.float32) / np.float32(n_freq))).astype(np.float32)
    cs = np.zeros((128, L), np.float32)
    sn = np.zeros((128, L), np.float32)
    partner = np.zeros(128, np.int64) - 1
    for p in range(nrows):
        d = p % rot_dim if base_part == 0 else p
        pp = p + base_part
        dd = d % rot_dim
        pos = row if dd < half else col
        e = dd % half
        i = e % n_freq
        ang = (pos * inv[i]).astype(np.float32)
        cs[pp] = np.cos(ang)
        sgn = -1.0 if e < n_freq else 1.0
        sn[pp] = sgn * np.sin(ang)
        partner[pp] = pp + n_freq if e < n_freq else pp - n_freq
    return cs, sn, partner


def prepare_shared(inp):
    f = lambda k: np.asarray(inp[k], np.float32)
    sh = {}
    sh["n1g"] = np.ascontiguousarray(f("norm1").reshape(4, 8, 128).transpose(2, 0, 1))
    sh["n2g"] = np.ascontiguousarray(f("norm2").reshape(4, 8, 128).transpose(2, 0, 1))
    sh["modw"] = np.ascontiguousarray(f("mod_w").reshape(4, 8, 128, 6144).transpose(0, 2, 1, 3))
    sh["modb"] = np.ascontiguousarray(f("mod_b").reshape(4, 48, 128).transpose(2, 0, 1))
    wup = f("ffn_w_up")
    def upl(w):
        return np.ascontiguousarray(w.reshape(4, 8, 128, NJ, 128).transpose(0, 3, 2, 1, 4).reshape(4, NJ, 128, 1024))
    sh["wug"] = upl(wup[:, :, :DFF])
    sh["wuv"] = upl(wup[:, :, DFF:])
    sh["wdn"] = np.ascontiguousarray(f("ffn_w_down").reshape(4, NJ, 128, 1024))
    sh["fcw"] = np.ascontiguousarray(f("ffn_conv_w").reshape(4, 3, NJ, 128).transpose(3, 0, 2, 1))
    sh["fcb"] = np.ascontiguousarray(f("ffn_conv_b").reshape(4, NJ, 128).transpose(2, 0, 1))
    wqkv = f("swa_w_qkv")
    wq = wqkv[:, :, :1024].reshape(2, 1024, 2, 8, 64).transpose(0, 1, 3, 2, 4).reshape(2, 1024, 1024)
    wk = wqkv[:, :, 1024:1280].reshape(2, 1024, 2, 2, 64).transpose(0, 1, 3, 2, 4).reshape(2, 1024, 256)
    wv = wqkv[:, :, 1280:1536]
    sh["swq"] = np.stack([_wfm(wq[i]) for i in range(2)])
    sh["swk"] = np.stack([_wfm(wk[i]) for i in range(2)])
    sh["swv"] = np.stack([_wfm(wv[i]) for i in range(2)])
    wo = f("swa_w_o").reshape(2, 2, 8, 64, 1024).transpose(0, 2, 1, 3, 4).reshape(2, 1024, 1024)
    sh["swo"] = np.stack([_wfm(wo[i]) for i in range(2)])
    sh["sqg"] = np.ascontiguousarray(np.tile(f("swa_q_gain"), (1, 2)).reshape(2, 128, 1))
    sh["skg"] = np.ascontiguousarray(np.tile(f("swa_k_gain"), (1, 2)).reshape(2, 128, 1))
    sh["ssink"] = np.ascontiguousarray(np.broadcast_to(f("swa_sink")[:, None, :], (2, 128, 16)))
    cs, sn, partner = _rope_tables(64, 0, 128)
    sh["rcs"], sh["rsn"] = cs, sn
    pm = np.zeros((128, 128), np.float32)
    for m in range(128):
        pm[partner[m], m] = 1.0
    sh["c_perm64"] = pm
    cs, sn, partner = _rope_tables(32, 64, 32)
    sh["mcs"], sh["msn"] = cs, sn
    pm = np.zeros((128, 128), np.float32)
    for m in range(64, 96):
        pm[partner[m], m] = 1.0
    sh["c_perm96"] = pm
    wd = f("mla_w_down")[0]
    sh["mdn"] = _wfm(wd[:, :640])
    wr = np.zeros((1024, 96), np.float32)
    wr[:, 64:96] = wd[:, 640:672]
    sh["mrp"] = _wfm(wr)
    sh["muq"] = _wfm(f("mla_w_uq")[0])
    sh["muk"] = _wfm(f("mla_w_uk")[0])
    sh["muv"] = _wfm(f("mla_w_uv")[0])
    sh["mwo"] = _wfm(f("mla_w_o")[0])
    gv = np.zeros((128, 8), np.float32)
    gv[:, 0:3] = _fm(f("mla_q_lora_gain")[0], 3)
    gv[:, 3:5] = _fm(f("mla_kv_lora_gain")[0], 2)
    gv[0:96, 5] = f("mla_q_gain")[0]
    gv[0:96, 6] = f("mla_k_gain")[0]
    sh["mgv"] = gv
    sh["lwin"] = _wfm(f("lru_w_in")[0])
    sh["lwout"] = _wfm(f("lru_w_out")[0])
    gw = f("lru_gate_w")[0]
    sh["lgw"] = np.ascontiguousarray(gw.reshape(2, 2, 4, 2, 128, 256).transpose(4, 0, 1, 2, 3, 5))
    sv = np.zeros((128, 64), np.float32)
    sv[:, 0:8] = _fm(f("lru_conv_b")[0], 8)
    cw = f("lru_conv_w")[0]
    sv[:, 8:40] = cw.reshape(4, 8, 128).transpose(2, 1, 0).reshape(128, 32)
    lam = f("lru_lam")[0]
    sv[:, 40:56] = lam.reshape(2, 8, 128).transpose(2, 0, 1).reshape(128, 16)
    sh["lsv"] = sv
    gb = f("lru_gate_b")[0]
    sh["lgb"] = np.ascontiguousarray(gb.reshape(2, 2, 8, 128).transpose(3, 0, 1, 2).reshape(128, 32))
    sh["c_ones1024"] = np.full((128, 128), 1.0 / 1024, np.float32)
    b = np.zeros((128, 128), np.float32)
    b[0:64, 0:64] = 1.0 / 64
    b[64:128, 64:128] = 1.0 / 64
    sh["c_blk64"] = b
    sh["c_ones384"] = np.full((128, 128), 1.0 / 384, np.float32)
    sh["c_ones256"] = np.full((128, 128), 1.0 / 256, np.float32)
    sh["c_ones96"] = np.full((128, 128), 1.0 / 96, np.float32)
    return sh


def prepare_core(inp, b):
    x = np.asarray(inp["x"][b], np.float32)
    ctx = np.asarray(inp["ctx"][b], np.float32)
    xin = np.ascontiguousarray(np.concatenate([x.T, ctx.T], axis=1))
    cond = np.stack([_fm(np.asarray(inp["c"][b]), 8), _fm(np.asarray(inp["c_ctx"]), 8)], axis=2)
    return {"xin": xin, "cond": np.ascontiguousarray(cond)}


_NC_CACHE = {}


def kernel(**inputs):
    sh = prepare_shared(inputs)
    if "nc" not in _NC_CACHE:
        _NC_CACHE["nc"] = Builder().build()
    nc = _NC_CACHE["nc"]
    in_maps = []
    for b in range(8):
        m = dict(sh)
        m.update(prepare_core(inputs, b))
        in_maps.append(m)
    res = run_bass_kernel_spmd(nc, in_maps, core_ids=list(range(8)))
    out = np.stack([np.ascontiguousarray(r["outT"].T) for r in res.results], axis=0)
    return out.astype(np.float32)
```
